# Optimizing a Trainium2 kernel written in Bass

```python
import jax, jax.numpy as jnp
from jax import lax
import numpy as np

D_MODEL = 2048
BATCH = 2
SEQ = 8192
DEPTH = 2

N_A_LAYERS = DEPTH // 2
N_B_LAYERS = DEPTH - N_A_LAYERS
PLE_DIM = 256
EPS = 1e-6

M_HEADS = 4
M_QK_DIM = D_MODEL // 2 // M_HEADS
M_V_DIM = D_MODEL // M_HEADS
M_CHUNK = 64
GATE_CAP = 15.0
M_IN_COLS = 2 * M_HEADS * M_QK_DIM + 2 * M_HEADS * M_V_DIM + 2 * M_HEADS

A_HEAD_DIM = 64
A_Q_HEADS = D_MODEL // A_HEAD_DIM
A_KV_HEADS = A_Q_HEADS // 8
A_GROUP = A_Q_HEADS // A_KV_HEADS
WINDOW = 128
A_BLOCK = WINDOW

P_HEADS = 8
P_NKEYS = 128
P_EXPERTS = P_NKEYS * P_NKEYS
P_QDIM = 256
P_HALF = P_QDIM // 2
P_TOPK = 16
P_TOKEN_CHUNK = 128

kernel_name = "yoco_mlstm_swa_sink_peer_ple"


def rms_norm(x, w):
    xf = x.astype(jnp.float32)
    y = xf * lax.rsqrt(jnp.mean(xf * xf, axis=-1, keepdims=True) + EPS)
    return (y * w.astype(jnp.float32)).astype(x.dtype)


def soft_cap(t):
    return GATE_CAP * jnp.tanh(t / GATE_CAP)


def mlstm_mixer(xn, w_in, gate_bias, head_norm, w_out):
    B, S, _ = xn.shape
    H, DK, DV, L = M_HEADS, M_QK_DIM, M_V_DIM, M_CHUNK
    NC = S // L
    proj = (xn @ w_in).astype(jnp.float32)
    o0 = H * DK
    o1 = 2 * H * DK
    o2 = o1 + H * DV
    o3 = o2 + H * DV
    o4 = o3 + H
    q = proj[..., :o0]
    k = proj[..., o0:o1]
    v = proj[..., o1:o2]
    og = proj[..., o2:o3]
    gb = gate_bias.astype(jnp.float32)
    log_i = soft_cap(proj[..., o3:o4] + gb[0])
    log_f = jax.nn.log_sigmoid(soft_cap(proj[..., o4:] + gb[1]))

    def to_chunks(t, d):
        return t.reshape(B, NC, L, H, d).transpose(1, 0, 3, 2, 4)

    def gate_chunks(t):
        return t.reshape(B, NC, L, H).transpose(1, 0, 3, 2)

    qc = to_chunks(q, DK) * (DK ** -0.5)
    kc = to_chunks(k, DK)
    vc = to_chunks(v, DV)
    gi = gate_chunks(log_i)
    gf = gate_chunks(log_f)
    causal = jnp.tril(jnp.ones((L, L), dtype=bool))

    def step(carry, inp):
        C, n, m = carry
        qb, kb, vb, li, lf = inp
        b = jnp.cumsum(lf, axis=-1)
        dmat = b[..., :, None] - b[..., None, :] + li[..., None, :]
        dmat = jnp.where(causal, dmat, -jnp.inf)
        inter = b + m[..., None]
        m_t = jnp.maximum(inter, jnp.max(dmat, axis=-1))
        w_intra = jnp.exp(dmat - m_t[..., None])
        w_inter = jnp.exp(inter - m_t)
        s = jnp.einsum('bhld,bhsd->bhls', qb, kb) * w_intra
        num = (w_inter[..., None] * jnp.einsum('bhld,bhde->bhle', qb, C)
               + jnp.einsum('bhls,bhse->bhle', s, vb))
        den = w_inter * jnp.einsum('bhld,bhd->bhl', qb, n) + jnp.sum(s, axis=-1)
        h = num / jnp.maximum(jnp.abs(den), jnp.exp(-m_t))[..., None]
        m_new = m_t[..., -1]
        w_state = jnp.exp(b[..., -1:] - b + li - m_new[..., None])
        decay = jnp.exp(b[..., -1] + m - m_new)
        C_new = decay[..., None, None] * C + jnp.einsum('bhs,bhsd,bhse->bhde', w_state, kb, vb)
        n_new = decay[..., None] * n + jnp.einsum('bhs,bhsd->bhd', w_state, kb)
        return (C_new, n_new, m_new), h

    init = (jnp.zeros((B, H, DK, DV), jnp.float32),
            jnp.zeros((B, H, DK), jnp.float32),
            jnp.zeros((B, H), jnp.float32))
    _, hs = lax.scan(step, init, (qc, kc, vc, gi, gf))
    h = hs.transpose(1, 0, 3, 2, 4).reshape(B, S, H, DV)
    h = h * lax.rsqrt(jnp.mean(h * h, axis=-1, keepdims=True) + EPS)
    h = h * head_norm.astype(jnp.float32).reshape(H, DV)
    h = h.reshape(B, S, H * DV) * jax.nn.sigmoid(og)
    return h.astype(xn.dtype) @ w_out


def swa_sink_mixer(xn, k, v, w_q, sinks, w_out):
    B, S, _ = xn.shape
    NBLK = S // A_BLOCK
    q = (xn @ w_q).reshape(B, NBLK, A_BLOCK, A_KV_HEADS, A_GROUP, A_HEAD_DIM)
    pad = ((0, 0), (A_BLOCK, 0), (0, 0), (0, 0))
    kb = jnp.pad(k, pad).reshape(B, NBLK + 1, A_BLOCK, A_KV_HEADS, A_HEAD_DIM)
    vb = jnp.pad(v, pad).reshape(B, NBLK + 1, A_BLOCK, A_KV_HEADS, A_HEAD_DIM)
    kw = jnp.concatenate([kb[:, :-1], kb[:, 1:]], axis=2)
    vw = jnp.concatenate([vb[:, :-1], vb[:, 1:]], axis=2)
    s = jnp.einsum('bnqkgd,bnskd->bnkgqs', q, kw).astype(jnp.float32) * (A_HEAD_DIM ** -0.5)
    qpos = jnp.arange(A_BLOCK)[:, None] + A_BLOCK
    kpos = jnp.arange(2 * A_BLOCK)[None, :]
    dist = qpos - kpos
    band = (dist >= 0) & (dist < WINDOW)
    valid = (jnp.arange(NBLK)[:, None, None] * A_BLOCK + kpos[None] - A_BLOCK) >= 0
    mask = band[None] & valid
    s = jnp.where(mask[None, :, None, None], s, -jnp.inf)
    sink = sinks.astype(jnp.float32).reshape(A_KV_HEADS, A_GROUP)[None, None, :, :, None, None]
    mx = jnp.maximum(jnp.max(s, axis=-1, keepdims=True), sink)
    pr = jnp.exp(s - mx)
    pr = pr / (jnp.sum(pr, axis=-1, keepdims=True) + jnp.exp(sink - mx))
    o = jnp.einsum('bnkgqs,bnskd->bnqkgd', pr.astype(vw.dtype), vw)
    return o.reshape(B, S, A_Q_HEADS * A_HEAD_DIM) @ w_out


def peer_mixer(xn, w_q, k1, k2, u, v):
    B, S, D = xn.shape
    xt = xn.reshape((B * S) // P_TOKEN_CHUNK, P_TOKEN_CHUNK, D)

    def chunk(xc):
        q = (xc @ w_q).reshape(-1, P_HEADS, 2, P_HALF)
        s1 = jnp.einsum('chd,hnd->chn', q[:, :, 0], k1).astype(jnp.float32)
        s2 = jnp.einsum('chd,hnd->chn', q[:, :, 1], k2).astype(jnp.float32)
        v1, i1 = lax.top_k(s1, P_TOPK)
        v2, i2 = lax.top_k(s2, P_TOPK)
        cand = (v1[..., :, None] + v2[..., None, :]).reshape(-1, P_HEADS, P_TOPK * P_TOPK)
        cidx = (i1[..., :, None] * P_NKEYS + i2[..., None, :]).reshape(-1, P_HEADS, P_TOPK * P_TOPK)
        top_s, pos = lax.top_k(cand, P_TOPK)
        eidx = jnp.take_along_axis(cidx, pos, axis=-1)
        g = jax.nn.softmax(top_s, axis=-1)
        ue = u[eidx]
        act = jax.nn.gelu(jnp.einsum('cd,chkd->chk', xc, ue).astype(jnp.float32), approximate=False)
        coef = (g * act).astype(xc.dtype)
        ve = v[eidx]
        return jnp.einsum('chk,chkd->cd', coef, ve)

    y = lax.map(chunk, xt)
    return y.reshape(B, S, D)


def setup_inputs(seed: int = 0) -> dict:
    key = jax.random.key(seed)
    ks = jax.random.split(key, 32)
    f32 = jnp.float32
    nrm = lambda k, shape, scale: jax.random.normal(k, shape, f32) * scale
    gain = lambda k, shape: 1.0 + 0.02 * jax.random.normal(k, shape, f32)
    gate_noise = nrm(ks[4], (N_A_LAYERS, 2, M_HEADS), 0.3)
    gate_bias = gate_noise + jnp.array([-1.0, 3.0], f32)[None, :, None]
    return {
        "x": nrm(ks[0], (BATCH, SEQ, D_MODEL), 1.0),
        "p": nrm(ks[1], (DEPTH, BATCH, SEQ, PLE_DIM), 1.0),
        "a_norm": gain(ks[2], (N_A_LAYERS, D_MODEL)),
        "a_w_in": nrm(ks[3], (N_A_LAYERS, D_MODEL, M_IN_COLS), D_MODEL ** -0.5),
        "a_gate_bias": gate_bias,
        "a_head_norm": gain(ks[5], (N_A_LAYERS, M_HEADS * M_V_DIM)),
        "a_w_out": nrm(ks[6], (N_A_LAYERS, M_HEADS * M_V_DIM, D_MODEL), (M_HEADS * M_V_DIM) ** -0.5),
        "kv_norm": gain(ks[7], (D_MODEL,)),
        "w_kv": nrm(ks[8], (D_MODEL, 2 * A_KV_HEADS * A_HEAD_DIM), D_MODEL ** -0.5),
        "b_norm": gain(ks[9], (N_B_LAYERS, D_MODEL)),
        "b_w_q": nrm(ks[10], (N_B_LAYERS, D_MODEL, A_Q_HEADS * A_HEAD_DIM), D_MODEL ** -0.5),
        "b_sinks": nrm(ks[11], (N_B_LAYERS, A_Q_HEADS), 0.5),
        "b_w_out": nrm(ks[12], (N_B_LAYERS, A_Q_HEADS * A_HEAD_DIM, D_MODEL), (A_Q_HEADS * A_HEAD_DIM) ** -0.5),
        "c_norm": gain(ks[13], (DEPTH, D_MODEL)),
        "peer_w_q": nrm(ks[14], (DEPTH, D_MODEL, P_HEADS * P_QDIM), D_MODEL ** -0.5),
        "peer_k1": nrm(ks[15], (DEPTH, P_HEADS, P_NKEYS, P_HALF), P_HALF ** -0.5),
        "peer_k2": nrm(ks[16], (DEPTH, P_HEADS, P_NKEYS, P_HALF), P_HALF ** -0.5),
        "peer_u": nrm(ks[17], (DEPTH, P_EXPERTS, D_MODEL), D_MODEL ** -0.5),
        "peer_v": nrm(ks[18], (DEPTH, P_EXPERTS, D_MODEL), 0.5 * P_HEADS ** -0.5),
        "ple_norm": gain(ks[19], (DEPTH, D_MODEL)),
        "ple_w_gate": nrm(ks[20], (DEPTH, D_MODEL, D_MODEL), D_MODEL ** -0.5),
        "ple_w_proj": nrm(ks[21], (DEPTH, PLE_DIM, D_MODEL), PLE_DIM ** -0.5),
        "final_norm": gain(ks[22], (D_MODEL,)),
    }


def reference(x, p, a_norm, a_w_in, a_gate_bias, a_head_norm, a_w_out, kv_norm, w_kv,
              b_norm, b_w_q, b_sinks, b_w_out, c_norm, peer_w_q, peer_k1, peer_k2,
              peer_u, peer_v, ple_norm, ple_w_gate, ple_w_proj, final_norm):
    B, S, _ = x.shape
    h = x
    k_sh = None
    v_sh = None
    for i in range(DEPTH):
        if i < N_A_LAYERS:
            h = h + mlstm_mixer(rms_norm(h, a_norm[i]), a_w_in[i], a_gate_bias[i],
                                a_head_norm[i], a_w_out[i])
        else:
            if i == N_A_LAYERS:
                kv = rms_norm(h, kv_norm) @ w_kv
                kv = kv.reshape(B, S, 2, A_KV_HEADS, A_HEAD_DIM)
                k_sh = kv[:, :, 0]
                v_sh = kv[:, :, 1]
            j = i - N_A_LAYERS
            h = h + swa_sink_mixer(rms_norm(h, b_norm[j]), k_sh, v_sh, b_w_q[j], b_sinks[j], b_w_out[j])
        h = h + peer_mixer(rms_norm(h, c_norm[i]), peer_w_q[i], peer_k1[i], peer_k2[i],
                           peer_u[i], peer_v[i])
        gate = jax.nn.sigmoid((rms_norm(h, ple_norm[i]) @ ple_w_gate[i]).astype(jnp.float32))
        h = h + (gate * (p[i] @ ple_w_proj[i]).astype(jnp.float32)).astype(h.dtype)
    return rms_norm(h, final_norm)
```

```python
from contextlib import ExitStack
import numpy as np
import concourse.bass as bass
import concourse.mybir as mybir
from concourse.bass_utils import run_bass_kernel_spmd

F32 = mybir.dt.float32
BF16 = mybir.dt.bfloat16
I32 = mybir.dt.int32
U32 = mybir.dt.uint32
AF = mybir.ActivationFunctionType
ALU = mybir.AluOpType
AX = mybir.AxisListType

SEM_LIMIT = 20000
ATT_STOP = 0
D = 2048
NCORES = 8


class Buf:
    __slots__ = ("name", "wr", "rd", "pend")

    def __init__(self, name):
        self.name = name
        self.wr = None
        self.rd = []
        self.pend = False


class T:
    def __init__(self, k, name, shape, dtype, psum=False):
        self.t = k.ps(name, shape, dtype) if psum else k.sb(name, shape, dtype)
        self.b = Buf(name)


class K:
    def __init__(self, nc, stack):
        self.nc = nc
        self.stack = stack
        self.sem_stack = stack
        self.eng = {"pe": nc.tensor, "act": nc.scalar, "dve": nc.vector,
                    "pool": nc.gpsimd, "sp": nc.sync}
        self.csem = {}
        self.known = {e: {} for e in self.eng}
        self.sems = {}
        self.nsem = 0
        self.dq = {}
        self.drr = {}
        self.ninst = 0
        self.pe_pend_r = []
        self.pe_pend_w = []
        self.retired = []

    def barrier(self):
        assert not self.pe_pend_r and not self.pe_pend_w
        toks = [(cs[0], cs[1], e) for e, cs in self.csem.items()]
        for q, slots in self.dq.items():
            for s in slots:
                if s[1] > 0:
                    toks.append((s[0], s[1], "dma"))
        for e in self.eng:
            for t in toks:
                if t[2] == e == "pe":
                    continue
                self._need(e, t)

    def new_sem(self, tag):
        key = "%s_%d" % (tag, self.nsem)
        self.nsem += 1
        h = self.sem_stack.enter_context(self.nc.semaphore(key))
        self.sems[key] = h
        return key

    def sb(self, name, shape, dtype):
        self.ntens = getattr(self, "ntens", 0) + 1
        return self.stack.enter_context(self.nc.sbuf_tensor("%s_%d" % (name, self.ntens), list(shape), dtype))

    def ps(self, name, shape, dtype):
        self.ntens = getattr(self, "ntens", 0) + 1
        return self.stack.enter_context(self.nc.psum_tensor("%s_%d" % (name, self.ntens), list(shape), dtype))

    def _need(self, e, tok):
        if tok is None:
            return
        key, val, teng = tok
        if teng == "pe" and e == "pe":
            return
        if self.known[e].get(key, 0) >= val:
            return
        self.eng[e].wait_ge(self.sems[key], val)
        self.known[e][key] = val

    def _waits(self, e, reads, writes):
        for b in reads:
            if b.pend and e != "pe":
                raise RuntimeError("pending PE access on %s" % b.name)
            self._need(e, b.wr)
        for b in writes:
            if b.pend and e != "pe":
                raise RuntimeError("pending PE access on %s" % b.name)
            self._need(e, b.wr)
            for t in b.rd:
                self._need(e, t)

    def _update(self, tok, reads, writes):
        for b in reads:
            b.rd = [t for t in b.rd if t[0] != tok[0]] + [tok]
        for b in writes:
            b.wr = tok
            b.rd = []

    def op(self, e, fn, reads=(), writes=(), tok=True):
        self._waits(e, reads, writes)
        inst = fn(self.eng[e])
        self.ninst += 1
        if not tok:
            assert e == "pe"
            for b in reads:
                b.pend = True
                self.pe_pend_r.append(b)
            for b in writes:
                b.pend = True
                self.pe_pend_w.append(b)
            return None
        cs = self.csem.get(e)
        if cs is None or cs[1] >= SEM_LIMIT:
            cs = [self.new_sem("c" + e), 0]
            self.csem[e] = cs
        cs[1] += 1
        inst.then_inc(self.sems[cs[0]], 1)
        t = (cs[0], cs[1], e)
        if e == "pe" and (self.pe_pend_r or self.pe_pend_w):
            for b in self.pe_pend_r + self.pe_pend_w:
                b.pend = False
            self._update(t, self.pe_pend_r, self.pe_pend_w)
            self.pe_pend_r = []
            self.pe_pend_w = []
        self._update(t, reads, writes)
        return t

    def dma(self, q, fn, reads=(), writes=(), nslots=8):
        self._waits(q, reads, writes)
        if q not in self.dq:
            self.dq[q] = [[self.new_sem("d" + q), 0] for _ in range(nslots)]
            self.drr[q] = 0
        slots = self.dq[q]
        i = self.drr[q]
        self.drr[q] = (i + 1) % len(slots)
        s = slots[i]
        if s[1] > 0:
            self._need(q, (s[0], s[1], "dma"))
        if s[1] >= 30000:
            self.retired.append((s[0], s[1], "dma"))
            s[0] = self.new_sem("d" + q)
            s[1] = 0
        inst = fn(self.eng[q])
        s[1] += 16
        inst.then_inc(self.sems[s[0]], 16)
        t = (s[0], s[1], "dma")
        self._update(t, reads, writes)
        self.ninst += 1
        return t

    def finish(self, bufs):
        for b in bufs:
            self._need("sp", b.wr)


class Common:
    def __init__(self, k, ident_d):
        self.k = k
        self.ident = T(k, "ident", [128, 128], BF16)
        self.identf = T(k, "identf", [128, 128], F32)
        k.dma("sp", lambda e: e.dma_start(out=self.identf.t[:], in_=ident_d[:, :]), writes=[self.identf.b])
        k.op("dve", lambda e: e.tensor_copy(out=self.ident.t[:], in_=self.identf.t[:]),
             reads=[self.identf.b], writes=[self.ident.b])
        self.wstage = [T(k, "wstage%d" % i, [128, 2048], F32) for i in range(2)]
        self.nw = 0
        self.psA = k.ps("psA", [128, 2048], F32)
        self.psB = k.ps("psB", [128, 2048], F32)
        self.ps_o = [V(self.psA[:, i * 512:(i + 1) * 512], "bank%d" % i) for i in range(4)]
        self.ps_tr = [V(self.psB[:, i * 512:(i + 1) * 512].bitcast(BF16), "bank%d" % (4 + i)) for i in range(2)]
        self.ps_x = [V(self.psB[:, (2 + i) * 512:(3 + i) * 512], "bank%d" % (6 + i)) for i in range(2)]
        self.ps_y = [V(self.psB[:, i * 512:(i + 1) * 512], None) for i in range(4)]
        self.ps_y[0].b = self.ps_tr[0].b
        self.ps_y[1].b = self.ps_tr[1].b
        self.ps_y[2].b = self.ps_x[0].b
        self.ps_y[3].b = self.ps_x[1].b
        self.bankA = [v.b for v in self.ps_o]
        self.bankB = [v.b for v in self.ps_y]


class V:
    def __init__(self, ap, name):
        self.t = ap
        self.b = Buf(name) if name is not None else None


def load_w(k, cm, W, wd, ncols, kchunks, scale=None, col0=0):
    for kc in range(kchunks):
        for c0 in range(0, ncols, 2048):
            cw = min(2048, ncols - c0)
            st = cm.wstage[cm.nw % 2]
            cm.nw += 1
            k.dma("sp", lambda e, st=st, kc=kc, c0=c0, cw=cw: e.dma_start(
                out=st.t[:, :cw], in_=wd[kc * 128:(kc + 1) * 128, c0:c0 + cw]), writes=[st.b])
            if scale is None:
                k.op("pool", lambda e, st=st, kc=kc, c0=c0, cw=cw: e.tensor_copy(
                    out=W.t[:, kc, col0 + c0:col0 + c0 + cw], in_=st.t[:, :cw]),
                    reads=[st.b], writes=[W.b])
            else:
                k.op("pool", lambda e, st=st, kc=kc, c0=c0, cw=cw: e.tensor_scalar(
                    out=W.t[:, kc, col0 + c0:col0 + c0 + cw], in0=st.t[:, :cw],
                    scalar1=scale.t[:, kc:kc + 1], scalar2=None, op0=ALU.mult),
                    reads=[st.b, scale.b], writes=[W.b])


def transpose_tile(k, cm, src_bf, dstT, nchunks=16):
    for half in range((nchunks + 7) // 8):
        pst = cm.ps_tr[half % 2]
        n = min(8, nchunks - half * 8)
        for j in range(n):
            kc = half * 8 + j
            k.op("pe", lambda e, j=j, kc=kc, pst=pst: e.transpose(
                out=pst.t[:, j * 128:(j + 1) * 128], in_=src_bf.t[:, kc * 128:(kc + 1) * 128],
                identity=cm.ident.t[:]), reads=[src_bf.b, cm.ident.b], writes=[pst.b], tok=(j == n - 1))
        k.op("dve", lambda e, half=half, pst=pst, n=n: e.tensor_copy(
            out=dstT.t[:, half * 1024:half * 1024 + n * 128], in_=pst.t[:, :n * 128]),
            reads=[pst.b], writes=[dstT.b])


def mm_group(k, out_ps, lhsT_t, lhsT_b, W, col0, ncols, kchunks=16):
    for kc in range(kchunks):
        k.op("pe", lambda e, kc=kc: e.matmul(
            out=out_ps.t[:, :ncols], lhsT=lhsT_t[:, kc * 128:(kc + 1) * 128],
            rhs=W.t[:, kc, col0:col0 + ncols], start=(kc == 0), stop=(kc == kchunks - 1)),
            reads=[lhsT_b, W.b], writes=[out_ps.b], tok=(kc == kchunks - 1))


def mmres_alloc(k):
    return {
        "W": T(k, "rW", [128, 16, 2048], BF16),
        "a": [T(k, "ra%d" % i, [128, 2048], F32) for i in range(2)],
        "x": [T(k, "rx%d" % i, [128, 2048], F32) for i in range(2)],
        "ab": T(k, "rab", [128, 2048], BF16),
        "aT": T(k, "raT", [128, 2048], BF16),
        "o": [T(k, "ro%d" % i, [128, 2048], F32) for i in range(2)],
    }


def phase_mmres(k, cm, bufs, w_d, ntiles, a_d, a_row0, a_bufs, x_d, x_row0, x_bufs, out_d, out_row0, out_bufs):
    W = bufs["W"]
    load_w(k, cm, W, w_d, 2048, 16)
    for i in range(ntiles):
        a = bufs["a"][i % 2]
        xt = bufs["x"][i % 2]
        ab = bufs["ab"]
        aT = bufs["aT"]
        ot = bufs["o"][i % 2]
        ra, rx, ro = a_row0 + i * 128, x_row0 + i * 128, out_row0 + i * 128
        k.dma("sp", lambda e: e.dma_start(out=a.t[:], in_=a_d[ra:ra + 128, :]),
              reads=[a_bufs[i]] if a_bufs else [], writes=[a.b])
        k.dma("sp", lambda e: e.dma_start(out=xt.t[:], in_=x_d[rx:rx + 128, :]),
              reads=[x_bufs[i]] if x_bufs else [], writes=[xt.b])
        k.op("act", lambda e: e.copy(out=ab.t[:], in_=a.t[:]), reads=[a.b], writes=[ab.b])
        transpose_tile(k, cm, ab, aT)
        for ng in range(4):
            mm_group(k, cm.ps_o[ng], aT.t, aT.b, W, ng * 512, 512)
            k.op("dve", lambda e, ng=ng: e.tensor_tensor(
                out=ot.t[:, ng * 512:(ng + 1) * 512], in0=cm.ps_o[ng].t[:],
                in1=xt.t[:, ng * 512:(ng + 1) * 512], op=ALU.add),
                reads=[cm.ps_o[ng].b, xt.b], writes=[ot.b])
        k.dma("pool", lambda e: e.dma_start(out=out_d[ro:ro + 128, :], in_=ot.t[:]), reads=[ot.b],
              writes=[out_bufs[i]] if out_bufs else [])


def build_test_b1(ntiles):
    nc = bass.Bass("TRN2", target_bir_lowering=False)
    x_d = nc.dram_tensor("x_in", [ntiles * 128, D], F32, kind="ExternalInput").ap()
    hg_d = nc.dram_tensor("hg_in", [ntiles * 128, D], F32, kind="ExternalInput").ap()
    w_d = nc.dram_tensor("w_in", [D, D], F32, kind="ExternalInput").ap()
    id_d = nc.dram_tensor("ident_in", [128, 128], F32, kind="ExternalInput").ap()
    o_d = nc.dram_tensor("out", [ntiles * 128, D], F32, kind="ExternalOutput").ap()
    with ExitStack() as stack:
        k = K(nc, stack)
        cm = Common(k, id_d)
        bufs = mmres_alloc(k)
        phase_mmres(k, cm, bufs, w_d, ntiles, hg_d, 0, None, x_d, 0, None, o_d, 0, None)
        drain(k)
        print("ninst", k.ninst, "nsem", k.nsem)
    return nc


def rms_rstd(k, h, junk, ss, sq, rstd, ncols=D):
    k.op("act", lambda e: e.activation(out=junk.t[:, :ncols], in_=h.t[:, :ncols], func=AF.Square,
                                       accum_out=ss.t[:, 0:1]),
         reads=[h.b], writes=[junk.b, ss.b])
    k.op("act", lambda e: e.activation(out=sq.t[:, 0:1], in_=ss.t[:, 0:1], func=AF.Sqrt,
                                       scale=1.0 / ncols, bias=cm_eps(k)),
         reads=[ss.b], writes=[sq.b])
    k.op("dve", lambda e: e.reciprocal(out=rstd.t[:, 0:1], in_=sq.t[:, 0:1]), reads=[sq.b], writes=[rstd.b])


_EPS_T = {}


def cm_eps(k):
    if id(k) not in _EPS_T:
        t = T(k, "eps_c", [128, 1], F32)
        k.op("pool", lambda e: e.memset(t.t[:], 1e-6), writes=[t.b])
        _EPS_T.clear()
        _EPS_T[id(k)] = t
    t = _EPS_T[id(k)]
    k._need("act", t.b.wr)
    return t.t[:, 0:1]


def peer_alloc(k):
    b = {}
    b["Wq"] = T(k, "pWq", [128, 16, 2048], BF16)
    b["kT"] = T(k, "pkT", [128, 16, 128], BF16)
    b["cw"] = T(k, "pcw", [128, 2048], F32)
    b["h"] = T(k, "ph", [128, 2048], F32)
    b["junk"] = T(k, "pjunk", [128, 2048], BF16)
    b["ss"] = T(k, "pss", [128, 1], F32)
    b["sq"] = T(k, "psq", [128, 1], F32)
    b["rstd"] = T(k, "prstd", [128, 1], F32)
    b["xn"] = T(k, "pxn", [128, 2048], BF16)
    b["xT"] = T(k, "pxT", [128, 2048], BF16)
    b["qb"] = T(k, "pqb", [128, 2048], BF16)
    b["qT"] = T(k, "pqT", [128, 2048], BF16)
    b["S"] = T(k, "pS", [128, 2048], F32)
    b["kst"] = V(b["S"].t[:].rearrange("p (c n) -> p c n", n=128), None)
    b["kst"].b = b["S"].b
    b["v"] = T(k, "pv", [128, 16, 16], F32)
    b["ix"] = T(k, "pix", [128, 16, 16], U32)
    b["ixf"] = T(k, "pixf", [128, 16, 16], F32)
    b["cand"] = T(k, "pcand", [128, 8, 256], F32)
    b["cand2"] = T(k, "pcand2", [128, 8, 256], F32)
    b["cidx"] = T(k, "pcidx", [128, 8, 256], F32)
    b["ts"] = T(k, "pts", [128, 8, 16], F32)
    b["tsm"] = T(k, "ptsm", [128, 8, 16], F32)
    b["e"] = T(k, "pe_", [128, 8, 16], F32)
    b["esum"] = T(k, "pesum", [128, 8], F32)
    b["rinv"] = T(k, "prinv", [128, 8], F32)
    b["g"] = T(k, "pg", [128, 128], F32)
    b["E"] = T(k, "pE", [128, 16, 256], F32)
    b["eidx"] = T(k, "peidx", [128, 128], F32)
    b["idxT"] = T(k, "pidxT", [128, 128], I32)
    b["gT"] = T(k, "pgT", [128, 128], F32)
    b["actv"] = T(k, "pactv", [128, 128], F32)
    b["gel"] = T(k, "pgel", [128, 128], F32)
    b["coefT"] = T(k, "pcoefT", [128, 128], F32)
    b["W2"] = T(k, "pW2", [128, 256], BF16)
    b["ue"] = [T(k, "pue%d" % i, [128, 2048], BF16) for i in range(3)]
    b["ve"] = [T(k, "pve%d" % i, [128, 2048], BF16) for i in range(3)]
    b["lh"] = [T(k, "plh%d" % i, [128, 128], BF16) for i in range(2)]
    b["o"] = b["S"]
    k.op("pool", lambda e: e.memset(b["W2"].t[:], 0.0), writes=[b["W2"].b])
    k.op("pool", lambda e: e.memset(b["W2"].t[:, 127:128], 1.0), writes=[b["W2"].b])
    return b


def phase_peer(k, cm, b, hs_d, hs_bufs, tiles, cw_d, wq_d, k1T_d, k2T_d, u_d, v_d):
    load_w(k, cm, b["Wq"], wq_d, 2048, 16)
    k.dma("sp", lambda e: e.dma_start(out=b["cw"].t[:], in_=cw_d.partition_broadcast(128)), writes=[b["cw"].b])
    for half, kd in enumerate((k1T_d, k2T_d)):
        for h in range(8):
            c = 2 * h + half
            k.dma("sp", lambda e, c=c, h=h, kd=kd: e.dma_start(out=b["kst"].t[:, c, :], in_=kd[h, :, :]),
                  writes=[b["kst"].b])
    k.op("pool", lambda e: e.tensor_copy(out=b["kT"].t[:], in_=b["kst"].t[:]), reads=[b["kst"].b], writes=[b["kT"].b])

    h_, junk, ss, sq, rstd = b["h"], b["junk"], b["ss"], b["sq"], b["rstd"]
    xn, xT, qb, qT, S = b["xn"], b["xT"], b["qb"], b["qT"], b["S"]
    v, ix, ixf, cand, cand2, cidx, ts = b["v"], b["ix"], b["ixf"], b["cand"], b["cand2"], b["cidx"], b["ts"]
    for i in tiles:
        r0 = i * 128
        k.dma("sp", lambda e: e.dma_start(out=h_.t[:], in_=hs_d[r0:r0 + 128, :]), reads=[hs_bufs[i]], writes=[h_.b])
        rms_rstd(k, h_, junk, ss, sq, rstd)
        k.op("dve", lambda e: e.scalar_tensor_tensor(out=xn.t[:], in0=h_.t[:], scalar=rstd.t[:, 0:1], in1=b["cw"].t[:],
                                                     op0=ALU.mult, op1=ALU.mult),
             reads=[h_.b, rstd.b, b["cw"].b], writes=[xn.b])
        transpose_tile(k, cm, xn, xT)
        for ng in range(4):
            mm_group(k, cm.ps_o[ng], xT.t, xT.b, b["Wq"], ng * 512, 512)
            k.op("act", lambda e, ng=ng: e.copy(out=qb.t[:, ng * 512:(ng + 1) * 512], in_=cm.ps_o[ng].t[:]),
                 reads=[cm.ps_o[ng].b], writes=[qb.b])
        transpose_tile(k, cm, qb, qT)
        for c in range(16):
            po = cm.ps_o[c // 4]
            k.op("pe", lambda e, c=c, po=po: e.matmul(out=po.t[:, (c % 4) * 128:(c % 4 + 1) * 128],
                                                      lhsT=qT.t[:, c * 128:(c + 1) * 128], rhs=b["kT"].t[:, c, :],
                                                      start=True, stop=True),
                 reads=[qT.b, b["kT"].b], writes=[po.b], tok=(c % 4 == 3))
        for ng in range(4):
            k.op("act", lambda e, ng=ng: e.copy(out=S.t[:, ng * 512:(ng + 1) * 512], in_=cm.ps_o[ng].t[:]),
                 reads=[cm.ps_o[ng].b], writes=[S.b])
        for c in range(16):
            Sc = S.t[:, c * 128:(c + 1) * 128]
            k.op("dve", lambda e, c=c, Sc=Sc: e.max(out=v.t[:, c, 0:8], in_=Sc), reads=[S.b], writes=[v.b])
            k.op("dve", lambda e, c=c, Sc=Sc: e.max_index(out=ix.t[:, c, 0:8], in_max=v.t[:, c, 0:8], in_values=Sc),
                 reads=[S.b, v.b], writes=[ix.b])
            k.op("dve", lambda e, c=c, Sc=Sc: e.match_replace(out=Sc, in_to_replace=v.t[:, c, 0:8], in_values=Sc,
                                                              imm_value=-1e30),
                 reads=[v.b], writes=[S.b])
            k.op("dve", lambda e, c=c, Sc=Sc: e.max(out=v.t[:, c, 8:16], in_=Sc), reads=[S.b], writes=[v.b])
            k.op("dve", lambda e, c=c, Sc=Sc: e.max_index(out=ix.t[:, c, 8:16], in_max=v.t[:, c, 8:16], in_values=Sc),
                 reads=[S.b, v.b], writes=[ix.b])
        k.op("dve", lambda e: e.tensor_copy(out=ixf.t[:], in_=ix.t[:]), reads=[ix.b], writes=[ixf.b])
        vv = v.t[:].rearrange("p (h two) k -> p h two k", two=2)
        iv = ixf.t[:].rearrange("p (h two) k -> p h two k", two=2)
        cand4 = cand.t[:].rearrange("p h (i j) -> p h i j", j=16)
        cidx4 = cidx.t[:].rearrange("p h (i j) -> p h i j", j=16)
        k.op("dve", lambda e: e.tensor_tensor(out=cand4, in0=vv[:, :, 0, :].unsqueeze(3).to_broadcast([128, 8, 16, 16]),
                                              in1=vv[:, :, 1, :].unsqueeze(2).to_broadcast([128, 8, 16, 16]), op=ALU.add),
             reads=[v.b], writes=[cand.b])
        k.op("dve", lambda e: e.tensor_scalar(out=iv[:, :, 0, :], in0=iv[:, :, 0, :], scalar1=128.0, scalar2=None,
                                              op0=ALU.mult), reads=[ixf.b], writes=[ixf.b])
        k.op("dve", lambda e: e.tensor_tensor(out=cidx4, in0=iv[:, :, 0, :].unsqueeze(3).to_broadcast([128, 8, 16, 16]),
                                              in1=iv[:, :, 1, :].unsqueeze(2).to_broadcast([128, 8, 16, 16]), op=ALU.add),
             reads=[ixf.b], writes=[cidx.b])
        k.op("pool", lambda e: e.tensor_copy(out=cand2.t[:], in_=cand.t[:]), reads=[cand.b], writes=[cand2.b])
        for h in range(8):
            ch = cand2.t[:, h, :]
            k.op("dve", lambda e, h=h, ch=ch: e.max(out=ts.t[:, h, 0:8], in_=ch), reads=[cand2.b], writes=[ts.b])
            k.op("dve", lambda e, h=h, ch=ch: e.match_replace(out=ch, in_to_replace=ts.t[:, h, 0:8], in_values=ch,
                                                              imm_value=-1e30), reads=[ts.b], writes=[cand2.b])
            k.op("dve", lambda e, h=h, ch=ch: e.max(out=ts.t[:, h, 8:16], in_=ch), reads=[cand2.b], writes=[ts.b])
        E = b["E"]
        eidx3 = b["eidx"].t[:].rearrange("p (h k) -> p h k", k=16)
        for h in range(8):
            k.op("dve", lambda e, h=h: e.tensor_tensor(
                out=E.t[:], in0=cand.t[:, h, :].unsqueeze(1).to_broadcast([128, 16, 256]),
                in1=ts.t[:, h, :].unsqueeze(2).to_broadcast([128, 16, 256]), op=ALU.is_equal),
                reads=[cand.b, ts.b], writes=[E.b])
            k.op("dve", lambda e, h=h: e.tensor_tensor(
                out=E.t[:], in0=E.t[:], in1=cidx.t[:, h, :].unsqueeze(1).to_broadcast([128, 16, 256]), op=ALU.mult),
                reads=[cidx.b], writes=[E.b])
            k.op("dve", lambda e, h=h: e.tensor_reduce(out=eidx3[:, h, :], in_=E.t[:], axis=AX.X, op=ALU.add),
                 reads=[E.b], writes=[b["eidx"].b])
        k.op("dve", lambda e: e.tensor_scalar(out=b["eidx"].t[:], in0=b["eidx"].t[:], scalar1=16383.0, scalar2=0.0,
                                              op0=ALU.min, op1=ALU.max), reads=[], writes=[b["eidx"].b])
        k.op("dve", lambda e: e.tensor_tensor(out=b["tsm"].t[:], in0=ts.t[:], in1=ts.t[:, :, 0:1].to_broadcast([128, 8, 16]),
                                              op=ALU.subtract), reads=[ts.b], writes=[b["tsm"].b])
        k.op("act", lambda e: e.activation(out=b["e"].t[:], in_=b["tsm"].t[:], func=AF.Exp),
             reads=[b["tsm"].b], writes=[b["e"].b])
        k.op("dve", lambda e: e.tensor_reduce(out=b["esum"].t[:], in_=b["e"].t[:], axis=AX.X, op=ALU.add),
             reads=[b["e"].b], writes=[b["esum"].b])
        k.op("dve", lambda e: e.reciprocal(out=b["rinv"].t[:], in_=b["esum"].t[:]), reads=[b["esum"].b], writes=[b["rinv"].b])
        g3 = b["g"].t[:].rearrange("p (h k) -> p h k", k=16)
        k.op("dve", lambda e: e.tensor_tensor(out=g3, in0=b["e"].t[:], in1=b["rinv"].t[:].unsqueeze(2).to_broadcast([128, 8, 16]),
                                              op=ALU.mult), reads=[b["e"].b, b["rinv"].b], writes=[b["g"].b])
        px0, px1 = cm.ps_x[0], cm.ps_x[1]
        k.op("pe", lambda e: e.transpose(out=px0.t[:, 0:128], in_=b["eidx"].t[:], identity=cm.identf.t[:]),
             reads=[b["eidx"].b, cm.identf.b], writes=[px0.b])
        k.op("dve", lambda e: e.tensor_copy(out=b["idxT"].t[:], in_=px0.t[:, 0:128]), reads=[px0.b], writes=[b["idxT"].b])
        k.op("pe", lambda e: e.transpose(out=px1.t[:, 0:128], in_=b["g"].t[:], identity=cm.identf.t[:]),
             reads=[b["g"].b, cm.identf.b], writes=[px1.b])
        k.op("dve", lambda e: e.tensor_copy(out=b["gT"].t[:], in_=px1.t[:, 0:128]), reads=[px1.b], writes=[b["gT"].b])
        actv = b["actv"]
        for t in range(128):
            ue = b["ue"][t % 3]
            k.dma("pool", lambda e, t=t, ue=ue: e.indirect_dma_start(
                out=ue.t[:], out_offset=None, in_=u_d[:, :],
                in_offset=bass.IndirectOffsetOnAxis(ap=b["idxT"].t[:, t:t + 1], axis=0)),
                reads=[b["idxT"].b], writes=[ue.b])
            for ng in range(4):
                k.op("pe", lambda e, t=t, ng=ng: e.matmul(
                    out=cm.ps_o[ng].t[:], lhsT=cm.ident.t[:, t:t + 1].to_broadcast([128, 128]),
                    rhs=xn.t[:, ng * 512:(ng + 1) * 512], start=True, stop=True),
                    reads=[cm.ident.b, xn.b], writes=[cm.ps_o[ng].b], tok=(ng == 3))
            k.op("dve", lambda e, t=t, ue=ue: e.scalar_tensor_tensor(
                out=junk.t[:], in0=ue.t[:], scalar=1.0, in1=cm.psA[:, :], op0=ALU.mult, op1=ALU.mult,
                accum_out=actv.t[:, t:t + 1]),
                reads=[ue.b] + cm.bankA, writes=[junk.b, actv.b])
        k.op("act", lambda e: e.activation(out=b["gel"].t[:], in_=actv.t[:], func=AF.Gelu),
             reads=[actv.b], writes=[b["gel"].b])
        k.op("dve", lambda e: e.tensor_tensor(out=b["coefT"].t[:], in0=b["gel"].t[:], in1=b["gT"].t[:], op=ALU.mult),
             reads=[b["gel"].b, b["gT"].b], writes=[b["coefT"].b])
        for t in range(128):
            ve = b["ve"][t % 3]
            lh = b["lh"][t % 2]
            k.dma("pool", lambda e, t=t, ve=ve: e.indirect_dma_start(
                out=ve.t[:], out_offset=None, in_=v_d[:, :],
                in_offset=bass.IndirectOffsetOnAxis(ap=b["idxT"].t[:, t:t + 1], axis=0)),
                reads=[b["idxT"].b], writes=[ve.b])
            k.op("dve", lambda e, t=t, lh=lh: e.tensor_scalar(
                out=lh.t[:], in0=b["W2"].t[:, 127 - t:255 - t], scalar1=b["coefT"].t[:, t:t + 1], scalar2=None,
                op0=ALU.mult), reads=[b["W2"].b, b["coefT"].b], writes=[lh.b])
            for ng in range(4):
                k.op("pe", lambda e, t=t, ng=ng, lh=lh, ve=ve: e.matmul(
                    out=cm.ps_y[ng].t[:], lhsT=lh.t[:], rhs=ve.t[:, ng * 512:(ng + 1) * 512],
                    start=(t == 0), stop=(t == 127)),
                    reads=[lh.b, ve.b], writes=[cm.ps_y[ng].b], tok=(ng == 3))
        o = b["o"]
        k.op("dve", lambda e: e.tensor_tensor(out=o.t[:], in0=cm.psB[:, :], in1=h_.t[:], op=ALU.add),
             reads=[h_.b] + cm.bankB, writes=[o.b])
        k.dma("pool", lambda e: e.dma_start(out=hs_d[r0:r0 + 128, :], in_=o.t[:]), reads=[o.b], writes=[hs_bufs[i]])


def drain(k):
    for q, slots in k.dq.items():
        for s in slots:
            if s[1] > 0:
                k._need("sp", (s[0], s[1], "dma"))


def build_test_peer(ntiles):
    nc = bass.Bass("TRN2", target_bir_lowering=False)
    hs_d = nc.dram_tensor("hs_io", [ntiles * 128, D], F32, kind="ExternalOutput").ap()
    hin_d = nc.dram_tensor("h_in", [ntiles * 128, D], F32, kind="ExternalInput").ap()
    cw_d = nc.dram_tensor("cw_in", [D], F32, kind="ExternalInput").ap()
    wq_d = nc.dram_tensor("wq_in", [D, D], F32, kind="ExternalInput").ap()
    k1_d = nc.dram_tensor("k1T_in", [8, 128, 128], F32, kind="ExternalInput").ap()
    k2_d = nc.dram_tensor("k2T_in", [8, 128, 128], F32, kind="ExternalInput").ap()
    u_d = nc.dram_tensor("u_in", [16384, D], F32, kind="ExternalInput").ap()
    v_d = nc.dram_tensor("v_in", [16384, D], F32, kind="ExternalInput").ap()
    id_d = nc.dram_tensor("ident_in", [128, 128], F32, kind="ExternalInput").ap()
    with ExitStack() as stack:
        k = K(nc, stack)
        cm = Common(k, id_d)
        b = peer_alloc(k)
        hs_bufs = [Buf("hs%d" % i) for i in range(ntiles)]
        for i in range(ntiles):
            k.dma("sp", lambda e, i=i: e.dma_start(out=b["o"].t[:], in_=hin_d[i * 128:(i + 1) * 128, :]), writes=[b["o"].b])
            k.dma("sp", lambda e, i=i: e.dma_start(out=hs_d[i * 128:(i + 1) * 128, :], in_=b["o"].t[:]), reads=[b["o"].b],
                  writes=[hs_bufs[i]])
        phase_peer(k, cm, b, hs_d, hs_bufs, list(range(ntiles)), cw_d, wq_d, k1_d, k2_d, u_d, v_d)
        drain(k)
        print("ninst", k.ninst, "nsem", k.nsem)
    return nc


def ple_alloc(k):
    b = {}
    b["Wg"] = T(k, "eWg", [128, 16, 2048], BF16)
    b["Wp"] = T(k, "eWp", [128, 2, 2048], BF16)
    b["nw"] = T(k, "enw", [128, 16], F32)
    b["fw"] = T(k, "efw", [128, 2048], F32)
    b["h"] = [T(k, "eh%d" % i, [128, 2048], F32) for i in range(2)]
    b["p"] = [T(k, "ep%d" % i, [128, 256], F32) for i in range(2)]
    b["pb"] = T(k, "epb", [128, 256], BF16)
    b["pT"] = T(k, "epT", [128, 256], BF16)
    b["junk"] = T(k, "ejunk", [128, 2048], BF16)
    b["ss"] = T(k, "ess", [128, 1], F32)
    b["sq"] = T(k, "esq", [128, 1], F32)
    b["rstd"] = T(k, "erstd", [128, 1], F32)
    b["xn"] = T(k, "exn", [128, 2048], BF16)
    b["xT"] = T(k, "exT", [128, 2048], BF16)
    b["sig"] = T(k, "esig", [128, 2048], F32)
    b["o"] = [T(k, "eo%d" % i, [128, 2048], F32) for i in range(2)]
    b["o2"] = [T(k, "eo2%d" % i, [128, 2048], F32) for i in range(2)]
    return b


def phase_ple(k, cm, b, hs_d, hs_bufs, tiles, p_d, prow0, nw_d, wg_d, wp_d, final=None):
    k.dma("sp", lambda e: e.dma_start(out=b["nw"].t[:], in_=nw_d[:, :]), writes=[b["nw"].b])
    load_w(k, cm, b["Wg"], wg_d, 2048, 16, scale=b["nw"])
    load_w(k, cm, b["Wp"], wp_d, 2048, 2)
    if final is not None:
        fw_d, out_d, orow0 = final
        k.dma("sp", lambda e: e.dma_start(out=b["fw"].t[:], in_=fw_d.partition_broadcast(128)), writes=[b["fw"].b])
    junk, ss, sq, rstd, xn, xT = b["junk"], b["ss"], b["sq"], b["rstd"], b["xn"], b["xT"]
    for n, i in enumerate(tiles):
        r0 = i * 128
        h_ = b["h"][n % 2]
        p_ = b["p"][n % 2]
        o = b["o"][n % 2]
        k.dma("sp", lambda e: e.dma_start(out=h_.t[:], in_=hs_d[r0:r0 + 128, :]), reads=[hs_bufs[i]], writes=[h_.b])
        k.dma("sp", lambda e: e.dma_start(out=p_.t[:], in_=p_d[prow0 + r0:prow0 + r0 + 128, :]), writes=[p_.b])
        rms_rstd(k, h_, junk, ss, sq, rstd)
        k.op("act", lambda e: e.activation(out=xn.t[:], in_=h_.t[:], func=AF.Copy, scale=rstd.t[:, 0:1]),
             reads=[h_.b, rstd.b], writes=[xn.b])
        transpose_tile(k, cm, xn, xT)
        k.op("pool", lambda e: e.tensor_copy(out=b["pb"].t[:], in_=p_.t[:]), reads=[p_.b], writes=[b["pb"].b])
        transpose_tile(k, cm, b["pb"], b["pT"], nchunks=2)
        for ng in range(4):
            mm_group(k, cm.ps_o[ng], xT.t, xT.b, b["Wg"], ng * 512, 512)
            px = cm.ps_x[ng % 2]
            mm_group(k, px, b["pT"].t, b["pT"].b, b["Wp"], ng * 512, 512, kchunks=2)
            sl = slice(ng * 512, (ng + 1) * 512)
            k.op("act", lambda e, ng=ng, sl=sl: e.activation(out=b["sig"].t[:, sl], in_=cm.ps_o[ng].t[:], func=AF.Sigmoid),
                 reads=[cm.ps_o[ng].b], writes=[b["sig"].b])
            k.op("dve", lambda e, sl=sl, px=px: e.tensor_tensor(out=b["sig"].t[:, sl], in0=px.t[:], in1=b["sig"].t[:, sl],
                                                                op=ALU.mult), reads=[px.b], writes=[b["sig"].b])
        k.op("pool", lambda e: e.tensor_tensor(out=o.t[:], in0=b["sig"].t[:], in1=h_.t[:], op=ALU.add),
             reads=[b["sig"].b, h_.b], writes=[o.b])
        if final is None:
            k.dma("pool", lambda e: e.dma_start(out=hs_d[r0:r0 + 128, :], in_=o.t[:]), reads=[o.b], writes=[hs_bufs[i]])
        else:
            o2 = b["o2"][n % 2]
            rms_rstd(k, o, junk, ss, sq, rstd)
            k.op("dve", lambda e: e.scalar_tensor_tensor(out=o2.t[:], in0=o.t[:], scalar=rstd.t[:, 0:1], in1=b["fw"].t[:],
                                                         op0=ALU.mult, op1=ALU.mult),
                 reads=[o.b, rstd.b, b["fw"].b], writes=[o2.b])
            rr = orow0 + n * 128
            k.dma("pool", lambda e: e.dma_start(out=out_d[rr:rr + 128, :], in_=o2.t[:]), reads=[o2.b])


def build_test_ple(ntiles, final):
    nc = bass.Bass("TRN2", target_bir_lowering=False)
    hs_d = nc.dram_tensor("hs_io", [ntiles * 128, D], F32, kind="ExternalOutput").ap()
    out_d = nc.dram_tensor("out", [ntiles * 128, D], F32, kind="ExternalOutput").ap()
    hin_d = nc.dram_tensor("h_in", [ntiles * 128, D], F32, kind="ExternalInput").ap()
    p_d = nc.dram_tensor("p_in", [ntiles * 128, 256], F32, kind="ExternalInput").ap()
    nw_d = nc.dram_tensor("nw_in", [128, 16], F32, kind="ExternalInput").ap()
    fw_d = nc.dram_tensor("fw_in", [D], F32, kind="ExternalInput").ap()
    wg_d = nc.dram_tensor("wg_in", [D, D], F32, kind="ExternalInput").ap()
    wp_d = nc.dram_tensor("wp_in", [256, D], F32, kind="ExternalInput").ap()
    id_d = nc.dram_tensor("ident_in", [128, 128], F32, kind="ExternalInput").ap()
    with ExitStack() as stack:
        k = K(nc, stack)
        cm = Common(k, id_d)
        b = ple_alloc(k)
        hs_bufs = [Buf("hs%d" % i) for i in range(ntiles)]
        for i in range(ntiles):
            k.dma("sp", lambda e, i=i: e.dma_start(out=b["sig"].t[:], in_=hin_d[i * 128:(i + 1) * 128, :]), writes=[b["sig"].b])
            k.dma("sp", lambda e, i=i: e.dma_start(out=hs_d[i * 128:(i + 1) * 128, :], in_=b["sig"].t[:]), reads=[b["sig"].b],
                  writes=[hs_bufs[i]])
        phase_ple(k, cm, b, hs_d, hs_bufs, list(range(ntiles)), p_d, 0, nw_d, wg_d, wp_d,
                  final=(fw_d, out_d, 0) if final else None)
        drain(k)
        print("ninst", k.ninst, "nsem", k.nsem)
    return nc


def att_alloc(k):
    b = {}
    b["Wq"] = T(k, "aWq", [128, 16, 2048], BF16)
    b["Wkv"] = T(k, "aWkv", [128, 16, 512], BF16)
    b["nq"] = T(k, "anq", [128, 16], F32)
    b["nkv"] = T(k, "ankv", [128, 16], F32)
    b["mask"] = T(k, "amask", [128, 256], F32)
    b["mask1"] = T(k, "amask1", [128, 256], F32)
    b["sinks"] = T(k, "asinks", [128, 32], F32)
    b["h"] = [T(k, "ah%d" % i, [128, 2048], F32) for i in range(2)]
    b["junk"] = T(k, "ajunk", [128, 2048], BF16)
    b["ss"] = T(k, "ass", [128, 1], F32)
    b["sq"] = T(k, "asq", [128, 1], F32)
    b["rstd"] = T(k, "arstd", [128, 1], F32)
    b["xn"] = T(k, "axn", [128, 2048], BF16)
    b["xT"] = T(k, "axT", [128, 2048], BF16)
    b["qb"] = T(k, "aqb", [128, 2048], BF16)
    b["qT"] = T(k, "aqT", [128, 2048], BF16)
    b["Kdup"] = T(k, "aKdup", [128, 8, 128], BF16)
    b["kT"] = [T(k, "akT%d" % i, [128, 1024], BF16) for i in range(2)]
    b["V"] = [T(k, "aV%d" % i, [128, 256], BF16) for i in range(2)]
    b["sm"] = T(k, "asm", [128, 8, 256], F32)
    b["pr"] = T(k, "apr", [128, 8, 256], BF16)
    b["prT"] = [T(k, "aprT%d" % i, [128, 1024], BF16) for i in range(2)]
    b["mx"] = T(k, "amx", [128, 8], F32)
    b["rs"] = T(k, "ars", [128, 8], F32)
    b["es"] = T(k, "aes", [128, 8], F32)
    b["rden"] = T(k, "arden", [128, 8], F32)
    b["o"] = [T(k, "ao%d" % i, [128, 2048], F32) for i in range(2)]
    return b


def phase_att(k, cm, b, hs_d, hs_bufs, ntiles, att_d, att_bufs, nq_d, nkv_d, wq_d, wkv_d, sinks_d, mask_d, mask1_d):
    k.dma("sp", lambda e: e.dma_start(out=b["nq"].t[:], in_=nq_d[:, :]), writes=[b["nq"].b])
    k.dma("sp", lambda e: e.dma_start(out=b["nkv"].t[:], in_=nkv_d[:, :]), writes=[b["nkv"].b])
    k.dma("sp", lambda e: e.dma_start(out=b["mask"].t[:], in_=mask_d[:, :]), writes=[b["mask"].b])
    k.dma("sp", lambda e: e.dma_start(out=b["mask1"].t[:], in_=mask1_d[:, :]), writes=[b["mask1"].b])
    k.dma("sp", lambda e: e.dma_start(out=b["sinks"].t[:], in_=sinks_d.partition_broadcast(128)), writes=[b["sinks"].b])
    load_w(k, cm, b["Wq"], wq_d, 2048, 16, scale=b["nq"])
    load_w(k, cm, b["Wkv"], wkv_d, 512, 16, scale=b["nkv"])
    k.op("pool", lambda e: e.memset(b["Kdup"].t[:], 0.0), writes=[b["Kdup"].b])
    junk, ss, sq, rstd, xn, xT, qb, qT = b["junk"], b["ss"], b["sq"], b["rstd"], b["xn"], b["xT"], b["qb"], b["qT"]
    sm, pr, mx, rs, es, rden = b["sm"], b["pr"], b["mx"], b["rs"], b["es"], b["rden"]
    for i in range(ntiles):
        r0 = i * 128
        h_ = b["h"][i % 2]
        kTc, kTp = b["kT"][i % 2], b["kT"][(i + 1) % 2]
        Vc, Vp = b["V"][i % 2], b["V"][(i + 1) % 2]
        k.dma("sp", lambda e: e.dma_start(out=h_.t[:], in_=hs_d[r0:r0 + 128, :]), reads=[hs_bufs[i]], writes=[h_.b])
        rms_rstd(k, h_, junk, ss, sq, rstd)
        k.op("act", lambda e: e.activation(out=xn.t[:], in_=h_.t[:], func=AF.Copy, scale=rstd.t[:, 0:1]),
             reads=[h_.b, rstd.b], writes=[xn.b])
        transpose_tile(k, cm, xn, xT)
        pkv = cm.ps_x[0]
        mm_group(k, pkv, xT.t, xT.b, b["Wkv"], 0, 512)
        kview = pkv.t[:, 0:256].rearrange("p (g d) -> p g d", d=64)
        kz4 = b["Kdup"].t[:].rearrange("p (g two) d -> p g two d", two=2)
        k.op("act", lambda e: e.copy(out=kz4[:, :, 0, 0:64], in_=kview), reads=[pkv.b], writes=[b["Kdup"].b])
        k.op("dve", lambda e: e.tensor_copy(out=kz4[:, :, 1, 64:128], in_=kview), reads=[pkv.b], writes=[b["Kdup"].b])
        k.op("act", lambda e: e.copy(out=Vc.t[:], in_=pkv.t[:, 256:512]), reads=[pkv.b], writes=[Vc.b])
        kd2 = V(b["Kdup"].t[:].rearrange("p g d -> p (g d)"), None)
        kd2.b = b["Kdup"].b
        transpose_tile(k, cm, kd2, kTc, nchunks=8)
        if i == 0:
            continue
        for ng in range(4):
            mm_group(k, cm.ps_o[ng], xT.t, xT.b, b["Wq"], ng * 512, 512)
            k.op("act", lambda e, ng=ng: e.activation(out=qb.t[:, ng * 512:(ng + 1) * 512], in_=cm.ps_o[ng].t[:],
                                                      func=AF.Copy, scale=0.125),
                 reads=[cm.ps_o[ng].b], writes=[qb.b])
        transpose_tile(k, cm, qb, qT)
        if ATT_STOP == 1:
            continue
        msk = b["mask1"] if i == 1 else b["mask"]
        o = b["o"][i % 2]
        for g in range(4):
            for j in range(8):
                hq = 8 * g + j
                c = hq // 2
                off = (hq % 2) * 64
                po = cm.ps_o[j // 2]
                cb = (j % 2) * 256
                k.op("pe", lambda e, c=c, off=off, po=po, cb=cb, g=g: e.matmul(
                    out=po.t[:, cb:cb + 128], lhsT=qT.t[:, c * 128:(c + 1) * 128],
                    rhs=kTp.t[:, (2 * g + off // 64) * 128:(2 * g + off // 64 + 1) * 128], start=True, stop=True),
                    reads=[qT.b, kTp.b], writes=[po.b], tok=False)
                k.op("pe", lambda e, c=c, off=off, po=po, cb=cb, g=g: e.matmul(
                    out=po.t[:, cb + 128:cb + 256], lhsT=qT.t[:, c * 128:(c + 1) * 128],
                    rhs=kTc.t[:, (2 * g + off // 64) * 128:(2 * g + off // 64 + 1) * 128], start=True, stop=True),
                    reads=[qT.b, kTc.b], writes=[po.b], tok=(j % 2 == 1))
            if ATT_STOP == 2:
                k.op("pe", lambda e: e.transpose(out=cm.ps_x[1].t[:, 0:128], in_=cm.identf.t[:], identity=cm.identf.t[:]),
                     reads=[cm.identf.b], writes=[cm.ps_x[1].b])
                continue
            sA = cm.psA[:, :].rearrange("p (j n) -> p j n", n=256)
            k.op("dve", lambda e: e.tensor_tensor(out=sm.t[:], in0=sA, in1=msk.t[:].unsqueeze(1).to_broadcast([128, 8, 256]),
                                                  op=ALU.add), reads=cm.bankA + [msk.b], writes=[sm.b])
            k.op("dve", lambda e: e.tensor_reduce(out=mx.t[:], in_=sm.t[:], axis=AX.X, op=ALU.max), reads=[sm.b], writes=[mx.b])
            k.op("dve", lambda e, g=g: e.tensor_tensor(out=mx.t[:], in0=mx.t[:], in1=b["sinks"].t[:, 8 * g:8 * g + 8], op=ALU.max),
                 reads=[b["sinks"].b], writes=[mx.b])
            k.op("dve", lambda e: e.tensor_tensor(out=sm.t[:], in0=sm.t[:], in1=mx.t[:].unsqueeze(2).to_broadcast([128, 8, 256]),
                                                  op=ALU.subtract), reads=[mx.b], writes=[sm.b])
            k.op("act", lambda e: e.activation(out=pr.t[:], in_=sm.t[:], func=AF.Exp), reads=[sm.b], writes=[pr.b])
            k.op("dve", lambda e: e.tensor_reduce(out=rs.t[:], in_=pr.t[:], axis=AX.X, op=ALU.add), reads=[pr.b], writes=[rs.b])
            k.op("dve", lambda e, g=g: e.tensor_tensor(out=es.t[:], in0=b["sinks"].t[:, 8 * g:8 * g + 8], in1=mx.t[:],
                                                       op=ALU.subtract), reads=[b["sinks"].b, mx.b], writes=[es.b])
            k.op("act", lambda e: e.activation(out=es.t[:], in_=es.t[:], func=AF.Exp), reads=[], writes=[es.b])
            k.op("dve", lambda e: e.tensor_tensor(out=rs.t[:], in0=rs.t[:], in1=es.t[:], op=ALU.add), reads=[es.b], writes=[rs.b])
            k.op("dve", lambda e: e.reciprocal(out=rden.t[:], in_=rs.t[:]), reads=[rs.b], writes=[rden.b])
            if ATT_STOP == 3:
                continue
            for half in range(2):
                pst = cm.ps_tr[half]
                for j in range(8):
                    k.op("pe", lambda e, j=j, half=half, pst=pst: e.transpose(
                        out=pst.t[:, j * 128:(j + 1) * 128], in_=pr.t[:, j, half * 128:(half + 1) * 128],
                        identity=cm.ident.t[:]), reads=[pr.b, cm.ident.b], writes=[pst.b], tok=(j == 7))
                eng = "dve" if half == 0 else "act"
                if half == 0:
                    k.op("dve", lambda e, pst=pst: e.tensor_copy(out=b["prT"][0].t[:], in_=pst.t[:]), reads=[pst.b],
                         writes=[b["prT"][0].b])
                else:
                    k.op("act", lambda e, pst=pst: e.copy(out=b["prT"][1].t[:], in_=pst.t[:]), reads=[pst.b],
                         writes=[b["prT"][1].b])
            if ATT_STOP == 4:
                continue
            pov = cm.ps_x[1]
            for j in range(8):
                k.op("pe", lambda e, j=j, g=g: e.matmul(out=pov.t[:, j * 64:(j + 1) * 64], lhsT=b["prT"][0].t[:, j * 128:(j + 1) * 128],
                                                        rhs=Vp.t[:, g * 64:(g + 1) * 64], start=True, stop=False),
                     reads=[b["prT"][0].b, Vp.b], writes=[pov.b], tok=False)
                k.op("pe", lambda e, j=j, g=g: e.matmul(out=pov.t[:, j * 64:(j + 1) * 64], lhsT=b["prT"][1].t[:, j * 128:(j + 1) * 128],
                                                        rhs=Vc.t[:, g * 64:(g + 1) * 64], start=False, stop=True),
                     reads=[b["prT"][1].b, Vc.b], writes=[pov.b], tok=(j == 7))
            k.op("dve", lambda e, g=g: e.tensor_tensor(
                out=o.t[:, g * 512:(g + 1) * 512].rearrange("p (j d) -> p j d", d=64),
                in0=pov.t[:, :].rearrange("p (j d) -> p j d", d=64),
                in1=rden.t[:].unsqueeze(2).to_broadcast([128, 8, 64]), op=ALU.mult),
                reads=[pov.b, rden.b], writes=[o.b])
        ro = (i - 1) * 128
        k.dma("pool", lambda e: e.dma_start(out=att_d[ro:ro + 128, :], in_=o.t[:]), reads=[o.b], writes=[att_bufs[i - 1]])


def build_test_att(ntiles):
    nc = bass.Bass("TRN2", target_bir_lowering=False)
    hin_d = nc.dram_tensor("h_in", [ntiles * 128, D], F32, kind="ExternalInput").ap()
    att_d = nc.dram_tensor("att_out", [(ntiles - 1) * 128, D], F32, kind="ExternalOutput").ap()
    nq_d = nc.dram_tensor("nq_in", [128, 16], F32, kind="ExternalInput").ap()
    nkv_d = nc.dram_tensor("nkv_in", [128, 16], F32, kind="ExternalInput").ap()
    wq_d = nc.dram_tensor("wq_in", [D, D], F32, kind="ExternalInput").ap()
    wkv_d = nc.dram_tensor("wkv_in", [D, 512], F32, kind="ExternalInput").ap()
    sinks_d = nc.dram_tensor("sinks_in", [32], F32, kind="ExternalInput").ap()
    mask_d = nc.dram_tensor("mask_in", [128, 256], F32, kind="ExternalInput").ap()
    mask1_d = nc.dram_tensor("mask1_in", [128, 256], F32, kind="ExternalInput").ap()
    id_d = nc.dram_tensor("ident_in", [128, 128], F32, kind="ExternalInput").ap()
    with ExitStack() as stack:
        k = K(nc, stack)
        cm = Common(k, id_d)
        b = att_alloc(k)
        hs_bufs = [Buf("hs%d" % i) for i in range(ntiles)]
        att_bufs = [Buf("at%d" % i) for i in range(ntiles)]
        phase_att(k, cm, b, hin_d, hs_bufs, ntiles, att_d, att_bufs, nq_d, nkv_d, wq_d, wkv_d, sinks_d, mask_d, mask1_d)
        drain(k)
        print("ninst", k.ninst, "nsem", k.nsem)
    return nc


def band_masks():
    t = np.arange(128)[:, None]
    kk = np.arange(256)[None, :]
    valid = (kk >= t + 1) & (kk <= t + 128)
    m = np.where(valid, 0.0, -1e30).astype(np.float32)
    m1 = m.copy()
    m1[:, :128] = -1e30
    return m, m1


def mlstm_alloc(k):
    mk = lambda name, shape, dt=F32: T(k, name, shape, dt)
    W = mk("mW", [128, 16, 1538], BF16)
    nw = mk("mnw", [128, 16])
    gb = mk("mgb", [128, 2])
    gb15 = mk("mgb15", [128, 2])
    hnw = mk("mhnw", [128, 512])
    tri = mk("mtri", [128, 128])
    sel = mk("msel", [128, 128])
    cmask = mk("mcmask", [128, 128])
    ones = mk("mones", [128, 128])
    one1 = mk("mone1", [128, 1])
    onesb = mk("monesb", [128, 1], BF16)
    C = mk("mC", [128, 2, 512])
    Cb = mk("mCb", [128, 2, 512], BF16)
    n_ = mk("mn", [128, 2])
    nb = mk("mnb", [128, 2], BF16)
    mprev = mk("mmprev", [128, 1])
    xs = [mk("mx%d" % i, [128, 2048]) for i in range(2)]
    junk = mk("mjunk", [128, 2048], BF16)
    ss, sq, rstd = mk("mss", [128, 1]), mk("msq", [128, 1]), mk("mrstd", [128, 1])
    xn = mk("mxn", [128, 2048], BF16)
    xT = mk("mxT", [128, 2048], BF16)
    qkb = mk("mqkb", [128, 512], BF16)
    qkT = mk("mqkT", [128, 512], BF16)
    vb = mk("mvb", [128, 512], BF16)
    sog = mk("msog", [128, 512])
    gs = mk("mgs", [128, 2])
    th = mk("mth", [128, 2])
    li, z, ez, sp, lf = mk("mli", [128, 1]), mk("mz", [128, 1]), mk("mez", [128, 1]), mk("msp", [128, 1]), mk("mlf", [128, 1])
    bcs, gvec, mrow, u, negu, mt = (mk("mbcs", [128, 1]), mk("mgvec", [128, 1]), mk("mmrow", [128, 1]), mk("mu", [128, 1]),
                                    mk("mnegu", [128, 1]), mk("mmt", [128, 1]))
    dg = mk("mdg", [128, 128])
    A = mk("mA", [128, 128])
    wintra = mk("mwintra", [128, 128])
    wia, winter = mk("mwia", [128, 1]), mk("mwinter", [128, 1])
    Pb = mk("mPb", [128, 128], BF16)
    PT = mk("mPT", [128, 128], BF16)
    dintra = mk("mdintra", [128, 1])
    tmp = mk("mtmp", [128, 512])
    num = mk("mnum", [128, 512])
    den, aden, emt, dmax, rden = mk("mden", [128, 1]), mk("maden", [128, 1]), mk("memt", [128, 1]), mk("mdmax", [128, 1]), mk("mrden", [128, 1])
    ssn, t1, sqv, rstd2, sc = mk("mssn", [128, 1]), mk("mt1", [128, 1]), mk("msqv", [128, 1]), mk("mrstd2", [128, 1]), mk("msc", [128, 1])
    ots = [mk("mot%d" % i, [128, 512]) for i in range(2)]
    mb = mk("mmb", [128, 2])
    last2 = mk("mlast2", [128, 2])
    dlt, wstate, deca, decay = mk("mdlt", [128, 1]), mk("mwstate", [128, 1]), mk("mdeca", [128, 1]), mk("mdecay", [128, 1])
    kw = mk("mkw", [128, 256], BF16)

    padneg = mk("mpadneg", [128, 80])
    npm = mk("mnpm", [128, 80])
    return dict(W=W, nw=nw, gb=gb, gb15=gb15, hnw=hnw, tri=tri, sel=sel, cmask=cmask, ones=ones, one1=one1, onesb=onesb, C=C, Cb=Cb, n_=n_, nb=nb, mprev=mprev, xs=xs, junk=junk, ss=ss, sq=sq, rstd=rstd, xn=xn, xT=xT, qkb=qkb, qkT=qkT, vb=vb, sog=sog, gs=gs, th=th, li=li, z=z, ez=ez, sp=sp, lf=lf, bcs=bcs, gvec=gvec, mrow=mrow, u=u, negu=negu, mt=mt, dg=dg, A=A, wintra=wintra, wia=wia, winter=winter, Pb=Pb, PT=PT, dintra=dintra, tmp=tmp, num=num, den=den, aden=aden, emt=emt, dmax=dmax, rden=rden, ssn=ssn, t1=t1, sqv=sqv, rstd2=rstd2, sc=sc, ots=ots, mb=mb, last2=last2, dlt=dlt, wstate=wstate, deca=deca, decay=decay, kw=kw, padneg=padneg, npm=npm)


def phase_mlstm(k, cm, m, xw_d, w4_d, nw_d, gb4_d, hnw_d, tri_d, sel_d, cmask_d, padneg_d, npm_d, nchunks, out_from,
                hg_d, hg_bufs, bg=None):
    g = globals()
    loc = dict(m)
    W = m["W"]
    nw = m["nw"]
    gb = m["gb"]
    gb15 = m["gb15"]
    hnw = m["hnw"]
    tri = m["tri"]
    sel = m["sel"]
    cmask = m["cmask"]
    ones = m["ones"]
    one1 = m["one1"]
    onesb = m["onesb"]
    C = m["C"]
    Cb = m["Cb"]
    n_ = m["n_"]
    nb = m["nb"]
    mprev = m["mprev"]
    xs = m["xs"]
    junk = m["junk"]
    ss = m["ss"]
    sq = m["sq"]
    rstd = m["rstd"]
    xn = m["xn"]
    xT = m["xT"]
    qkb = m["qkb"]
    qkT = m["qkT"]
    vb = m["vb"]
    sog = m["sog"]
    gs = m["gs"]
    th = m["th"]
    li = m["li"]
    z = m["z"]
    ez = m["ez"]
    sp = m["sp"]
    lf = m["lf"]
    bcs = m["bcs"]
    gvec = m["gvec"]
    mrow = m["mrow"]
    u = m["u"]
    negu = m["negu"]
    mt = m["mt"]
    dg = m["dg"]
    A = m["A"]
    wintra = m["wintra"]
    wia = m["wia"]
    winter = m["winter"]
    Pb = m["Pb"]
    PT = m["PT"]
    dintra = m["dintra"]
    tmp = m["tmp"]
    num = m["num"]
    den = m["den"]
    aden = m["aden"]
    emt = m["emt"]
    dmax = m["dmax"]
    rden = m["rden"]
    ssn = m["ssn"]
    t1 = m["t1"]
    sqv = m["sqv"]
    rstd2 = m["rstd2"]
    sc = m["sc"]
    ots = m["ots"]
    mb = m["mb"]
    last2 = m["last2"]
    dlt = m["dlt"]
    wstate = m["wstate"]
    deca = m["deca"]
    decay = m["decay"]
    kw = m["kw"]
    padneg = m["padneg"]
    npm = m["npm"]

    dl = lambda t, src: k.dma("sp", lambda e: e.dma_start(out=t.t[:], in_=src), writes=[t.b])
    dl(nw, nw_d[:, :])
    dl(tri, tri_d[:, :])
    dl(sel, sel_d[:, :])
    dl(cmask, cmask_d[:, :])
    k.dma("sp", lambda e: e.dma_start(out=padneg.t[:, :nchunks], in_=padneg_d[:, :]), writes=[padneg.b])
    k.dma("sp", lambda e: e.dma_start(out=npm.t[:, :nchunks], in_=npm_d[:, :]), writes=[npm.b])
    for t_, val in ((ones, 1.0), (one1, 1.0), (onesb, 1.0)):
        k.op("pool", lambda e, t_=t_, val=val: e.memset(t_.t[:], val), writes=[t_.b])
    eps_ap = cm_eps(k)
    P0, P1, P2, P3 = cm.ps_o
    X0, X1 = cm.ps_x
    for hd in range(4):
        dl(gb, gb4_d[hd, :, :])
        k.dma("sp", lambda e: e.dma_start(out=hnw.t[:], in_=hnw_d[hd * 512:(hd + 1) * 512].partition_broadcast(128)),
              writes=[hnw.b])
        k.op("dve", lambda e: e.tensor_scalar(out=gb15.t[:], in0=gb.t[:], scalar1=1.0 / 15.0, scalar2=None, op0=ALU.mult),
             reads=[gb.b], writes=[gb15.b])
        for t_, val in ((C, 0.0), (Cb, 0.0), (n_, 0.0), (nb, 0.0), (mprev, 0.0)):
            k.op("pool", lambda e, t_=t_, val=val: e.memset(t_.t[:], val), writes=[t_.b])
        load_w(k, cm, W, w4_d[hd], 1538, 16, scale=nw)
        for c in range(nchunks):
            r0 = c * 128
            xt = xs[c % 2]
            ot = ots[c % 2]
            k.dma("sp", lambda e: e.dma_start(out=xt.t[:], in_=xw_d[r0:r0 + 128, :]), writes=[xt.b])
            if bg is not None:
                bg()
            rms_rstd(k, xt, junk, ss, sq, rstd)
            k.op("act", lambda e: e.activation(out=xn.t[:], in_=xt.t[:], func=AF.Copy, scale=rstd.t[:, 0:1]),
                 reads=[xt.b, rstd.b], writes=[xn.b])
            transpose_tile(k, cm, xn, xT)
            full = c >= out_from
            if full:
                mm_group(k, P0, xT.t, xT.b, W, 0, 512)
            else:
                for kc in range(16):
                    k.op("pe", lambda e, kc=kc: e.matmul(out=P0.t[:, 256:512], lhsT=xT.t[:, kc * 128:(kc + 1) * 128],
                                                         rhs=W.t[:, kc, 256:512], start=(kc == 0), stop=(kc == 15)),
                         reads=[xT.b, W.b], writes=[P0.b], tok=(kc == 15))
            mm_group(k, P1, xT.t, xT.b, W, 512, 512)
            if full:
                mm_group(k, P2, xT.t, xT.b, W, 1024, 512)
            mm_group(k, X0, xT.t, xT.b, W, 1536, 2)
            if full:
                k.op("act", lambda e: e.activation(out=qkb.t[:, 0:256], in_=P0.t[:, 0:256], func=AF.Copy, scale=1.0 / 16.0),
                     reads=[P0.b], writes=[qkb.b])
            k.op("dve", lambda e: e.tensor_copy(out=qkb.t[:, 256:512], in_=P0.t[:, 256:512]), reads=[P0.b], writes=[qkb.b])
            k.op("dve", lambda e: e.tensor_copy(out=vb.t[:], in_=P1.t[:]), reads=[P1.b], writes=[vb.b])
            if full:
                k.op("act", lambda e: e.activation(out=sog.t[:], in_=P2.t[:], func=AF.Sigmoid), reads=[P2.b], writes=[sog.b])
                k.op("pool", lambda e: e.tensor_tensor(out=sog.t[:], in0=sog.t[:], in1=hnw.t[:], op=ALU.mult),
                     reads=[hnw.b], writes=[sog.b])
            k.op("dve", lambda e: e.tensor_copy(out=gs.t[:], in_=X0.t[:, 0:2]), reads=[X0.b], writes=[gs.b])
            if full:
                transpose_tile(k, cm, qkb, qkT, nchunks=4)
            for col in range(2):
                k.op("act", lambda e, col=col: e.activation(out=th.t[:, col:col + 1], in_=gs.t[:, col:col + 1], func=AF.Tanh,
                                                            scale=1.0 / 15.0, bias=gb15.t[:, col:col + 1]),
                     reads=[gs.b, gb15.b], writes=[th.b])
            k.op("dve", lambda e: e.tensor_scalar(out=li.t[:], in0=th.t[:, 0:1], scalar1=15.0, scalar2=None, op0=ALU.mult),
                 reads=[th.b], writes=[li.b])
            k.op("dve", lambda e: e.tensor_tensor(out=li.t[:], in0=li.t[:], in1=padneg.t[:, c:c + 1], op=ALU.add),
                 reads=[padneg.b], writes=[li.b])
            k.op("act", lambda e: e.activation(out=ez.t[:], in_=th.t[:, 1:2], func=AF.Exp, scale=-15.0), reads=[th.b], writes=[ez.b])
            k.op("act", lambda e: e.activation(out=sp.t[:], in_=ez.t[:], func=AF.Ln, bias=one1.t[:, 0:1]),
                 reads=[ez.b, one1.b], writes=[sp.b])
            k.op("dve", lambda e: e.tensor_tensor(out=lf.t[:], in0=sp.t[:], in1=npm.t[:, c:c + 1], op=ALU.mult),
                 reads=[sp.b, npm.b], writes=[lf.b])
            k.op("pe", lambda e: e.matmul(out=X0.t[:, 4:5], lhsT=tri.t[:], rhs=lf.t[:], start=True, stop=True),
                 reads=[tri.b, lf.b], writes=[X0.b])
            k.op("dve", lambda e: e.tensor_copy(out=bcs.t[:], in_=X0.t[:, 4:5]), reads=[X0.b], writes=[bcs.b])
            k.op("dve", lambda e: e.tensor_tensor(out=gvec.t[:], in0=li.t[:], in1=bcs.t[:], op=ALU.subtract),
                 reads=[li.b, bcs.b], writes=[gvec.b])
            k.op("dve", lambda e: e.tensor_scalar(out=dg.t[:], in0=cm.identf.t[:], scalar1=gvec.t[:, 0:1], scalar2=None, op0=ALU.mult),
                 reads=[cm.identf.b, gvec.b], writes=[dg.b])
            k.op("pe", lambda e: e.matmul(out=X1.t[:, 0:128], lhsT=ones.t[:], rhs=dg.t[:], start=True, stop=True),
                 reads=[ones.b, dg.b], writes=[X1.b])
            k.op("dve", lambda e: e.tensor_tensor(out=A.t[:], in0=X1.t[:, 0:128], in1=cmask.t[:], op=ALU.add),
                 reads=[X1.b, cmask.b], writes=[A.b])
            k.op("dve", lambda e: e.tensor_reduce(out=mrow.t[:], in_=A.t[:], axis=AX.X, op=ALU.max), reads=[A.b], writes=[mrow.b])
            k.op("dve", lambda e: e.tensor_tensor(out=u.t[:], in0=mrow.t[:], in1=mprev.t[:], op=ALU.max),
                 reads=[mrow.b, mprev.b], writes=[u.b])
            k.op("dve", lambda e: e.tensor_scalar(out=negu.t[:], in0=u.t[:], scalar1=-1.0, scalar2=None, op0=ALU.mult),
                 reads=[u.b], writes=[negu.b])
            k.op("dve", lambda e: e.tensor_tensor(out=mt.t[:], in0=bcs.t[:], in1=u.t[:], op=ALU.add), reads=[bcs.b, u.b], writes=[mt.b])
            if full:
                k.op("act", lambda e: e.activation(out=wintra.t[:], in_=A.t[:], func=AF.Exp, bias=negu.t[:, 0:1]),
                     reads=[A.b, negu.b], writes=[wintra.b])
                k.op("dve", lambda e: e.tensor_tensor(out=wia.t[:], in0=mprev.t[:], in1=u.t[:], op=ALU.subtract),
                     reads=[mprev.b, u.b], writes=[wia.b])
                k.op("act", lambda e: e.activation(out=winter.t[:], in_=wia.t[:], func=AF.Exp), reads=[wia.b], writes=[winter.b])
                for dc in range(2):
                    k.op("pe", lambda e, dc=dc: e.matmul(out=X1.t[:, 128:256], lhsT=qkT.t[:, dc * 128:(dc + 1) * 128],
                                                         rhs=qkT.t[:, (2 + dc) * 128:(3 + dc) * 128], start=(dc == 0), stop=(dc == 1)),
                         reads=[qkT.b], writes=[X1.b], tok=(dc == 1))
                k.op("dve", lambda e: e.scalar_tensor_tensor(out=Pb.t[:], in0=X1.t[:, 128:256], scalar=1.0, in1=wintra.t[:],
                                                             op0=ALU.mult, op1=ALU.mult, accum_out=dintra.t[:, 0:1]),
                     reads=[X1.b, wintra.b], writes=[Pb.b, dintra.b])
                pst = cm.ps_tr[0]
                k.op("pe", lambda e: e.transpose(out=pst.t[:, 0:128], in_=Pb.t[:], identity=cm.ident.t[:]),
                     reads=[Pb.b, cm.ident.b], writes=[pst.b])
                k.op("dve", lambda e: e.tensor_copy(out=PT.t[:], in_=pst.t[:, 0:128]), reads=[pst.b], writes=[PT.b])
                for dc in range(2):
                    k.op("pe", lambda e, dc=dc: e.matmul(out=P0.t[:], lhsT=qkT.t[:, dc * 128:(dc + 1) * 128], rhs=Cb.t[:, dc, :],
                                                         start=(dc == 0), stop=(dc == 1)),
                         reads=[qkT.b, Cb.b], writes=[P0.b], tok=(dc == 1))
                for dc in range(2):
                    k.op("pe", lambda e, dc=dc: e.matmul(out=X0.t[:, 12:13], lhsT=qkT.t[:, dc * 128:(dc + 1) * 128], rhs=nb.t[:, dc:dc + 1],
                                                         start=(dc == 0), stop=(dc == 1)),
                         reads=[qkT.b, nb.b], writes=[X0.b], tok=(dc == 1))
                k.op("pe", lambda e: e.matmul(out=P1.t[:], lhsT=PT.t[:], rhs=vb.t[:], start=True, stop=True),
                     reads=[PT.b, vb.b], writes=[P1.b])
                k.op("act", lambda e: e.activation(out=tmp.t[:], in_=P0.t[:], func=AF.Copy, scale=winter.t[:, 0:1]),
                     reads=[P0.b, winter.b], writes=[tmp.b])
                k.op("dve", lambda e: e.tensor_tensor(out=num.t[:], in0=P1.t[:], in1=tmp.t[:], op=ALU.add),
                     reads=[P1.b, tmp.b], writes=[num.b])
                k.op("dve", lambda e: e.scalar_tensor_tensor(out=den.t[:], in0=X0.t[:, 12:13], scalar=winter.t[:, 0:1], in1=dintra.t[:],
                                                             op0=ALU.mult, op1=ALU.add),
                     reads=[X0.b, winter.b, dintra.b], writes=[den.b])
                k.op("dve", lambda e: e.tensor_scalar(out=aden.t[:], in0=den.t[:], scalar1=-1.0, scalar2=None, op0=ALU.mult),
                     reads=[den.b], writes=[aden.b])
                k.op("dve", lambda e: e.tensor_tensor(out=aden.t[:], in0=aden.t[:], in1=den.t[:], op=ALU.max),
                     reads=[den.b], writes=[aden.b])
                k.op("act", lambda e: e.activation(out=emt.t[:], in_=mt.t[:], func=AF.Exp, scale=-1.0), reads=[mt.b], writes=[emt.b])
                k.op("dve", lambda e: e.tensor_tensor(out=dmax.t[:], in0=aden.t[:], in1=emt.t[:], op=ALU.max),
                     reads=[aden.b, emt.b], writes=[dmax.b])
                k.op("dve", lambda e: e.reciprocal(out=rden.t[:], in_=dmax.t[:]), reads=[dmax.b], writes=[rden.b])
                k.op("act", lambda e: e.activation(out=junk.t[:, 0:512], in_=num.t[:], func=AF.Square, accum_out=ssn.t[:, 0:1]),
                     reads=[num.b], writes=[junk.b, ssn.b])
                k.op("dve", lambda e: e.tensor_tensor(out=t1.t[:], in0=ssn.t[:], in1=rden.t[:], op=ALU.mult), reads=[ssn.b, rden.b], writes=[t1.b])
                k.op("dve", lambda e: e.tensor_tensor(out=t1.t[:], in0=t1.t[:], in1=rden.t[:], op=ALU.mult), reads=[rden.b], writes=[t1.b])
                k.op("act", lambda e: e.activation(out=sqv.t[:], in_=t1.t[:], func=AF.Sqrt, scale=1.0 / 512.0, bias=eps_ap),
                     reads=[t1.b], writes=[sqv.b])
                k.op("dve", lambda e: e.reciprocal(out=rstd2.t[:], in_=sqv.t[:]), reads=[sqv.b], writes=[rstd2.b])
                k.op("dve", lambda e: e.tensor_tensor(out=sc.t[:], in0=rden.t[:], in1=rstd2.t[:], op=ALU.mult), reads=[rden.b, rstd2.b], writes=[sc.b])
                k.op("dve", lambda e: e.scalar_tensor_tensor(out=ot.t[:], in0=num.t[:], scalar=sc.t[:, 0:1], in1=sog.t[:],
                                                             op0=ALU.mult, op1=ALU.mult),
                     reads=[num.b, sc.b, sog.b], writes=[ot.b])
                ti = c - out_from
                k.dma("pool", lambda e: e.dma_start(out=hg_d[ti * 128:(ti + 1) * 128, hd * 512:(hd + 1) * 512], in_=ot.t[:]),
                      reads=[ot.b], writes=[hg_bufs[ti]])
            k.op("dve", lambda e: e.tensor_copy(out=mb.t[:, 0:1], in_=mt.t[:]), reads=[mt.b], writes=[mb.b])
            k.op("dve", lambda e: e.tensor_copy(out=mb.t[:, 1:2], in_=bcs.t[:]), reads=[bcs.b], writes=[mb.b])
            k.op("pe", lambda e: e.matmul(out=X0.t[:, 8:10], lhsT=sel.t[:], rhs=mb.t[:], start=True, stop=True),
                 reads=[sel.b, mb.b], writes=[X0.b])
            k.op("dve", lambda e: e.tensor_copy(out=last2.t[:], in_=X0.t[:, 8:10]), reads=[X0.b], writes=[last2.b])
            k.op("dve", lambda e: e.tensor_tensor(out=dlt.t[:], in0=last2.t[:, 1:2], in1=last2.t[:, 0:1], op=ALU.subtract),
                 reads=[last2.b], writes=[dlt.b])
            k.op("act", lambda e: e.activation(out=wstate.t[:], in_=gvec.t[:], func=AF.Exp, bias=dlt.t[:, 0:1]),
                 reads=[gvec.b, dlt.b], writes=[wstate.b])
            k.op("dve", lambda e: e.tensor_tensor(out=deca.t[:], in0=dlt.t[:], in1=mprev.t[:], op=ALU.add),
                 reads=[dlt.b, mprev.b], writes=[deca.b])
            k.op("act", lambda e: e.activation(out=decay.t[:], in_=deca.t[:], func=AF.Exp), reads=[deca.b], writes=[decay.b])
            k.op("dve", lambda e: e.tensor_scalar(out=kw.t[:], in0=qkb.t[:, 256:512], scalar1=wstate.t[:, 0:1], scalar2=None, op0=ALU.mult),
                 reads=[qkb.b, wstate.b], writes=[kw.b])
            k.op("pe", lambda e: e.matmul(out=P2.t[:], lhsT=kw.t[:, 0:128], rhs=vb.t[:], start=True, stop=True),
                 reads=[kw.b, vb.b], writes=[P2.b])
            k.op("pe", lambda e: e.matmul(out=P3.t[:], lhsT=kw.t[:, 128:256], rhs=vb.t[:], start=True, stop=True),
                 reads=[kw.b, vb.b], writes=[P3.b])
            k.op("pe", lambda e: e.matmul(out=X0.t[:, 16:17], lhsT=kw.t[:, 0:128], rhs=onesb.t[:], start=True, stop=True),
                 reads=[kw.b, onesb.b], writes=[X0.b], tok=False)
            k.op("pe", lambda e: e.matmul(out=X0.t[:, 17:18], lhsT=kw.t[:, 128:256], rhs=onesb.t[:], start=True, stop=True),
                 reads=[kw.b, onesb.b], writes=[X0.b])
            k.op("dve", lambda e: e.scalar_tensor_tensor(out=C.t[:, 0, :], in0=C.t[:, 0, :], scalar=decay.t[:, 0:1], in1=P2.t[:],
                                                         op0=ALU.mult, op1=ALU.add), reads=[decay.b, P2.b], writes=[C.b])
            k.op("dve", lambda e: e.scalar_tensor_tensor(out=C.t[:, 1, :], in0=C.t[:, 1, :], scalar=decay.t[:, 0:1], in1=P3.t[:],
                                                         op0=ALU.mult, op1=ALU.add), reads=[decay.b, P3.b], writes=[C.b])
            k.op("act", lambda e: e.copy(out=Cb.t[:], in_=C.t[:]), reads=[C.b], writes=[Cb.b])
            k.op("dve", lambda e: e.scalar_tensor_tensor(out=n_.t[:], in0=n_.t[:], scalar=decay.t[:, 0:1], in1=X0.t[:, 16:18],
                                                         op0=ALU.mult, op1=ALU.add), reads=[decay.b, X0.b], writes=[n_.b])
            k.op("dve", lambda e: e.tensor_copy(out=nb.t[:], in_=n_.t[:]), reads=[n_.b], writes=[nb.b])
            k.op("dve", lambda e: e.tensor_copy(out=mprev.t[:], in_=last2.t[:, 0:1]), reads=[last2.b], writes=[mprev.b])


def mlstm_consts():
    s = np.arange(128)[:, None]
    t = np.arange(128)[None, :]
    tri = (s <= t).astype(np.float32)
    sel = np.zeros((128, 128), np.float32)
    sel[127, :] = 1.0
    cmask = np.where(t <= s, 0.0, -1e30).astype(np.float32)
    return tri, sel, cmask


def mlstm_inmap(x_b, a_norm, a_w_in, gate_bias, head_norm, hd):
    o0, o1, o2, o3 = 1024, 2048, 4096, 6144
    cols = np.concatenate([np.arange(hd * 256, (hd + 1) * 256), o0 + np.arange(hd * 256, (hd + 1) * 256),
                           o1 + np.arange(hd * 512, (hd + 1) * 512), o2 + np.arange(hd * 512, (hd + 1) * 512),
                           np.array([o3 + hd, o3 + 4 + hd])])
    tri, sel, cmask = mlstm_consts()
    return {
        "x_in": np.ascontiguousarray(x_b),
        "w_in": np.ascontiguousarray(a_w_in[:, cols]),
        "nw_in": np.ascontiguousarray(a_norm.reshape(16, 128).T),
        "gb_in": np.ascontiguousarray(np.broadcast_to(gate_bias[:, hd][None, :], (128, 2))).astype(np.float32),
        "hnw_in": np.ascontiguousarray(head_norm[hd * 512:(hd + 1) * 512]),
        "tri_in": tri, "sel_in": sel, "cmask_in": cmask, "ident_in": np.eye(128, dtype=np.float32),
    }


NT0 = 17
NT1 = 16
NW = 65


def build_fused():
    nc = bass.Bass("TRN2", target_bir_lowering=False)
    di = lambda name, shape: nc.dram_tensor(name, list(shape), F32, kind="ExternalInput").ap()
    xw_d = di("xw", [NW * 128, D])
    p0_d = di("p0_sh", [NT0 * 128, 256])
    p1_d = di("p1_sh", [NT1 * 128, 256])
    w4_d = di("mw4", [4, D, 1538])
    mnw_d = di("mnw", [128, 16])
    gb4_d = di("mgb4", [4, 128, 2])
    hnw_d = di("mhnw", [D])
    tri_d = di("tri_in", [128, 128])
    sel_d = di("sel_in", [128, 128])
    cmask_d = di("cmask_in", [128, 128])
    padneg_d = di("padneg", [128, NW])
    npm_d = di("npm", [128, NW])
    wout0_d = di("wout0", [D, D])
    wout1_d = di("wout1", [D, D])
    peer = []
    for L in range(2):
        peer.append(dict(cw=di("cw%d" % L, [D]), wq=di("pwq%d" % L, [D, D]), k1=di("k1T%d" % L, [8, 128, 128]),
                         k2=di("k2T%d" % L, [8, 128, 128]), u=di("u%d" % L, [16384, D]), v=di("v%d" % L, [16384, D])))
    ple = []
    for L in range(2):
        ple.append(dict(nw=di("enw%d" % L, [128, 16]), wg=di("ewg%d" % L, [D, D]), wp=di("ewp%d" % L, [256, D])))
    fw_d = di("fw", [D])
    nq_d = di("nq", [128, 16])
    nkv_d = di("nkv", [128, 16])
    bwq_d = di("bwq", [D, D])
    wkv_d = di("wkv", [D, 512])
    sinks_d = di("sinks", [32])
    mask_d = di("mask", [128, 256])
    mask1_d = di("mask1", [128, 256])
    id_d = di("ident_in", [128, 128])
    out_d = nc.dram_tensor("out", [NT1 * 128, D], F32, kind="ExternalOutput").ap()
    hg_d = nc.dram_tensor("hg_scr", [NT0 * 128, D], F32, kind="Internal").ap()
    hs_d = nc.dram_tensor("hs_scr", [NT0 * 128, D], F32, kind="Internal").ap()
    att_d = nc.dram_tensor("att_scr", [NT1 * 128, D], F32, kind="Internal").ap()
    x0 = (NW - NT0) * 128
    tbl = []
    for L in range(2):
        for nm in ("u", "v"):
            tb = nc.dram_tensor("tb_%s%d" % (nm, L), [16384, D], BF16, kind="Internal").ap()
            tbl.append((peer[L][nm], tb))
            peer[L][nm + "b"] = tb
    with ExitStack() as stack:
        k = K(nc, stack)
        cm = Common(k, id_d)
        cm_eps(k)
        hg_bufs = [Buf("hg%d" % i) for i in range(NT0)]
        hs_bufs = [Buf("hs%d" % i) for i in range(NT0)]
        att_bufs = [Buf("at%d" % i) for i in range(NT1)]

        def phase(alloc, fn):
            with ExitStack() as ps:
                k.stack = ps
                b = alloc(k)
                fn(b)
                k.barrier()
            k.stack = stack

        conv = [(src_, dst_, i) for (src_, dst_) in tbl for i in range(128)]
        conv.reverse()

        def run_mlstm(m):
            stg = [T(k, "cstg%d" % i, [128, 2048], BF16) for i in range(4)]
            cnt = [0]

            def bg(n=2):
                for _ in range(n):
                    if not conv:
                        return
                    src_, dst_, i = conv.pop()
                    s = stg[cnt[0] % 4]
                    cnt[0] += 1
                    k.dma("pool", lambda e: e.dma_start(out=s.t[:], in_=src_[i * 128:(i + 1) * 128, :]), writes=[s.b])
                    k.dma("sp", lambda e: e.dma_start(out=dst_[i * 128:(i + 1) * 128, :], in_=s.t[:]), reads=[s.b])
            phase_mlstm(k, cm, m, xw_d, w4_d, mnw_d, gb4_d, hnw_d, tri_d, sel_d, cmask_d,
                        padneg_d, npm_d, NW, NW - NT0, hg_d, hg_bufs, bg=bg)
            while conv:
                bg()
        phase(mlstm_alloc, run_mlstm)
        phase(mmres_alloc, lambda b: phase_mmres(k, cm, b, wout0_d, NT0, hg_d, 0, hg_bufs, xw_d, x0, None, hs_d, 0, hs_bufs))
        phase(peer_alloc, lambda b: phase_peer(k, cm, b, hs_d, hs_bufs, list(range(NT0)), peer[0]["cw"], peer[0]["wq"],
                                               peer[0]["k1"], peer[0]["k2"], peer[0]["ub"], peer[0]["vb"]))
        phase(ple_alloc, lambda b: phase_ple(k, cm, b, hs_d, hs_bufs, list(range(NT0)), p0_d, 0, ple[0]["nw"], ple[0]["wg"],
                                             ple[0]["wp"]))
        phase(att_alloc, lambda b: phase_att(k, cm, b, hs_d, hs_bufs, NT0, att_d, att_bufs, nq_d, nkv_d, bwq_d, wkv_d,
                                             sinks_d, mask_d, mask1_d))
        phase(mmres_alloc, lambda b: phase_mmres(k, cm, b, wout1_d, NT1, att_d, 0, att_bufs, hs_d, 128, hs_bufs[1:],
                                                 hs_d, 128, hs_bufs[1:]))
        phase(peer_alloc, lambda b: phase_peer(k, cm, b, hs_d, hs_bufs, list(range(1, NT0)), peer[1]["cw"], peer[1]["wq"],
                                               peer[1]["k1"], peer[1]["k2"], peer[1]["ub"], peer[1]["vb"]))
        phase(ple_alloc, lambda b: phase_ple(k, cm, b, hs_d, hs_bufs, list(range(1, NT0)), p1_d, -128, ple[1]["nw"],
                                             ple[1]["wg"], ple[1]["wp"], final=(fw_d, out_d, 0)))
        drain(k)
        print("fused ninst", k.ninst, "nsem", k.nsem)
    return nc


def _r16(v):
    return np.ascontiguousarray(np.asarray(v, np.float32).reshape(16, 128).T)


def kernel(x, p, a_norm, a_w_in, a_gate_bias, a_head_norm, a_w_out, kv_norm, w_kv,
           b_norm, b_w_q, b_sinks, b_w_out, c_norm, peer_w_q, peer_k1, peer_k2,
           peer_u, peer_v, ple_norm, ple_w_gate, ple_w_proj, final_norm):
    f = lambda a: np.ascontiguousarray(np.asarray(a, dtype=np.float32))
    x = f(x)
    p = f(p)
    B, S, _ = x.shape
    nc = build_fused()
    m, m1 = band_masks()
    tri, sel, cmask = mlstm_consts()
    a_w_in0 = f(a_w_in)[0]
    gbias = f(a_gate_bias)[0]
    o0, o1, o2, o3 = 1024, 2048, 4096, 6144
    w4 = np.zeros((4, D, 1538), np.float32)
    gb4 = np.zeros((4, 128, 2), np.float32)
    for hd in range(4):
        cols = np.concatenate([np.arange(hd * 256, (hd + 1) * 256), o0 + np.arange(hd * 256, (hd + 1) * 256),
                               o1 + np.arange(hd * 512, (hd + 1) * 512), o2 + np.arange(hd * 512, (hd + 1) * 512),
                               np.array([o3 + hd, o3 + 4 + hd])])
        w4[hd] = a_w_in0[:, cols]
        gb4[hd] = np.broadcast_to(gbias[:, hd][None, :], (128, 2))
    shared = {
        "mw4": w4, "mnw": _r16(f(a_norm)[0]), "mgb4": gb4, "mhnw": f(a_head_norm)[0],
        "tri_in": tri, "sel_in": sel, "cmask_in": cmask,
        "wout0": f(a_w_out)[0], "wout1": f(b_w_out)[0], "fw": f(final_norm),
        "nq": _r16(f(b_norm)[0]), "nkv": _r16(kv_norm), "bwq": f(b_w_q)[0], "wkv": f(w_kv), "sinks": f(b_sinks)[0],
        "mask": m, "ident_in": np.eye(128, dtype=np.float32),
    }
    for L in range(2):
        shared["cw%d" % L] = f(c_norm)[L]
        shared["pwq%d" % L] = f(peer_w_q)[L]
        shared["k1T%d" % L] = np.ascontiguousarray(f(peer_k1)[L].transpose(0, 2, 1))
        shared["k2T%d" % L] = np.ascontiguousarray(f(peer_k2)[L].transpose(0, 2, 1))
        shared["u%d" % L] = f(peer_u)[L]
        shared["v%d" % L] = f(peer_v)[L]
        shared["enw%d" % L] = _r16(f(ple_norm)[L])
        shared["ewg%d" % L] = f(ple_w_gate)[L]
        shared["ewp%d" % L] = f(ple_w_proj)[L]
    TPC = S // 4
    WT = NW * 128
    maps = []
    for c in range(NCORES):
        bb, qd = c // 4, c % 4
        e0 = (qd + 1) * TPC
        npad = max(0, WT - e0)

        def window(arr, width, ntok):
            o = np.zeros((ntok, width), np.float32)
            lo = e0 - ntok
            if lo < 0:
                o[-lo:] = arr[0:e0]
            else:
                o[:] = arr[lo:e0]
            return o
        mp = dict(shared)
        mp["xw"] = window(x[bb], D, WT)
        mp["p0_sh"] = window(p[0, bb], 256, NT0 * 128)
        mp["p1_sh"] = np.ascontiguousarray(p[1, bb, e0 - TPC:e0])
        padc = npad // 128
        pn = np.zeros((128, NW), np.float32)
        pn[:, :padc] = -1e30
        pm = -np.ones((128, NW), np.float32)
        pm[:, :padc] = 0.0
        mp["padneg"] = pn
        mp["npm"] = pm
        mp["mask1"] = m1 if qd == 0 else m
        maps.append(mp)
    res = run_bass_kernel_spmd(nc, maps, core_ids=list(range(NCORES)))
    out = np.zeros((B, S, D), np.float32)
    for c in range(NCORES):
        bb, qd = c // 4, c % 4
        out[bb, qd * TPC:(qd + 1) * TPC] = res.results[c]["out"]
    return out
```

```python
from contextlib import ExitStack
import numpy as np
import concourse.bass as bass
import concourse.mybir as mybir
from concourse.bass_utils import run_bass_kernel_spmd

F32 = mybir.dt.float32
BF16 = mybir.dt.bfloat16
I32 = mybir.dt.int32
U32 = mybir.dt.uint32
AF = mybir.ActivationFunctionType
ALU = mybir.AluOpType
AX = mybir.AxisListType

SEM_LIMIT = 20000
ATT_STOP = 0
D = 2048
NCORES = 8


class Buf:
    __slots__ = ("name", "wr", "rd", "pend")

    def __init__(self, name):
        self.name = name
        self.wr = None
        self.rd = []
        self.pend = False


class T:
    def __init__(self, k, name, shape, dtype, psum=False):
        self.t = k.ps(name, shape, dtype) if psum else k.sb(name, shape, dtype)
        self.b = Buf(name)


class K:
    def __init__(self, nc, stack):
        self.nc = nc
        self.stack = stack
        self.sem_stack = stack
        self.eng = {"pe": nc.tensor, "act": nc.scalar, "dve": nc.vector,
                    "pool": nc.gpsimd, "sp": nc.sync}
        self.csem = {}
        self.known = {e: {} for e in self.eng}
        self.sems = {}
        self.nsem = 0
        self.dq = {}
        self.drr = {}
        self.ninst = 0
        self.pe_pend_r = []
        self.pe_pend_w = []
        self.retired = []

    def barrier(self):
        assert not self.pe_pend_r and not self.pe_pend_w
        toks = [(cs[0], cs[1], e) for e, cs in self.csem.items()]
        for q, slots in self.dq.items():
            for s in slots:
                if s[1] > 0:
                    toks.append((s[0], s[1], "dma"))
        for e in self.eng:
            for t in toks:
                if t[2] == e == "pe":
                    continue
                self._need(e, t)

    def new_sem(self, tag):
        key = "%s_%d" % (tag, self.nsem)
        self.nsem += 1
        h = self.sem_stack.enter_context(self.nc.semaphore(key))
        self.sems[key] = h
        return key

    def sb(self, name, shape, dtype):
        self.ntens = getattr(self, "ntens", 0) + 1
        return self.stack.enter_context(self.nc.sbuf_tensor("%s_%d" % (name, self.ntens), list(shape), dtype))

    def ps(self, name, shape, dtype):
        self.ntens = getattr(self, "ntens", 0) + 1
        return self.stack.enter_context(self.nc.psum_tensor("%s_%d" % (name, self.ntens), list(shape), dtype))

    def _need(self, e, tok):
        if tok is None:
            return
        key, val, teng = tok
        if teng == "pe" and e == "pe":
            return
        if self.known[e].get(key, 0) >= val:
            return
        self.eng[e].wait_ge(self.sems[key], val)
        self.known[e][key] = val

    def _waits(self, e, reads, writes):
        for b in reads:
            if b.pend and e != "pe":
                raise RuntimeError("pending PE access on %s" % b.name)
            self._need(e, b.wr)
        for b in writes:
            if b.pend and e != "pe":
                raise RuntimeError("pending PE access on %s" % b.name)
            self._need(e, b.wr)
            for t in b.rd:
                self._need(e, t)

    def _update(self, tok, reads, writes):
        for b in reads:
            b.rd = [t for t in b.rd if t[0] != tok[0]] + [tok]
        for b in writes:
            b.wr = tok
            b.rd = []

    def op(self, e, fn, reads=(), writes=(), tok=True):
        self._waits(e, reads, writes)
        inst = fn(self.eng[e])
        self.ninst += 1
        if not tok:
            assert e == "pe"
            for b in reads:
                b.pend = True
                self.pe_pend_r.append(b)
            for b in writes:
                b.pend = True
                self.pe_pend_w.append(b)
            return None
        cs = self.csem.get(e)
        if cs is None or cs[1] >= SEM_LIMIT:
            cs = [self.new_sem("c" + e), 0]
            self.csem[e] = cs
        cs[1] += 1
        inst.then_inc(self.sems[cs[0]], 1)
        t = (cs[0], cs[1], e)
        if e == "pe" and (self.pe_pend_r or self.pe_pend_w):
            for b in self.pe_pend_r + self.pe_pend_w:
                b.pend = False
            self._update(t, self.pe_pend_r, self.pe_pend_w)
            self.pe_pend_r = []
            self.pe_pend_w = []
        self._update(t, reads, writes)
        return t

    def dma(self, q, fn, reads=(), writes=(), nslots=8):
        self._waits(q, reads, writes)
        if q not in self.dq:
            self.dq[q] = [[self.new_sem("d" + q), 0] for _ in range(nslots)]
            self.drr[q] = 0
        slots = self.dq[q]
        i = self.drr[q]
        self.drr[q] = (i + 1) % len(slots)
        s = slots[i]
        if s[1] > 0:
            self._need(q, (s[0], s[1], "dma"))
        if s[1] >= 30000:
            self.retired.append((s[0], s[1], "dma"))
            s[0] = self.new_sem("d" + q)
            s[1] = 0
        inst = fn(self.eng[q])
        s[1] += 16
        inst.then_inc(self.sems[s[0]], 16)
        t = (s[0], s[1], "dma")
        self._update(t, reads, writes)
        self.ninst += 1
        return t

    def finish(self, bufs):
        for b in bufs:
            self._need("sp", b.wr)


class Common:
    def __init__(self, k, ident_d):
        self.k = k
        self.ident = T(k, "ident", [128, 128], BF16)
        self.identf = T(k, "identf", [128, 128], F32)
        k.dma("sp", lambda e: e.dma_start(out=self.identf.t[:], in_=ident_d[:, :]), writes=[self.identf.b])
        k.op("dve", lambda e: e.tensor_copy(out=self.ident.t[:], in_=self.identf.t[:]),
             reads=[self.identf.b], writes=[self.ident.b])
        self.wstage = [T(k, "wstage%d" % i, [128, 2048], F32) for i in range(2)]
        self.nw = 0
        self.psA = k.ps("psA", [128, 2048], F32)
        self.psB = k.ps("psB", [128, 2048], F32)
        self.ps_o = [V(self.psA[:, i * 512:(i + 1) * 512], "bank%d" % i) for i in range(4)]
        self.ps_tr = [V(self.psB[:, i * 512:(i + 1) * 512].bitcast(BF16), "bank%d" % (4 + i)) for i in range(2)]
        self.ps_x = [V(self.psB[:, (2 + i) * 512:(3 + i) * 512], "bank%d" % (6 + i)) for i in range(2)]
        self.ps_y = [V(self.psB[:, i * 512:(i + 1) * 512], None) for i in range(4)]
        self.ps_y[0].b = self.ps_tr[0].b
        self.ps_y[1].b = self.ps_tr[1].b
        self.ps_y[2].b = self.ps_x[0].b
        self.ps_y[3].b = self.ps_x[1].b
        self.bankA = [v.b for v in self.ps_o]
        self.bankB = [v.b for v in self.ps_y]


class V:
    def __init__(self, ap, name):
        self.t = ap
        self.b = Buf(name) if name is not None else None


def load_w(k, cm, W, wd, ncols, kchunks, scale=None, col0=0):
    for kc in range(kchunks):
        for c0 in range(0, ncols, 2048):
            cw = min(2048, ncols - c0)
            st = cm.wstage[cm.nw % 2]
            cm.nw += 1
            k.dma("sp", lambda e, st=st, kc=kc, c0=c0, cw=cw: e.dma_start(
                out=st.t[:, :cw], in_=wd[kc * 128:(kc + 1) * 128, c0:c0 + cw]), writes=[st.b])
            if scale is None:
                k.op("pool", lambda e, st=st, kc=kc, c0=c0, cw=cw: e.tensor_copy(
                    out=W.t[:, kc, col0 + c0:col0 + c0 + cw], in_=st.t[:, :cw]),
                    reads=[st.b], writes=[W.b])
            else:
                k.op("pool", lambda e, st=st, kc=kc, c0=c0, cw=cw: e.tensor_scalar(
                    out=W.t[:, kc, col0 + c0:col0 + c0 + cw], in0=st.t[:, :cw],
                    scalar1=scale.t[:, kc:kc + 1], scalar2=None, op0=ALU.mult),
                    reads=[st.b, scale.b], writes=[W.b])


def transpose_tile(k, cm, src_bf, dstT, nchunks=16):
    for half in range((nchunks + 7) // 8):
        pst = cm.ps_tr[half % 2]
        n = min(8, nchunks - half * 8)
        for j in range(n):
            kc = half * 8 + j
            k.op("pe", lambda e, j=j, kc=kc, pst=pst: e.transpose(
                out=pst.t[:, j * 128:(j + 1) * 128], in_=src_bf.t[:, kc * 128:(kc + 1) * 128],
                identity=cm.ident.t[:]), reads=[src_bf.b, cm.ident.b], writes=[pst.b], tok=(j == n - 1))
        k.op("dve", lambda e, half=half, pst=pst, n=n: e.tensor_copy(
            out=dstT.t[:, half * 1024:half * 1024 + n * 128], in_=pst.t[:, :n * 128]),
            reads=[pst.b], writes=[dstT.b])


def mm_group(k, out_ps, lhsT_t, lhsT_b, W, col0, ncols, kchunks=16):
    for kc in range(kchunks):
        k.op("pe", lambda e, kc=kc: e.matmul(
            out=out_ps.t[:, :ncols], lhsT=lhsT_t[:, kc * 128:(kc + 1) * 128],
            rhs=W.t[:, kc, col0:col0 + ncols], start=(kc == 0), stop=(kc == kchunks - 1)),
            reads=[lhsT_b, W.b], writes=[out_ps.b], tok=(kc == kchunks - 1))


def mmres_alloc(k):
    return {
        "W": T(k, "rW", [128, 16, 2048], BF16),
        "a": [T(k, "ra%d" % i, [128, 2048], F32) for i in range(2)],
        "x": [T(k, "rx%d" % i, [128, 2048], F32) for i in range(2)],
        "ab": T(k, "rab", [128, 2048], BF16),
        "aT": T(k, "raT", [128, 2048], BF16),
        "o": [T(k, "ro%d" % i, [128, 2048], F32) for i in range(2)],
    }


def phase_mmres(k, cm, bufs, w_d, ntiles, a_d, a_row0, a_bufs, x_d, x_row0, x_bufs, out_d, out_row0, out_bufs):
    W = bufs["W"]
    load_w(k, cm, W, w_d, 2048, 16)
    for i in range(ntiles):
        a = bufs["a"][i % 2]
        xt = bufs["x"][i % 2]
        ab = bufs["ab"]
        aT = bufs["aT"]
        ot = bufs["o"][i % 2]
        ra, rx, ro = a_row0 + i * 128, x_row0 + i * 128, out_row0 + i * 128
        k.dma("sp", lambda e: e.dma_start(out=a.t[:], in_=a_d[ra:ra + 128, :]),
              reads=[a_bufs[i]] if a_bufs else [], writes=[a.b])
        k.dma("sp", lambda e: e.dma_start(out=xt.t[:], in_=x_d[rx:rx + 128, :]),
              reads=[x_bufs[i]] if x_bufs else [], writes=[xt.b])
        k.op("act", lambda e: e.copy(out=ab.t[:], in_=a.t[:]), reads=[a.b], writes=[ab.b])
        transpose_tile(k, cm, ab, aT)
        for ng in range(4):
            mm_group(k, cm.ps_o[ng], aT.t, aT.b, W, ng * 512, 512)
            k.op("dve", lambda e, ng=ng: e.tensor_tensor(
                out=ot.t[:, ng * 512:(ng + 1) * 512], in0=cm.ps_o[ng].t[:],
                in1=xt.t[:, ng * 512:(ng + 1) * 512], op=ALU.add),
                reads=[cm.ps_o[ng].b, xt.b], writes=[ot.b])
        k.dma("pool", lambda e: e.dma_start(out=out_d[ro:ro + 128, :], in_=ot.t[:]), reads=[ot.b],
              writes=[out_bufs[i]] if out_bufs else [])


def build_test_b1(ntiles):
    nc = bass.Bass("TRN2", target_bir_lowering=False)
    x_d = nc.dram_tensor("x_in", [ntiles * 128, D], F32, kind="ExternalInput").ap()
    hg_d = nc.dram_tensor("hg_in", [ntiles * 128, D], F32, kind="ExternalInput").ap()
    w_d = nc.dram_tensor("w_in", [D, D], F32, kind="ExternalInput").ap()
    id_d = nc.dram_tensor("ident_in", [128, 128], F32, kind="ExternalInput").ap()
    o_d = nc.dram_tensor("out", [ntiles * 128, D], F32, kind="ExternalOutput").ap()
    with ExitStack() as stack:
        k = K(nc, stack)
        cm = Common(k, id_d)
        bufs = mmres_alloc(k)
        phase_mmres(k, cm, bufs, w_d, ntiles, hg_d, 0, None, x_d, 0, None, o_d, 0, None)
        drain(k)
        print("ninst", k.ninst, "nsem", k.nsem)
    return nc


def rms_rstd(k, h, junk, ss, sq, rstd, ncols=D):
    k.op("act", lambda e: e.activation(out=junk.t[:, :ncols], in_=h.t[:, :ncols], func=AF.Square,
                                       accum_out=ss.t[:, 0:1]),
         reads=[h.b], writes=[junk.b, ss.b])
    k.op("act", lambda e: e.activation(out=sq.t[:, 0:1], in_=ss.t[:, 0:1], func=AF.Sqrt,
                                       scale=1.0 / ncols, bias=cm_eps(k)),
         reads=[ss.b], writes=[sq.b])
    k.op("dve", lambda e: e.reciprocal(out=rstd.t[:, 0:1], in_=sq.t[:, 0:1]), reads=[sq.b], writes=[rstd.b])


_EPS_T = {}


def cm_eps(k):
    if id(k) not in _EPS_T:
        t = T(k, "eps_c", [128, 1], F32)
        k.op("pool", lambda e: e.memset(t.t[:], 1e-6), writes=[t.b])
        _EPS_T.clear()
        _EPS_T[id(k)] = t
    t = _EPS_T[id(k)]
    k._need("act", t.b.wr)
    return t.t[:, 0:1]


def peer_alloc(k):
    b = {}
    b["Wq"] = T(k, "pWq", [128, 16, 2048], BF16)
    b["kT"] = T(k, "pkT", [128, 16, 128], BF16)
    b["cw"] = T(k, "pcw", [128, 2048], F32)
    b["h"] = T(k, "ph", [128, 2048], F32)
    b["junk"] = T(k, "pjunk", [128, 2048], BF16)
    b["ss"] = T(k, "pss", [128, 1], F32)
    b["sq"] = T(k, "psq", [128, 1], F32)
    b["rstd"] = T(k, "prstd", [128, 1], F32)
    b["xn"] = T(k, "pxn", [128, 2048], BF16)
    b["xT"] = T(k, "pxT", [128, 2048], BF16)
    b["qb"] = T(k, "pqb", [128, 2048], BF16)
    b["qT"] = b["xT"]
    b["S"] = T(k, "pS", [128, 2048], F32)
    b["kst"] = V(b["S"].t[:].rearrange("p (c n) -> p c n", n=128), None)
    b["kst"].b = b["S"].b
    b["v"] = T(k, "pv", [128, 16, 16], F32)
    b["ix"] = T(k, "pix", [128, 16, 16], U32)
    b["ixf"] = T(k, "pixf", [128, 16, 16], F32)
    b["cand"] = T(k, "pcand", [128, 8, 256], F32)
    b["cand2"] = T(k, "pcand2", [128, 8, 256], F32)
    b["cidx"] = T(k, "pcidx", [128, 8, 256], F32)
    b["ts"] = T(k, "pts", [128, 8, 16], F32)
    b["tsm"] = T(k, "ptsm", [128, 8, 16], F32)
    b["e"] = T(k, "pe_", [128, 8, 16], F32)
    b["esum"] = T(k, "pesum", [128, 8], F32)
    b["rinv"] = T(k, "prinv", [128, 8], F32)
    b["g"] = T(k, "pg", [128, 128], F32)
    b["E"] = T(k, "pE", [128, 16, 256], F32)
    b["eidx"] = T(k, "peidx", [128, 128], F32)
    b["idxT"] = T(k, "pidxT", [128, 128], I32)
    b["gT"] = T(k, "pgT", [128, 128], F32)
    b["actv"] = T(k, "pactv", [128, 128], F32)
    b["gel"] = T(k, "pgel", [128, 128], F32)
    b["coefT"] = T(k, "pcoefT", [128, 128], F32)
    b["W2"] = T(k, "pW2", [128, 256], BF16)
    b["ue"] = [T(k, "pue%d" % i, [128, 2048], BF16) for i in range(4)]
    b["ve"] = [T(k, "pve%d" % i, [128, 2048], BF16) for i in range(4)]
    b["lh"] = [T(k, "plh%d" % i, [128, 128], BF16) for i in range(2)]
    b["o"] = b["S"]
    k.op("pool", lambda e: e.memset(b["W2"].t[:], 0.0), writes=[b["W2"].b])
    k.op("pool", lambda e: e.memset(b["W2"].t[:, 127:128], 1.0), writes=[b["W2"].b])
    return b


def phase_peer(k, cm, b, hs_d, hs_bufs, tiles, cw_d, wq_d, k1T_d, k2T_d, u_d, v_d):
    load_w(k, cm, b["Wq"], wq_d, 2048, 16)
    k.dma("sp", lambda e: e.dma_start(out=b["cw"].t[:], in_=cw_d.partition_broadcast(128)), writes=[b["cw"].b])
    for half, kd in enumerate((k1T_d, k2T_d)):
        for h in range(8):
            c = 2 * h + half
            k.dma("sp", lambda e, c=c, h=h, kd=kd: e.dma_start(out=b["kst"].t[:, c, :], in_=kd[h, :, :]),
                  writes=[b["kst"].b])
    k.op("pool", lambda e: e.tensor_copy(out=b["kT"].t[:], in_=b["kst"].t[:]), reads=[b["kst"].b], writes=[b["kT"].b])

    h_, junk, ss, sq, rstd = b["h"], b["junk"], b["ss"], b["sq"], b["rstd"]
    xn, xT, qb, qT, S = b["xn"], b["xT"], b["qb"], b["qT"], b["S"]
    v, ix, ixf, cand, cand2, cidx, ts = b["v"], b["ix"], b["ixf"], b["cand"], b["cand2"], b["cidx"], b["ts"]
    for i in tiles:
        r0 = i * 128
        k.dma("sp", lambda e: e.dma_start(out=h_.t[:], in_=hs_d[r0:r0 + 128, :]), reads=[hs_bufs[i]], writes=[h_.b])
        rms_rstd(k, h_, junk, ss, sq, rstd)
        k.op("dve", lambda e: e.scalar_tensor_tensor(out=xn.t[:], in0=h_.t[:], scalar=rstd.t[:, 0:1], in1=b["cw"].t[:],
                                                     op0=ALU.mult, op1=ALU.mult),
             reads=[h_.b, rstd.b, b["cw"].b], writes=[xn.b])
        transpose_tile(k, cm, xn, xT)
        for ng in range(4):
            mm_group(k, cm.ps_o[ng], xT.t, xT.b, b["Wq"], ng * 512, 512)
            k.op("act", lambda e, ng=ng: e.copy(out=qb.t[:, ng * 512:(ng + 1) * 512], in_=cm.ps_o[ng].t[:]),
                 reads=[cm.ps_o[ng].b], writes=[qb.b])
        transpose_tile(k, cm, qb, qT)
        for c in range(16):
            po = cm.ps_o[c // 4]
            k.op("pe", lambda e, c=c, po=po: e.matmul(out=po.t[:, (c % 4) * 128:(c % 4 + 1) * 128],
                                                      lhsT=qT.t[:, c * 128:(c + 1) * 128], rhs=b["kT"].t[:, c, :],
                                                      start=True, stop=True),
                 reads=[qT.b, b["kT"].b], writes=[po.b], tok=(c % 4 == 3))
        for ng in range(4):
            k.op("act", lambda e, ng=ng: e.copy(out=S.t[:, ng * 512:(ng + 1) * 512], in_=cm.ps_o[ng].t[:]),
                 reads=[cm.ps_o[ng].b], writes=[S.b])
        for c in range(16):
            Sc = S.t[:, c * 128:(c + 1) * 128]
            k.op("dve", lambda e, c=c, Sc=Sc: e.max(out=v.t[:, c, 0:8], in_=Sc), reads=[S.b], writes=[v.b])
            k.op("dve", lambda e, c=c, Sc=Sc: e.max_index(out=ix.t[:, c, 0:8], in_max=v.t[:, c, 0:8], in_values=Sc),
                 reads=[S.b, v.b], writes=[ix.b])
            k.op("dve", lambda e, c=c, Sc=Sc: e.match_replace(out=Sc, in_to_replace=v.t[:, c, 0:8], in_values=Sc,
                                                              imm_value=-1e30),
                 reads=[v.b], writes=[S.b])
            k.op("dve", lambda e, c=c, Sc=Sc: e.max(out=v.t[:, c, 8:16], in_=Sc), reads=[S.b], writes=[v.b])
            k.op("dve", lambda e, c=c, Sc=Sc: e.max_index(out=ix.t[:, c, 8:16], in_max=v.t[:, c, 8:16], in_values=Sc),
                 reads=[S.b, v.b], writes=[ix.b])
        k.op("dve", lambda e: e.tensor_copy(out=ixf.t[:], in_=ix.t[:]), reads=[ix.b], writes=[ixf.b])
        vv = v.t[:].rearrange("p (h two) k -> p h two k", two=2)
        iv = ixf.t[:].rearrange("p (h two) k -> p h two k", two=2)
        cand4 = cand.t[:].rearrange("p h (i j) -> p h i j", j=16)
        cidx4 = cidx.t[:].rearrange("p h (i j) -> p h i j", j=16)
        k.op("dve", lambda e: e.tensor_tensor(out=cand4, in0=vv[:, :, 0, :].unsqueeze(3).to_broadcast([128, 8, 16, 16]),
                                              in1=vv[:, :, 1, :].unsqueeze(2).to_broadcast([128, 8, 16, 16]), op=ALU.add),
             reads=[v.b], writes=[cand.b])
        k.op("dve", lambda e: e.tensor_scalar(out=iv[:, :, 0, :], in0=iv[:, :, 0, :], scalar1=128.0, scalar2=None,
                                              op0=ALU.mult), reads=[ixf.b], writes=[ixf.b])
        k.op("dve", lambda e: e.tensor_tensor(out=cidx4, in0=iv[:, :, 0, :].unsqueeze(3).to_broadcast([128, 8, 16, 16]),
                                              in1=iv[:, :, 1, :].unsqueeze(2).to_broadcast([128, 8, 16, 16]), op=ALU.add),
             reads=[ixf.b], writes=[cidx.b])
        k.op("pool", lambda e: e.tensor_copy(out=cand2.t[:], in_=cand.t[:]), reads=[cand.b], writes=[cand2.b])
        for h in range(8):
            ch = cand2.t[:, h, :]
            k.op("dve", lambda e, h=h, ch=ch: e.max(out=ts.t[:, h, 0:8], in_=ch), reads=[cand2.b], writes=[ts.b])
            k.op("dve", lambda e, h=h, ch=ch: e.match_replace(out=ch, in_to_replace=ts.t[:, h, 0:8], in_values=ch,
                                                              imm_value=-1e30), reads=[ts.b], writes=[cand2.b])
            k.op("dve", lambda e, h=h, ch=ch: e.max(out=ts.t[:, h, 8:16], in_=ch), reads=[cand2.b], writes=[ts.b])
        E = b["E"]
        eidx3 = b["eidx"].t[:].rearrange("p (h k) -> p h k", k=16)
        for h in range(8):
            k.op("dve", lambda e, h=h: e.tensor_tensor(
                out=E.t[:], in0=cand.t[:, h, :].unsqueeze(1).to_broadcast([128, 16, 256]),
                in1=ts.t[:, h, :].unsqueeze(2).to_broadcast([128, 16, 256]), op=ALU.is_equal),
                reads=[cand.b, ts.b], writes=[E.b])
            k.op("dve", lambda e, h=h: e.tensor_tensor(
                out=E.t[:], in0=E.t[:], in1=cidx.t[:, h, :].unsqueeze(1).to_broadcast([128, 16, 256]), op=ALU.mult),
                reads=[cidx.b], writes=[E.b])
            k.op("dve", lambda e, h=h: e.tensor_reduce(out=eidx3[:, h, :], in_=E.t[:], axis=AX.X, op=ALU.add),
                 reads=[E.b], writes=[b["eidx"].b])
        k.op("dve", lambda e: e.tensor_scalar(out=b["eidx"].t[:], in0=b["eidx"].t[:], scalar1=16383.0, scalar2=0.0,
                                              op0=ALU.min, op1=ALU.max), reads=[], writes=[b["eidx"].b])
        k.op("dve", lambda e: e.tensor_tensor(out=b["tsm"].t[:], in0=ts.t[:], in1=ts.t[:, :, 0:1].to_broadcast([128, 8, 16]),
                                              op=ALU.subtract), reads=[ts.b], writes=[b["tsm"].b])
        k.op("act", lambda e: e.activation(out=b["e"].t[:], in_=b["tsm"].t[:], func=AF.Exp),
             reads=[b["tsm"].b], writes=[b["e"].b])
        k.op("dve", lambda e: e.tensor_reduce(out=b["esum"].t[:], in_=b["e"].t[:], axis=AX.X, op=ALU.add),
             reads=[b["e"].b], writes=[b["esum"].b])
        k.op("dve", lambda e: e.reciprocal(out=b["rinv"].t[:], in_=b["esum"].t[:]), reads=[b["esum"].b], writes=[b["rinv"].b])
        g3 = b["g"].t[:].rearrange("p (h k) -> p h k", k=16)
        k.op("dve", lambda e: e.tensor_tensor(out=g3, in0=b["e"].t[:], in1=b["rinv"].t[:].unsqueeze(2).to_broadcast([128, 8, 16]),
                                              op=ALU.mult), reads=[b["e"].b, b["rinv"].b], writes=[b["g"].b])
        px0, px1 = cm.ps_x[0], cm.ps_x[1]
        k.op("pe", lambda e: e.transpose(out=px0.t[:, 0:128], in_=b["eidx"].t[:], identity=cm.identf.t[:]),
             reads=[b["eidx"].b, cm.identf.b], writes=[px0.b])
        k.op("dve", lambda e: e.tensor_copy(out=b["idxT"].t[:], in_=px0.t[:, 0:128]), reads=[px0.b], writes=[b["idxT"].b])
        k.op("pe", lambda e: e.transpose(out=px1.t[:, 0:128], in_=b["g"].t[:], identity=cm.identf.t[:]),
             reads=[b["g"].b, cm.identf.b], writes=[px1.b])
        k.op("dve", lambda e: e.tensor_copy(out=b["gT"].t[:], in_=px1.t[:, 0:128]), reads=[px1.b], writes=[b["gT"].b])
        actv = b["actv"]
        for t in range(128):
            ue = b["ue"][t % 4]
            k.dma("pool", lambda e, t=t, ue=ue: e.indirect_dma_start(
                out=ue.t[:], out_offset=None, in_=u_d[:, :],
                in_offset=bass.IndirectOffsetOnAxis(ap=b["idxT"].t[:, t:t + 1], axis=0)),
                reads=[b["idxT"].b], writes=[ue.b])
            pbk = cm.ps_o if t % 2 == 0 else cm.ps_y
            pall = cm.psA if t % 2 == 0 else cm.psB
            pbufs = cm.bankA if t % 2 == 0 else cm.bankB
            for ng in range(4):
                k.op("pe", lambda e, t=t, ng=ng, pbk=pbk: e.matmul(
                    out=pbk[ng].t[:], lhsT=cm.ident.t[:, t:t + 1].to_broadcast([128, 128]),
                    rhs=xn.t[:, ng * 512:(ng + 1) * 512], start=True, stop=True),
                    reads=[cm.ident.b, xn.b], writes=[pbk[ng].b], tok=(ng == 3))
            k.op("dve", lambda e, t=t, ue=ue, pall=pall: e.scalar_tensor_tensor(
                out=junk.t[:], in0=ue.t[:], scalar=1.0, in1=pall[:, :], op0=ALU.mult, op1=ALU.mult,
                accum_out=actv.t[:, t:t + 1]),
                reads=[ue.b] + pbufs, writes=[junk.b, actv.b])
        k.op("act", lambda e: e.activation(out=b["gel"].t[:], in_=actv.t[:], func=AF.Gelu),
             reads=[actv.b], writes=[b["gel"].b])
        k.op("dve", lambda e: e.tensor_tensor(out=b["coefT"].t[:], in0=b["gel"].t[:], in1=b["gT"].t[:], op=ALU.mult),
             reads=[b["gel"].b, b["gT"].b], writes=[b["coefT"].b])
        for t in range(128):
            ve = b["ve"][t % 4]
            lh = b["lh"][t % 2]
            k.dma("pool", lambda e, t=t, ve=ve: e.indirect_dma_start(
                out=ve.t[:], out_offset=None, in_=v_d[:, :],
                in_offset=bass.IndirectOffsetOnAxis(ap=b["idxT"].t[:, t:t + 1], axis=0)),
                reads=[b["idxT"].b], writes=[ve.b])
            k.op("dve", lambda e, t=t, lh=lh: e.tensor_scalar(
                out=lh.t[:], in0=b["W2"].t[:, 127 - t:255 - t], scalar1=b["coefT"].t[:, t:t + 1], scalar2=None,
                op0=ALU.mult), reads=[b["W2"].b, b["coefT"].b], writes=[lh.b])
            for ng in range(4):
                k.op("pe", lambda e, t=t, ng=ng, lh=lh, ve=ve: e.matmul(
                    out=cm.ps_y[ng].t[:], lhsT=lh.t[:], rhs=ve.t[:, ng * 512:(ng + 1) * 512],
                    start=(t == 0), stop=(t == 127)),
                    reads=[lh.b, ve.b], writes=[cm.ps_y[ng].b], tok=(ng == 3))
        o = b["o"]
        k.op("dve", lambda e: e.tensor_tensor(out=o.t[:], in0=cm.psB[:, :], in1=h_.t[:], op=ALU.add),
             reads=[h_.b] + cm.bankB, writes=[o.b])
        k.dma("pool", lambda e: e.dma_start(out=hs_d[r0:r0 + 128, :], in_=o.t[:]), reads=[o.b], writes=[hs_bufs[i]])


def drain(k):
    for q, slots in k.dq.items():
        for s in slots:
            if s[1] > 0:
                k._need("sp", (s[0], s[1], "dma"))


def build_test_peer(ntiles):
    nc = bass.Bass("TRN2", target_bir_lowering=False)
    hs_d = nc.dram_tensor("hs_io", [ntiles * 128, D], F32, kind="ExternalOutput").ap()
    hin_d = nc.dram_tensor("h_in", [ntiles * 128, D], F32, kind="ExternalInput").ap()
    cw_d = nc.dram_tensor("cw_in", [D], F32, kind="ExternalInput").ap()
    wq_d = nc.dram_tensor("wq_in", [D, D], F32, kind="ExternalInput").ap()
    k1_d = nc.dram_tensor("k1T_in", [8, 128, 128], F32, kind="ExternalInput").ap()
    k2_d = nc.dram_tensor("k2T_in", [8, 128, 128], F32, kind="ExternalInput").ap()
    u_d = nc.dram_tensor("u_in", [16384, D], F32, kind="ExternalInput").ap()
    v_d = nc.dram_tensor("v_in", [16384, D], F32, kind="ExternalInput").ap()
    id_d = nc.dram_tensor("ident_in", [128, 128], F32, kind="ExternalInput").ap()
    with ExitStack() as stack:
        k = K(nc, stack)
        cm = Common(k, id_d)
        b = peer_alloc(k)
        hs_bufs = [Buf("hs%d" % i) for i in range(ntiles)]
        for i in range(ntiles):
            k.dma("sp", lambda e, i=i: e.dma_start(out=b["o"].t[:], in_=hin_d[i * 128:(i + 1) * 128, :]), writes=[b["o"].b])
            k.dma("sp", lambda e, i=i: e.dma_start(out=hs_d[i * 128:(i + 1) * 128, :], in_=b["o"].t[:]), reads=[b["o"].b],
                  writes=[hs_bufs[i]])
        phase_peer(k, cm, b, hs_d, hs_bufs, list(range(ntiles)), cw_d, wq_d, k1_d, k2_d, u_d, v_d)
        drain(k)
        print("ninst", k.ninst, "nsem", k.nsem)
    return nc


def ple_alloc(k):
    b = {}
    b["Wg"] = T(k, "eWg", [128, 16, 2048], BF16)
    b["Wp"] = T(k, "eWp", [128, 2, 2048], BF16)
    b["nw"] = T(k, "enw", [128, 16], F32)
    b["fw"] = T(k, "efw", [128, 2048], F32)
    b["h"] = [T(k, "eh%d" % i, [128, 2048], F32) for i in range(2)]
    b["p"] = [T(k, "ep%d" % i, [128, 256], F32) for i in range(2)]
    b["pb"] = T(k, "epb", [128, 256], BF16)
    b["pT"] = T(k, "epT", [128, 256], BF16)
    b["junk"] = T(k, "ejunk", [128, 2048], BF16)
    b["ss"] = T(k, "ess", [128, 1], F32)
    b["sq"] = T(k, "esq", [128, 1], F32)
    b["rstd"] = T(k, "erstd", [128, 1], F32)
    b["xn"] = T(k, "exn", [128, 2048], BF16)
    b["xT"] = T(k, "exT", [128, 2048], BF16)
    b["sig"] = T(k, "esig", [128, 2048], F32)
    b["o"] = [T(k, "eo%d" % i, [128, 2048], F32) for i in range(2)]
    b["o2"] = [T(k, "eo2%d" % i, [128, 2048], F32) for i in range(2)]
    return b


def phase_ple(k, cm, b, hs_d, hs_bufs, tiles, p_d, prow0, nw_d, wg_d, wp_d, final=None):
    k.dma("sp", lambda e: e.dma_start(out=b["nw"].t[:], in_=nw_d[:, :]), writes=[b["nw"].b])
    load_w(k, cm, b["Wg"], wg_d, 2048, 16, scale=b["nw"])
    load_w(k, cm, b["Wp"], wp_d, 2048, 2)
    if final is not None:
        fw_d, out_d, orow0 = final
        k.dma("sp", lambda e: e.dma_start(out=b["fw"].t[:], in_=fw_d.partition_broadcast(128)), writes=[b["fw"].b])
    junk, ss, sq, rstd, xn, xT = b["junk"], b["ss"], b["sq"], b["rstd"], b["xn"], b["xT"]
    for n, i in enumerate(tiles):
        r0 = i * 128
        h_ = b["h"][n % 2]
        p_ = b["p"][n % 2]
        o = b["o"][n % 2]
        k.dma("sp", lambda e: e.dma_start(out=h_.t[:], in_=hs_d[r0:r0 + 128, :]), reads=[hs_bufs[i]], writes=[h_.b])
        k.dma("sp", lambda e: e.dma_start(out=p_.t[:], in_=p_d[prow0 + r0:prow0 + r0 + 128, :]), writes=[p_.b])
        rms_rstd(k, h_, junk, ss, sq, rstd)
        k.op("act", lambda e: e.activation(out=xn.t[:], in_=h_.t[:], func=AF.Copy, scale=rstd.t[:, 0:1]),
             reads=[h_.b, rstd.b], writes=[xn.b])
        transpose_tile(k, cm, xn, xT)
        k.op("pool", lambda e: e.tensor_copy(out=b["pb"].t[:], in_=p_.t[:]), reads=[p_.b], writes=[b["pb"].b])
        transpose_tile(k, cm, b["pb"], b["pT"], nchunks=2)
        for ng in range(4):
            mm_group(k, cm.ps_o[ng], xT.t, xT.b, b["Wg"], ng * 512, 512)
            px = cm.ps_x[ng % 2]
            mm_group(k, px, b["pT"].t, b["pT"].b, b["Wp"], ng * 512, 512, kchunks=2)
            sl = slice(ng * 512, (ng + 1) * 512)
            k.op("act", lambda e, ng=ng, sl=sl: e.activation(out=b["sig"].t[:, sl], in_=cm.ps_o[ng].t[:], func=AF.Sigmoid),
                 reads=[cm.ps_o[ng].b], writes=[b["sig"].b])
            k.op("dve", lambda e, sl=sl, px=px: e.tensor_tensor(out=b["sig"].t[:, sl], in0=px.t[:], in1=b["sig"].t[:, sl],
                                                                op=ALU.mult), reads=[px.b], writes=[b["sig"].b])
        k.op("pool", lambda e: e.tensor_tensor(out=o.t[:], in0=b["sig"].t[:], in1=h_.t[:], op=ALU.add),
             reads=[b["sig"].b, h_.b], writes=[o.b])
        if final is None:
            k.dma("pool", lambda e: e.dma_start(out=hs_d[r0:r0 + 128, :], in_=o.t[:]), reads=[o.b], writes=[hs_bufs[i]])
        else:
            o2 = b["o2"][n % 2]
            rms_rstd(k, o, junk, ss, sq, rstd)
            k.op("dve", lambda e: e.scalar_tensor_tensor(out=o2.t[:], in0=o.t[:], scalar=rstd.t[:, 0:1], in1=b["fw"].t[:],
                                                         op0=ALU.mult, op1=ALU.mult),
                 reads=[o.b, rstd.b, b["fw"].b], writes=[o2.b])
            rr = orow0 + n * 128
            k.dma("pool", lambda e: e.dma_start(out=out_d[rr:rr + 128, :], in_=o2.t[:]), reads=[o2.b])


def build_test_ple(ntiles, final):
    nc = bass.Bass("TRN2", target_bir_lowering=False)
    hs_d = nc.dram_tensor("hs_io", [ntiles * 128, D], F32, kind="ExternalOutput").ap()
    out_d = nc.dram_tensor("out", [ntiles * 128, D], F32, kind="ExternalOutput").ap()
    hin_d = nc.dram_tensor("h_in", [ntiles * 128, D], F32, kind="ExternalInput").ap()
    p_d = nc.dram_tensor("p_in", [ntiles * 128, 256], F32, kind="ExternalInput").ap()
    nw_d = nc.dram_tensor("nw_in", [128, 16], F32, kind="ExternalInput").ap()
    fw_d = nc.dram_tensor("fw_in", [D], F32, kind="ExternalInput").ap()
    wg_d = nc.dram_tensor("wg_in", [D, D], F32, kind="ExternalInput").ap()
    wp_d = nc.dram_tensor("wp_in", [256, D], F32, kind="ExternalInput").ap()
    id_d = nc.dram_tensor("ident_in", [128, 128], F32, kind="ExternalInput").ap()
    with ExitStack() as stack:
        k = K(nc, stack)
        cm = Common(k, id_d)
        b = ple_alloc(k)
        hs_bufs = [Buf("hs%d" % i) for i in range(ntiles)]
        for i in range(ntiles):
            k.dma("sp", lambda e, i=i: e.dma_start(out=b["sig"].t[:], in_=hin_d[i * 128:(i + 1) * 128, :]), writes=[b["sig"].b])
            k.dma("sp", lambda e, i=i: e.dma_start(out=hs_d[i * 128:(i + 1) * 128, :], in_=b["sig"].t[:]), reads=[b["sig"].b],
                  writes=[hs_bufs[i]])
        phase_ple(k, cm, b, hs_d, hs_bufs, list(range(ntiles)), p_d, 0, nw_d, wg_d, wp_d,
                  final=(fw_d, out_d, 0) if final else None)
        drain(k)
        print("ninst", k.ninst, "nsem", k.nsem)
    return nc


def att_alloc(k):
    b = {}
    b["Wq"] = T(k, "aWq", [128, 16, 2048], BF16)
    b["Wkv"] = T(k, "aWkv", [128, 16, 512], BF16)
    b["nq"] = T(k, "anq", [128, 16], F32)
    b["nkv"] = T(k, "ankv", [128, 16], F32)
    b["mask"] = T(k, "amask", [128, 256], F32)
    b["mask1"] = T(k, "amask1", [128, 256], F32)
    b["sinks"] = T(k, "asinks", [128, 32], F32)
    b["h"] = [T(k, "ah%d" % i, [128, 2048], F32) for i in range(2)]
    b["junk"] = T(k, "ajunk", [128, 2048], BF16)
    b["ss"] = T(k, "ass", [128, 1], F32)
    b["sq"] = T(k, "asq", [128, 1], F32)
    b["rstd"] = T(k, "arstd", [128, 1], F32)
    b["xn"] = T(k, "axn", [128, 2048], BF16)
    b["xT"] = T(k, "axT", [128, 2048], BF16)
    b["qb"] = T(k, "aqb", [128, 2048], BF16)
    b["qT"] = T(k, "aqT", [128, 2048], BF16)
    b["Kdup"] = T(k, "aKdup", [128, 8, 128], BF16)
    b["kT"] = [T(k, "akT%d" % i, [128, 1024], BF16) for i in range(2)]
    b["V"] = [T(k, "aV%d" % i, [128, 256], BF16) for i in range(2)]
    b["sm"] = T(k, "asm", [128, 8, 256], F32)
    b["pr"] = T(k, "apr", [128, 8, 256], BF16)
    b["prT"] = [T(k, "aprT%d" % i, [128, 1024], BF16) for i in range(2)]
    b["mx"] = T(k, "amx", [128, 8], F32)
    b["rs"] = T(k, "ars", [128, 8], F32)
    b["es"] = T(k, "aes", [128, 8], F32)
    b["rden"] = T(k, "arden", [128, 8], F32)
    b["o"] = [T(k, "ao%d" % i, [128, 2048], F32) for i in range(2)]
    return b


def phase_att(k, cm, b, hs_d, hs_bufs, ntiles, att_d, att_bufs, nq_d, nkv_d, wq_d, wkv_d, sinks_d, mask_d, mask1_d):
    k.dma("sp", lambda e: e.dma_start(out=b["nq"].t[:], in_=nq_d[:, :]), writes=[b["nq"].b])
    k.dma("sp", lambda e: e.dma_start(out=b["nkv"].t[:], in_=nkv_d[:, :]), writes=[b["nkv"].b])
    k.dma("sp", lambda e: e.dma_start(out=b["mask"].t[:], in_=mask_d[:, :]), writes=[b["mask"].b])
    k.dma("sp", lambda e: e.dma_start(out=b["mask1"].t[:], in_=mask1_d[:, :]), writes=[b["mask1"].b])
    k.dma("sp", lambda e: e.dma_start(out=b["sinks"].t[:], in_=sinks_d.partition_broadcast(128)), writes=[b["sinks"].b])
    load_w(k, cm, b["Wq"], wq_d, 2048, 16, scale=b["nq"])
    load_w(k, cm, b["Wkv"], wkv_d, 512, 16, scale=b["nkv"])
    k.op("pool", lambda e: e.memset(b["Kdup"].t[:], 0.0), writes=[b["Kdup"].b])
    junk, ss, sq, rstd, xn, xT, qb, qT = b["junk"], b["ss"], b["sq"], b["rstd"], b["xn"], b["xT"], b["qb"], b["qT"]
    sm, pr, mx, rs, es, rden = b["sm"], b["pr"], b["mx"], b["rs"], b["es"], b["rden"]
    for i in range(ntiles):
        r0 = i * 128
        h_ = b["h"][i % 2]
        kTc, kTp = b["kT"][i % 2], b["kT"][(i + 1) % 2]
        Vc, Vp = b["V"][i % 2], b["V"][(i + 1) % 2]
        k.dma("sp", lambda e: e.dma_start(out=h_.t[:], in_=hs_d[r0:r0 + 128, :]), reads=[hs_bufs[i]], writes=[h_.b])
        rms_rstd(k, h_, junk, ss, sq, rstd)
        k.op("act", lambda e: e.activation(out=xn.t[:], in_=h_.t[:], func=AF.Copy, scale=rstd.t[:, 0:1]),
             reads=[h_.b, rstd.b], writes=[xn.b])
        transpose_tile(k, cm, xn, xT)
        pkv = cm.ps_x[0]
        mm_group(k, pkv, xT.t, xT.b, b["Wkv"], 0, 512)
        kview = pkv.t[:, 0:256].rearrange("p (g d) -> p g d", d=64)
        kz4 = b["Kdup"].t[:].rearrange("p (g two) d -> p g two d", two=2)
        k.op("act", lambda e: e.copy(out=kz4[:, :, 0, 0:64], in_=kview), reads=[pkv.b], writes=[b["Kdup"].b])
        k.op("dve", lambda e: e.tensor_copy(out=kz4[:, :, 1, 64:128], in_=kview), reads=[pkv.b], writes=[b["Kdup"].b])
        k.op("act", lambda e: e.copy(out=Vc.t[:], in_=pkv.t[:, 256:512]), reads=[pkv.b], writes=[Vc.b])
        kd2 = V(b["Kdup"].t[:].rearrange("p g d -> p (g d)"), None)
        kd2.b = b["Kdup"].b
        transpose_tile(k, cm, kd2, kTc, nchunks=8)
        if i == 0:
            continue
        for ng in range(4):
            mm_group(k, cm.ps_o[ng], xT.t, xT.b, b["Wq"], ng * 512, 512)
            k.op("act", lambda e, ng=ng: e.activation(out=qb.t[:, ng * 512:(ng + 1) * 512], in_=cm.ps_o[ng].t[:],
                                                      func=AF.Copy, scale=0.125),
                 reads=[cm.ps_o[ng].b], writes=[qb.b])
        transpose_tile(k, cm, qb, qT)
        if ATT_STOP == 1:
            continue
        msk = b["mask1"] if i == 1 else b["mask"]
        o = b["o"][i % 2]
        for g in range(4):
            for j in range(8):
                hq = 8 * g + j
                c = hq // 2
                off = (hq % 2) * 64
                po = cm.ps_o[j // 2]
                cb = (j % 2) * 256
                k.op("pe", lambda e, c=c, off=off, po=po, cb=cb, g=g: e.matmul(
                    out=po.t[:, cb:cb + 128], lhsT=qT.t[:, c * 128:(c + 1) * 128],
                    rhs=kTp.t[:, (2 * g + off // 64) * 128:(2 * g + off // 64 + 1) * 128], start=True, stop=True),
                    reads=[qT.b, kTp.b], writes=[po.b], tok=False)
                k.op("pe", lambda e, c=c, off=off, po=po, cb=cb, g=g: e.matmul(
                    out=po.t[:, cb + 128:cb + 256], lhsT=qT.t[:, c * 128:(c + 1) * 128],
                    rhs=kTc.t[:, (2 * g + off // 64) * 128:(2 * g + off // 64 + 1) * 128], start=True, stop=True),
                    reads=[qT.b, kTc.b], writes=[po.b], tok=(j % 2 == 1))
            if ATT_STOP == 2:
                k.op("pe", lambda e: e.transpose(out=cm.ps_x[1].t[:, 0:128], in_=cm.identf.t[:], identity=cm.identf.t[:]),
                     reads=[cm.identf.b], writes=[cm.ps_x[1].b])
                continue
            sA = cm.psA[:, :].rearrange("p (j n) -> p j n", n=256)
            k.op("dve", lambda e: e.tensor_tensor(out=sm.t[:], in0=sA, in1=msk.t[:].unsqueeze(1).to_broadcast([128, 8, 256]),
                                                  op=ALU.add), reads=cm.bankA + [msk.b], writes=[sm.b])
            k.op("dve", lambda e: e.tensor_reduce(out=mx.t[:], in_=sm.t[:], axis=AX.X, op=ALU.max), reads=[sm.b], writes=[mx.b])
            k.op("dve", lambda e, g=g: e.tensor_tensor(out=mx.t[:], in0=mx.t[:], in1=b["sinks"].t[:, 8 * g:8 * g + 8], op=ALU.max),
                 reads=[b["sinks"].b], writes=[mx.b])
            k.op("dve", lambda e: e.tensor_tensor(out=sm.t[:], in0=sm.t[:], in1=mx.t[:].unsqueeze(2).to_broadcast([128, 8, 256]),
                                                  op=ALU.subtract), reads=[mx.b], writes=[sm.b])
            k.op("act", lambda e: e.activation(out=pr.t[:], in_=sm.t[:], func=AF.Exp), reads=[sm.b], writes=[pr.b])
            k.op("dve", lambda e: e.tensor_reduce(out=rs.t[:], in_=pr.t[:], axis=AX.X, op=ALU.add), reads=[pr.b], writes=[rs.b])
            k.op("dve", lambda e, g=g: e.tensor_tensor(out=es.t[:], in0=b["sinks"].t[:, 8 * g:8 * g + 8], in1=mx.t[:],
                                                       op=ALU.subtract), reads=[b["sinks"].b, mx.b], writes=[es.b])
            k.op("act", lambda e: e.activation(out=es.t[:], in_=es.t[:], func=AF.Exp), reads=[], writes=[es.b])
            k.op("dve", lambda e: e.tensor_tensor(out=rs.t[:], in0=rs.t[:], in1=es.t[:], op=ALU.add), reads=[es.b], writes=[rs.b])
            k.op("dve", lambda e: e.reciprocal(out=rden.t[:], in_=rs.t[:]), reads=[rs.b], writes=[rden.b])
            if ATT_STOP == 3:
                continue
            for half in range(2):
                pst = cm.ps_tr[half]
                for j in range(8):
                    k.op("pe", lambda e, j=j, half=half, pst=pst: e.transpose(
                        out=pst.t[:, j * 128:(j + 1) * 128], in_=pr.t[:, j, half * 128:(half + 1) * 128],
                        identity=cm.ident.t[:]), reads=[pr.b, cm.ident.b], writes=[pst.b], tok=(j == 7))
                eng = "dve" if half == 0 else "act"
                if half == 0:
                    k.op("dve", lambda e, pst=pst: e.tensor_copy(out=b["prT"][0].t[:], in_=pst.t[:]), reads=[pst.b],
                         writes=[b["prT"][0].b])
                else:
                    k.op("act", lambda e, pst=pst: e.copy(out=b["prT"][1].t[:], in_=pst.t[:]), reads=[pst.b],
                         writes=[b["prT"][1].b])
            if ATT_STOP == 4:
                continue
            pov = cm.ps_x[1]
            for j in range(8):
                k.op("pe", lambda e, j=j, g=g: e.matmul(out=pov.t[:, j * 64:(j + 1) * 64], lhsT=b["prT"][0].t[:, j * 128:(j + 1) * 128],
                                                        rhs=Vp.t[:, g * 64:(g + 1) * 64], start=True, stop=False),
                     reads=[b["prT"][0].b, Vp.b], writes=[pov.b], tok=False)
                k.op("pe", lambda e, j=j, g=g: e.matmul(out=pov.t[:, j * 64:(j + 1) * 64], lhsT=b["prT"][1].t[:, j * 128:(j + 1) * 128],
                                                        rhs=Vc.t[:, g * 64:(g + 1) * 64], start=False, stop=True),
                     reads=[b["prT"][1].b, Vc.b], writes=[pov.b], tok=(j == 7))
            k.op("dve", lambda e, g=g: e.tensor_tensor(
                out=o.t[:, g * 512:(g + 1) * 512].rearrange("p (j d) -> p j d", d=64),
                in0=pov.t[:, :].rearrange("p (j d) -> p j d", d=64),
                in1=rden.t[:].unsqueeze(2).to_broadcast([128, 8, 64]), op=ALU.mult),
                reads=[pov.b, rden.b], writes=[o.b])
        ro = (i - 1) * 128
        k.dma("pool", lambda e: e.dma_start(out=att_d[ro:ro + 128, :], in_=o.t[:]), reads=[o.b], writes=[att_bufs[i - 1]])


def build_test_att(ntiles):
    nc = bass.Bass("TRN2", target_bir_lowering=False)
    hin_d = nc.dram_tensor("h_in", [ntiles * 128, D], F32, kind="ExternalInput").ap()
    att_d = nc.dram_tensor("att_out", [(ntiles - 1) * 128, D], F32, kind="ExternalOutput").ap()
    nq_d = nc.dram_tensor("nq_in", [128, 16], F32, kind="ExternalInput").ap()
    nkv_d = nc.dram_tensor("nkv_in", [128, 16], F32, kind="ExternalInput").ap()
    wq_d = nc.dram_tensor("wq_in", [D, D], F32, kind="ExternalInput").ap()
    wkv_d = nc.dram_tensor("wkv_in", [D, 512], F32, kind="ExternalInput").ap()
    sinks_d = nc.dram_tensor("sinks_in", [32], F32, kind="ExternalInput").ap()
    mask_d = nc.dram_tensor("mask_in", [128, 256], F32, kind="ExternalInput").ap()
    mask1_d = nc.dram_tensor("mask1_in", [128, 256], F32, kind="ExternalInput").ap()
    id_d = nc.dram_tensor("ident_in", [128, 128], F32, kind="ExternalInput").ap()
    with ExitStack() as stack:
        k = K(nc, stack)
        cm = Common(k, id_d)
        b = att_alloc(k)
        hs_bufs = [Buf("hs%d" % i) for i in range(ntiles)]
        att_bufs = [Buf("at%d" % i) for i in range(ntiles)]
        phase_att(k, cm, b, hin_d, hs_bufs, ntiles, att_d, att_bufs, nq_d, nkv_d, wq_d, wkv_d, sinks_d, mask_d, mask1_d)
        drain(k)
        print("ninst", k.ninst, "nsem", k.nsem)
    return nc


def band_masks():
    t = np.arange(128)[:, None]
    kk = np.arange(256)[None, :]
    valid = (kk >= t + 1) & (kk <= t + 128)
    m = np.where(valid, 0.0, -1e30).astype(np.float32)
    m1 = m.copy()
    m1[:, :128] = -1e30
    return m, m1


def mlstm_alloc(k):
    mk = lambda name, shape, dt=F32: T(k, name, shape, dt)
    W = mk("mW", [128, 16, 1538], BF16)
    nw = mk("mnw", [128, 16])
    gb = mk("mgb", [128, 2])
    gb15 = mk("mgb15", [128, 2])
    hnw = mk("mhnw", [128, 512])
    tri = mk("mtri", [128, 128])
    sel = mk("msel", [128, 128])
    cmask = mk("mcmask", [128, 128])
    ones = mk("mones", [128, 128])
    one1 = mk("mone1", [128, 1])
    onesb = mk("monesb", [128, 1], BF16)
    C = mk("mC", [128, 2, 512])
    Cb = mk("mCb", [128, 2, 512], BF16)
    n_ = mk("mn", [128, 2])
    nb = mk("mnb", [128, 2], BF16)
    mprev = mk("mmprev", [128, 1])
    xs = [mk("mx%d" % i, [128, 2048]) for i in range(2)]
    junk = mk("mjunk", [128, 2048], BF16)
    ss, sq, rstd = mk("mss", [128, 1]), mk("msq", [128, 1]), mk("mrstd", [128, 1])
    xn = mk("mxn", [128, 2048], BF16)
    xT = mk("mxT", [128, 2048], BF16)
    qkb = mk("mqkb", [128, 512], BF16)
    qkT = mk("mqkT", [128, 512], BF16)
    vb = mk("mvb", [128, 512], BF16)
    sog = mk("msog", [128, 512])
    gs = mk("mgs", [128, 2])
    th = mk("mth", [128, 2])
    li, z, ez, sp, lf = mk("mli", [128, 1]), mk("mz", [128, 1]), mk("mez", [128, 1]), mk("msp", [128, 1]), mk("mlf", [128, 1])
    bcs, gvec, mrow, u, negu, mt = (mk("mbcs", [128, 1]), mk("mgvec", [128, 1]), mk("mmrow", [128, 1]), mk("mu", [128, 1]),
                                    mk("mnegu", [128, 1]), mk("mmt", [128, 1]))
    dg = mk("mdg", [128, 128])
    A = mk("mA", [128, 128])
    wintra = mk("mwintra", [128, 128])
    wia, winter = mk("mwia", [128, 1]), mk("mwinter", [128, 1])
    Pb = mk("mPb", [128, 128], BF16)
    PT = mk("mPT", [128, 128], BF16)
    dintra = mk("mdintra", [128, 1])
    tmp = mk("mtmp", [128, 512])
    num = mk("mnum", [128, 512])
    den, aden, emt, dmax, rden = mk("mden", [128, 1]), mk("maden", [128, 1]), mk("memt", [128, 1]), mk("mdmax", [128, 1]), mk("mrden", [128, 1])
    ssn, t1, sqv, rstd2, sc = mk("mssn", [128, 1]), mk("mt1", [128, 1]), mk("msqv", [128, 1]), mk("mrstd2", [128, 1]), mk("msc", [128, 1])
    ots = [mk("mot%d" % i, [128, 512]) for i in range(2)]
    mb = mk("mmb", [128, 2])
    last2 = mk("mlast2", [128, 2])
    dlt, wstate, deca, decay = mk("mdlt", [128, 1]), mk("mwstate", [128, 1]), mk("mdeca", [128, 1]), mk("mdecay", [128, 1])
    kw = mk("mkw", [128, 256], BF16)

    padneg = mk("mpadneg", [128, 80])
    npm = mk("mnpm", [128, 80])
    return dict(W=W, nw=nw, gb=gb, gb15=gb15, hnw=hnw, tri=tri, sel=sel, cmask=cmask, ones=ones, one1=one1, onesb=onesb, C=C, Cb=Cb, n_=n_, nb=nb, mprev=mprev, xs=xs, junk=junk, ss=ss, sq=sq, rstd=rstd, xn=xn, xT=xT, qkb=qkb, qkT=qkT, vb=vb, sog=sog, gs=gs, th=th, li=li, z=z, ez=ez, sp=sp, lf=lf, bcs=bcs, gvec=gvec, mrow=mrow, u=u, negu=negu, mt=mt, dg=dg, A=A, wintra=wintra, wia=wia, winter=winter, Pb=Pb, PT=PT, dintra=dintra, tmp=tmp, num=num, den=den, aden=aden, emt=emt, dmax=dmax, rden=rden, ssn=ssn, t1=t1, sqv=sqv, rstd2=rstd2, sc=sc, ots=ots, mb=mb, last2=last2, dlt=dlt, wstate=wstate, deca=deca, decay=decay, kw=kw, padneg=padneg, npm=npm)


def phase_mlstm(k, cm, m, xw_d, w4_d, nw_d, gb4_d, hnw_d, tri_d, sel_d, cmask_d, padneg_d, npm_d, nchunks, out_from,
                hg_d, hg_bufs, bg=None):
    g = globals()
    loc = dict(m)
    W = m["W"]
    nw = m["nw"]
    gb = m["gb"]
    gb15 = m["gb15"]
    hnw = m["hnw"]
    tri = m["tri"]
    sel = m["sel"]
    cmask = m["cmask"]
    ones = m["ones"]
    one1 = m["one1"]
    onesb = m["onesb"]
    C = m["C"]
    Cb = m["Cb"]
    n_ = m["n_"]
    nb = m["nb"]
    mprev = m["mprev"]
    xs = m["xs"]
    junk = m["junk"]
    ss = m["ss"]
    sq = m["sq"]
    rstd = m["rstd"]
    xn = m["xn"]
    xT = m["xT"]
    qkb = m["qkb"]
    qkT = m["qkT"]
    vb = m["vb"]
    sog = m["sog"]
    gs = m["gs"]
    th = m["th"]
    li = m["li"]
    z = m["z"]
    ez = m["ez"]
    sp = m["sp"]
    lf = m["lf"]
    bcs = m["bcs"]
    gvec = m["gvec"]
    mrow = m["mrow"]
    u = m["u"]
    negu = m["negu"]
    mt = m["mt"]
    dg = m["dg"]
    A = m["A"]
    wintra = m["wintra"]
    wia = m["wia"]
    winter = m["winter"]
    Pb = m["Pb"]
    PT = m["PT"]
    dintra = m["dintra"]
    tmp = m["tmp"]
    num = m["num"]
    den = m["den"]
    aden = m["aden"]
    emt = m["emt"]
    dmax = m["dmax"]
    rden = m["rden"]
    ssn = m["ssn"]
    t1 = m["t1"]
    sqv = m["sqv"]
    rstd2 = m["rstd2"]
    sc = m["sc"]
    ots = m["ots"]
    mb = m["mb"]
    last2 = m["last2"]
    dlt = m["dlt"]
    wstate = m["wstate"]
    deca = m["deca"]
    decay = m["decay"]
    kw = m["kw"]
    padneg = m["padneg"]
    npm = m["npm"]

    dl = lambda t, src: k.dma("sp", lambda e: e.dma_start(out=t.t[:], in_=src), writes=[t.b])
    dl(nw, nw_d[:, :])
    dl(tri, tri_d[:, :])
    dl(sel, sel_d[:, :])
    dl(cmask, cmask_d[:, :])
    k.dma("sp", lambda e: e.dma_start(out=padneg.t[:, :nchunks], in_=padneg_d[:, :]), writes=[padneg.b])
    k.dma("sp", lambda e: e.dma_start(out=npm.t[:, :nchunks], in_=npm_d[:, :]), writes=[npm.b])
    for t_, val in ((ones, 1.0), (one1, 1.0), (onesb, 1.0)):
        k.op("pool", lambda e, t_=t_, val=val: e.memset(t_.t[:], val), writes=[t_.b])
    eps_ap = cm_eps(k)
    P0, P1, P2, P3 = cm.ps_o
    X0, X1 = cm.ps_x
    for hd in range(4):
        dl(gb, gb4_d[hd, :, :])
        k.dma("sp", lambda e: e.dma_start(out=hnw.t[:], in_=hnw_d[hd * 512:(hd + 1) * 512].partition_broadcast(128)),
              writes=[hnw.b])
        k.op("dve", lambda e: e.tensor_scalar(out=gb15.t[:], in0=gb.t[:], scalar1=1.0 / 15.0, scalar2=None, op0=ALU.mult),
             reads=[gb.b], writes=[gb15.b])
        for t_, val in ((C, 0.0), (Cb, 0.0), (n_, 0.0), (nb, 0.0), (mprev, 0.0)):
            k.op("pool", lambda e, t_=t_, val=val: e.memset(t_.t[:], val), writes=[t_.b])
        load_w(k, cm, W, w4_d[hd], 1538, 16, scale=nw)
        for c in range(nchunks):
            r0 = c * 128
            xt = xs[c % 2]
            ot = ots[c % 2]
            k.dma("sp", lambda e: e.dma_start(out=xt.t[:], in_=xw_d[r0:r0 + 128, :]), writes=[xt.b])
            if bg is not None:
                bg()
            rms_rstd(k, xt, junk, ss, sq, rstd)
            k.op("act", lambda e: e.activation(out=xn.t[:], in_=xt.t[:], func=AF.Copy, scale=rstd.t[:, 0:1]),
                 reads=[xt.b, rstd.b], writes=[xn.b])
            transpose_tile(k, cm, xn, xT)
            full = c >= out_from
            if full:
                mm_group(k, P0, xT.t, xT.b, W, 0, 512)
            else:
                for kc in range(16):
                    k.op("pe", lambda e, kc=kc: e.matmul(out=P0.t[:, 256:512], lhsT=xT.t[:, kc * 128:(kc + 1) * 128],
                                                         rhs=W.t[:, kc, 256:512], start=(kc == 0), stop=(kc == 15)),
                         reads=[xT.b, W.b], writes=[P0.b], tok=(kc == 15))
            mm_group(k, P1, xT.t, xT.b, W, 512, 512)
            if full:
                mm_group(k, P2, xT.t, xT.b, W, 1024, 512)
            mm_group(k, X0, xT.t, xT.b, W, 1536, 2)
            if full:
                k.op("act", lambda e: e.activation(out=qkb.t[:, 0:256], in_=P0.t[:, 0:256], func=AF.Copy, scale=1.0 / 16.0),
                     reads=[P0.b], writes=[qkb.b])
            k.op("dve", lambda e: e.tensor_copy(out=qkb.t[:, 256:512], in_=P0.t[:, 256:512]), reads=[P0.b], writes=[qkb.b])
            k.op("dve", lambda e: e.tensor_copy(out=vb.t[:], in_=P1.t[:]), reads=[P1.b], writes=[vb.b])
            if full:
                k.op("act", lambda e: e.activation(out=sog.t[:], in_=P2.t[:], func=AF.Sigmoid), reads=[P2.b], writes=[sog.b])
                k.op("pool", lambda e: e.tensor_tensor(out=sog.t[:], in0=sog.t[:], in1=hnw.t[:], op=ALU.mult),
                     reads=[hnw.b], writes=[sog.b])
            k.op("dve", lambda e: e.tensor_copy(out=gs.t[:], in_=X0.t[:, 0:2]), reads=[X0.b], writes=[gs.b])
            if full:
                transpose_tile(k, cm, qkb, qkT, nchunks=4)
            for col in range(2):
                k.op("act", lambda e, col=col: e.activation(out=th.t[:, col:col + 1], in_=gs.t[:, col:col + 1], func=AF.Tanh,
                                                            scale=1.0 / 15.0, bias=gb15.t[:, col:col + 1]),
                     reads=[gs.b, gb15.b], writes=[th.b])
            k.op("dve", lambda e: e.tensor_scalar(out=li.t[:], in0=th.t[:, 0:1], scalar1=15.0, scalar2=None, op0=ALU.mult),
                 reads=[th.b], writes=[li.b])
            k.op("dve", lambda e: e.tensor_tensor(out=li.t[:], in0=li.t[:], in1=padneg.t[:, c:c + 1], op=ALU.add),
                 reads=[padneg.b], writes=[li.b])
            k.op("act", lambda e: e.activation(out=ez.t[:], in_=th.t[:, 1:2], func=AF.Exp, scale=-15.0), reads=[th.b], writes=[ez.b])
            k.op("act", lambda e: e.activation(out=sp.t[:], in_=ez.t[:], func=AF.Ln, bias=one1.t[:, 0:1]),
                 reads=[ez.b, one1.b], writes=[sp.b])
            k.op("dve", lambda e: e.tensor_tensor(out=lf.t[:], in0=sp.t[:], in1=npm.t[:, c:c + 1], op=ALU.mult),
                 reads=[sp.b, npm.b], writes=[lf.b])
            k.op("pe", lambda e: e.matmul(out=X0.t[:, 4:5], lhsT=tri.t[:], rhs=lf.t[:], start=True, stop=True),
                 reads=[tri.b, lf.b], writes=[X0.b])
            k.op("dve", lambda e: e.tensor_copy(out=bcs.t[:], in_=X0.t[:, 4:5]), reads=[X0.b], writes=[bcs.b])
            k.op("dve", lambda e: e.tensor_tensor(out=gvec.t[:], in0=li.t[:], in1=bcs.t[:], op=ALU.subtract),
                 reads=[li.b, bcs.b], writes=[gvec.b])
            k.op("dve", lambda e: e.tensor_scalar(out=dg.t[:], in0=cm.identf.t[:], scalar1=gvec.t[:, 0:1], scalar2=None, op0=ALU.mult),
                 reads=[cm.identf.b, gvec.b], writes=[dg.b])
            k.op("pe", lambda e: e.matmul(out=X1.t[:, 0:128], lhsT=ones.t[:], rhs=dg.t[:], start=True, stop=True),
                 reads=[ones.b, dg.b], writes=[X1.b])
            k.op("dve", lambda e: e.tensor_tensor(out=A.t[:], in0=X1.t[:, 0:128], in1=cmask.t[:], op=ALU.add),
                 reads=[X1.b, cmask.b], writes=[A.b])
            k.op("dve", lambda e: e.tensor_reduce(out=mrow.t[:], in_=A.t[:], axis=AX.X, op=ALU.max), reads=[A.b], writes=[mrow.b])
            k.op("dve", lambda e: e.tensor_tensor(out=u.t[:], in0=mrow.t[:], in1=mprev.t[:], op=ALU.max),
                 reads=[mrow.b, mprev.b], writes=[u.b])
            k.op("dve", lambda e: e.tensor_scalar(out=negu.t[:], in0=u.t[:], scalar1=-1.0, scalar2=None, op0=ALU.mult),
                 reads=[u.b], writes=[negu.b])
            k.op("dve", lambda e: e.tensor_tensor(out=mt.t[:], in0=bcs.t[:], in1=u.t[:], op=ALU.add), reads=[bcs.b, u.b], writes=[mt.b])
            if full:
                k.op("act", lambda e: e.activation(out=wintra.t[:], in_=A.t[:], func=AF.Exp, bias=negu.t[:, 0:1]),
                     reads=[A.b, negu.b], writes=[wintra.b])
                k.op("dve", lambda e: e.tensor_tensor(out=wia.t[:], in0=mprev.t[:], in1=u.t[:], op=ALU.subtract),
                     reads=[mprev.b, u.b], writes=[wia.b])
                k.op("act", lambda e: e.activation(out=winter.t[:], in_=wia.t[:], func=AF.Exp), reads=[wia.b], writes=[winter.b])
                for dc in range(2):
                    k.op("pe", lambda e, dc=dc: e.matmul(out=X1.t[:, 128:256], lhsT=qkT.t[:, dc * 128:(dc + 1) * 128],
                                                         rhs=qkT.t[:, (2 + dc) * 128:(3 + dc) * 128], start=(dc == 0), stop=(dc == 1)),
                         reads=[qkT.b], writes=[X1.b], tok=(dc == 1))
                k.op("dve", lambda e: e.scalar_tensor_tensor(out=Pb.t[:], in0=X1.t[:, 128:256], scalar=1.0, in1=wintra.t[:],
                                                             op0=ALU.mult, op1=ALU.mult, accum_out=dintra.t[:, 0:1]),
                     reads=[X1.b, wintra.b], writes=[Pb.b, dintra.b])
                pst = cm.ps_tr[0]
                k.op("pe", lambda e: e.transpose(out=pst.t[:, 0:128], in_=Pb.t[:], identity=cm.ident.t[:]),
                     reads=[Pb.b, cm.ident.b], writes=[pst.b])
                k.op("dve", lambda e: e.tensor_copy(out=PT.t[:], in_=pst.t[:, 0:128]), reads=[pst.b], writes=[PT.b])
                for dc in range(2):
                    k.op("pe", lambda e, dc=dc: e.matmul(out=P0.t[:], lhsT=qkT.t[:, dc * 128:(dc + 1) * 128], rhs=Cb.t[:, dc, :],
                                                         start=(dc == 0), stop=(dc == 1)),
                         reads=[qkT.b, Cb.b], writes=[P0.b], tok=(dc == 1))
                for dc in range(2):
                    k.op("pe", lambda e, dc=dc: e.matmul(out=X0.t[:, 12:13], lhsT=qkT.t[:, dc * 128:(dc + 1) * 128], rhs=nb.t[:, dc:dc + 1],
                                                         start=(dc == 0), stop=(dc == 1)),
                         reads=[qkT.b, nb.b], writes=[X0.b], tok=(dc == 1))
                k.op("pe", lambda e: e.matmul(out=P1.t[:], lhsT=PT.t[:], rhs=vb.t[:], start=True, stop=True),
                     reads=[PT.b, vb.b], writes=[P1.b])
                k.op("act", lambda e: e.activation(out=tmp.t[:], in_=P0.t[:], func=AF.Copy, scale=winter.t[:, 0:1]),
                     reads=[P0.b, winter.b], writes=[tmp.b])
                k.op("dve", lambda e: e.tensor_tensor(out=num.t[:], in0=P1.t[:], in1=tmp.t[:], op=ALU.add),
                     reads=[P1.b, tmp.b], writes=[num.b])
                k.op("dve", lambda e: e.scalar_tensor_tensor(out=den.t[:], in0=X0.t[:, 12:13], scalar=winter.t[:, 0:1], in1=dintra.t[:],
                                                             op0=ALU.mult, op1=ALU.add),
                     reads=[X0.b, winter.b, dintra.b], writes=[den.b])
                k.op("dve", lambda e: e.tensor_scalar(out=aden.t[:], in0=den.t[:], scalar1=-1.0, scalar2=None, op0=ALU.mult),
                     reads=[den.b], writes=[aden.b])
                k.op("dve", lambda e: e.tensor_tensor(out=aden.t[:], in0=aden.t[:], in1=den.t[:], op=ALU.max),
                     reads=[den.b], writes=[aden.b])
                k.op("act", lambda e: e.activation(out=emt.t[:], in_=mt.t[:], func=AF.Exp, scale=-1.0), reads=[mt.b], writes=[emt.b])
                k.op("dve", lambda e: e.tensor_tensor(out=dmax.t[:], in0=aden.t[:], in1=emt.t[:], op=ALU.max),
                     reads=[aden.b, emt.b], writes=[dmax.b])
                k.op("dve", lambda e: e.reciprocal(out=rden.t[:], in_=dmax.t[:]), reads=[dmax.b], writes=[rden.b])
                k.op("act", lambda e: e.activation(out=junk.t[:, 0:512], in_=num.t[:], func=AF.Square, accum_out=ssn.t[:, 0:1]),
                     reads=[num.b], writes=[junk.b, ssn.b])
                k.op("dve", lambda e: e.tensor_tensor(out=t1.t[:], in0=ssn.t[:], in1=rden.t[:], op=ALU.mult), reads=[ssn.b, rden.b], writes=[t1.b])
                k.op("dve", lambda e: e.tensor_tensor(out=t1.t[:], in0=t1.t[:], in1=rden.t[:], op=ALU.mult), reads=[rden.b], writes=[t1.b])
                k.op("act", lambda e: e.activation(out=sqv.t[:], in_=t1.t[:], func=AF.Sqrt, scale=1.0 / 512.0, bias=eps_ap),
                     reads=[t1.b], writes=[sqv.b])
                k.op("dve", lambda e: e.reciprocal(out=rstd2.t[:], in_=sqv.t[:]), reads=[sqv.b], writes=[rstd2.b])
                k.op("dve", lambda e: e.tensor_tensor(out=sc.t[:], in0=rden.t[:], in1=rstd2.t[:], op=ALU.mult), reads=[rden.b, rstd2.b], writes=[sc.b])
                k.op("dve", lambda e: e.scalar_tensor_tensor(out=ot.t[:], in0=num.t[:], scalar=sc.t[:, 0:1], in1=sog.t[:],
                                                             op0=ALU.mult, op1=ALU.mult),
                     reads=[num.b, sc.b, sog.b], writes=[ot.b])
                ti = c - out_from
                k.dma("pool", lambda e: e.dma_start(out=hg_d[ti * 128:(ti + 1) * 128, hd * 512:(hd + 1) * 512], in_=ot.t[:]),
                      reads=[ot.b], writes=[hg_bufs[ti]])
            k.op("dve", lambda e: e.tensor_copy(out=mb.t[:, 0:1], in_=mt.t[:]), reads=[mt.b], writes=[mb.b])
            k.op("dve", lambda e: e.tensor_copy(out=mb.t[:, 1:2], in_=bcs.t[:]), reads=[bcs.b], writes=[mb.b])
            k.op("pe", lambda e: e.matmul(out=X0.t[:, 8:10], lhsT=sel.t[:], rhs=mb.t[:], start=True, stop=True),
                 reads=[sel.b, mb.b], writes=[X0.b])
            k.op("dve", lambda e: e.tensor_copy(out=last2.t[:], in_=X0.t[:, 8:10]), reads=[X0.b], writes=[last2.b])
            k.op("dve", lambda e: e.tensor_tensor(out=dlt.t[:], in0=last2.t[:, 1:2], in1=last2.t[:, 0:1], op=ALU.subtract),
                 reads=[last2.b], writes=[dlt.b])
            k.op("act", lambda e: e.activation(out=wstate.t[:], in_=gvec.t[:], func=AF.Exp, bias=dlt.t[:, 0:1]),
                 reads=[gvec.b, dlt.b], writes=[wstate.b])
            k.op("dve", lambda e: e.tensor_tensor(out=deca.t[:], in0=dlt.t[:], in1=mprev.t[:], op=ALU.add),
                 reads=[dlt.b, mprev.b], writes=[deca.b])
            k.op("act", lambda e: e.activation(out=decay.t[:], in_=deca.t[:], func=AF.Exp), reads=[deca.b], writes=[decay.b])
            k.op("dve", lambda e: e.tensor_scalar(out=kw.t[:], in0=qkb.t[:, 256:512], scalar1=wstate.t[:, 0:1], scalar2=None, op0=ALU.mult),
                 reads=[qkb.b, wstate.b], writes=[kw.b])
            k.op("pe", lambda e: e.matmul(out=P2.t[:], lhsT=kw.t[:, 0:128], rhs=vb.t[:], start=True, stop=True),
                 reads=[kw.b, vb.b], writes=[P2.b])
            k.op("pe", lambda e: e.matmul(out=P3.t[:], lhsT=kw.t[:, 128:256], rhs=vb.t[:], start=True, stop=True),
                 reads=[kw.b, vb.b], writes=[P3.b])
            k.op("pe", lambda e: e.matmul(out=X0.t[:, 16:17], lhsT=kw.t[:, 0:128], rhs=onesb.t[:], start=True, stop=True),
                 reads=[kw.b, onesb.b], writes=[X0.b], tok=False)
            k.op("pe", lambda e: e.matmul(out=X0.t[:, 17:18], lhsT=kw.t[:, 128:256], rhs=onesb.t[:], start=True, stop=True),
                 reads=[kw.b, onesb.b], writes=[X0.b])
            k.op("dve", lambda e: e.scalar_tensor_tensor(out=C.t[:, 0, :], in0=C.t[:, 0, :], scalar=decay.t[:, 0:1], in1=P2.t[:],
                                                         op0=ALU.mult, op1=ALU.add), reads=[decay.b, P2.b], writes=[C.b])
            k.op("dve", lambda e: e.scalar_tensor_tensor(out=C.t[:, 1, :], in0=C.t[:, 1, :], scalar=decay.t[:, 0:1], in1=P3.t[:],
                                                         op0=ALU.mult, op1=ALU.add), reads=[decay.b, P3.b], writes=[C.b])
            k.op("act", lambda e: e.copy(out=Cb.t[:], in_=C.t[:]), reads=[C.b], writes=[Cb.b])
            k.op("dve", lambda e: e.scalar_tensor_tensor(out=n_.t[:], in0=n_.t[:], scalar=decay.t[:, 0:1], in1=X0.t[:, 16:18],
                                                         op0=ALU.mult, op1=ALU.add), reads=[decay.b, X0.b], writes=[n_.b])
            k.op("dve", lambda e: e.tensor_copy(out=nb.t[:], in_=n_.t[:]), reads=[n_.b], writes=[nb.b])
            k.op("dve", lambda e: e.tensor_copy(out=mprev.t[:], in_=last2.t[:, 0:1]), reads=[last2.b], writes=[mprev.b])


def mlstm_consts():
    s = np.arange(128)[:, None]
    t = np.arange(128)[None, :]
    tri = (s <= t).astype(np.float32)
    sel = np.zeros((128, 128), np.float32)
    sel[127, :] = 1.0
    cmask = np.where(t <= s, 0.0, -1e30).astype(np.float32)
    return tri, sel, cmask


def mlstm_inmap(x_b, a_norm, a_w_in, gate_bias, head_norm, hd):
    o0, o1, o2, o3 = 1024, 2048, 4096, 6144
    cols = np.concatenate([np.arange(hd * 256, (hd + 1) * 256), o0 + np.arange(hd * 256, (hd + 1) * 256),
                           o1 + np.arange(hd * 512, (hd + 1) * 512), o2 + np.arange(hd * 512, (hd + 1) * 512),
                           np.array([o3 + hd, o3 + 4 + hd])])
    tri, sel, cmask = mlstm_consts()
    return {
        "x_in": np.ascontiguousarray(x_b),
        "w_in": np.ascontiguousarray(a_w_in[:, cols]),
        "nw_in": np.ascontiguousarray(a_norm.reshape(16, 128).T),
        "gb_in": np.ascontiguousarray(np.broadcast_to(gate_bias[:, hd][None, :], (128, 2))).astype(np.float32),
        "hnw_in": np.ascontiguousarray(head_norm[hd * 512:(hd + 1) * 512]),
        "tri_in": tri, "sel_in": sel, "cmask_in": cmask, "ident_in": np.eye(128, dtype=np.float32),
    }


NT0 = 17
NT1 = 16
NW = 65


def build_fused():
    nc = bass.Bass("TRN2", target_bir_lowering=False)
    di = lambda name, shape: nc.dram_tensor(name, list(shape), F32, kind="ExternalInput").ap()
    xw_d = di("xw", [NW * 128, D])
    p0_d = di("p0_sh", [NT0 * 128, 256])
    p1_d = di("p1_sh", [NT1 * 128, 256])
    w4_d = di("mw4", [4, D, 1538])
    mnw_d = di("mnw", [128, 16])
    gb4_d = di("mgb4", [4, 128, 2])
    hnw_d = di("mhnw", [D])
    tri_d = di("tri_in", [128, 128])
    sel_d = di("sel_in", [128, 128])
    cmask_d = di("cmask_in", [128, 128])
    padneg_d = di("padneg", [128, NW])
    npm_d = di("npm", [128, NW])
    wout0_d = di("wout0", [D, D])
    wout1_d = di("wout1", [D, D])
    peer = []
    for L in range(2):
        peer.append(dict(cw=di("cw%d" % L, [D]), wq=di("pwq%d" % L, [D, D]), k1=di("k1T%d" % L, [8, 128, 128]),
                         k2=di("k2T%d" % L, [8, 128, 128]), u=di("u%d" % L, [16384, D]), v=di("v%d" % L, [16384, D])))
    ple = []
    for L in range(2):
        ple.append(dict(nw=di("enw%d" % L, [128, 16]), wg=di("ewg%d" % L, [D, D]), wp=di("ewp%d" % L, [256, D])))
    fw_d = di("fw", [D])
    nq_d = di("nq", [128, 16])
    nkv_d = di("nkv", [128, 16])
    bwq_d = di("bwq", [D, D])
    wkv_d = di("wkv", [D, 512])
    sinks_d = di("sinks", [32])
    mask_d = di("mask", [128, 256])
    mask1_d = di("mask1", [128, 256])
    id_d = di("ident_in", [128, 128])
    out_d = nc.dram_tensor("out", [NT1 * 128, D], F32, kind="ExternalOutput").ap()
    hg_d = nc.dram_tensor("hg_scr", [NT0 * 128, D], F32, kind="Internal").ap()
    hs_d = nc.dram_tensor("hs_scr", [NT0 * 128, D], F32, kind="Internal").ap()
    att_d = nc.dram_tensor("att_scr", [NT1 * 128, D], F32, kind="Internal").ap()
    x0 = (NW - NT0) * 128
    tbl = []
    for L in range(2):
        for nm in ("u", "v"):
            tb = nc.dram_tensor("tb_%s%d" % (nm, L), [16384, D], BF16, kind="Internal").ap()
            tbl.append((peer[L][nm], tb))
            peer[L][nm + "b"] = tb
    with ExitStack() as stack:
        k = K(nc, stack)
        cm = Common(k, id_d)
        cm_eps(k)
        hg_bufs = [Buf("hg%d" % i) for i in range(NT0)]
        hs_bufs = [Buf("hs%d" % i) for i in range(NT0)]
        att_bufs = [Buf("at%d" % i) for i in range(NT1)]

        def phase(alloc, fn):
            with ExitStack() as ps:
                k.stack = ps
                b = alloc(k)
                fn(b)
                k.barrier()
            k.stack = stack

        conv = [(src_, dst_, i) for (src_, dst_) in tbl for i in range(128)]
        conv.reverse()

        def run_mlstm(m):
            stg = [T(k, "cstg%d" % i, [128, 2048], BF16) for i in range(4)]
            cnt = [0]

            def bg(n=2):
                for _ in range(n):
                    if not conv:
                        return
                    src_, dst_, i = conv.pop()
                    s = stg[cnt[0] % 4]
                    cnt[0] += 1
                    k.dma("pool", lambda e: e.dma_start(out=s.t[:], in_=src_[i * 128:(i + 1) * 128, :]), writes=[s.b])
                    k.dma("sp", lambda e: e.dma_start(out=dst_[i * 128:(i + 1) * 128, :], in_=s.t[:]), reads=[s.b])
            phase_mlstm(k, cm, m, xw_d, w4_d, mnw_d, gb4_d, hnw_d, tri_d, sel_d, cmask_d,
                        padneg_d, npm_d, NW, NW - NT0, hg_d, hg_bufs, bg=bg)
            while conv:
                bg()
        phase(mlstm_alloc, run_mlstm)
        phase(mmres_alloc, lambda b: phase_mmres(k, cm, b, wout0_d, NT0, hg_d, 0, hg_bufs, xw_d, x0, None, hs_d, 0, hs_bufs))
        phase(peer_alloc, lambda b: phase_peer(k, cm, b, hs_d, hs_bufs, list(range(NT0)), peer[0]["cw"], peer[0]["wq"],
                                               peer[0]["k1"], peer[0]["k2"], peer[0]["ub"], peer[0]["vb"]))
        phase(ple_alloc, lambda b: phase_ple(k, cm, b, hs_d, hs_bufs, list(range(NT0)), p0_d, 0, ple[0]["nw"], ple[0]["wg"],
                                             ple[0]["wp"]))
        phase(att_alloc, lambda b: phase_att(k, cm, b, hs_d, hs_bufs, NT0, att_d, att_bufs, nq_d, nkv_d, bwq_d, wkv_d,
                                             sinks_d, mask_d, mask1_d))
        phase(mmres_alloc, lambda b: phase_mmres(k, cm, b, wout1_d, NT1, att_d, 0, att_bufs, hs_d, 128, hs_bufs[1:],
                                                 hs_d, 128, hs_bufs[1:]))
        phase(peer_alloc, lambda b: phase_peer(k, cm, b, hs_d, hs_bufs, list(range(1, NT0)), peer[1]["cw"], peer[1]["wq"],
                                               peer[1]["k1"], peer[1]["k2"], peer[1]["ub"], peer[1]["vb"]))
        phase(ple_alloc, lambda b: phase_ple(k, cm, b, hs_d, hs_bufs, list(range(1, NT0)), p1_d, -128, ple[1]["nw"],
                                             ple[1]["wg"], ple[1]["wp"], final=(fw_d, out_d, 0)))
        drain(k)
        print("fused ninst", k.ninst, "nsem", k.nsem)
    return nc


def _r16(v):
    return np.ascontiguousarray(np.asarray(v, np.float32).reshape(16, 128).T)


def kernel(x, p, a_norm, a_w_in, a_gate_bias, a_head_norm, a_w_out, kv_norm, w_kv,
           b_norm, b_w_q, b_sinks, b_w_out, c_norm, peer_w_q, peer_k1, peer_k2,
           peer_u, peer_v, ple_norm, ple_w_gate, ple_w_proj, final_norm):
    f = lambda a: np.ascontiguousarray(np.asarray(a, dtype=np.float32))
    x = f(x)
    p = f(p)
    B, S, _ = x.shape
    nc = build_fused()
    m, m1 = band_masks()
    tri, sel, cmask = mlstm_consts()
    a_w_in0 = f(a_w_in)[0]
    gbias = f(a_gate_bias)[0]
    o0, o1, o2, o3 = 1024, 2048, 4096, 6144
    w4 = np.zeros((4, D, 1538), np.float32)
    gb4 = np.zeros((4, 128, 2), np.float32)
    for hd in range(4):
        cols = np.concatenate([np.arange(hd * 256, (hd + 1) * 256), o0 + np.arange(hd * 256, (hd + 1) * 256),
                               o1 + np.arange(hd * 512, (hd + 1) * 512), o2 + np.arange(hd * 512, (hd + 1) * 512),
                               np.array([o3 + hd, o3 + 4 + hd])])
        w4[hd] = a_w_in0[:, cols]
        gb4[hd] = np.broadcast_to(gbias[:, hd][None, :], (128, 2))
    shared = {
        "mw4": w4, "mnw": _r16(f(a_norm)[0]), "mgb4": gb4, "mhnw": f(a_head_norm)[0],
        "tri_in": tri, "sel_in": sel, "cmask_in": cmask,
        "wout0": f(a_w_out)[0], "wout1": f(b_w_out)[0], "fw": f(final_norm),
        "nq": _r16(f(b_norm)[0]), "nkv": _r16(kv_norm), "bwq": f(b_w_q)[0], "wkv": f(w_kv), "sinks": f(b_sinks)[0],
        "mask": m, "ident_in": np.eye(128, dtype=np.float32),
    }
    for L in range(2):
        shared["cw%d" % L] = f(c_norm)[L]
        shared["pwq%d" % L] = f(peer_w_q)[L]
        shared["k1T%d" % L] = np.ascontiguousarray(f(peer_k1)[L].transpose(0, 2, 1))
        shared["k2T%d" % L] = np.ascontiguousarray(f(peer_k2)[L].transpose(0, 2, 1))
        shared["u%d" % L] = f(peer_u)[L]
        shared["v%d" % L] = f(peer_v)[L]
        shared["enw%d" % L] = _r16(f(ple_norm)[L])
        shared["ewg%d" % L] = f(ple_w_gate)[L]
        shared["ewp%d" % L] = f(ple_w_proj)[L]
    TPC = S // 4
    WT = NW * 128
    maps = []
    for c in range(NCORES):
        bb, qd = c // 4, c % 4
        e0 = (qd + 1) * TPC
        npad = max(0, WT - e0)

        def window(arr, width, ntok):
            o = np.zeros((ntok, width), np.float32)
            lo = e0 - ntok
            if lo < 0:
                o[-lo:] = arr[0:e0]
            else:
                o[:] = arr[lo:e0]
            return o
        mp = dict(shared)
        mp["xw"] = window(x[bb], D, WT)
        mp["p0_sh"] = window(p[0, bb], 256, NT0 * 128)
        mp["p1_sh"] = np.ascontiguousarray(p[1, bb, e0 - TPC:e0])
        padc = npad // 128
        pn = np.zeros((128, NW), np.float32)
        pn[:, :padc] = -1e30
        pm = -np.ones((128, NW), np.float32)
        pm[:, :padc] = 0.0
        mp["padneg"] = pn
        mp["npm"] = pm
        mp["mask1"] = m1 if qd == 0 else m
        maps.append(mp)
    res = run_bass_kernel_spmd(nc, maps, core_ids=list(range(NCORES)))
    out = np.zeros((B, S, D), np.float32)
    for c in range(NCORES):
        bb, qd = c // 4, c % 4
        out[bb, qd * TPC:(qd + 1) * TPC] = res.results[c]["out"]
    return out
```

```python
from contextlib import ExitStack
import numpy as np
import concourse.bass as bass
import concourse.mybir as mybir
from concourse.bass_utils import run_bass_kernel_spmd

F32 = mybir.dt.float32
BF16 = mybir.dt.bfloat16
I32 = mybir.dt.int32
U32 = mybir.dt.uint32
AF = mybir.ActivationFunctionType
ALU = mybir.AluOpType
AX = mybir.AxisListType

SEM_LIMIT = 20000
ATT_STOP = 0
D = 2048
NCORES = 8


class Buf:
    __slots__ = ("name", "wr", "rd", "pend")

    def __init__(self, name):
        self.name = name
        self.wr = None
        self.rd = []
        self.pend = False


class T:
    def __init__(self, k, name, shape, dtype, psum=False):
        self.t = k.ps(name, shape, dtype) if psum else k.sb(name, shape, dtype)
        self.b = Buf(name)


class K:
    def __init__(self, nc, stack):
        self.nc = nc
        self.stack = stack
        self.sem_stack = stack
        self.eng = {"pe": nc.tensor, "act": nc.scalar, "dve": nc.vector,
                    "pool": nc.gpsimd, "sp": nc.sync}
        self.csem = {}
        self.known = {e: {} for e in self.eng}
        self.sems = {}
        self.nsem = 0
        self.dq = {}
        self.drr = {}
        self.ninst = 0
        self.pe_pend_r = []
        self.pe_pend_w = []
        self.retired = []

    def barrier(self):
        assert not self.pe_pend_r and not self.pe_pend_w
        toks = [(cs[0], cs[1], e) for e, cs in self.csem.items()]
        for q, slots in self.dq.items():
            for s in slots:
                if s[1] > 0:
                    toks.append((s[0], s[1], "dma"))
        for e in self.eng:
            for t in toks:
                if t[2] == e == "pe":
                    continue
                self._need(e, t)

    def new_sem(self, tag):
        key = "%s_%d" % (tag, self.nsem)
        self.nsem += 1
        h = self.sem_stack.enter_context(self.nc.semaphore(key))
        self.sems[key] = h
        return key

    def sb(self, name, shape, dtype):
        self.ntens = getattr(self, "ntens", 0) + 1
        return self.stack.enter_context(self.nc.sbuf_tensor("%s_%d" % (name, self.ntens), list(shape), dtype))

    def ps(self, name, shape, dtype):
        self.ntens = getattr(self, "ntens", 0) + 1
        return self.stack.enter_context(self.nc.psum_tensor("%s_%d" % (name, self.ntens), list(shape), dtype))

    def _need(self, e, tok):
        if tok is None:
            return
        key, val, teng = tok
        if teng == "pe" and e == "pe":
            return
        if self.known[e].get(key, 0) >= val:
            return
        self.eng[e].wait_ge(self.sems[key], val)
        self.known[e][key] = val

    def _waits(self, e, reads, writes):
        for b in reads:
            if b.pend and e != "pe":
                raise RuntimeError("pending PE access on %s" % b.name)
            self._need(e, b.wr)
        for b in writes:
            if b.pend and e != "pe":
                raise RuntimeError("pending PE access on %s" % b.name)
            self._need(e, b.wr)
            for t in b.rd:
                self._need(e, t)

    def _update(self, tok, reads, writes):
        for b in reads:
            b.rd = [t for t in b.rd if t[0] != tok[0]] + [tok]
        for b in writes:
            b.wr = tok
            b.rd = []

    def op(self, e, fn, reads=(), writes=(), tok=True):
        self._waits(e, reads, writes)
        inst = fn(self.eng[e])
        self.ninst += 1
        if not tok:
            assert e == "pe"
            for b in reads:
                b.pend = True
                self.pe_pend_r.append(b)
            for b in writes:
                b.pend = True
                self.pe_pend_w.append(b)
            return None
        cs = self.csem.get(e)
        if cs is None or cs[1] >= SEM_LIMIT:
            cs = [self.new_sem("c" + e), 0]
            self.csem[e] = cs
        cs[1] += 1
        inst.then_inc(self.sems[cs[0]], 1)
        t = (cs[0], cs[1], e)
        if e == "pe" and (self.pe_pend_r or self.pe_pend_w):
            for b in self.pe_pend_r + self.pe_pend_w:
                b.pend = False
            self._update(t, self.pe_pend_r, self.pe_pend_w)
            self.pe_pend_r = []
            self.pe_pend_w = []
        self._update(t, reads, writes)
        return t

    def dma(self, q, fn, reads=(), writes=(), nslots=8):
        self._waits(q, reads, writes)
        if q not in self.dq:
            self.dq[q] = [[self.new_sem("d" + q), 0] for _ in range(nslots)]
            self.drr[q] = 0
        slots = self.dq[q]
        i = self.drr[q]
        self.drr[q] = (i + 1) % len(slots)
        s = slots[i]
        if s[1] > 0:
            self._need(q, (s[0], s[1], "dma"))
        if s[1] >= 30000:
            self.retired.append((s[0], s[1], "dma"))
            s[0] = self.new_sem("d" + q)
            s[1] = 0
        inst = fn(self.eng[q])
        s[1] += 16
        inst.then_inc(self.sems[s[0]], 16)
        t = (s[0], s[1], "dma")
        self._update(t, reads, writes)
        self.ninst += 1
        return t

    def finish(self, bufs):
        for b in bufs:
            self._need("sp", b.wr)


class Common:
    def __init__(self, k, ident_d):
        self.k = k
        self.ident = T(k, "ident", [128, 128], BF16)
        self.identf = T(k, "identf", [128, 128], F32)
        k.dma("sp", lambda e: e.dma_start(out=self.identf.t[:], in_=ident_d[:, :]), writes=[self.identf.b])
        k.op("dve", lambda e: e.tensor_copy(out=self.ident.t[:], in_=self.identf.t[:]),
             reads=[self.identf.b], writes=[self.ident.b])
        self.wstage = [T(k, "wstage%d" % i, [128, 2048], F32) for i in range(2)]
        self.nw = 0
        self.psA = k.ps("psA", [128, 2048], F32)
        self.psB = k.ps("psB", [128, 2048], F32)
        self.ps_o = [V(self.psA[:, i * 512:(i + 1) * 512], "bank%d" % i) for i in range(4)]
        self.ps_tr = [V(self.psB[:, i * 512:(i + 1) * 512].bitcast(BF16), "bank%d" % (4 + i)) for i in range(2)]
        self.ps_x = [V(self.psB[:, (2 + i) * 512:(3 + i) * 512], "bank%d" % (6 + i)) for i in range(2)]
        self.ps_y = [V(self.psB[:, i * 512:(i + 1) * 512], None) for i in range(4)]
        self.ps_y[0].b = self.ps_tr[0].b
        self.ps_y[1].b = self.ps_tr[1].b
        self.ps_y[2].b = self.ps_x[0].b
        self.ps_y[3].b = self.ps_x[1].b
        self.bankA = [v.b for v in self.ps_o]
        self.bankB = [v.b for v in self.ps_y]


class V:
    def __init__(self, ap, name):
        self.t = ap
        self.b = Buf(name) if name is not None else None


def load_w(k, cm, W, wd, ncols, kchunks, scale=None, col0=0):
    for kc in range(kchunks):
        for c0 in range(0, ncols, 2048):
            cw = min(2048, ncols - c0)
            st = cm.wstage[cm.nw % 2]
            cm.nw += 1
            k.dma("sp", lambda e, st=st, kc=kc, c0=c0, cw=cw: e.dma_start(
                out=st.t[:, :cw], in_=wd[kc * 128:(kc + 1) * 128, c0:c0 + cw]), writes=[st.b])
            if scale is None:
                k.op("pool", lambda e, st=st, kc=kc, c0=c0, cw=cw: e.tensor_copy(
                    out=W.t[:, kc, col0 + c0:col0 + c0 + cw], in_=st.t[:, :cw]),
                    reads=[st.b], writes=[W.b])
            else:
                k.op("pool", lambda e, st=st, kc=kc, c0=c0, cw=cw: e.tensor_scalar(
                    out=W.t[:, kc, col0 + c0:col0 + c0 + cw], in0=st.t[:, :cw],
                    scalar1=scale.t[:, kc:kc + 1], scalar2=None, op0=ALU.mult),
                    reads=[st.b, scale.b], writes=[W.b])


def transpose_tile(k, cm, src_bf, dstT, nchunks=16):
    for half in range((nchunks + 7) // 8):
        pst = cm.ps_tr[half % 2]
        n = min(8, nchunks - half * 8)
        for j in range(n):
            kc = half * 8 + j
            k.op("pe", lambda e, j=j, kc=kc, pst=pst: e.transpose(
                out=pst.t[:, j * 128:(j + 1) * 128], in_=src_bf.t[:, kc * 128:(kc + 1) * 128],
                identity=cm.ident.t[:]), reads=[src_bf.b, cm.ident.b], writes=[pst.b], tok=(j == n - 1))
        k.op("dve", lambda e, half=half, pst=pst, n=n: e.tensor_copy(
            out=dstT.t[:, half * 1024:half * 1024 + n * 128], in_=pst.t[:, :n * 128]),
            reads=[pst.b], writes=[dstT.b])


def mm_group(k, out_ps, lhsT_t, lhsT_b, W, col0, ncols, kchunks=16):
    for kc in range(kchunks):
        k.op("pe", lambda e, kc=kc: e.matmul(
            out=out_ps.t[:, :ncols], lhsT=lhsT_t[:, kc * 128:(kc + 1) * 128],
            rhs=W.t[:, kc, col0:col0 + ncols], start=(kc == 0), stop=(kc == kchunks - 1)),
            reads=[lhsT_b, W.b], writes=[out_ps.b], tok=(kc == kchunks - 1))


def mmres_alloc(k):
    return {
        "W": T(k, "rW", [128, 16, 2048], BF16),
        "a": [T(k, "ra%d" % i, [128, 2048], F32) for i in range(2)],
        "x": [T(k, "rx%d" % i, [128, 2048], F32) for i in range(2)],
        "ab": T(k, "rab", [128, 2048], BF16),
        "aT": T(k, "raT", [128, 2048], BF16),
        "o": [T(k, "ro%d" % i, [128, 2048], F32) for i in range(2)],
    }


def phase_mmres(k, cm, bufs, w_d, ntiles, a_d, a_row0, a_bufs, x_d, x_row0, x_bufs, out_d, out_row0, out_bufs):
    W = bufs["W"]
    load_w(k, cm, W, w_d, 2048, 16)
    for i in range(ntiles):
        a = bufs["a"][i % 2]
        xt = bufs["x"][i % 2]
        ab = bufs["ab"]
        aT = bufs["aT"]
        ot = bufs["o"][i % 2]
        ra, rx, ro = a_row0 + i * 128, x_row0 + i * 128, out_row0 + i * 128
        k.dma("sp", lambda e: e.dma_start(out=a.t[:], in_=a_d[ra:ra + 128, :]),
              reads=[a_bufs[i]] if a_bufs else [], writes=[a.b])
        k.dma("sp", lambda e: e.dma_start(out=xt.t[:], in_=x_d[rx:rx + 128, :]),
              reads=[x_bufs[i]] if x_bufs else [], writes=[xt.b])
        k.op("act", lambda e: e.copy(out=ab.t[:], in_=a.t[:]), reads=[a.b], writes=[ab.b])
        transpose_tile(k, cm, ab, aT)
        for ng in range(4):
            mm_group(k, cm.ps_o[ng], aT.t, aT.b, W, ng * 512, 512)
            k.op("dve", lambda e, ng=ng: e.tensor_tensor(
                out=ot.t[:, ng * 512:(ng + 1) * 512], in0=cm.ps_o[ng].t[:],
                in1=xt.t[:, ng * 512:(ng + 1) * 512], op=ALU.add),
                reads=[cm.ps_o[ng].b, xt.b], writes=[ot.b])
        k.dma("pool", lambda e: e.dma_start(out=out_d[ro:ro + 128, :], in_=ot.t[:]), reads=[ot.b],
              writes=[out_bufs[i]] if out_bufs else [])


def build_test_b1(ntiles):
    nc = bass.Bass("TRN2", target_bir_lowering=False)
    x_d = nc.dram_tensor("x_in", [ntiles * 128, D], F32, kind="ExternalInput").ap()
    hg_d = nc.dram_tensor("hg_in", [ntiles * 128, D], F32, kind="ExternalInput").ap()
    w_d = nc.dram_tensor("w_in", [D, D], F32, kind="ExternalInput").ap()
    id_d = nc.dram_tensor("ident_in", [128, 128], F32, kind="ExternalInput").ap()
    o_d = nc.dram_tensor("out", [ntiles * 128, D], F32, kind="ExternalOutput").ap()
    with ExitStack() as stack:
        k = K(nc, stack)
        cm = Common(k, id_d)
        bufs = mmres_alloc(k)
        phase_mmres(k, cm, bufs, w_d, ntiles, hg_d, 0, None, x_d, 0, None, o_d, 0, None)
        drain(k)
        print("ninst", k.ninst, "nsem", k.nsem)
    return nc


def rms_rstd(k, h, junk, ss, sq, rstd, ncols=D):
    k.op("act", lambda e: e.activation(out=junk.t[:, :ncols], in_=h.t[:, :ncols], func=AF.Square,
                                       accum_out=ss.t[:, 0:1]),
         reads=[h.b], writes=[junk.b, ss.b])
    k.op("act", lambda e: e.activation(out=sq.t[:, 0:1], in_=ss.t[:, 0:1], func=AF.Sqrt,
                                       scale=1.0 / ncols, bias=cm_eps(k)),
         reads=[ss.b], writes=[sq.b])
    k.op("dve", lambda e: e.reciprocal(out=rstd.t[:, 0:1], in_=sq.t[:, 0:1]), reads=[sq.b], writes=[rstd.b])


_EPS_T = {}


def cm_eps(k):
    if id(k) not in _EPS_T:
        t = T(k, "eps_c", [128, 1], F32)
        k.op("pool", lambda e: e.memset(t.t[:], 1e-6), writes=[t.b])
        _EPS_T.clear()
        _EPS_T[id(k)] = t
    t = _EPS_T[id(k)]
    k._need("act", t.b.wr)
    return t.t[:, 0:1]


def peer_alloc(k):
    b = {}
    b["Wq"] = T(k, "pWq", [128, 16, 2048], BF16)
    b["kT"] = T(k, "pkT", [128, 16, 128], BF16)
    b["cw"] = T(k, "pcw", [128, 2048], F32)
    b["h"] = T(k, "ph", [128, 2048], F32)
    b["junk"] = T(k, "pjunk", [128, 2048], BF16)
    b["ss"] = T(k, "pss", [128, 1], F32)
    b["sq"] = T(k, "psq", [128, 1], F32)
    b["rstd"] = T(k, "prstd", [128, 1], F32)
    b["xn"] = T(k, "pxn", [128, 2048], BF16)
    b["xT"] = T(k, "pxT", [128, 2048], BF16)
    b["qb"] = T(k, "pqb", [128, 2048], BF16)
    b["qT"] = b["xT"]
    b["S"] = T(k, "pS", [128, 2048], F32)
    b["kst"] = V(b["S"].t[:].rearrange("p (c n) -> p c n", n=128), None)
    b["kst"].b = b["S"].b
    b["v"] = T(k, "pv", [128, 16, 16], F32)
    b["ix"] = T(k, "pix", [128, 16, 16], U32)
    b["ixf"] = T(k, "pixf", [128, 16, 16], F32)
    b["cand"] = T(k, "pcand", [128, 8, 256], F32)
    b["cand2"] = T(k, "pcand2", [128, 8, 256], F32)
    b["cidx"] = T(k, "pcidx", [128, 8, 256], F32)
    b["ts"] = T(k, "pts", [128, 8, 16], F32)
    b["tsm"] = T(k, "ptsm", [128, 8, 16], F32)
    b["e"] = T(k, "pe_", [128, 8, 16], F32)
    b["esum"] = T(k, "pesum", [128, 8], F32)
    b["rinv"] = T(k, "prinv", [128, 8], F32)
    b["g"] = T(k, "pg", [128, 128], F32)
    b["E"] = T(k, "pE", [128, 16, 256], F32)
    b["eidx"] = T(k, "peidx", [128, 128], F32)
    b["idxT"] = T(k, "pidxT", [128, 128], I32)
    b["gT"] = T(k, "pgT", [128, 128], F32)
    b["actv"] = T(k, "pactv", [128, 128], F32)
    b["gel"] = T(k, "pgel", [128, 128], F32)
    b["coefT"] = T(k, "pcoefT", [128, 128], F32)
    b["W2"] = T(k, "pW2", [128, 256], BF16)
    b["ue"] = [T(k, "pue%d" % i, [128, 2048], BF16) for i in range(4)]
    b["ve"] = [T(k, "pve%d" % i, [128, 2048], BF16) for i in range(4)]
    b["lh"] = [T(k, "plh%d" % i, [128, 128], BF16) for i in range(2)]
    Eb = b["E"].t[:].rearrange("p a c -> p (a c)").bitcast(BF16)
    b["E_alias"] = []
    for i in range(4):
        v_ = V(Eb[:, i * 2048:(i + 1) * 2048], "ueA%d" % i)
        b["ue"].append(v_)
        b["E_alias"].append(v_.b)
    for nm in ("cand2", "cidx"):
        cb = b[nm].t[:].rearrange("p a c -> p (a c)").bitcast(BF16)
        b[nm + "_alias"] = []
        for i in range(2):
            v_ = V(cb[:, i * 2048:(i + 1) * 2048], "veA_%s%d" % (nm, i))
            b["ve"].append(v_)
            b[nm + "_alias"].append(v_.b)
    b["o"] = b["S"]
    k.op("pool", lambda e: e.memset(b["W2"].t[:], 0.0), writes=[b["W2"].b])
    k.op("pool", lambda e: e.memset(b["W2"].t[:, 127:128], 1.0), writes=[b["W2"].b])
    return b


def phase_peer(k, cm, b, hs_d, hs_bufs, tiles, cw_d, wq_d, k1T_d, k2T_d, u_d, v_d):
    load_w(k, cm, b["Wq"], wq_d, 2048, 16)
    k.dma("sp", lambda e: e.dma_start(out=b["cw"].t[:], in_=cw_d.partition_broadcast(128)), writes=[b["cw"].b])
    for half, kd in enumerate((k1T_d, k2T_d)):
        for h in range(8):
            c = 2 * h + half
            k.dma("sp", lambda e, c=c, h=h, kd=kd: e.dma_start(out=b["kst"].t[:, c, :], in_=kd[h, :, :]),
                  writes=[b["kst"].b])
    k.op("pool", lambda e: e.tensor_copy(out=b["kT"].t[:], in_=b["kst"].t[:]), reads=[b["kst"].b], writes=[b["kT"].b])

    h_, junk, ss, sq, rstd = b["h"], b["junk"], b["ss"], b["sq"], b["rstd"]
    xn, xT, qb, qT, S = b["xn"], b["xT"], b["qb"], b["qT"], b["S"]
    v, ix, ixf, cand, cand2, cidx, ts = b["v"], b["ix"], b["ixf"], b["cand"], b["cand2"], b["cidx"], b["ts"]
    for i in tiles:
        r0 = i * 128
        k.dma("sp", lambda e: e.dma_start(out=h_.t[:], in_=hs_d[r0:r0 + 128, :]), reads=[hs_bufs[i]], writes=[h_.b])
        rms_rstd(k, h_, junk, ss, sq, rstd)
        k.op("dve", lambda e: e.scalar_tensor_tensor(out=xn.t[:], in0=h_.t[:], scalar=rstd.t[:, 0:1], in1=b["cw"].t[:],
                                                     op0=ALU.mult, op1=ALU.mult),
             reads=[h_.b, rstd.b, b["cw"].b], writes=[xn.b])
        transpose_tile(k, cm, xn, xT)
        for ng in range(4):
            mm_group(k, cm.ps_o[ng], xT.t, xT.b, b["Wq"], ng * 512, 512)
            k.op("act", lambda e, ng=ng: e.copy(out=qb.t[:, ng * 512:(ng + 1) * 512], in_=cm.ps_o[ng].t[:]),
                 reads=[cm.ps_o[ng].b], writes=[qb.b])
        transpose_tile(k, cm, qb, qT)
        for c in range(16):
            po = cm.ps_o[c // 4]
            k.op("pe", lambda e, c=c, po=po: e.matmul(out=po.t[:, (c % 4) * 128:(c % 4 + 1) * 128],
                                                      lhsT=qT.t[:, c * 128:(c + 1) * 128], rhs=b["kT"].t[:, c, :],
                                                      start=True, stop=True),
                 reads=[qT.b, b["kT"].b], writes=[po.b], tok=(c % 4 == 3))
        for ng in range(4):
            k.op("act", lambda e, ng=ng: e.copy(out=S.t[:, ng * 512:(ng + 1) * 512], in_=cm.ps_o[ng].t[:]),
                 reads=[cm.ps_o[ng].b], writes=[S.b])
        for c in range(16):
            Sc = S.t[:, c * 128:(c + 1) * 128]
            k.op("dve", lambda e, c=c, Sc=Sc: e.max(out=v.t[:, c, 0:8], in_=Sc), reads=[S.b], writes=[v.b])
            k.op("dve", lambda e, c=c, Sc=Sc: e.max_index(out=ix.t[:, c, 0:8], in_max=v.t[:, c, 0:8], in_values=Sc),
                 reads=[S.b, v.b], writes=[ix.b])
            k.op("dve", lambda e, c=c, Sc=Sc: e.match_replace(out=Sc, in_to_replace=v.t[:, c, 0:8], in_values=Sc,
                                                              imm_value=-1e30),
                 reads=[v.b], writes=[S.b])
            k.op("dve", lambda e, c=c, Sc=Sc: e.max(out=v.t[:, c, 8:16], in_=Sc), reads=[S.b], writes=[v.b])
            k.op("dve", lambda e, c=c, Sc=Sc: e.max_index(out=ix.t[:, c, 8:16], in_max=v.t[:, c, 8:16], in_values=Sc),
                 reads=[S.b, v.b], writes=[ix.b])
        k.op("dve", lambda e: e.tensor_copy(out=ixf.t[:], in_=ix.t[:]), reads=[ix.b], writes=[ixf.b])
        vv = v.t[:].rearrange("p (h two) k -> p h two k", two=2)
        iv = ixf.t[:].rearrange("p (h two) k -> p h two k", two=2)
        cand4 = cand.t[:].rearrange("p h (i j) -> p h i j", j=16)
        cidx4 = cidx.t[:].rearrange("p h (i j) -> p h i j", j=16)
        k.op("dve", lambda e: e.tensor_tensor(out=cand4, in0=vv[:, :, 0, :].unsqueeze(3).to_broadcast([128, 8, 16, 16]),
                                              in1=vv[:, :, 1, :].unsqueeze(2).to_broadcast([128, 8, 16, 16]), op=ALU.add),
             reads=[v.b], writes=[cand.b])
        k.op("dve", lambda e: e.tensor_scalar(out=iv[:, :, 0, :], in0=iv[:, :, 0, :], scalar1=128.0, scalar2=None,
                                              op0=ALU.mult), reads=[ixf.b], writes=[ixf.b])
        k.op("dve", lambda e: e.tensor_tensor(out=cidx4, in0=iv[:, :, 0, :].unsqueeze(3).to_broadcast([128, 8, 16, 16]),
                                              in1=iv[:, :, 1, :].unsqueeze(2).to_broadcast([128, 8, 16, 16]), op=ALU.add),
             reads=[ixf.b], writes=[cidx.b] + b["cidx_alias"])
        k.op("pool", lambda e: e.tensor_copy(out=cand2.t[:], in_=cand.t[:]), reads=[cand.b], writes=[cand2.b] + b["cand2_alias"])
        for h in range(8):
            ch = cand2.t[:, h, :]
            k.op("dve", lambda e, h=h, ch=ch: e.max(out=ts.t[:, h, 0:8], in_=ch), reads=[cand2.b], writes=[ts.b])
            k.op("dve", lambda e, h=h, ch=ch: e.match_replace(out=ch, in_to_replace=ts.t[:, h, 0:8], in_values=ch,
                                                              imm_value=-1e30), reads=[ts.b], writes=[cand2.b])
            k.op("dve", lambda e, h=h, ch=ch: e.max(out=ts.t[:, h, 8:16], in_=ch), reads=[cand2.b], writes=[ts.b])
        E = b["E"]
        eidx3 = b["eidx"].t[:].rearrange("p (h k) -> p h k", k=16)
        for h in range(8):
            k.op("dve", lambda e, h=h: e.tensor_tensor(
                out=E.t[:], in0=cand.t[:, h, :].unsqueeze(1).to_broadcast([128, 16, 256]),
                in1=ts.t[:, h, :].unsqueeze(2).to_broadcast([128, 16, 256]), op=ALU.is_equal),
                reads=[cand.b, ts.b], writes=[E.b] + b["E_alias"])
            k.op("dve", lambda e, h=h: e.tensor_tensor(
                out=E.t[:], in0=E.t[:], in1=cidx.t[:, h, :].unsqueeze(1).to_broadcast([128, 16, 256]), op=ALU.mult),
                reads=[cidx.b], writes=[E.b])
            k.op("dve", lambda e, h=h: e.tensor_reduce(out=eidx3[:, h, :], in_=E.t[:], axis=AX.X, op=ALU.add),
                 reads=[E.b], writes=[b["eidx"].b])
        k.op("dve", lambda e: e.tensor_scalar(out=b["eidx"].t[:], in0=b["eidx"].t[:], scalar1=16383.0, scalar2=0.0,
                                              op0=ALU.min, op1=ALU.max), reads=[], writes=[b["eidx"].b])
        k.op("dve", lambda e: e.tensor_tensor(out=b["tsm"].t[:], in0=ts.t[:], in1=ts.t[:, :, 0:1].to_broadcast([128, 8, 16]),
                                              op=ALU.subtract), reads=[ts.b], writes=[b["tsm"].b])
        k.op("act", lambda e: e.activation(out=b["e"].t[:], in_=b["tsm"].t[:], func=AF.Exp),
             reads=[b["tsm"].b], writes=[b["e"].b])
        k.op("dve", lambda e: e.tensor_reduce(out=b["esum"].t[:], in_=b["e"].t[:], axis=AX.X, op=ALU.add),
             reads=[b["e"].b], writes=[b["esum"].b])
        k.op("dve", lambda e: e.reciprocal(out=b["rinv"].t[:], in_=b["esum"].t[:]), reads=[b["esum"].b], writes=[b["rinv"].b])
        g3 = b["g"].t[:].rearrange("p (h k) -> p h k", k=16)
        k.op("dve", lambda e: e.tensor_tensor(out=g3, in0=b["e"].t[:], in1=b["rinv"].t[:].unsqueeze(2).to_broadcast([128, 8, 16]),
                                              op=ALU.mult), reads=[b["e"].b, b["rinv"].b], writes=[b["g"].b])
        px0, px1 = cm.ps_x[0], cm.ps_x[1]
        k.op("pe", lambda e: e.transpose(out=px0.t[:, 0:128], in_=b["eidx"].t[:], identity=cm.identf.t[:]),
             reads=[b["eidx"].b, cm.identf.b], writes=[px0.b])
        k.op("dve", lambda e: e.tensor_copy(out=b["idxT"].t[:], in_=px0.t[:, 0:128]), reads=[px0.b], writes=[b["idxT"].b])
        k.op("pe", lambda e: e.transpose(out=px1.t[:, 0:128], in_=b["g"].t[:], identity=cm.identf.t[:]),
             reads=[b["g"].b, cm.identf.b], writes=[px1.b])
        k.op("dve", lambda e: e.tensor_copy(out=b["gT"].t[:], in_=px1.t[:, 0:128]), reads=[px1.b], writes=[b["gT"].b])
        actv = b["actv"]
        for t in range(128):
            ue = b["ue"][t % 8]
            k.dma("pool", lambda e, t=t, ue=ue: e.indirect_dma_start(
                out=ue.t[:], out_offset=None, in_=u_d[:, :],
                in_offset=bass.IndirectOffsetOnAxis(ap=b["idxT"].t[:, t:t + 1], axis=0)),
                reads=[b["idxT"].b], writes=[ue.b])
            pbk = cm.ps_o if t % 2 == 0 else cm.ps_y
            pall = cm.psA if t % 2 == 0 else cm.psB
            pbufs = cm.bankA if t % 2 == 0 else cm.bankB
            for ng in range(4):
                k.op("pe", lambda e, t=t, ng=ng, pbk=pbk: e.matmul(
                    out=pbk[ng].t[:], lhsT=cm.ident.t[:, t:t + 1].to_broadcast([128, 128]),
                    rhs=xn.t[:, ng * 512:(ng + 1) * 512], start=True, stop=True),
                    reads=[cm.ident.b, xn.b], writes=[pbk[ng].b], tok=(ng == 3))
            k.op("dve", lambda e, t=t, ue=ue, pall=pall: e.scalar_tensor_tensor(
                out=junk.t[:], in0=ue.t[:], scalar=1.0, in1=pall[:, :], op0=ALU.mult, op1=ALU.mult,
                accum_out=actv.t[:, t:t + 1]),
                reads=[ue.b] + pbufs, writes=[junk.b, actv.b])
        k.op("act", lambda e: e.activation(out=b["gel"].t[:], in_=actv.t[:], func=AF.Gelu),
             reads=[actv.b], writes=[b["gel"].b])
        k.op("dve", lambda e: e.tensor_tensor(out=b["coefT"].t[:], in0=b["gel"].t[:], in1=b["gT"].t[:], op=ALU.mult),
             reads=[b["gel"].b, b["gT"].b], writes=[b["coefT"].b])
        for t in range(128):
            ve = b["ve"][t % 8]
            lh = b["lh"][t % 2]
            k.dma("pool", lambda e, t=t, ve=ve: e.indirect_dma_start(
                out=ve.t[:], out_offset=None, in_=v_d[:, :],
                in_offset=bass.IndirectOffsetOnAxis(ap=b["idxT"].t[:, t:t + 1], axis=0)),
                reads=[b["idxT"].b], writes=[ve.b])
            k.op("dve", lambda e, t=t, lh=lh: e.tensor_scalar(
                out=lh.t[:], in0=b["W2"].t[:, 127 - t:255 - t], scalar1=b["coefT"].t[:, t:t + 1], scalar2=None,
                op0=ALU.mult), reads=[b["W2"].b, b["coefT"].b], writes=[lh.b])
            for ng in range(4):
                k.op("pe", lambda e, t=t, ng=ng, lh=lh, ve=ve: e.matmul(
                    out=cm.ps_y[ng].t[:], lhsT=lh.t[:], rhs=ve.t[:, ng * 512:(ng + 1) * 512],
                    start=(t == 0), stop=(t == 127)),
                    reads=[lh.b, ve.b], writes=[cm.ps_y[ng].b], tok=(ng == 3))
        o = b["o"]
        k.op("dve", lambda e: e.tensor_tensor(out=o.t[:], in0=cm.psB[:, :], in1=h_.t[:], op=ALU.add),
             reads=[h_.b] + cm.bankB, writes=[o.b])
        k.dma("pool", lambda e: e.dma_start(out=hs_d[r0:r0 + 128, :], in_=o.t[:]), reads=[o.b], writes=[hs_bufs[i]])


def drain(k):
    for q, slots in k.dq.items():
        for s in slots:
            if s[1] > 0:
                k._need("sp", (s[0], s[1], "dma"))


def build_test_peer(ntiles):
    nc = bass.Bass("TRN2", target_bir_lowering=False)
    hs_d = nc.dram_tensor("hs_io", [ntiles * 128, D], F32, kind="ExternalOutput").ap()
    hin_d = nc.dram_tensor("h_in", [ntiles * 128, D], F32, kind="ExternalInput").ap()
    cw_d = nc.dram_tensor("cw_in", [D], F32, kind="ExternalInput").ap()
    wq_d = nc.dram_tensor("wq_in", [D, D], F32, kind="ExternalInput").ap()
    k1_d = nc.dram_tensor("k1T_in", [8, 128, 128], F32, kind="ExternalInput").ap()
    k2_d = nc.dram_tensor("k2T_in", [8, 128, 128], F32, kind="ExternalInput").ap()
    u_d = nc.dram_tensor("u_in", [16384, D], F32, kind="ExternalInput").ap()
    v_d = nc.dram_tensor("v_in", [16384, D], F32, kind="ExternalInput").ap()
    id_d = nc.dram_tensor("ident_in", [128, 128], F32, kind="ExternalInput").ap()
    with ExitStack() as stack:
        k = K(nc, stack)
        cm = Common(k, id_d)
        b = peer_alloc(k)
        hs_bufs = [Buf("hs%d" % i) for i in range(ntiles)]
        for i in range(ntiles):
            k.dma("sp", lambda e, i=i: e.dma_start(out=b["o"].t[:], in_=hin_d[i * 128:(i + 1) * 128, :]), writes=[b["o"].b])
            k.dma("sp", lambda e, i=i: e.dma_start(out=hs_d[i * 128:(i + 1) * 128, :], in_=b["o"].t[:]), reads=[b["o"].b],
                  writes=[hs_bufs[i]])
        phase_peer(k, cm, b, hs_d, hs_bufs, list(range(ntiles)), cw_d, wq_d, k1_d, k2_d, u_d, v_d)
        drain(k)
        print("ninst", k.ninst, "nsem", k.nsem)
    return nc


def ple_alloc(k):
    b = {}
    b["Wg"] = T(k, "eWg", [128, 16, 2048], BF16)
    b["Wp"] = T(k, "eWp", [128, 2, 2048], BF16)
    b["nw"] = T(k, "enw", [128, 16], F32)
    b["fw"] = T(k, "efw", [128, 2048], F32)
    b["h"] = [T(k, "eh%d" % i, [128, 2048], F32) for i in range(2)]
    b["p"] = [T(k, "ep%d" % i, [128, 256], F32) for i in range(2)]
    b["pb"] = T(k, "epb", [128, 256], BF16)
    b["pT"] = T(k, "epT", [128, 256], BF16)
    b["junk"] = T(k, "ejunk", [128, 2048], BF16)
    b["ss"] = T(k, "ess", [128, 1], F32)
    b["sq"] = T(k, "esq", [128, 1], F32)
    b["rstd"] = T(k, "erstd", [128, 1], F32)
    b["xn"] = T(k, "exn", [128, 2048], BF16)
    b["xT"] = T(k, "exT", [128, 2048], BF16)
    b["sig"] = T(k, "esig", [128, 2048], F32)
    b["o"] = [T(k, "eo%d" % i, [128, 2048], F32) for i in range(2)]
    b["o2"] = [T(k, "eo2%d" % i, [128, 2048], F32) for i in range(2)]
    return b


def phase_ple(k, cm, b, hs_d, hs_bufs, tiles, p_d, prow0, nw_d, wg_d, wp_d, final=None):
    k.dma("sp", lambda e: e.dma_start(out=b["nw"].t[:], in_=nw_d[:, :]), writes=[b["nw"].b])
    load_w(k, cm, b["Wg"], wg_d, 2048, 16, scale=b["nw"])
    load_w(k, cm, b["Wp"], wp_d, 2048, 2)
    if final is not None:
        fw_d, out_d, orow0 = final
        k.dma("sp", lambda e: e.dma_start(out=b["fw"].t[:], in_=fw_d.partition_broadcast(128)), writes=[b["fw"].b])
    junk, ss, sq, rstd, xn, xT = b["junk"], b["ss"], b["sq"], b["rstd"], b["xn"], b["xT"]
    for n, i in enumerate(tiles):
        r0 = i * 128
        h_ = b["h"][n % 2]
        p_ = b["p"][n % 2]
        o = b["o"][n % 2]
        k.dma("sp", lambda e: e.dma_start(out=h_.t[:], in_=hs_d[r0:r0 + 128, :]), reads=[hs_bufs[i]], writes=[h_.b])
        k.dma("sp", lambda e: e.dma_start(out=p_.t[:], in_=p_d[prow0 + r0:prow0 + r0 + 128, :]), writes=[p_.b])
        rms_rstd(k, h_, junk, ss, sq, rstd)
        k.op("act", lambda e: e.activation(out=xn.t[:], in_=h_.t[:], func=AF.Copy, scale=rstd.t[:, 0:1]),
             reads=[h_.b, rstd.b], writes=[xn.b])
        transpose_tile(k, cm, xn, xT)
        k.op("pool", lambda e: e.tensor_copy(out=b["pb"].t[:], in_=p_.t[:]), reads=[p_.b], writes=[b["pb"].b])
        transpose_tile(k, cm, b["pb"], b["pT"], nchunks=2)
        for ng in range(4):
            mm_group(k, cm.ps_o[ng], xT.t, xT.b, b["Wg"], ng * 512, 512)
            px = cm.ps_x[ng % 2]
            mm_group(k, px, b["pT"].t, b["pT"].b, b["Wp"], ng * 512, 512, kchunks=2)
            sl = slice(ng * 512, (ng + 1) * 512)
            k.op("act", lambda e, ng=ng, sl=sl: e.activation(out=b["sig"].t[:, sl], in_=cm.ps_o[ng].t[:], func=AF.Sigmoid),
                 reads=[cm.ps_o[ng].b], writes=[b["sig"].b])
            k.op("dve", lambda e, sl=sl, px=px: e.tensor_tensor(out=b["sig"].t[:, sl], in0=px.t[:], in1=b["sig"].t[:, sl],
                                                                op=ALU.mult), reads=[px.b], writes=[b["sig"].b])
        k.op("pool", lambda e: e.tensor_tensor(out=o.t[:], in0=b["sig"].t[:], in1=h_.t[:], op=ALU.add),
             reads=[b["sig"].b, h_.b], writes=[o.b])
        if final is None:
            k.dma("pool", lambda e: e.dma_start(out=hs_d[r0:r0 + 128, :], in_=o.t[:]), reads=[o.b], writes=[hs_bufs[i]])
        else:
            o2 = b["o2"][n % 2]
            rms_rstd(k, o, junk, ss, sq, rstd)
            k.op("dve", lambda e: e.scalar_tensor_tensor(out=o2.t[:], in0=o.t[:], scalar=rstd.t[:, 0:1], in1=b["fw"].t[:],
                                                         op0=ALU.mult, op1=ALU.mult),
                 reads=[o.b, rstd.b, b["fw"].b], writes=[o2.b])
            rr = orow0 + n * 128
            k.dma("pool", lambda e: e.dma_start(out=out_d[rr:rr + 128, :], in_=o2.t[:]), reads=[o2.b])


def build_test_ple(ntiles, final):
    nc = bass.Bass("TRN2", target_bir_lowering=False)
    hs_d = nc.dram_tensor("hs_io", [ntiles * 128, D], F32, kind="ExternalOutput").ap()
    out_d = nc.dram_tensor("out", [ntiles * 128, D], F32, kind="ExternalOutput").ap()
    hin_d = nc.dram_tensor("h_in", [ntiles * 128, D], F32, kind="ExternalInput").ap()
    p_d = nc.dram_tensor("p_in", [ntiles * 128, 256], F32, kind="ExternalInput").ap()
    nw_d = nc.dram_tensor("nw_in", [128, 16], F32, kind="ExternalInput").ap()
    fw_d = nc.dram_tensor("fw_in", [D], F32, kind="ExternalInput").ap()
    wg_d = nc.dram_tensor("wg_in", [D, D], F32, kind="ExternalInput").ap()
    wp_d = nc.dram_tensor("wp_in", [256, D], F32, kind="ExternalInput").ap()
    id_d = nc.dram_tensor("ident_in", [128, 128], F32, kind="ExternalInput").ap()
    with ExitStack() as stack:
        k = K(nc, stack)
        cm = Common(k, id_d)
        b = ple_alloc(k)
        hs_bufs = [Buf("hs%d" % i) for i in range(ntiles)]
        for i in range(ntiles):
            k.dma("sp", lambda e, i=i: e.dma_start(out=b["sig"].t[:], in_=hin_d[i * 128:(i + 1) * 128, :]), writes=[b["sig"].b])
            k.dma("sp", lambda e, i=i: e.dma_start(out=hs_d[i * 128:(i + 1) * 128, :], in_=b["sig"].t[:]), reads=[b["sig"].b],
                  writes=[hs_bufs[i]])
        phase_ple(k, cm, b, hs_d, hs_bufs, list(range(ntiles)), p_d, 0, nw_d, wg_d, wp_d,
                  final=(fw_d, out_d, 0) if final else None)
        drain(k)
        print("ninst", k.ninst, "nsem", k.nsem)
    return nc


def att_alloc(k):
    b = {}
    b["Wq"] = T(k, "aWq", [128, 16, 2048], BF16)
    b["Wkv"] = T(k, "aWkv", [128, 16, 512], BF16)
    b["nq"] = T(k, "anq", [128, 16], F32)
    b["nkv"] = T(k, "ankv", [128, 16], F32)
    b["mask"] = T(k, "amask", [128, 256], F32)
    b["mask1"] = T(k, "amask1", [128, 256], F32)
    b["sinks"] = T(k, "asinks", [128, 32], F32)
    b["h"] = [T(k, "ah%d" % i, [128, 2048], F32) for i in range(2)]
    b["junk"] = T(k, "ajunk", [128, 2048], BF16)
    b["ss"] = T(k, "ass", [128, 1], F32)
    b["sq"] = T(k, "asq", [128, 1], F32)
    b["rstd"] = T(k, "arstd", [128, 1], F32)
    b["xn"] = T(k, "axn", [128, 2048], BF16)
    b["xT"] = T(k, "axT", [128, 2048], BF16)
    b["qb"] = T(k, "aqb", [128, 2048], BF16)
    b["qT"] = T(k, "aqT", [128, 2048], BF16)
    b["Kdup"] = T(k, "aKdup", [128, 8, 128], BF16)
    b["kT"] = [T(k, "akT%d" % i, [128, 1024], BF16) for i in range(2)]
    b["V"] = [T(k, "aV%d" % i, [128, 256], BF16) for i in range(2)]
    b["sm"] = T(k, "asm", [128, 8, 256], F32)
    b["pr"] = T(k, "apr", [128, 8, 256], BF16)
    b["prT"] = [T(k, "aprT%d" % i, [128, 1024], BF16) for i in range(2)]
    b["mx"] = T(k, "amx", [128, 8], F32)
    b["rs"] = T(k, "ars", [128, 8], F32)
    b["es"] = T(k, "aes", [128, 8], F32)
    b["rden"] = T(k, "arden", [128, 8], F32)
    b["o"] = [T(k, "ao%d" % i, [128, 2048], F32) for i in range(2)]
    return b


def phase_att(k, cm, b, hs_d, hs_bufs, ntiles, att_d, att_bufs, nq_d, nkv_d, wq_d, wkv_d, sinks_d, mask_d, mask1_d):
    k.dma("sp", lambda e: e.dma_start(out=b["nq"].t[:], in_=nq_d[:, :]), writes=[b["nq"].b])
    k.dma("sp", lambda e: e.dma_start(out=b["nkv"].t[:], in_=nkv_d[:, :]), writes=[b["nkv"].b])
    k.dma("sp", lambda e: e.dma_start(out=b["mask"].t[:], in_=mask_d[:, :]), writes=[b["mask"].b])
    k.dma("sp", lambda e: e.dma_start(out=b["mask1"].t[:], in_=mask1_d[:, :]), writes=[b["mask1"].b])
    k.dma("sp", lambda e: e.dma_start(out=b["sinks"].t[:], in_=sinks_d.partition_broadcast(128)), writes=[b["sinks"].b])
    load_w(k, cm, b["Wq"], wq_d, 2048, 16, scale=b["nq"])
    load_w(k, cm, b["Wkv"], wkv_d, 512, 16, scale=b["nkv"])
    k.op("pool", lambda e: e.memset(b["Kdup"].t[:], 0.0), writes=[b["Kdup"].b])
    junk, ss, sq, rstd, xn, xT, qb, qT = b["junk"], b["ss"], b["sq"], b["rstd"], b["xn"], b["xT"], b["qb"], b["qT"]
    sm, pr, mx, rs, es, rden = b["sm"], b["pr"], b["mx"], b["rs"], b["es"], b["rden"]
    for i in range(ntiles):
        r0 = i * 128
        h_ = b["h"][i % 2]
        kTc, kTp = b["kT"][i % 2], b["kT"][(i + 1) % 2]
        Vc, Vp = b["V"][i % 2], b["V"][(i + 1) % 2]
        k.dma("sp", lambda e: e.dma_start(out=h_.t[:], in_=hs_d[r0:r0 + 128, :]), reads=[hs_bufs[i]], writes=[h_.b])
        rms_rstd(k, h_, junk, ss, sq, rstd)
        k.op("act", lambda e: e.activation(out=xn.t[:], in_=h_.t[:], func=AF.Copy, scale=rstd.t[:, 0:1]),
             reads=[h_.b, rstd.b], writes=[xn.b])
        transpose_tile(k, cm, xn, xT)
        pkv = cm.ps_x[0]
        mm_group(k, pkv, xT.t, xT.b, b["Wkv"], 0, 512)
        kview = pkv.t[:, 0:256].rearrange("p (g d) -> p g d", d=64)
        kz4 = b["Kdup"].t[:].rearrange("p (g two) d -> p g two d", two=2)
        k.op("act", lambda e: e.copy(out=kz4[:, :, 0, 0:64], in_=kview), reads=[pkv.b], writes=[b["Kdup"].b])
        k.op("dve", lambda e: e.tensor_copy(out=kz4[:, :, 1, 64:128], in_=kview), reads=[pkv.b], writes=[b["Kdup"].b])
        k.op("act", lambda e: e.copy(out=Vc.t[:], in_=pkv.t[:, 256:512]), reads=[pkv.b], writes=[Vc.b])
        kd2 = V(b["Kdup"].t[:].rearrange("p g d -> p (g d)"), None)
        kd2.b = b["Kdup"].b
        transpose_tile(k, cm, kd2, kTc, nchunks=8)
        if i == 0:
            continue
        for ng in range(4):
            mm_group(k, cm.ps_o[ng], xT.t, xT.b, b["Wq"], ng * 512, 512)
            k.op("act", lambda e, ng=ng: e.activation(out=qb.t[:, ng * 512:(ng + 1) * 512], in_=cm.ps_o[ng].t[:],
                                                      func=AF.Copy, scale=0.125),
                 reads=[cm.ps_o[ng].b], writes=[qb.b])
        transpose_tile(k, cm, qb, qT)
        if ATT_STOP == 1:
            continue
        msk = b["mask1"] if i == 1 else b["mask"]
        o = b["o"][i % 2]
        for g in range(4):
            for j in range(8):
                hq = 8 * g + j
                c = hq // 2
                off = (hq % 2) * 64
                po = cm.ps_o[j // 2]
                cb = (j % 2) * 256
                k.op("pe", lambda e, c=c, off=off, po=po, cb=cb, g=g: e.matmul(
                    out=po.t[:, cb:cb + 128], lhsT=qT.t[:, c * 128:(c + 1) * 128],
                    rhs=kTp.t[:, (2 * g + off // 64) * 128:(2 * g + off // 64 + 1) * 128], start=True, stop=True),
                    reads=[qT.b, kTp.b], writes=[po.b], tok=False)
                k.op("pe", lambda e, c=c, off=off, po=po, cb=cb, g=g: e.matmul(
                    out=po.t[:, cb + 128:cb + 256], lhsT=qT.t[:, c * 128:(c + 1) * 128],
                    rhs=kTc.t[:, (2 * g + off // 64) * 128:(2 * g + off // 64 + 1) * 128], start=True, stop=True),
                    reads=[qT.b, kTc.b], writes=[po.b], tok=(j % 2 == 1))
            if ATT_STOP == 2:
                k.op("pe", lambda e: e.transpose(out=cm.ps_x[1].t[:, 0:128], in_=cm.identf.t[:], identity=cm.identf.t[:]),
                     reads=[cm.identf.b], writes=[cm.ps_x[1].b])
                continue
            sA = cm.psA[:, :].rearrange("p (j n) -> p j n", n=256)
            k.op("dve", lambda e: e.tensor_tensor(out=sm.t[:], in0=sA, in1=msk.t[:].unsqueeze(1).to_broadcast([128, 8, 256]),
                                                  op=ALU.add), reads=cm.bankA + [msk.b], writes=[sm.b])
            k.op("dve", lambda e: e.tensor_reduce(out=mx.t[:], in_=sm.t[:], axis=AX.X, op=ALU.max), reads=[sm.b], writes=[mx.b])
            k.op("dve", lambda e, g=g: e.tensor_tensor(out=mx.t[:], in0=mx.t[:], in1=b["sinks"].t[:, 8 * g:8 * g + 8], op=ALU.max),
                 reads=[b["sinks"].b], writes=[mx.b])
            k.op("dve", lambda e: e.tensor_tensor(out=sm.t[:], in0=sm.t[:], in1=mx.t[:].unsqueeze(2).to_broadcast([128, 8, 256]),
                                                  op=ALU.subtract), reads=[mx.b], writes=[sm.b])
            k.op("act", lambda e: e.activation(out=pr.t[:], in_=sm.t[:], func=AF.Exp), reads=[sm.b], writes=[pr.b])
            k.op("dve", lambda e: e.tensor_reduce(out=rs.t[:], in_=pr.t[:], axis=AX.X, op=ALU.add), reads=[pr.b], writes=[rs.b])
            k.op("dve", lambda e, g=g: e.tensor_tensor(out=es.t[:], in0=b["sinks"].t[:, 8 * g:8 * g + 8], in1=mx.t[:],
                                                       op=ALU.subtract), reads=[b["sinks"].b, mx.b], writes=[es.b])
            k.op("act", lambda e: e.activation(out=es.t[:], in_=es.t[:], func=AF.Exp), reads=[], writes=[es.b])
            k.op("dve", lambda e: e.tensor_tensor(out=rs.t[:], in0=rs.t[:], in1=es.t[:], op=ALU.add), reads=[es.b], writes=[rs.b])
            k.op("dve", lambda e: e.reciprocal(out=rden.t[:], in_=rs.t[:]), reads=[rs.b], writes=[rden.b])
            if ATT_STOP == 3:
                continue
            for half in range(2):
                pst = cm.ps_tr[half]
                for j in range(8):
                    k.op("pe", lambda e, j=j, half=half, pst=pst: e.transpose(
                        out=pst.t[:, j * 128:(j + 1) * 128], in_=pr.t[:, j, half * 128:(half + 1) * 128],
                        identity=cm.ident.t[:]), reads=[pr.b, cm.ident.b], writes=[pst.b], tok=(j == 7))
                eng = "dve" if half == 0 else "act"
                if half == 0:
                    k.op("dve", lambda e, pst=pst: e.tensor_copy(out=b["prT"][0].t[:], in_=pst.t[:]), reads=[pst.b],
                         writes=[b["prT"][0].b])
                else:
                    k.op("act", lambda e, pst=pst: e.copy(out=b["prT"][1].t[:], in_=pst.t[:]), reads=[pst.b],
                         writes=[b["prT"][1].b])
            if ATT_STOP == 4:
                continue
            pov = cm.ps_x[1]
            for j in range(8):
                k.op("pe", lambda e, j=j, g=g: e.matmul(out=pov.t[:, j * 64:(j + 1) * 64], lhsT=b["prT"][0].t[:, j * 128:(j + 1) * 128],
                                                        rhs=Vp.t[:, g * 64:(g + 1) * 64], start=True, stop=False),
                     reads=[b["prT"][0].b, Vp.b], writes=[pov.b], tok=False)
                k.op("pe", lambda e, j=j, g=g: e.matmul(out=pov.t[:, j * 64:(j + 1) * 64], lhsT=b["prT"][1].t[:, j * 128:(j + 1) * 128],
                                                        rhs=Vc.t[:, g * 64:(g + 1) * 64], start=False, stop=True),
                     reads=[b["prT"][1].b, Vc.b], writes=[pov.b], tok=(j == 7))
            k.op("dve", lambda e, g=g: e.tensor_tensor(
                out=o.t[:, g * 512:(g + 1) * 512].rearrange("p (j d) -> p j d", d=64),
                in0=pov.t[:, :].rearrange("p (j d) -> p j d", d=64),
                in1=rden.t[:].unsqueeze(2).to_broadcast([128, 8, 64]), op=ALU.mult),
                reads=[pov.b, rden.b], writes=[o.b])
        ro = (i - 1) * 128
        k.dma("pool", lambda e: e.dma_start(out=att_d[ro:ro + 128, :], in_=o.t[:]), reads=[o.b], writes=[att_bufs[i - 1]])


def build_test_att(ntiles):
    nc = bass.Bass("TRN2", target_bir_lowering=False)
    hin_d = nc.dram_tensor("h_in", [ntiles * 128, D], F32, kind="ExternalInput").ap()
    att_d = nc.dram_tensor("att_out", [(ntiles - 1) * 128, D], F32, kind="ExternalOutput").ap()
    nq_d = nc.dram_tensor("nq_in", [128, 16], F32, kind="ExternalInput").ap()
    nkv_d = nc.dram_tensor("nkv_in", [128, 16], F32, kind="ExternalInput").ap()
    wq_d = nc.dram_tensor("wq_in", [D, D], F32, kind="ExternalInput").ap()
    wkv_d = nc.dram_tensor("wkv_in", [D, 512], F32, kind="ExternalInput").ap()
    sinks_d = nc.dram_tensor("sinks_in", [32], F32, kind="ExternalInput").ap()
    mask_d = nc.dram_tensor("mask_in", [128, 256], F32, kind="ExternalInput").ap()
    mask1_d = nc.dram_tensor("mask1_in", [128, 256], F32, kind="ExternalInput").ap()
    id_d = nc.dram_tensor("ident_in", [128, 128], F32, kind="ExternalInput").ap()
    with ExitStack() as stack:
        k = K(nc, stack)
        cm = Common(k, id_d)
        b = att_alloc(k)
        hs_bufs = [Buf("hs%d" % i) for i in range(ntiles)]
        att_bufs = [Buf("at%d" % i) for i in range(ntiles)]
        phase_att(k, cm, b, hin_d, hs_bufs, ntiles, att_d, att_bufs, nq_d, nkv_d, wq_d, wkv_d, sinks_d, mask_d, mask1_d)
        drain(k)
        print("ninst", k.ninst, "nsem", k.nsem)
    return nc


def band_masks():
    t = np.arange(128)[:, None]
    kk = np.arange(256)[None, :]
    valid = (kk >= t + 1) & (kk <= t + 128)
    m = np.where(valid, 0.0, -1e30).astype(np.float32)
    m1 = m.copy()
    m1[:, :128] = -1e30
    return m, m1


def mlstm_alloc(k):
    mk = lambda name, shape, dt=F32: T(k, name, shape, dt)
    W = mk("mW", [128, 16, 1538], BF16)
    nw = mk("mnw", [128, 16])
    gb = mk("mgb", [128, 2])
    gb15 = mk("mgb15", [128, 2])
    hnw = mk("mhnw", [128, 512])
    tri = mk("mtri", [128, 128])
    sel = mk("msel", [128, 128])
    cmask = mk("mcmask", [128, 128])
    ones = mk("mones", [128, 128])
    one1 = mk("mone1", [128, 1])
    onesb = mk("monesb", [128, 1], BF16)
    C = mk("mC", [128, 2, 512])
    Cb = mk("mCb", [128, 2, 512], BF16)
    n_ = mk("mn", [128, 2])
    nb = mk("mnb", [128, 2], BF16)
    mprev = mk("mmprev", [128, 1])
    xs = [mk("mx%d" % i, [128, 2048]) for i in range(2)]
    junk = mk("mjunk", [128, 2048], BF16)
    ss, sq, rstd = mk("mss", [128, 1]), mk("msq", [128, 1]), mk("mrstd", [128, 1])
    xn = mk("mxn", [128, 2048], BF16)
    xT = mk("mxT", [128, 2048], BF16)
    qkb = mk("mqkb", [128, 512], BF16)
    qkT = mk("mqkT", [128, 512], BF16)
    vb = mk("mvb", [128, 512], BF16)
    sog = mk("msog", [128, 512])
    gs = mk("mgs", [128, 2])
    th = mk("mth", [128, 2])
    li, z, ez, sp, lf = mk("mli", [128, 1]), mk("mz", [128, 1]), mk("mez", [128, 1]), mk("msp", [128, 1]), mk("mlf", [128, 1])
    bcs, gvec, mrow, u, negu, mt = (mk("mbcs", [128, 1]), mk("mgvec", [128, 1]), mk("mmrow", [128, 1]), mk("mu", [128, 1]),
                                    mk("mnegu", [128, 1]), mk("mmt", [128, 1]))
    dg = mk("mdg", [128, 128])
    A = mk("mA", [128, 128])
    wintra = mk("mwintra", [128, 128])
    wia, winter = mk("mwia", [128, 1]), mk("mwinter", [128, 1])
    Pb = mk("mPb", [128, 128], BF16)
    PT = mk("mPT", [128, 128], BF16)
    dintra = mk("mdintra", [128, 1])
    tmp = mk("mtmp", [128, 512])
    num = mk("mnum", [128, 512])
    den, aden, emt, dmax, rden = mk("mden", [128, 1]), mk("maden", [128, 1]), mk("memt", [128, 1]), mk("mdmax", [128, 1]), mk("mrden", [128, 1])
    ssn, t1, sqv, rstd2, sc = mk("mssn", [128, 1]), mk("mt1", [128, 1]), mk("msqv", [128, 1]), mk("mrstd2", [128, 1]), mk("msc", [128, 1])
    ots = [mk("mot%d" % i, [128, 512]) for i in range(2)]
    mb = mk("mmb", [128, 2])
    last2 = mk("mlast2", [128, 2])
    dlt, wstate, deca, decay = mk("mdlt", [128, 1]), mk("mwstate", [128, 1]), mk("mdeca", [128, 1]), mk("mdecay", [128, 1])
    kw = mk("mkw", [128, 256], BF16)

    padneg = mk("mpadneg", [128, 80])
    kall = mk("mkall", [128, 48, 256], BF16)
    vall = mk("mvall", [128, 48, 512], BF16)
    gall = mk("mgall", [128, 48, 2])
    pv = {nm: mk("mpv_" + nm, [128, 48]) for nm in ("thi", "li", "thf", "ez", "sp", "lf", "bloc", "btot", "off", "bg", "wv", "wst")}
    pM, pMall, pMp, pnegMp, pBend = mk("mpM", [128, 1]), mk("mpMall", [128, 1]), mk("mpMp", [128, 1]), mk("mpnegMp", [128, 1]), mk("mpBend", [128, 1])
    npm = mk("mnpm", [128, 80])
    return dict(kall=kall, vall=vall, gall=gall, pv=pv, pM=pM, pMall=pMall, pMp=pMp, pnegMp=pnegMp, pBend=pBend, W=W, nw=nw, gb=gb, gb15=gb15, hnw=hnw, tri=tri, sel=sel, cmask=cmask, ones=ones, one1=one1, onesb=onesb, C=C, Cb=Cb, n_=n_, nb=nb, mprev=mprev, xs=xs, junk=junk, ss=ss, sq=sq, rstd=rstd, xn=xn, xT=xT, qkb=qkb, qkT=qkT, vb=vb, sog=sog, gs=gs, th=th, li=li, z=z, ez=ez, sp=sp, lf=lf, bcs=bcs, gvec=gvec, mrow=mrow, u=u, negu=negu, mt=mt, dg=dg, A=A, wintra=wintra, wia=wia, winter=winter, Pb=Pb, PT=PT, dintra=dintra, tmp=tmp, num=num, den=den, aden=aden, emt=emt, dmax=dmax, rden=rden, ssn=ssn, t1=t1, sqv=sqv, rstd2=rstd2, sc=sc, ots=ots, mb=mb, last2=last2, dlt=dlt, wstate=wstate, deca=deca, decay=decay, kw=kw, padneg=padneg, npm=npm)


def phase_mlstm(k, cm, m, xw_d, w4_d, nw_d, gb4_d, hnw_d, tri_d, sel_d, cmask_d, padneg_d, npm_d, nchunks, out_from,
                hg_d, hg_bufs, bg=None):
    g = globals()
    loc = dict(m)
    W = m["W"]
    nw = m["nw"]
    gb = m["gb"]
    gb15 = m["gb15"]
    hnw = m["hnw"]
    tri = m["tri"]
    sel = m["sel"]
    cmask = m["cmask"]
    ones = m["ones"]
    one1 = m["one1"]
    onesb = m["onesb"]
    C = m["C"]
    Cb = m["Cb"]
    n_ = m["n_"]
    nb = m["nb"]
    mprev = m["mprev"]
    xs = m["xs"]
    junk = m["junk"]
    ss = m["ss"]
    sq = m["sq"]
    rstd = m["rstd"]
    xn = m["xn"]
    xT = m["xT"]
    qkb = m["qkb"]
    qkT = m["qkT"]
    vb = m["vb"]
    sog = m["sog"]
    gs = m["gs"]
    th = m["th"]
    li = m["li"]
    z = m["z"]
    ez = m["ez"]
    sp = m["sp"]
    lf = m["lf"]
    bcs = m["bcs"]
    gvec = m["gvec"]
    mrow = m["mrow"]
    u = m["u"]
    negu = m["negu"]
    mt = m["mt"]
    dg = m["dg"]
    A = m["A"]
    wintra = m["wintra"]
    wia = m["wia"]
    winter = m["winter"]
    Pb = m["Pb"]
    PT = m["PT"]
    dintra = m["dintra"]
    tmp = m["tmp"]
    num = m["num"]
    den = m["den"]
    aden = m["aden"]
    emt = m["emt"]
    dmax = m["dmax"]
    rden = m["rden"]
    ssn = m["ssn"]
    t1 = m["t1"]
    sqv = m["sqv"]
    rstd2 = m["rstd2"]
    sc = m["sc"]
    ots = m["ots"]
    mb = m["mb"]
    last2 = m["last2"]
    dlt = m["dlt"]
    wstate = m["wstate"]
    deca = m["deca"]
    decay = m["decay"]
    kw = m["kw"]
    padneg = m["padneg"]
    npm = m["npm"]

    dl = lambda t, src: k.dma("sp", lambda e: e.dma_start(out=t.t[:], in_=src), writes=[t.b])
    dl(nw, nw_d[:, :])
    dl(tri, tri_d[:, :])
    dl(sel, sel_d[:, :])
    dl(cmask, cmask_d[:, :])
    k.dma("sp", lambda e: e.dma_start(out=padneg.t[:, :nchunks], in_=padneg_d[:, :]), writes=[padneg.b])
    k.dma("sp", lambda e: e.dma_start(out=npm.t[:, :nchunks], in_=npm_d[:, :]), writes=[npm.b])
    for t_, val in ((ones, 1.0), (one1, 1.0), (onesb, 1.0)):
        k.op("pool", lambda e, t_=t_, val=val: e.memset(t_.t[:], val), writes=[t_.b])
    eps_ap = cm_eps(k)
    P0, P1, P2, P3 = cm.ps_o
    X0, X1 = cm.ps_x
    for hd in range(4):
        dl(gb, gb4_d[hd, :, :])
        k.dma("sp", lambda e: e.dma_start(out=hnw.t[:], in_=hnw_d[hd * 512:(hd + 1) * 512].partition_broadcast(128)),
              writes=[hnw.b])
        k.op("dve", lambda e: e.tensor_scalar(out=gb15.t[:], in0=gb.t[:], scalar1=1.0 / 15.0, scalar2=None, op0=ALU.mult),
             reads=[gb.b], writes=[gb15.b])
        for t_, val in ((C, 0.0), (Cb, 0.0), (n_, 0.0), (nb, 0.0), (mprev, 0.0)):
            k.op("pool", lambda e, t_=t_, val=val: e.memset(t_.t[:], val), writes=[t_.b])
        load_w(k, cm, W, w4_d[hd], 1538, 16, scale=nw)
        NP = out_from
        kall, vall, gall, pv = m["kall"], m["vall"], m["gall"], m["pv"]
        pM, pMall, pMp, pnegMp, pBend = m["pM"], m["pMall"], m["pMp"], m["pnegMp"], m["pBend"]
        for c in range(NP):
            r0 = c * 128
            xt = xs[c % 2]
            k.dma("sp", lambda e: e.dma_start(out=xt.t[:], in_=xw_d[r0:r0 + 128, :]), writes=[xt.b])
            if bg is not None:
                bg()
            rms_rstd(k, xt, junk, ss, sq, rstd)
            k.op("act", lambda e: e.activation(out=xn.t[:], in_=xt.t[:], func=AF.Copy, scale=rstd.t[:, 0:1]),
                 reads=[xt.b, rstd.b], writes=[xn.b])
            transpose_tile(k, cm, xn, xT)
            pk = P0 if c % 2 == 0 else P2
            pvv = P1 if c % 2 == 0 else P3
            for kc in range(16):
                k.op("pe", lambda e, kc=kc: e.matmul(out=pk.t[:, 0:256], lhsT=xT.t[:, kc * 128:(kc + 1) * 128],
                                                     rhs=W.t[:, kc, 256:512], start=(kc == 0), stop=(kc == 15)),
                     reads=[xT.b, W.b], writes=[pk.b], tok=(kc == 15))
            mm_group(k, pvv, xT.t, xT.b, W, 512, 512)
            mm_group(k, X0, xT.t, xT.b, W, 1536, 2)
            k.op("dve", lambda e: e.tensor_copy(out=kall.t[:, c, :], in_=pk.t[:, 0:256]), reads=[pk.b], writes=[kall.b])
            k.op("act", lambda e: e.copy(out=vall.t[:, c, :], in_=pvv.t[:]), reads=[pvv.b], writes=[vall.b])
            k.op("dve", lambda e: e.tensor_copy(out=gall.t[:, c, :], in_=X0.t[:, 0:2]), reads=[X0.b], writes=[gall.b])
        if NP > 0:
            o1 = lambda eng, fn, r, w: k.op(eng, fn, reads=[t_.b for t_ in r], writes=[t_.b for t_ in w])
            def _sl(t_):
                v_ = V(t_.t[:, 0:NP], None)
                v_.b = t_.b
                return v_
            thi, liA, thf, ezA, spA, lfA = [_sl(pv[n]) for n in ("thi", "li", "thf", "ez", "sp", "lf")]
            bloc, btot, off, bgl, wv, wst = [_sl(pv[n]) for n in ("bloc", "btot", "off", "bg", "wv", "wst")]
            gall = V(m["gall"].t[:, 0:NP, :], None)
            gall.b = m["gall"].b
            kall = V(m["kall"].t[:, 0:NP, :], None)
            kall.b = m["kall"].b
            o1("act", lambda e: e.activation(out=thi.t[:], in_=gall.t[:, :, 0], func=AF.Tanh, scale=1.0 / 15.0, bias=gb15.t[:, 0:1]),
               [gall, gb15], [thi])
            o1("act", lambda e: e.activation(out=thf.t[:], in_=gall.t[:, :, 1], func=AF.Tanh, scale=1.0 / 15.0, bias=gb15.t[:, 1:2]),
               [gall, gb15], [thf])
            o1("dve", lambda e: e.tensor_scalar(out=liA.t[:], in0=thi.t[:], scalar1=15.0, scalar2=None, op0=ALU.mult), [thi], [liA])
            o1("dve", lambda e: e.tensor_tensor(out=liA.t[:], in0=liA.t[:], in1=padneg.t[:, 0:NP], op=ALU.add), [padneg], [liA])
            o1("act", lambda e: e.activation(out=ezA.t[:], in_=thf.t[:], func=AF.Exp, scale=-15.0), [thf], [ezA])
            o1("act", lambda e: e.activation(out=spA.t[:], in_=ezA.t[:], func=AF.Ln, bias=one1.t[:, 0:1]), [ezA, one1], [spA])
            o1("dve", lambda e: e.tensor_tensor(out=lfA.t[:], in0=spA.t[:], in1=npm.t[:, 0:NP], op=ALU.mult), [spA, npm], [lfA])
            k.op("pe", lambda e: e.matmul(out=X1.t[:, 0:NP], lhsT=tri.t[:], rhs=lfA.t[:], start=True, stop=True),
                 reads=[tri.b, lfA.b], writes=[X1.b])
            k.op("dve", lambda e: e.tensor_copy(out=bloc.t[:], in_=X1.t[:, 0:NP]), reads=[X1.b], writes=[bloc.b])
            k.op("pe", lambda e: e.matmul(out=X1.t[:, 64:64 + NP], lhsT=sel.t[:], rhs=bloc.t[:], start=True, stop=True),
                 reads=[sel.b, bloc.b], writes=[X1.b])
            k.op("dve", lambda e: e.tensor_copy(out=btot.t[:], in_=X1.t[:, 64:64 + NP]), reads=[X1.b], writes=[btot.b])
            k.op("pool", lambda e: e.memset(off.t[:], 0.0), writes=[off.b])
            for c in range(1, NP):
                k.op("dve", lambda e, c=c: e.tensor_tensor(out=off.t[:, c:c + 1], in0=off.t[:, c - 1:c], in1=btot.t[:, c - 1:c], op=ALU.add),
                     reads=[btot.b], writes=[off.b])
            o1("dve", lambda e: e.tensor_tensor(out=bgl.t[:], in0=off.t[:], in1=bloc.t[:], op=ALU.add), [off, bloc], [bgl])
            o1("dve", lambda e: e.tensor_tensor(out=wv.t[:], in0=liA.t[:], in1=bgl.t[:], op=ALU.subtract), [liA, bgl], [wv])
            o1("dve", lambda e: e.tensor_reduce(out=pM.t[:], in_=wv.t[:], axis=AX.X, op=ALU.max), [wv], [pM])
            o1("dve", lambda e: e.tensor_scalar(out=dg.t[:], in0=cm.identf.t[:], scalar1=pM.t[:, 0:1], scalar2=None, op0=ALU.mult),
               [cm.identf, pM], [dg])
            k.op("pe", lambda e: e.matmul(out=X1.t[:, 128:256], lhsT=ones.t[:], rhs=dg.t[:], start=True, stop=True),
                 reads=[ones.b, dg.b], writes=[X1.b])
            k.op("dve", lambda e: e.tensor_reduce(out=pMp.t[:], in_=X1.t[:, 128:256], axis=AX.X, op=ALU.max), reads=[X1.b], writes=[pMp.b])
            o1("dve", lambda e: e.tensor_scalar(out=pMp.t[:], in0=pMp.t[:], scalar1=0.0, scalar2=None, op0=ALU.max), [], [pMp])
            o1("dve", lambda e: e.tensor_scalar(out=pnegMp.t[:], in0=pMp.t[:], scalar1=-1.0, scalar2=None, op0=ALU.mult), [pMp], [pnegMp])
            o1("dve", lambda e: e.tensor_tensor(out=pBend.t[:], in0=off.t[:, NP - 1:NP], in1=btot.t[:, NP - 1:NP], op=ALU.add),
               [off, btot], [pBend])
            o1("dve", lambda e: e.tensor_tensor(out=mprev.t[:], in0=pBend.t[:], in1=pMp.t[:], op=ALU.add), [pBend, pMp], [mprev])
            o1("act", lambda e: e.activation(out=wst.t[:], in_=wv.t[:], func=AF.Exp, bias=pnegMp.t[:, 0:1]), [wv, pnegMp], [wst])
            o1("dve", lambda e: e.tensor_tensor(out=kall.t[:], in0=kall.t[:], in1=wst.t[:].unsqueeze(2).to_broadcast([128, NP, 256]),
                                                op=ALU.mult), [wst], [kall])
            for dc, pc in ((0, P2), (1, P3)):
                for c in range(NP):
                    k.op("pe", lambda e, c=c, dc=dc, pc=pc: e.matmul(out=pc.t[:], lhsT=kall.t[:, c, dc * 128:(dc + 1) * 128],
                                                                     rhs=vall.t[:, c, :], start=(c == 0), stop=(c == NP - 1)),
                         reads=[kall.b, vall.b], writes=[pc.b], tok=(c == NP - 1))
            for dc in range(2):
                for c in range(NP):
                    k.op("pe", lambda e, c=c, dc=dc: e.matmul(out=X0.t[:, 16 + dc:17 + dc], lhsT=kall.t[:, c, dc * 128:(dc + 1) * 128],
                                                              rhs=onesb.t[:], start=(c == 0), stop=(c == NP - 1)),
                         reads=[kall.b, onesb.b], writes=[X0.b], tok=(c == NP - 1))
            k.op("dve", lambda e: e.tensor_copy(out=C.t[:, 0, :], in_=P2.t[:]), reads=[P2.b], writes=[C.b])
            k.op("dve", lambda e: e.tensor_copy(out=C.t[:, 1, :], in_=P3.t[:]), reads=[P3.b], writes=[C.b])
            k.op("act", lambda e: e.copy(out=Cb.t[:], in_=C.t[:]), reads=[C.b], writes=[Cb.b])
            k.op("dve", lambda e: e.tensor_copy(out=n_.t[:], in_=X0.t[:, 16:18]), reads=[X0.b], writes=[n_.b])
            k.op("dve", lambda e: e.tensor_copy(out=nb.t[:], in_=n_.t[:]), reads=[n_.b], writes=[nb.b])
        for c in range(NP, nchunks):
            r0 = c * 128
            xt = xs[c % 2]
            ot = ots[c % 2]
            k.dma("sp", lambda e: e.dma_start(out=xt.t[:], in_=xw_d[r0:r0 + 128, :]), writes=[xt.b])
            if bg is not None:
                bg()
            rms_rstd(k, xt, junk, ss, sq, rstd)
            k.op("act", lambda e: e.activation(out=xn.t[:], in_=xt.t[:], func=AF.Copy, scale=rstd.t[:, 0:1]),
                 reads=[xt.b, rstd.b], writes=[xn.b])
            transpose_tile(k, cm, xn, xT)
            full = c >= out_from
            if full:
                mm_group(k, P0, xT.t, xT.b, W, 0, 512)
            else:
                for kc in range(16):
                    k.op("pe", lambda e, kc=kc: e.matmul(out=P0.t[:, 256:512], lhsT=xT.t[:, kc * 128:(kc + 1) * 128],
                                                         rhs=W.t[:, kc, 256:512], start=(kc == 0), stop=(kc == 15)),
                         reads=[xT.b, W.b], writes=[P0.b], tok=(kc == 15))
            mm_group(k, P1, xT.t, xT.b, W, 512, 512)
            if full:
                mm_group(k, P2, xT.t, xT.b, W, 1024, 512)
            mm_group(k, X0, xT.t, xT.b, W, 1536, 2)
            if full:
                k.op("act", lambda e: e.activation(out=qkb.t[:, 0:256], in_=P0.t[:, 0:256], func=AF.Copy, scale=1.0 / 16.0),
                     reads=[P0.b], writes=[qkb.b])
            k.op("dve", lambda e: e.tensor_copy(out=qkb.t[:, 256:512], in_=P0.t[:, 256:512]), reads=[P0.b], writes=[qkb.b])
            k.op("dve", lambda e: e.tensor_copy(out=vb.t[:], in_=P1.t[:]), reads=[P1.b], writes=[vb.b])
            if full:
                k.op("act", lambda e: e.activation(out=sog.t[:], in_=P2.t[:], func=AF.Sigmoid), reads=[P2.b], writes=[sog.b])
                k.op("pool", lambda e: e.tensor_tensor(out=sog.t[:], in0=sog.t[:], in1=hnw.t[:], op=ALU.mult),
                     reads=[hnw.b], writes=[sog.b])
            k.op("dve", lambda e: e.tensor_copy(out=gs.t[:], in_=X0.t[:, 0:2]), reads=[X0.b], writes=[gs.b])
            if full:
                transpose_tile(k, cm, qkb, qkT, nchunks=4)
            for col in range(2):
                k.op("act", lambda e, col=col: e.activation(out=th.t[:, col:col + 1], in_=gs.t[:, col:col + 1], func=AF.Tanh,
                                                            scale=1.0 / 15.0, bias=gb15.t[:, col:col + 1]),
                     reads=[gs.b, gb15.b], writes=[th.b])
            k.op("dve", lambda e: e.tensor_scalar(out=li.t[:], in0=th.t[:, 0:1], scalar1=15.0, scalar2=None, op0=ALU.mult),
                 reads=[th.b], writes=[li.b])
            k.op("dve", lambda e: e.tensor_tensor(out=li.t[:], in0=li.t[:], in1=padneg.t[:, c:c + 1], op=ALU.add),
                 reads=[padneg.b], writes=[li.b])
            k.op("act", lambda e: e.activation(out=ez.t[:], in_=th.t[:, 1:2], func=AF.Exp, scale=-15.0), reads=[th.b], writes=[ez.b])
            k.op("act", lambda e: e.activation(out=sp.t[:], in_=ez.t[:], func=AF.Ln, bias=one1.t[:, 0:1]),
                 reads=[ez.b, one1.b], writes=[sp.b])
            k.op("dve", lambda e: e.tensor_tensor(out=lf.t[:], in0=sp.t[:], in1=npm.t[:, c:c + 1], op=ALU.mult),
                 reads=[sp.b, npm.b], writes=[lf.b])
            k.op("pe", lambda e: e.matmul(out=X0.t[:, 4:5], lhsT=tri.t[:], rhs=lf.t[:], start=True, stop=True),
                 reads=[tri.b, lf.b], writes=[X0.b])
            k.op("dve", lambda e: e.tensor_copy(out=bcs.t[:], in_=X0.t[:, 4:5]), reads=[X0.b], writes=[bcs.b])
            k.op("dve", lambda e: e.tensor_tensor(out=gvec.t[:], in0=li.t[:], in1=bcs.t[:], op=ALU.subtract),
                 reads=[li.b, bcs.b], writes=[gvec.b])
            k.op("dve", lambda e: e.tensor_scalar(out=dg.t[:], in0=cm.identf.t[:], scalar1=gvec.t[:, 0:1], scalar2=None, op0=ALU.mult),
                 reads=[cm.identf.b, gvec.b], writes=[dg.b])
            k.op("pe", lambda e: e.matmul(out=X1.t[:, 0:128], lhsT=ones.t[:], rhs=dg.t[:], start=True, stop=True),
                 reads=[ones.b, dg.b], writes=[X1.b])
            k.op("dve", lambda e: e.tensor_tensor(out=A.t[:], in0=X1.t[:, 0:128], in1=cmask.t[:], op=ALU.add),
                 reads=[X1.b, cmask.b], writes=[A.b])
            k.op("dve", lambda e: e.tensor_reduce(out=mrow.t[:], in_=A.t[:], axis=AX.X, op=ALU.max), reads=[A.b], writes=[mrow.b])
            k.op("dve", lambda e: e.tensor_tensor(out=u.t[:], in0=mrow.t[:], in1=mprev.t[:], op=ALU.max),
                 reads=[mrow.b, mprev.b], writes=[u.b])
            k.op("dve", lambda e: e.tensor_scalar(out=negu.t[:], in0=u.t[:], scalar1=-1.0, scalar2=None, op0=ALU.mult),
                 reads=[u.b], writes=[negu.b])
            k.op("dve", lambda e: e.tensor_tensor(out=mt.t[:], in0=bcs.t[:], in1=u.t[:], op=ALU.add), reads=[bcs.b, u.b], writes=[mt.b])
            if full:
                k.op("act", lambda e: e.activation(out=wintra.t[:], in_=A.t[:], func=AF.Exp, bias=negu.t[:, 0:1]),
                     reads=[A.b, negu.b], writes=[wintra.b])
                k.op("dve", lambda e: e.tensor_tensor(out=wia.t[:], in0=mprev.t[:], in1=u.t[:], op=ALU.subtract),
                     reads=[mprev.b, u.b], writes=[wia.b])
                k.op("act", lambda e: e.activation(out=winter.t[:], in_=wia.t[:], func=AF.Exp), reads=[wia.b], writes=[winter.b])
                for dc in range(2):
                    k.op("pe", lambda e, dc=dc: e.matmul(out=X1.t[:, 128:256], lhsT=qkT.t[:, dc * 128:(dc + 1) * 128],
                                                         rhs=qkT.t[:, (2 + dc) * 128:(3 + dc) * 128], start=(dc == 0), stop=(dc == 1)),
                         reads=[qkT.b], writes=[X1.b], tok=(dc == 1))
                k.op("dve", lambda e: e.scalar_tensor_tensor(out=Pb.t[:], in0=X1.t[:, 128:256], scalar=1.0, in1=wintra.t[:],
                                                             op0=ALU.mult, op1=ALU.mult, accum_out=dintra.t[:, 0:1]),
                     reads=[X1.b, wintra.b], writes=[Pb.b, dintra.b])
                pst = cm.ps_tr[0]
                k.op("pe", lambda e: e.transpose(out=pst.t[:, 0:128], in_=Pb.t[:], identity=cm.ident.t[:]),
                     reads=[Pb.b, cm.ident.b], writes=[pst.b])
                k.op("dve", lambda e: e.tensor_copy(out=PT.t[:], in_=pst.t[:, 0:128]), reads=[pst.b], writes=[PT.b])
                for dc in range(2):
                    k.op("pe", lambda e, dc=dc: e.matmul(out=P0.t[:], lhsT=qkT.t[:, dc * 128:(dc + 1) * 128], rhs=Cb.t[:, dc, :],
                                                         start=(dc == 0), stop=(dc == 1)),
                         reads=[qkT.b, Cb.b], writes=[P0.b], tok=(dc == 1))
                for dc in range(2):
                    k.op("pe", lambda e, dc=dc: e.matmul(out=X0.t[:, 12:13], lhsT=qkT.t[:, dc * 128:(dc + 1) * 128], rhs=nb.t[:, dc:dc + 1],
                                                         start=(dc == 0), stop=(dc == 1)),
                         reads=[qkT.b, nb.b], writes=[X0.b], tok=(dc == 1))
                k.op("pe", lambda e: e.matmul(out=P1.t[:], lhsT=PT.t[:], rhs=vb.t[:], start=True, stop=True),
                     reads=[PT.b, vb.b], writes=[P1.b])
                k.op("act", lambda e: e.activation(out=tmp.t[:], in_=P0.t[:], func=AF.Copy, scale=winter.t[:, 0:1]),
                     reads=[P0.b, winter.b], writes=[tmp.b])
                k.op("dve", lambda e: e.tensor_tensor(out=num.t[:], in0=P1.t[:], in1=tmp.t[:], op=ALU.add),
                     reads=[P1.b, tmp.b], writes=[num.b])
                k.op("dve", lambda e: e.scalar_tensor_tensor(out=den.t[:], in0=X0.t[:, 12:13], scalar=winter.t[:, 0:1], in1=dintra.t[:],
                                                             op0=ALU.mult, op1=ALU.add),
                     reads=[X0.b, winter.b, dintra.b], writes=[den.b])
                k.op("dve", lambda e: e.tensor_scalar(out=aden.t[:], in0=den.t[:], scalar1=-1.0, scalar2=None, op0=ALU.mult),
                     reads=[den.b], writes=[aden.b])
                k.op("dve", lambda e: e.tensor_tensor(out=aden.t[:], in0=aden.t[:], in1=den.t[:], op=ALU.max),
                     reads=[den.b], writes=[aden.b])
                k.op("act", lambda e: e.activation(out=emt.t[:], in_=mt.t[:], func=AF.Exp, scale=-1.0), reads=[mt.b], writes=[emt.b])
                k.op("dve", lambda e: e.tensor_tensor(out=dmax.t[:], in0=aden.t[:], in1=emt.t[:], op=ALU.max),
                     reads=[aden.b, emt.b], writes=[dmax.b])
                k.op("dve", lambda e: e.reciprocal(out=rden.t[:], in_=dmax.t[:]), reads=[dmax.b], writes=[rden.b])
                k.op("act", lambda e: e.activation(out=junk.t[:, 0:512], in_=num.t[:], func=AF.Square, accum_out=ssn.t[:, 0:1]),
                     reads=[num.b], writes=[junk.b, ssn.b])
                k.op("dve", lambda e: e.tensor_tensor(out=t1.t[:], in0=ssn.t[:], in1=rden.t[:], op=ALU.mult), reads=[ssn.b, rden.b], writes=[t1.b])
                k.op("dve", lambda e: e.tensor_tensor(out=t1.t[:], in0=t1.t[:], in1=rden.t[:], op=ALU.mult), reads=[rden.b], writes=[t1.b])
                k.op("act", lambda e: e.activation(out=sqv.t[:], in_=t1.t[:], func=AF.Sqrt, scale=1.0 / 512.0, bias=eps_ap),
                     reads=[t1.b], writes=[sqv.b])
                k.op("dve", lambda e: e.reciprocal(out=rstd2.t[:], in_=sqv.t[:]), reads=[sqv.b], writes=[rstd2.b])
                k.op("dve", lambda e: e.tensor_tensor(out=sc.t[:], in0=rden.t[:], in1=rstd2.t[:], op=ALU.mult), reads=[rden.b, rstd2.b], writes=[sc.b])
                k.op("dve", lambda e: e.scalar_tensor_tensor(out=ot.t[:], in0=num.t[:], scalar=sc.t[:, 0:1], in1=sog.t[:],
                                                             op0=ALU.mult, op1=ALU.mult),
                     reads=[num.b, sc.b, sog.b], writes=[ot.b])
                ti = c - out_from
                k.dma("pool", lambda e: e.dma_start(out=hg_d[ti * 128:(ti + 1) * 128, hd * 512:(hd + 1) * 512], in_=ot.t[:]),
                      reads=[ot.b], writes=[hg_bufs[ti]])
            k.op("dve", lambda e: e.tensor_copy(out=mb.t[:, 0:1], in_=mt.t[:]), reads=[mt.b], writes=[mb.b])
            k.op("dve", lambda e: e.tensor_copy(out=mb.t[:, 1:2], in_=bcs.t[:]), reads=[bcs.b], writes=[mb.b])
            k.op("pe", lambda e: e.matmul(out=X0.t[:, 8:10], lhsT=sel.t[:], rhs=mb.t[:], start=True, stop=True),
                 reads=[sel.b, mb.b], writes=[X0.b])
            k.op("dve", lambda e: e.tensor_copy(out=last2.t[:], in_=X0.t[:, 8:10]), reads=[X0.b], writes=[last2.b])
            k.op("dve", lambda e: e.tensor_tensor(out=dlt.t[:], in0=last2.t[:, 1:2], in1=last2.t[:, 0:1], op=ALU.subtract),
                 reads=[last2.b], writes=[dlt.b])
            k.op("act", lambda e: e.activation(out=wstate.t[:], in_=gvec.t[:], func=AF.Exp, bias=dlt.t[:, 0:1]),
                 reads=[gvec.b, dlt.b], writes=[wstate.b])
            k.op("dve", lambda e: e.tensor_tensor(out=deca.t[:], in0=dlt.t[:], in1=mprev.t[:], op=ALU.add),
                 reads=[dlt.b, mprev.b], writes=[deca.b])
            k.op("act", lambda e: e.activation(out=decay.t[:], in_=deca.t[:], func=AF.Exp), reads=[deca.b], writes=[decay.b])
            k.op("dve", lambda e: e.tensor_scalar(out=kw.t[:], in0=qkb.t[:, 256:512], scalar1=wstate.t[:, 0:1], scalar2=None, op0=ALU.mult),
                 reads=[qkb.b, wstate.b], writes=[kw.b])
            k.op("pe", lambda e: e.matmul(out=P2.t[:], lhsT=kw.t[:, 0:128], rhs=vb.t[:], start=True, stop=True),
                 reads=[kw.b, vb.b], writes=[P2.b])
            k.op("pe", lambda e: e.matmul(out=P3.t[:], lhsT=kw.t[:, 128:256], rhs=vb.t[:], start=True, stop=True),
                 reads=[kw.b, vb.b], writes=[P3.b])
            k.op("pe", lambda e: e.matmul(out=X0.t[:, 16:17], lhsT=kw.t[:, 0:128], rhs=onesb.t[:], start=True, stop=True),
                 reads=[kw.b, onesb.b], writes=[X0.b], tok=False)
            k.op("pe", lambda e: e.matmul(out=X0.t[:, 17:18], lhsT=kw.t[:, 128:256], rhs=onesb.t[:], start=True, stop=True),
                 reads=[kw.b, onesb.b], writes=[X0.b])
            k.op("dve", lambda e: e.scalar_tensor_tensor(out=C.t[:, 0, :], in0=C.t[:, 0, :], scalar=decay.t[:, 0:1], in1=P2.t[:],
                                                         op0=ALU.mult, op1=ALU.add), reads=[decay.b, P2.b], writes=[C.b])
            k.op("dve", lambda e: e.scalar_tensor_tensor(out=C.t[:, 1, :], in0=C.t[:, 1, :], scalar=decay.t[:, 0:1], in1=P3.t[:],
                                                         op0=ALU.mult, op1=ALU.add), reads=[decay.b, P3.b], writes=[C.b])
            k.op("act", lambda e: e.copy(out=Cb.t[:], in_=C.t[:]), reads=[C.b], writes=[Cb.b])
            k.op("dve", lambda e: e.scalar_tensor_tensor(out=n_.t[:], in0=n_.t[:], scalar=decay.t[:, 0:1], in1=X0.t[:, 16:18],
                                                         op0=ALU.mult, op1=ALU.add), reads=[decay.b, X0.b], writes=[n_.b])
            k.op("dve", lambda e: e.tensor_copy(out=nb.t[:], in_=n_.t[:]), reads=[n_.b], writes=[nb.b])
            k.op("dve", lambda e: e.tensor_copy(out=mprev.t[:], in_=last2.t[:, 0:1]), reads=[last2.b], writes=[mprev.b])


def mlstm_consts():
    s = np.arange(128)[:, None]
    t = np.arange(128)[None, :]
    tri = (s <= t).astype(np.float32)
    sel = np.zeros((128, 128), np.float32)
    sel[127, :] = 1.0
    cmask = np.where(t <= s, 0.0, -1e30).astype(np.float32)
    return tri, sel, cmask


def mlstm_inmap(x_b, a_norm, a_w_in, gate_bias, head_norm, hd):
    o0, o1, o2, o3 = 1024, 2048, 4096, 6144
    cols = np.concatenate([np.arange(hd * 256, (hd + 1) * 256), o0 + np.arange(hd * 256, (hd + 1) * 256),
                           o1 + np.arange(hd * 512, (hd + 1) * 512), o2 + np.arange(hd * 512, (hd + 1) * 512),
                           np.array([o3 + hd, o3 + 4 + hd])])
    tri, sel, cmask = mlstm_consts()
    return {
        "x_in": np.ascontiguousarray(x_b),
        "w_in": np.ascontiguousarray(a_w_in[:, cols]),
        "nw_in": np.ascontiguousarray(a_norm.reshape(16, 128).T),
        "gb_in": np.ascontiguousarray(np.broadcast_to(gate_bias[:, hd][None, :], (128, 2))).astype(np.float32),
        "hnw_in": np.ascontiguousarray(head_norm[hd * 512:(hd + 1) * 512]),
        "tri_in": tri, "sel_in": sel, "cmask_in": cmask, "ident_in": np.eye(128, dtype=np.float32),
    }


NT0 = 17
NT1 = 16
NW = 65


def build_fused():
    nc = bass.Bass("TRN2", target_bir_lowering=False)
    di = lambda name, shape: nc.dram_tensor(name, list(shape), F32, kind="ExternalInput").ap()
    xw_d = di("xw", [NW * 128, D])
    p0_d = di("p0_sh", [NT0 * 128, 256])
    p1_d = di("p1_sh", [NT1 * 128, 256])
    w4_d = di("mw4", [4, D, 1538])
    mnw_d = di("mnw", [128, 16])
    gb4_d = di("mgb4", [4, 128, 2])
    hnw_d = di("mhnw", [D])
    tri_d = di("tri_in", [128, 128])
    sel_d = di("sel_in", [128, 128])
    cmask_d = di("cmask_in", [128, 128])
    padneg_d = di("padneg", [128, NW])
    npm_d = di("npm", [128, NW])
    wout0_d = di("wout0", [D, D])
    wout1_d = di("wout1", [D, D])
    peer = []
    for L in range(2):
        peer.append(dict(cw=di("cw%d" % L, [D]), wq=di("pwq%d" % L, [D, D]), k1=di("k1T%d" % L, [8, 128, 128]),
                         k2=di("k2T%d" % L, [8, 128, 128]), u=di("u%d" % L, [16384, D]), v=di("v%d" % L, [16384, D])))
    ple = []
    for L in range(2):
        ple.append(dict(nw=di("enw%d" % L, [128, 16]), wg=di("ewg%d" % L, [D, D]), wp=di("ewp%d" % L, [256, D])))
    fw_d = di("fw", [D])
    nq_d = di("nq", [128, 16])
    nkv_d = di("nkv", [128, 16])
    bwq_d = di("bwq", [D, D])
    wkv_d = di("wkv", [D, 512])
    sinks_d = di("sinks", [32])
    mask_d = di("mask", [128, 256])
    mask1_d = di("mask1", [128, 256])
    id_d = di("ident_in", [128, 128])
    out_d = nc.dram_tensor("out", [NT1 * 128, D], F32, kind="ExternalOutput").ap()
    hg_d = nc.dram_tensor("hg_scr", [NT0 * 128, D], F32, kind="Internal").ap()
    hs_d = nc.dram_tensor("hs_scr", [NT0 * 128, D], F32, kind="Internal").ap()
    att_d = nc.dram_tensor("att_scr", [NT1 * 128, D], F32, kind="Internal").ap()
    x0 = (NW - NT0) * 128
    tbl = []
    for L in range(2):
        for nm in ("u", "v"):
            tb = nc.dram_tensor("tb_%s%d" % (nm, L), [16384, D], BF16, kind="Internal").ap()
            tbl.append((peer[L][nm], tb))
            peer[L][nm + "b"] = tb
    with ExitStack() as stack:
        k = K(nc, stack)
        cm = Common(k, id_d)
        cm_eps(k)
        hg_bufs = [Buf("hg%d" % i) for i in range(NT0)]
        hs_bufs = [Buf("hs%d" % i) for i in range(NT0)]
        att_bufs = [Buf("at%d" % i) for i in range(NT1)]

        def phase(alloc, fn):
            with ExitStack() as ps:
                k.stack = ps
                b = alloc(k)
                fn(b)
                k.barrier()
            k.stack = stack

        conv = [(src_, dst_, i) for (src_, dst_) in tbl for i in range(128)]
        conv.reverse()

        def run_mlstm(m):
            stg = [T(k, "cstg%d" % i, [128, 2048], BF16) for i in range(3)]
            cnt = [0]

            def bg(n=2):
                for _ in range(n):
                    if not conv:
                        return
                    src_, dst_, i = conv.pop()
                    s = stg[cnt[0] % 3]
                    cnt[0] += 1
                    k.dma("pool", lambda e: e.dma_start(out=s.t[:], in_=src_[i * 128:(i + 1) * 128, :]), writes=[s.b])
                    k.dma("sp", lambda e: e.dma_start(out=dst_[i * 128:(i + 1) * 128, :], in_=s.t[:]), reads=[s.b])
            phase_mlstm(k, cm, m, xw_d, w4_d, mnw_d, gb4_d, hnw_d, tri_d, sel_d, cmask_d,
                        padneg_d, npm_d, NW, NW - NT0, hg_d, hg_bufs, bg=bg)
            while conv:
                bg()
        phase(mlstm_alloc, run_mlstm)
        phase(mmres_alloc, lambda b: phase_mmres(k, cm, b, wout0_d, NT0, hg_d, 0, hg_bufs, xw_d, x0, None, hs_d, 0, hs_bufs))
        phase(peer_alloc, lambda b: phase_peer(k, cm, b, hs_d, hs_bufs, list(range(NT0)), peer[0]["cw"], peer[0]["wq"],
                                               peer[0]["k1"], peer[0]["k2"], peer[0]["ub"], peer[0]["vb"]))
        phase(ple_alloc, lambda b: phase_ple(k, cm, b, hs_d, hs_bufs, list(range(NT0)), p0_d, 0, ple[0]["nw"], ple[0]["wg"],
                                             ple[0]["wp"]))
        phase(att_alloc, lambda b: phase_att(k, cm, b, hs_d, hs_bufs, NT0, att_d, att_bufs, nq_d, nkv_d, bwq_d, wkv_d,
                                             sinks_d, mask_d, mask1_d))
        phase(mmres_alloc, lambda b: phase_mmres(k, cm, b, wout1_d, NT1, att_d, 0, att_bufs, hs_d, 128, hs_bufs[1:],
                                                 hs_d, 128, hs_bufs[1:]))
        phase(peer_alloc, lambda b: phase_peer(k, cm, b, hs_d, hs_bufs, list(range(1, NT0)), peer[1]["cw"], peer[1]["wq"],
                                               peer[1]["k1"], peer[1]["k2"], peer[1]["ub"], peer[1]["vb"]))
        phase(ple_alloc, lambda b: phase_ple(k, cm, b, hs_d, hs_bufs, list(range(1, NT0)), p1_d, -128, ple[1]["nw"],
                                             ple[1]["wg"], ple[1]["wp"], final=(fw_d, out_d, 0)))
        drain(k)
        print("fused ninst", k.ninst, "nsem", k.nsem)
    return nc


def _r16(v):
    return np.ascontiguousarray(np.asarray(v, np.float32).reshape(16, 128).T)


def kernel(x, p, a_norm, a_w_in, a_gate_bias, a_head_norm, a_w_out, kv_norm, w_kv,
           b_norm, b_w_q, b_sinks, b_w_out, c_norm, peer_w_q, peer_k1, peer_k2,
           peer_u, peer_v, ple_norm, ple_w_gate, ple_w_proj, final_norm):
    f = lambda a: np.ascontiguousarray(np.asarray(a, dtype=np.float32))
    x = f(x)
    p = f(p)
    B, S, _ = x.shape
    nc = build_fused()
    m, m1 = band_masks()
    tri, sel, cmask = mlstm_consts()
    a_w_in0 = f(a_w_in)[0]
    gbias = f(a_gate_bias)[0]
    o0, o1, o2, o3 = 1024, 2048, 4096, 6144
    w4 = np.zeros((4, D, 1538), np.float32)
    gb4 = np.zeros((4, 128, 2), np.float32)
    for hd in range(4):
        cols = np.concatenate([np.arange(hd * 256, (hd + 1) * 256), o0 + np.arange(hd * 256, (hd + 1) * 256),
                               o1 + np.arange(hd * 512, (hd + 1) * 512), o2 + np.arange(hd * 512, (hd + 1) * 512),
                               np.array([o3 + hd, o3 + 4 + hd])])
        w4[hd] = a_w_in0[:, cols]
        gb4[hd] = np.broadcast_to(gbias[:, hd][None, :], (128, 2))
    shared = {
        "mw4": w4, "mnw": _r16(f(a_norm)[0]), "mgb4": gb4, "mhnw": f(a_head_norm)[0],
        "tri_in": tri, "sel_in": sel, "cmask_in": cmask,
        "wout0": f(a_w_out)[0], "wout1": f(b_w_out)[0], "fw": f(final_norm),
        "nq": _r16(f(b_norm)[0]), "nkv": _r16(kv_norm), "bwq": f(b_w_q)[0], "wkv": f(w_kv), "sinks": f(b_sinks)[0],
        "mask": m, "ident_in": np.eye(128, dtype=np.float32),
    }
    for L in range(2):
        shared["cw%d" % L] = f(c_norm)[L]
        shared["pwq%d" % L] = f(peer_w_q)[L]
        shared["k1T%d" % L] = np.ascontiguousarray(f(peer_k1)[L].transpose(0, 2, 1))
        shared["k2T%d" % L] = np.ascontiguousarray(f(peer_k2)[L].transpose(0, 2, 1))
        shared["u%d" % L] = f(peer_u)[L]
        shared["v%d" % L] = f(peer_v)[L]
        shared["enw%d" % L] = _r16(f(ple_norm)[L])
        shared["ewg%d" % L] = f(ple_w_gate)[L]
        shared["ewp%d" % L] = f(ple_w_proj)[L]
    TPC = S // 4
    WT = NW * 128
    maps = []
    for c in range(NCORES):
        bb, qd = c // 4, c % 4
        e0 = (qd + 1) * TPC
        npad = max(0, WT - e0)

        def window(arr, width, ntok):
            o = np.zeros((ntok, width), np.float32)
            lo = e0 - ntok
            if lo < 0:
                o[-lo:] = arr[0:e0]
            else:
                o[:] = arr[lo:e0]
            return o
        mp = dict(shared)
        mp["xw"] = window(x[bb], D, WT)
        mp["p0_sh"] = window(p[0, bb], 256, NT0 * 128)
        mp["p1_sh"] = np.ascontiguousarray(p[1, bb, e0 - TPC:e0])
        padc = npad // 128
        pn = np.zeros((128, NW), np.float32)
        pn[:, :padc] = -1e30
        pm = -np.ones((128, NW), np.float32)
        pm[:, :padc] = 0.0
        mp["padneg"] = pn
        mp["npm"] = pm
        mp["mask1"] = m1 if qd == 0 else m
        maps.append(mp)
    res = run_bass_kernel_spmd(nc, maps, core_ids=list(range(NCORES)))
    out = np.zeros((B, S, D), np.float32)
    for c in range(NCORES):
        bb, qd = c // 4, c % 4
        out[bb, qd * TPC:(qd + 1) * TPC] = res.results[c]["out"]
    return out
```

```python
from contextlib import ExitStack
import numpy as np
import concourse.bass as bass
import concourse.mybir as mybir
from concourse.bass_utils import run_bass_kernel_spmd

F32 = mybir.dt.float32
BF16 = mybir.dt.bfloat16
I32 = mybir.dt.int32
U32 = mybir.dt.uint32
AF = mybir.ActivationFunctionType
ALU = mybir.AluOpType
AX = mybir.AxisListType

SEM_LIMIT = 20000
ATT_STOP = 0
D = 2048
NCORES = 8


class Buf:
    __slots__ = ("name", "wr", "rd", "pend")

    def __init__(self, name):
        self.name = name
        self.wr = None
        self.rd = []
        self.pend = False


class T:
    def __init__(self, k, name, shape, dtype, psum=False):
        self.t = k.ps(name, shape, dtype) if psum else k.sb(name, shape, dtype)
        self.b = Buf(name)


class K:
    def __init__(self, nc, stack):
        self.nc = nc
        self.stack = stack
        self.sem_stack = stack
        self.eng = {"pe": nc.tensor, "act": nc.scalar, "dve": nc.vector,
                    "pool": nc.gpsimd, "sp": nc.sync}
        self.csem = {}
        self.known = {e: {} for e in self.eng}
        self.sems = {}
        self.nsem = 0
        self.dq = {}
        self.drr = {}
        self.ninst = 0
        self.pe_pend_r = []
        self.pe_pend_w = []
        self.retired = []

    def barrier(self):
        assert not self.pe_pend_r and not self.pe_pend_w
        toks = [(cs[0], cs[1], e) for e, cs in self.csem.items()]
        for q, slots in self.dq.items():
            for s in slots:
                if s[1] > 0:
                    toks.append((s[0], s[1], "dma"))
        for e in self.eng:
            for t in toks:
                if t[2] == e == "pe":
                    continue
                self._need(e, t)

    def new_sem(self, tag):
        key = "%s_%d" % (tag, self.nsem)
        self.nsem += 1
        h = self.sem_stack.enter_context(self.nc.semaphore(key))
        self.sems[key] = h
        return key

    def sb(self, name, shape, dtype):
        self.ntens = getattr(self, "ntens", 0) + 1
        return self.stack.enter_context(self.nc.sbuf_tensor("%s_%d" % (name, self.ntens), list(shape), dtype))

    def ps(self, name, shape, dtype):
        self.ntens = getattr(self, "ntens", 0) + 1
        return self.stack.enter_context(self.nc.psum_tensor("%s_%d" % (name, self.ntens), list(shape), dtype))

    def _need(self, e, tok):
        if tok is None:
            return
        key, val, teng = tok
        if teng == "pe" and e == "pe":
            return
        if self.known[e].get(key, 0) >= val:
            return
        self.eng[e].wait_ge(self.sems[key], val)
        self.known[e][key] = val

    def _waits(self, e, reads, writes):
        for b in reads:
            if b.pend and e != "pe":
                raise RuntimeError("pending PE access on %s" % b.name)
            self._need(e, b.wr)
        for b in writes:
            if b.pend and e != "pe":
                raise RuntimeError("pending PE access on %s" % b.name)
            self._need(e, b.wr)
            for t in b.rd:
                self._need(e, t)

    def _update(self, tok, reads, writes):
        for b in reads:
            b.rd = [t for t in b.rd if t[0] != tok[0]] + [tok]
        for b in writes:
            b.wr = tok
            b.rd = []

    def op(self, e, fn, reads=(), writes=(), tok=True):
        self._waits(e, reads, writes)
        inst = fn(self.eng[e])
        self.ninst += 1
        if not tok:
            assert e == "pe"
            for b in reads:
                b.pend = True
                self.pe_pend_r.append(b)
            for b in writes:
                b.pend = True
                self.pe_pend_w.append(b)
            return None
        cs = self.csem.get(e)
        if cs is None or cs[1] >= SEM_LIMIT:
            cs = [self.new_sem("c" + e), 0]
            self.csem[e] = cs
        cs[1] += 1
        inst.then_inc(self.sems[cs[0]], 1)
        t = (cs[0], cs[1], e)
        if e == "pe" and (self.pe_pend_r or self.pe_pend_w):
            for b in self.pe_pend_r + self.pe_pend_w:
                b.pend = False
            self._update(t, self.pe_pend_r, self.pe_pend_w)
            self.pe_pend_r = []
            self.pe_pend_w = []
        self._update(t, reads, writes)
        return t

    def dma(self, q, fn, reads=(), writes=(), nslots=8):
        self._waits(q, reads, writes)
        if q not in self.dq:
            self.dq[q] = [[self.new_sem("d" + q), 0] for _ in range(nslots)]
            self.drr[q] = 0
        slots = self.dq[q]
        i = self.drr[q]
        self.drr[q] = (i + 1) % len(slots)
        s = slots[i]
        if s[1] > 0:
            self._need(q, (s[0], s[1], "dma"))
        if s[1] >= 30000:
            self.retired.append((s[0], s[1], "dma"))
            s[0] = self.new_sem("d" + q)
            s[1] = 0
        inst = fn(self.eng[q])
        s[1] += 16
        inst.then_inc(self.sems[s[0]], 16)
        t = (s[0], s[1], "dma")
        self._update(t, reads, writes)
        self.ninst += 1
        return t

    def finish(self, bufs):
        for b in bufs:
            self._need("sp", b.wr)


class Common:
    def __init__(self, k, ident_d):
        self.k = k
        self.ident = T(k, "ident", [128, 128], BF16)
        self.identf = T(k, "identf", [128, 128], F32)
        k.dma("sp", lambda e: e.dma_start(out=self.identf.t[:], in_=ident_d[:, :]), writes=[self.identf.b])
        k.op("dve", lambda e: e.tensor_copy(out=self.ident.t[:], in_=self.identf.t[:]),
             reads=[self.identf.b], writes=[self.ident.b])
        self.wstage = [T(k, "wstage%d" % i, [128, 2048], F32) for i in range(2)]
        self.nw = 0
        self.psA = k.ps("psA", [128, 2048], F32)
        self.psB = k.ps("psB", [128, 2048], F32)
        self.ps_o = [V(self.psA[:, i * 512:(i + 1) * 512], "bank%d" % i) for i in range(4)]
        self.ps_tr = [V(self.psB[:, i * 512:(i + 1) * 512].bitcast(BF16), "bank%d" % (4 + i)) for i in range(2)]
        self.ps_x = [V(self.psB[:, (2 + i) * 512:(3 + i) * 512], "bank%d" % (6 + i)) for i in range(2)]
        self.ps_y = [V(self.psB[:, i * 512:(i + 1) * 512], None) for i in range(4)]
        self.ps_y[0].b = self.ps_tr[0].b
        self.ps_y[1].b = self.ps_tr[1].b
        self.ps_y[2].b = self.ps_x[0].b
        self.ps_y[3].b = self.ps_x[1].b
        self.bankA = [v.b for v in self.ps_o]
        self.bankB = [v.b for v in self.ps_y]


class V:
    def __init__(self, ap, name):
        self.t = ap
        self.b = Buf(name) if name is not None else None


def load_w(k, cm, W, wd, ncols, kchunks, scale=None, col0=0):
    for kc in range(kchunks):
        for c0 in range(0, ncols, 2048):
            cw = min(2048, ncols - c0)
            st = cm.wstage[cm.nw % 2]
            cm.nw += 1
            k.dma("sp", lambda e, st=st, kc=kc, c0=c0, cw=cw: e.dma_start(
                out=st.t[:, :cw], in_=wd[kc * 128:(kc + 1) * 128, c0:c0 + cw]), writes=[st.b])
            if scale is None:
                k.op("pool", lambda e, st=st, kc=kc, c0=c0, cw=cw: e.tensor_copy(
                    out=W.t[:, kc, col0 + c0:col0 + c0 + cw], in_=st.t[:, :cw]),
                    reads=[st.b], writes=[W.b])
            else:
                k.op("pool", lambda e, st=st, kc=kc, c0=c0, cw=cw: e.tensor_scalar(
                    out=W.t[:, kc, col0 + c0:col0 + c0 + cw], in0=st.t[:, :cw],
                    scalar1=scale.t[:, kc:kc + 1], scalar2=None, op0=ALU.mult),
                    reads=[st.b, scale.b], writes=[W.b])


def transpose_tile(k, cm, src_bf, dstT, nchunks=16):
    for half in range((nchunks + 7) // 8):
        pst = cm.ps_tr[half % 2]
        n = min(8, nchunks - half * 8)
        for j in range(n):
            kc = half * 8 + j
            k.op("pe", lambda e, j=j, kc=kc, pst=pst: e.transpose(
                out=pst.t[:, j * 128:(j + 1) * 128], in_=src_bf.t[:, kc * 128:(kc + 1) * 128],
                identity=cm.ident.t[:]), reads=[src_bf.b, cm.ident.b], writes=[pst.b], tok=(j == n - 1))
        k.op("dve", lambda e, half=half, pst=pst, n=n: e.tensor_copy(
            out=dstT.t[:, half * 1024:half * 1024 + n * 128], in_=pst.t[:, :n * 128]),
            reads=[pst.b], writes=[dstT.b])


def mm_group(k, out_ps, lhsT_t, lhsT_b, W, col0, ncols, kchunks=16):
    for kc in range(kchunks):
        k.op("pe", lambda e, kc=kc: e.matmul(
            out=out_ps.t[:, :ncols], lhsT=lhsT_t[:, kc * 128:(kc + 1) * 128],
            rhs=W.t[:, kc, col0:col0 + ncols], start=(kc == 0), stop=(kc == kchunks - 1)),
            reads=[lhsT_b, W.b], writes=[out_ps.b], tok=(kc == kchunks - 1))


def mmres_alloc(k):
    return {
        "W": T(k, "rW", [128, 16, 2048], BF16),
        "a": [T(k, "ra%d" % i, [128, 2048], F32) for i in range(2)],
        "x": [T(k, "rx%d" % i, [128, 2048], F32) for i in range(2)],
        "ab": T(k, "rab", [128, 2048], BF16),
        "aT": T(k, "raT", [128, 2048], BF16),
        "o": [T(k, "ro%d" % i, [128, 2048], F32) for i in range(2)],
    }


def phase_mmres(k, cm, bufs, w_d, ntiles, a_d, a_row0, a_bufs, x_d, x_row0, x_bufs, out_d, out_row0, out_bufs):
    W = bufs["W"]
    load_w(k, cm, W, w_d, 2048, 16)
    for i in range(ntiles):
        a = bufs["a"][i % 2]
        xt = bufs["x"][i % 2]
        ab = bufs["ab"]
        aT = bufs["aT"]
        ot = bufs["o"][i % 2]
        ra, rx, ro = a_row0 + i * 128, x_row0 + i * 128, out_row0 + i * 128
        k.dma("sp", lambda e: e.dma_start(out=a.t[:], in_=a_d[ra:ra + 128, :]),
              reads=[a_bufs[i]] if a_bufs else [], writes=[a.b])
        k.dma("sp", lambda e: e.dma_start(out=xt.t[:], in_=x_d[rx:rx + 128, :]),
              reads=[x_bufs[i]] if x_bufs else [], writes=[xt.b])
        k.op("act", lambda e: e.copy(out=ab.t[:], in_=a.t[:]), reads=[a.b], writes=[ab.b])
        transpose_tile(k, cm, ab, aT)
        for ng in range(4):
            mm_group(k, cm.ps_o[ng], aT.t, aT.b, W, ng * 512, 512)
            k.op("dve", lambda e, ng=ng: e.tensor_tensor(
                out=ot.t[:, ng * 512:(ng + 1) * 512], in0=cm.ps_o[ng].t[:],
                in1=xt.t[:, ng * 512:(ng + 1) * 512], op=ALU.add),
                reads=[cm.ps_o[ng].b, xt.b], writes=[ot.b])
        k.dma("pool", lambda e: e.dma_start(out=out_d[ro:ro + 128, :], in_=ot.t[:]), reads=[ot.b],
              writes=[out_bufs[i]] if out_bufs else [])


def build_test_b1(ntiles):
    nc = bass.Bass("TRN2", target_bir_lowering=False)
    x_d = nc.dram_tensor("x_in", [ntiles * 128, D], F32, kind="ExternalInput").ap()
    hg_d = nc.dram_tensor("hg_in", [ntiles * 128, D], F32, kind="ExternalInput").ap()
    w_d = nc.dram_tensor("w_in", [D, D], F32, kind="ExternalInput").ap()
    id_d = nc.dram_tensor("ident_in", [128, 128], F32, kind="ExternalInput").ap()
    o_d = nc.dram_tensor("out", [ntiles * 128, D], F32, kind="ExternalOutput").ap()
    with ExitStack() as stack:
        k = K(nc, stack)
        cm = Common(k, id_d)
        bufs = mmres_alloc(k)
        phase_mmres(k, cm, bufs, w_d, ntiles, hg_d, 0, None, x_d, 0, None, o_d, 0, None)
        drain(k)
        print("ninst", k.ninst, "nsem", k.nsem)
    return nc


def rms_rstd(k, h, junk, ss, sq, rstd, ncols=D):
    k.op("act", lambda e: e.activation(out=junk.t[:, :ncols], in_=h.t[:, :ncols], func=AF.Square,
                                       accum_out=ss.t[:, 0:1]),
         reads=[h.b], writes=[junk.b, ss.b])
    k.op("act", lambda e: e.activation(out=sq.t[:, 0:1], in_=ss.t[:, 0:1], func=AF.Sqrt,
                                       scale=1.0 / ncols, bias=cm_eps(k)),
         reads=[ss.b], writes=[sq.b])
    k.op("dve", lambda e: e.reciprocal(out=rstd.t[:, 0:1], in_=sq.t[:, 0:1]), reads=[sq.b], writes=[rstd.b])


_EPS_T = {}


def cm_eps(k):
    if id(k) not in _EPS_T:
        t = T(k, "eps_c", [128, 1], F32)
        k.op("pool", lambda e: e.memset(t.t[:], 1e-6), writes=[t.b])
        _EPS_T.clear()
        _EPS_T[id(k)] = t
    t = _EPS_T[id(k)]
    k._need("act", t.b.wr)
    return t.t[:, 0:1]


def peer_alloc(k):
    b = {}
    b["Wq"] = T(k, "pWq", [128, 16, 2048], BF16)
    b["kT"] = T(k, "pkT", [128, 16, 128], BF16)
    b["cw"] = T(k, "pcw", [128, 2048], F32)
    b["h"] = T(k, "ph", [128, 2048], F32)
    b["junk"] = T(k, "pjunk", [128, 2048], BF16)
    b["ss"] = T(k, "pss", [128, 1], F32)
    b["sq"] = T(k, "psq", [128, 1], F32)
    b["rstd"] = T(k, "prstd", [128, 1], F32)
    b["xn"] = T(k, "pxn", [128, 2048], BF16)
    b["xT"] = T(k, "pxT", [128, 2048], BF16)
    b["qb"] = T(k, "pqb", [128, 2048], BF16)
    b["qT"] = b["xT"]
    b["S"] = T(k, "pS", [128, 2048], F32)
    b["kst"] = V(b["S"].t[:].rearrange("p (c n) -> p c n", n=128), None)
    b["kst"].b = b["S"].b
    b["v"] = T(k, "pv", [128, 16, 16], F32)
    b["ix"] = T(k, "pix", [128, 16, 16], U32)
    b["ixf"] = T(k, "pixf", [128, 16, 16], F32)
    b["cand"] = T(k, "pcand", [128, 8, 256], F32)
    b["cand2"] = T(k, "pcand2", [128, 8, 256], F32)
    b["cidx"] = T(k, "pcidx", [128, 8, 256], F32)
    b["ts"] = T(k, "pts", [128, 8, 16], F32)
    b["pos"] = T(k, "ppos", [128, 8, 16], U32)
    b["pcc"] = T(k, "ppcc", [128, 32], F32)
    b["tsm"] = T(k, "ptsm", [128, 8, 16], F32)
    b["e"] = T(k, "pe_", [128, 8, 16], F32)
    b["esum"] = T(k, "pesum", [128, 8], F32)
    b["rinv"] = T(k, "prinv", [128, 8], F32)
    b["g"] = T(k, "pg", [128, 128], F32)
    b["E"] = T(k, "pE", [128, 16, 256], F32)
    b["eidx"] = T(k, "peidx", [128, 128], F32)
    b["idxT"] = T(k, "pidxT", [128, 128], I32)
    b["gT"] = T(k, "pgT", [128, 128], F32)
    b["actv"] = T(k, "pactv", [128, 128], F32)
    b["gel"] = T(k, "pgel", [128, 128], F32)
    b["coefT"] = T(k, "pcoefT", [128, 128], F32)
    b["W2"] = T(k, "pW2", [128, 256], BF16)
    b["ue"] = [T(k, "pue%d" % i, [128, 2048], BF16) for i in range(4)]
    b["ve"] = [T(k, "pve%d" % i, [128, 2048], BF16) for i in range(4)]
    b["lh"] = [T(k, "plh%d" % i, [128, 128], BF16) for i in range(2)]
    b["hi"] = b["tsm"]
    b["lo"] = b["e"]
    for nm, src_ in (("posf", "actv"), ("esel", "gel")):
        v_ = V(b[src_].t[:].rearrange("p (h k) -> p h k", k=16), None)
        v_.b = b[src_].b
        b[nm] = v_
    Eb = b["E"].t[:].rearrange("p a c -> p (a c)").bitcast(BF16)
    b["E_alias"] = []
    for i in range(4):
        v_ = V(Eb[:, i * 2048:(i + 1) * 2048], "ueA%d" % i)
        b["ue"].append(v_)
        b["E_alias"].append(v_.b)
    for nm in ("cand2", "cidx"):
        cb = b[nm].t[:].rearrange("p a c -> p (a c)").bitcast(BF16)
        b[nm + "_alias"] = []
        for i in range(2):
            v_ = V(cb[:, i * 2048:(i + 1) * 2048], "veA_%s%d" % (nm, i))
            b["ve"].append(v_)
            b[nm + "_alias"].append(v_.b)
    b["o"] = b["S"]
    k.op("pool", lambda e: e.memset(b["W2"].t[:], 0.0), writes=[b["W2"].b])
    k.op("pool", lambda e: e.memset(b["W2"].t[:, 127:128], 1.0), writes=[b["W2"].b])
    return b


def phase_peer(k, cm, b, hs_d, hs_bufs, tiles, cw_d, wq_d, k1T_d, k2T_d, u_d, v_d):
    load_w(k, cm, b["Wq"], wq_d, 2048, 16)
    k.dma("sp", lambda e: e.dma_start(out=b["cw"].t[:], in_=cw_d.partition_broadcast(128)), writes=[b["cw"].b])
    k.dma("sp", lambda e: e.dma_start(out=b["pcc"].t[:], in_=cm.pcc_d[:, :]), writes=[b["pcc"].b])
    for half, kd in enumerate((k1T_d, k2T_d)):
        for h in range(8):
            c = 2 * h + half
            k.dma("sp", lambda e, c=c, h=h, kd=kd: e.dma_start(out=b["kst"].t[:, c, :], in_=kd[h, :, :]),
                  writes=[b["kst"].b])
    k.op("pool", lambda e: e.tensor_copy(out=b["kT"].t[:], in_=b["kst"].t[:]), reads=[b["kst"].b], writes=[b["kT"].b])

    h_, junk, ss, sq, rstd = b["h"], b["junk"], b["ss"], b["sq"], b["rstd"]
    xn, xT, qb, qT, S = b["xn"], b["xT"], b["qb"], b["qT"], b["S"]
    v, ix, ixf, cand, cand2, cidx, ts = b["v"], b["ix"], b["ixf"], b["cand"], b["cand2"], b["cidx"], b["ts"]
    for i in tiles:
        r0 = i * 128
        k.dma("sp", lambda e: e.dma_start(out=h_.t[:], in_=hs_d[r0:r0 + 128, :]), reads=[hs_bufs[i]], writes=[h_.b])
        rms_rstd(k, h_, junk, ss, sq, rstd)
        k.op("dve", lambda e: e.scalar_tensor_tensor(out=xn.t[:], in0=h_.t[:], scalar=rstd.t[:, 0:1], in1=b["cw"].t[:],
                                                     op0=ALU.mult, op1=ALU.mult),
             reads=[h_.b, rstd.b, b["cw"].b], writes=[xn.b])
        transpose_tile(k, cm, xn, xT)
        for ng in range(4):
            mm_group(k, cm.ps_o[ng], xT.t, xT.b, b["Wq"], ng * 512, 512)
            k.op("act", lambda e, ng=ng: e.copy(out=qb.t[:, ng * 512:(ng + 1) * 512], in_=cm.ps_o[ng].t[:]),
                 reads=[cm.ps_o[ng].b], writes=[qb.b])
        transpose_tile(k, cm, qb, qT)
        for c in range(16):
            po = cm.ps_o[c // 4]
            k.op("pe", lambda e, c=c, po=po: e.matmul(out=po.t[:, (c % 4) * 128:(c % 4 + 1) * 128],
                                                      lhsT=qT.t[:, c * 128:(c + 1) * 128], rhs=b["kT"].t[:, c, :],
                                                      start=True, stop=True),
                 reads=[qT.b, b["kT"].b], writes=[po.b], tok=(c % 4 == 3))
        for ng in range(4):
            k.op("act", lambda e, ng=ng: e.copy(out=S.t[:, ng * 512:(ng + 1) * 512], in_=cm.ps_o[ng].t[:]),
                 reads=[cm.ps_o[ng].b], writes=[S.b])
        for c in range(16):
            Sc = S.t[:, c * 128:(c + 1) * 128]
            k.op("dve", lambda e, c=c, Sc=Sc: e.max(out=v.t[:, c, 0:8], in_=Sc), reads=[S.b], writes=[v.b])
            k.op("dve", lambda e, c=c, Sc=Sc: e.max_index(out=ix.t[:, c, 0:8], in_max=v.t[:, c, 0:8], in_values=Sc),
                 reads=[S.b, v.b], writes=[ix.b])
            k.op("dve", lambda e, c=c, Sc=Sc: e.match_replace(out=Sc, in_to_replace=v.t[:, c, 0:8], in_values=Sc,
                                                              imm_value=-1e30),
                 reads=[v.b], writes=[S.b])
            k.op("dve", lambda e, c=c, Sc=Sc: e.max(out=v.t[:, c, 8:16], in_=Sc), reads=[S.b], writes=[v.b])
            k.op("dve", lambda e, c=c, Sc=Sc: e.max_index(out=ix.t[:, c, 8:16], in_max=v.t[:, c, 8:16], in_values=Sc),
                 reads=[S.b, v.b], writes=[ix.b])
        k.op("dve", lambda e: e.tensor_copy(out=ixf.t[:], in_=ix.t[:]), reads=[ix.b], writes=[ixf.b])
        vv = v.t[:].rearrange("p (h two) k -> p h two k", two=2)
        iv = ixf.t[:].rearrange("p (h two) k -> p h two k", two=2)
        cand4 = cand.t[:].rearrange("p h (i j) -> p h i j", j=16)
        k.op("dve", lambda e: e.tensor_tensor(out=cand4, in0=vv[:, :, 0, :].unsqueeze(3).to_broadcast([128, 8, 16, 16]),
                                              in1=vv[:, :, 1, :].unsqueeze(2).to_broadcast([128, 8, 16, 16]), op=ALU.add),
             reads=[v.b], writes=[cand.b])
        k.op("dve", lambda e: e.tensor_scalar(out=iv[:, :, 0, :], in0=iv[:, :, 0, :], scalar1=128.0, scalar2=None,
                                              op0=ALU.mult), reads=[ixf.b], writes=[ixf.b])
        pos, posf, hi, lo = b["pos"], b["posf"], b["hi"], b["lo"]
        for h in range(8):
            ch = cand.t[:, h, :]
            k.op("dve", lambda e, h=h, ch=ch: e.max(out=ts.t[:, h, 0:8], in_=ch), reads=[cand.b], writes=[ts.b])
            k.op("dve", lambda e, h=h, ch=ch: e.max_index(out=pos.t[:, h, 0:8], in_max=ts.t[:, h, 0:8], in_values=ch),
                 reads=[cand.b, ts.b], writes=[pos.b])
            k.op("dve", lambda e, h=h, ch=ch: e.match_replace(out=ch, in_to_replace=ts.t[:, h, 0:8], in_values=ch,
                                                              imm_value=-1e30), reads=[ts.b], writes=[cand.b])
            k.op("dve", lambda e, h=h, ch=ch: e.max(out=ts.t[:, h, 8:16], in_=ch), reads=[cand.b], writes=[ts.b])
            k.op("dve", lambda e, h=h, ch=ch: e.max_index(out=pos.t[:, h, 8:16], in_max=ts.t[:, h, 8:16], in_values=ch),
                 reads=[cand.b, ts.b], writes=[pos.b])
        k.op("dve", lambda e: e.tensor_copy(out=posf.t[:], in_=pos.t[:]), reads=[pos.b], writes=[posf.b])
        sc1 = cand2.t[:].rearrange("p h (k i) -> p h k i", i=16)
        sc2 = cidx.t[:].rearrange("p h (k i) -> p h k i", i=16)
        s1w = [cand2.b] + b["cand2_alias"]
        s2w = [cidx.b] + b["cidx_alias"]
        thr = b["pcc"].t[:, 0:16]
        io16 = b["pcc"].t[:, 16:32]
        bc_k = lambda ap3: ap3.unsqueeze(3).to_broadcast([128, 8, 16, 16])
        bc_c = lambda ap1: ap1.unsqueeze(1).unsqueeze(1).to_broadcast([128, 8, 16, 16])
        k.op("dve", lambda e: e.tensor_tensor(out=sc1, in0=bc_k(posf.t[:]), in1=bc_c(thr), op=ALU.is_ge),
             reads=[posf.b, b["pcc"].b], writes=s1w)
        k.op("dve", lambda e: e.tensor_reduce(out=hi.t[:], in_=sc1, axis=AX.X, op=ALU.add), reads=[cand2.b], writes=[hi.b])
        k.op("dve", lambda e: e.scalar_tensor_tensor(out=lo.t[:].rearrange("p h k -> p (h k)"),
                                                     in0=hi.t[:].rearrange("p h k -> p (h k)"), scalar=-16.0,
                                                     in1=posf.t[:].rearrange("p h k -> p (h k)"), op0=ALU.mult, op1=ALU.add),
             reads=[hi.b, posf.b], writes=[lo.b])
        eidx3 = b["eidx"].t[:].rearrange("p (h k) -> p h k", k=16)
        esel = b["esel"]
        for which, (src_, scr, scw, dst) in enumerate(((hi, sc1, s1w, eidx3), (lo, sc2, s2w, esel.t[:]))):
            tab = iv[:, :, which, :].unsqueeze(2).to_broadcast([128, 8, 16, 16])
            k.op("dve", lambda e, src_=src_, scr=scr: e.tensor_tensor(out=scr, in0=bc_k(src_.t[:]), in1=bc_c(io16), op=ALU.is_equal),
                 reads=[src_.b, b["pcc"].b], writes=scw)
            k.op("dve", lambda e, scr=scr, tab=tab: e.tensor_tensor(out=scr, in0=scr, in1=tab, op=ALU.mult),
                 reads=[ixf.b], writes=scw)
            k.op("dve", lambda e, scr=scr, dst=dst: e.tensor_reduce(out=dst, in_=scr, axis=AX.X, op=ALU.add),
                 reads=[scw[0]], writes=[b["eidx"].b if which == 0 else esel.b])
        k.op("dve", lambda e: e.tensor_tensor(out=eidx3, in0=eidx3, in1=esel.t[:], op=ALU.add), reads=[esel.b], writes=[b["eidx"].b])
        k.op("dve", lambda e: e.tensor_scalar(out=b["eidx"].t[:], in0=b["eidx"].t[:], scalar1=16383.0, scalar2=0.0,
                                              op0=ALU.min, op1=ALU.max), reads=[], writes=[b["eidx"].b])
        k.op("dve", lambda e: e.tensor_tensor(out=b["tsm"].t[:], in0=ts.t[:], in1=ts.t[:, :, 0:1].to_broadcast([128, 8, 16]),
                                              op=ALU.subtract), reads=[ts.b], writes=[b["tsm"].b])
        k.op("act", lambda e: e.activation(out=b["e"].t[:], in_=b["tsm"].t[:], func=AF.Exp),
             reads=[b["tsm"].b], writes=[b["e"].b])
        k.op("dve", lambda e: e.tensor_reduce(out=b["esum"].t[:], in_=b["e"].t[:], axis=AX.X, op=ALU.add),
             reads=[b["e"].b], writes=[b["esum"].b])
        k.op("dve", lambda e: e.reciprocal(out=b["rinv"].t[:], in_=b["esum"].t[:]), reads=[b["esum"].b], writes=[b["rinv"].b])
        g3 = b["g"].t[:].rearrange("p (h k) -> p h k", k=16)
        k.op("dve", lambda e: e.tensor_tensor(out=g3, in0=b["e"].t[:], in1=b["rinv"].t[:].unsqueeze(2).to_broadcast([128, 8, 16]),
                                              op=ALU.mult), reads=[b["e"].b, b["rinv"].b], writes=[b["g"].b])
        px0, px1 = cm.ps_x[0], cm.ps_x[1]
        k.op("pe", lambda e: e.transpose(out=px0.t[:, 0:128], in_=b["eidx"].t[:], identity=cm.identf.t[:]),
             reads=[b["eidx"].b, cm.identf.b], writes=[px0.b])
        k.op("dve", lambda e: e.tensor_copy(out=b["idxT"].t[:], in_=px0.t[:, 0:128]), reads=[px0.b], writes=[b["idxT"].b])
        k.op("pe", lambda e: e.transpose(out=px1.t[:, 0:128], in_=b["g"].t[:], identity=cm.identf.t[:]),
             reads=[b["g"].b, cm.identf.b], writes=[px1.b])
        k.op("dve", lambda e: e.tensor_copy(out=b["gT"].t[:], in_=px1.t[:, 0:128]), reads=[px1.b], writes=[b["gT"].b])
        actv = b["actv"]
        for t in range(128):
            ue = b["ue"][t % 8]
            k.dma("pool", lambda e, t=t, ue=ue: e.indirect_dma_start(
                out=ue.t[:], out_offset=None, in_=u_d[:, :],
                in_offset=bass.IndirectOffsetOnAxis(ap=b["idxT"].t[:, t:t + 1], axis=0)),
                reads=[b["idxT"].b], writes=[ue.b])
            pbk = cm.ps_o if t % 2 == 0 else cm.ps_y
            pall = cm.psA if t % 2 == 0 else cm.psB
            pbufs = cm.bankA if t % 2 == 0 else cm.bankB
            for ng in range(4):
                k.op("pe", lambda e, t=t, ng=ng, pbk=pbk: e.matmul(
                    out=pbk[ng].t[:], lhsT=cm.ident.t[:, t:t + 1].to_broadcast([128, 128]),
                    rhs=xn.t[:, ng * 512:(ng + 1) * 512], start=True, stop=True),
                    reads=[cm.ident.b, xn.b], writes=[pbk[ng].b], tok=(ng == 3))
            k.op("dve", lambda e, t=t, ue=ue, pall=pall: e.scalar_tensor_tensor(
                out=junk.t[:], in0=ue.t[:], scalar=1.0, in1=pall[:, :], op0=ALU.mult, op1=ALU.mult,
                accum_out=actv.t[:, t:t + 1]),
                reads=[ue.b] + pbufs, writes=[junk.b, actv.b])
        k.op("act", lambda e: e.activation(out=b["gel"].t[:], in_=actv.t[:], func=AF.Gelu),
             reads=[actv.b], writes=[b["gel"].b])
        k.op("dve", lambda e: e.tensor_tensor(out=b["coefT"].t[:], in0=b["gel"].t[:], in1=b["gT"].t[:], op=ALU.mult),
             reads=[b["gel"].b, b["gT"].b], writes=[b["coefT"].b])
        for t in range(128):
            ve = b["ve"][t % 8]
            lh = b["lh"][t % 2]
            k.dma("pool", lambda e, t=t, ve=ve: e.indirect_dma_start(
                out=ve.t[:], out_offset=None, in_=v_d[:, :],
                in_offset=bass.IndirectOffsetOnAxis(ap=b["idxT"].t[:, t:t + 1], axis=0)),
                reads=[b["idxT"].b], writes=[ve.b])
            k.op("dve", lambda e, t=t, lh=lh: e.tensor_scalar(
                out=lh.t[:], in0=b["W2"].t[:, 127 - t:255 - t], scalar1=b["coefT"].t[:, t:t + 1], scalar2=None,
                op0=ALU.mult), reads=[b["W2"].b, b["coefT"].b], writes=[lh.b])
            for ng in range(4):
                k.op("pe", lambda e, t=t, ng=ng, lh=lh, ve=ve: e.matmul(
                    out=cm.ps_y[ng].t[:], lhsT=lh.t[:], rhs=ve.t[:, ng * 512:(ng + 1) * 512],
                    start=(t == 0), stop=(t == 127)),
                    reads=[lh.b, ve.b], writes=[cm.ps_y[ng].b], tok=(ng == 3))
        o = b["o"]
        k.op("dve", lambda e: e.tensor_tensor(out=o.t[:], in0=cm.psB[:, :], in1=h_.t[:], op=ALU.add),
             reads=[h_.b] + cm.bankB, writes=[o.b])
        k.dma("pool", lambda e: e.dma_start(out=hs_d[r0:r0 + 128, :], in_=o.t[:]), reads=[o.b], writes=[hs_bufs[i]])


def drain(k):
    for q, slots in k.dq.items():
        for s in slots:
            if s[1] > 0:
                k._need("sp", (s[0], s[1], "dma"))


def peer_consts():
    c = np.zeros((128, 32), np.float32)
    c[:, 0:16] = 16.0 * np.arange(1, 17)[None, :]
    c[:, 16:32] = np.arange(16)[None, :]
    return c


def build_test_peer(ntiles):
    nc = bass.Bass("TRN2", target_bir_lowering=False)
    pcc_d = nc.dram_tensor("pcc_in", [128, 32], F32, kind="ExternalInput").ap()
    hs_d = nc.dram_tensor("hs_io", [ntiles * 128, D], F32, kind="ExternalOutput").ap()
    hin_d = nc.dram_tensor("h_in", [ntiles * 128, D], F32, kind="ExternalInput").ap()
    cw_d = nc.dram_tensor("cw_in", [D], F32, kind="ExternalInput").ap()
    wq_d = nc.dram_tensor("wq_in", [D, D], F32, kind="ExternalInput").ap()
    k1_d = nc.dram_tensor("k1T_in", [8, 128, 128], F32, kind="ExternalInput").ap()
    k2_d = nc.dram_tensor("k2T_in", [8, 128, 128], F32, kind="ExternalInput").ap()
    u_d = nc.dram_tensor("u_in", [16384, D], F32, kind="ExternalInput").ap()
    v_d = nc.dram_tensor("v_in", [16384, D], F32, kind="ExternalInput").ap()
    id_d = nc.dram_tensor("ident_in", [128, 128], F32, kind="ExternalInput").ap()
    with ExitStack() as stack:
        k = K(nc, stack)
        cm = Common(k, id_d)
        cm.pcc_d = pcc_d
        b = peer_alloc(k)
        hs_bufs = [Buf("hs%d" % i) for i in range(ntiles)]
        for i in range(ntiles):
            k.dma("sp", lambda e, i=i: e.dma_start(out=b["o"].t[:], in_=hin_d[i * 128:(i + 1) * 128, :]), writes=[b["o"].b])
            k.dma("sp", lambda e, i=i: e.dma_start(out=hs_d[i * 128:(i + 1) * 128, :], in_=b["o"].t[:]), reads=[b["o"].b],
                  writes=[hs_bufs[i]])
        phase_peer(k, cm, b, hs_d, hs_bufs, list(range(ntiles)), cw_d, wq_d, k1_d, k2_d, u_d, v_d)
        drain(k)
        print("ninst", k.ninst, "nsem", k.nsem)
    return nc


def ple_alloc(k):
    b = {}
    b["Wg"] = T(k, "eWg", [128, 16, 2048], BF16)
    b["Wp"] = T(k, "eWp", [128, 2, 2048], BF16)
    b["nw"] = T(k, "enw", [128, 16], F32)
    b["fw"] = T(k, "efw", [128, 2048], F32)
    b["h"] = [T(k, "eh%d" % i, [128, 2048], F32) for i in range(2)]
    b["p"] = [T(k, "ep%d" % i, [128, 256], F32) for i in range(2)]
    b["pb"] = T(k, "epb", [128, 256], BF16)
    b["pT"] = T(k, "epT", [128, 256], BF16)
    b["junk"] = T(k, "ejunk", [128, 2048], BF16)
    b["ss"] = T(k, "ess", [128, 1], F32)
    b["sq"] = T(k, "esq", [128, 1], F32)
    b["rstd"] = T(k, "erstd", [128, 1], F32)
    b["xn"] = T(k, "exn", [128, 2048], BF16)
    b["xT"] = T(k, "exT", [128, 2048], BF16)
    b["sig"] = T(k, "esig", [128, 2048], F32)
    b["o"] = [T(k, "eo%d" % i, [128, 2048], F32) for i in range(2)]
    b["o2"] = [T(k, "eo2%d" % i, [128, 2048], F32) for i in range(2)]
    return b


def phase_ple(k, cm, b, hs_d, hs_bufs, tiles, p_d, prow0, nw_d, wg_d, wp_d, final=None):
    k.dma("sp", lambda e: e.dma_start(out=b["nw"].t[:], in_=nw_d[:, :]), writes=[b["nw"].b])
    load_w(k, cm, b["Wg"], wg_d, 2048, 16, scale=b["nw"])
    load_w(k, cm, b["Wp"], wp_d, 2048, 2)
    if final is not None:
        fw_d, out_d, orow0 = final
        k.dma("sp", lambda e: e.dma_start(out=b["fw"].t[:], in_=fw_d.partition_broadcast(128)), writes=[b["fw"].b])
    junk, ss, sq, rstd, xn, xT = b["junk"], b["ss"], b["sq"], b["rstd"], b["xn"], b["xT"]
    for n, i in enumerate(tiles):
        r0 = i * 128
        h_ = b["h"][n % 2]
        p_ = b["p"][n % 2]
        o = b["o"][n % 2]
        k.dma("sp", lambda e: e.dma_start(out=h_.t[:], in_=hs_d[r0:r0 + 128, :]), reads=[hs_bufs[i]], writes=[h_.b])
        k.dma("sp", lambda e: e.dma_start(out=p_.t[:], in_=p_d[prow0 + r0:prow0 + r0 + 128, :]), writes=[p_.b])
        rms_rstd(k, h_, junk, ss, sq, rstd)
        k.op("act", lambda e: e.activation(out=xn.t[:], in_=h_.t[:], func=AF.Copy, scale=rstd.t[:, 0:1]),
             reads=[h_.b, rstd.b], writes=[xn.b])
        transpose_tile(k, cm, xn, xT)
        k.op("pool", lambda e: e.tensor_copy(out=b["pb"].t[:], in_=p_.t[:]), reads=[p_.b], writes=[b["pb"].b])
        transpose_tile(k, cm, b["pb"], b["pT"], nchunks=2)
        for ng in range(4):
            mm_group(k, cm.ps_o[ng], xT.t, xT.b, b["Wg"], ng * 512, 512)
            px = cm.ps_x[ng % 2]
            mm_group(k, px, b["pT"].t, b["pT"].b, b["Wp"], ng * 512, 512, kchunks=2)
            sl = slice(ng * 512, (ng + 1) * 512)
            k.op("act", lambda e, ng=ng, sl=sl: e.activation(out=b["sig"].t[:, sl], in_=cm.ps_o[ng].t[:], func=AF.Sigmoid),
                 reads=[cm.ps_o[ng].b], writes=[b["sig"].b])
            k.op("dve", lambda e, sl=sl, px=px: e.tensor_tensor(out=b["sig"].t[:, sl], in0=px.t[:], in1=b["sig"].t[:, sl],
                                                                op=ALU.mult), reads=[px.b], writes=[b["sig"].b])
        k.op("pool", lambda e: e.tensor_tensor(out=o.t[:], in0=b["sig"].t[:], in1=h_.t[:], op=ALU.add),
             reads=[b["sig"].b, h_.b], writes=[o.b])
        if final is None:
            k.dma("pool", lambda e: e.dma_start(out=hs_d[r0:r0 + 128, :], in_=o.t[:]), reads=[o.b], writes=[hs_bufs[i]])
        else:
            o2 = b["o2"][n % 2]
            rms_rstd(k, o, junk, ss, sq, rstd)
            k.op("dve", lambda e: e.scalar_tensor_tensor(out=o2.t[:], in0=o.t[:], scalar=rstd.t[:, 0:1], in1=b["fw"].t[:],
                                                         op0=ALU.mult, op1=ALU.mult),
                 reads=[o.b, rstd.b, b["fw"].b], writes=[o2.b])
            rr = orow0 + n * 128
            k.dma("pool", lambda e: e.dma_start(out=out_d[rr:rr + 128, :], in_=o2.t[:]), reads=[o2.b])


def build_test_ple(ntiles, final):
    nc = bass.Bass("TRN2", target_bir_lowering=False)
    hs_d = nc.dram_tensor("hs_io", [ntiles * 128, D], F32, kind="ExternalOutput").ap()
    out_d = nc.dram_tensor("out", [ntiles * 128, D], F32, kind="ExternalOutput").ap()
    hin_d = nc.dram_tensor("h_in", [ntiles * 128, D], F32, kind="ExternalInput").ap()
    p_d = nc.dram_tensor("p_in", [ntiles * 128, 256], F32, kind="ExternalInput").ap()
    nw_d = nc.dram_tensor("nw_in", [128, 16], F32, kind="ExternalInput").ap()
    fw_d = nc.dram_tensor("fw_in", [D], F32, kind="ExternalInput").ap()
    wg_d = nc.dram_tensor("wg_in", [D, D], F32, kind="ExternalInput").ap()
    wp_d = nc.dram_tensor("wp_in", [256, D], F32, kind="ExternalInput").ap()
    id_d = nc.dram_tensor("ident_in", [128, 128], F32, kind="ExternalInput").ap()
    with ExitStack() as stack:
        k = K(nc, stack)
        cm = Common(k, id_d)
        b = ple_alloc(k)
        hs_bufs = [Buf("hs%d" % i) for i in range(ntiles)]
        for i in range(ntiles):
            k.dma("sp", lambda e, i=i: e.dma_start(out=b["sig"].t[:], in_=hin_d[i * 128:(i + 1) * 128, :]), writes=[b["sig"].b])
            k.dma("sp", lambda e, i=i: e.dma_start(out=hs_d[i * 128:(i + 1) * 128, :], in_=b["sig"].t[:]), reads=[b["sig"].b],
                  writes=[hs_bufs[i]])
        phase_ple(k, cm, b, hs_d, hs_bufs, list(range(ntiles)), p_d, 0, nw_d, wg_d, wp_d,
                  final=(fw_d, out_d, 0) if final else None)
        drain(k)
        print("ninst", k.ninst, "nsem", k.nsem)
    return nc


def att_alloc(k):
    b = {}
    b["Wq"] = T(k, "aWq", [128, 16, 2048], BF16)
    b["Wkv"] = T(k, "aWkv", [128, 16, 512], BF16)
    b["nq"] = T(k, "anq", [128, 16], F32)
    b["nkv"] = T(k, "ankv", [128, 16], F32)
    b["mask"] = T(k, "amask", [128, 256], F32)
    b["mask1"] = T(k, "amask1", [128, 256], F32)
    b["sinks"] = T(k, "asinks", [128, 32], F32)
    b["h"] = [T(k, "ah%d" % i, [128, 2048], F32) for i in range(2)]
    b["junk"] = T(k, "ajunk", [128, 2048], BF16)
    b["ss"] = T(k, "ass", [128, 1], F32)
    b["sq"] = T(k, "asq", [128, 1], F32)
    b["rstd"] = T(k, "arstd", [128, 1], F32)
    b["xn"] = T(k, "axn", [128, 2048], BF16)
    b["xT"] = T(k, "axT", [128, 2048], BF16)
    b["qb"] = T(k, "aqb", [128, 2048], BF16)
    b["qT"] = T(k, "aqT", [128, 2048], BF16)
    b["Kdup"] = T(k, "aKdup", [128, 8, 128], BF16)
    b["kT"] = [T(k, "akT%d" % i, [128, 1024], BF16) for i in range(2)]
    b["V"] = [T(k, "aV%d" % i, [128, 256], BF16) for i in range(2)]
    b["sm"] = T(k, "asm", [128, 8, 256], F32)
    b["pr"] = T(k, "apr", [128, 8, 256], BF16)
    b["prT"] = [T(k, "aprT%d" % i, [128, 1024], BF16) for i in range(2)]
    b["mx"] = T(k, "amx", [128, 8], F32)
    b["rs"] = T(k, "ars", [128, 8], F32)
    b["es"] = T(k, "aes", [128, 8], F32)
    b["rden"] = T(k, "arden", [128, 8], F32)
    b["o"] = [T(k, "ao%d" % i, [128, 2048], F32) for i in range(2)]
    return b


def phase_att(k, cm, b, hs_d, hs_bufs, ntiles, att_d, att_bufs, nq_d, nkv_d, wq_d, wkv_d, sinks_d, mask_d, mask1_d):
    k.dma("sp", lambda e: e.dma_start(out=b["nq"].t[:], in_=nq_d[:, :]), writes=[b["nq"].b])
    k.dma("sp", lambda e: e.dma_start(out=b["nkv"].t[:], in_=nkv_d[:, :]), writes=[b["nkv"].b])
    k.dma("sp", lambda e: e.dma_start(out=b["mask"].t[:], in_=mask_d[:, :]), writes=[b["mask"].b])
    k.dma("sp", lambda e: e.dma_start(out=b["mask1"].t[:], in_=mask1_d[:, :]), writes=[b["mask1"].b])
    k.dma("sp", lambda e: e.dma_start(out=b["sinks"].t[:], in_=sinks_d.partition_broadcast(128)), writes=[b["sinks"].b])
    load_w(k, cm, b["Wq"], wq_d, 2048, 16, scale=b["nq"])
    load_w(k, cm, b["Wkv"], wkv_d, 512, 16, scale=b["nkv"])
    k.op("pool", lambda e: e.memset(b["Kdup"].t[:], 0.0), writes=[b["Kdup"].b])
    junk, ss, sq, rstd, xn, xT, qb, qT = b["junk"], b["ss"], b["sq"], b["rstd"], b["xn"], b["xT"], b["qb"], b["qT"]
    sm, pr, mx, rs, es, rden = b["sm"], b["pr"], b["mx"], b["rs"], b["es"], b["rden"]
    for i in range(ntiles):
        r0 = i * 128
        h_ = b["h"][i % 2]
        kTc, kTp = b["kT"][i % 2], b["kT"][(i + 1) % 2]
        Vc, Vp = b["V"][i % 2], b["V"][(i + 1) % 2]
        k.dma("sp", lambda e: e.dma_start(out=h_.t[:], in_=hs_d[r0:r0 + 128, :]), reads=[hs_bufs[i]], writes=[h_.b])
        rms_rstd(k, h_, junk, ss, sq, rstd)
        k.op("act", lambda e: e.activation(out=xn.t[:], in_=h_.t[:], func=AF.Copy, scale=rstd.t[:, 0:1]),
             reads=[h_.b, rstd.b], writes=[xn.b])
        transpose_tile(k, cm, xn, xT)
        pkv = cm.ps_x[0]
        mm_group(k, pkv, xT.t, xT.b, b["Wkv"], 0, 512)
        kview = pkv.t[:, 0:256].rearrange("p (g d) -> p g d", d=64)
        kz4 = b["Kdup"].t[:].rearrange("p (g two) d -> p g two d", two=2)
        k.op("act", lambda e: e.copy(out=kz4[:, :, 0, 0:64], in_=kview), reads=[pkv.b], writes=[b["Kdup"].b])
        k.op("dve", lambda e: e.tensor_copy(out=kz4[:, :, 1, 64:128], in_=kview), reads=[pkv.b], writes=[b["Kdup"].b])
        k.op("act", lambda e: e.copy(out=Vc.t[:], in_=pkv.t[:, 256:512]), reads=[pkv.b], writes=[Vc.b])
        kd2 = V(b["Kdup"].t[:].rearrange("p g d -> p (g d)"), None)
        kd2.b = b["Kdup"].b
        transpose_tile(k, cm, kd2, kTc, nchunks=8)
        if i == 0:
            continue
        for ng in range(4):
            mm_group(k, cm.ps_o[ng], xT.t, xT.b, b["Wq"], ng * 512, 512)
            k.op("act", lambda e, ng=ng: e.activation(out=qb.t[:, ng * 512:(ng + 1) * 512], in_=cm.ps_o[ng].t[:],
                                                      func=AF.Copy, scale=0.125),
                 reads=[cm.ps_o[ng].b], writes=[qb.b])
        transpose_tile(k, cm, qb, qT)
        if ATT_STOP == 1:
            continue
        msk = b["mask1"] if i == 1 else b["mask"]
        o = b["o"][i % 2]
        for g in range(4):
            for j in range(8):
                hq = 8 * g + j
                c = hq // 2
                off = (hq % 2) * 64
                po = cm.ps_o[j // 2]
                cb = (j % 2) * 256
                k.op("pe", lambda e, c=c, off=off, po=po, cb=cb, g=g: e.matmul(
                    out=po.t[:, cb:cb + 128], lhsT=qT.t[:, c * 128:(c + 1) * 128],
                    rhs=kTp.t[:, (2 * g + off // 64) * 128:(2 * g + off // 64 + 1) * 128], start=True, stop=True),
                    reads=[qT.b, kTp.b], writes=[po.b], tok=False)
                k.op("pe", lambda e, c=c, off=off, po=po, cb=cb, g=g: e.matmul(
                    out=po.t[:, cb + 128:cb + 256], lhsT=qT.t[:, c * 128:(c + 1) * 128],
                    rhs=kTc.t[:, (2 * g + off // 64) * 128:(2 * g + off // 64 + 1) * 128], start=True, stop=True),
                    reads=[qT.b, kTc.b], writes=[po.b], tok=(j % 2 == 1))
            if ATT_STOP == 2:
                k.op("pe", lambda e: e.transpose(out=cm.ps_x[1].t[:, 0:128], in_=cm.identf.t[:], identity=cm.identf.t[:]),
                     reads=[cm.identf.b], writes=[cm.ps_x[1].b])
                continue
            sA = cm.psA[:, :].rearrange("p (j n) -> p j n", n=256)
            k.op("dve", lambda e: e.tensor_tensor(out=sm.t[:], in0=sA, in1=msk.t[:].unsqueeze(1).to_broadcast([128, 8, 256]),
                                                  op=ALU.add), reads=cm.bankA + [msk.b], writes=[sm.b])
            k.op("dve", lambda e: e.tensor_reduce(out=mx.t[:], in_=sm.t[:], axis=AX.X, op=ALU.max), reads=[sm.b], writes=[mx.b])
            k.op("dve", lambda e, g=g: e.tensor_tensor(out=mx.t[:], in0=mx.t[:], in1=b["sinks"].t[:, 8 * g:8 * g + 8], op=ALU.max),
                 reads=[b["sinks"].b], writes=[mx.b])
            k.op("dve", lambda e: e.tensor_tensor(out=sm.t[:], in0=sm.t[:], in1=mx.t[:].unsqueeze(2).to_broadcast([128, 8, 256]),
                                                  op=ALU.subtract), reads=[mx.b], writes=[sm.b])
            k.op("act", lambda e: e.activation(out=pr.t[:], in_=sm.t[:], func=AF.Exp), reads=[sm.b], writes=[pr.b])
            k.op("dve", lambda e: e.tensor_reduce(out=rs.t[:], in_=pr.t[:], axis=AX.X, op=ALU.add), reads=[pr.b], writes=[rs.b])
            k.op("dve", lambda e, g=g: e.tensor_tensor(out=es.t[:], in0=b["sinks"].t[:, 8 * g:8 * g + 8], in1=mx.t[:],
                                                       op=ALU.subtract), reads=[b["sinks"].b, mx.b], writes=[es.b])
            k.op("act", lambda e: e.activation(out=es.t[:], in_=es.t[:], func=AF.Exp), reads=[], writes=[es.b])
            k.op("dve", lambda e: e.tensor_tensor(out=rs.t[:], in0=rs.t[:], in1=es.t[:], op=ALU.add), reads=[es.b], writes=[rs.b])
            k.op("dve", lambda e: e.reciprocal(out=rden.t[:], in_=rs.t[:]), reads=[rs.b], writes=[rden.b])
            if ATT_STOP == 3:
                continue
            for half in range(2):
                pst = cm.ps_tr[half]
                for j in range(8):
                    k.op("pe", lambda e, j=j, half=half, pst=pst: e.transpose(
                        out=pst.t[:, j * 128:(j + 1) * 128], in_=pr.t[:, j, half * 128:(half + 1) * 128],
                        identity=cm.ident.t[:]), reads=[pr.b, cm.ident.b], writes=[pst.b], tok=(j == 7))
                eng = "dve" if half == 0 else "act"
                if half == 0:
                    k.op("dve", lambda e, pst=pst: e.tensor_copy(out=b["prT"][0].t[:], in_=pst.t[:]), reads=[pst.b],
                         writes=[b["prT"][0].b])
                else:
                    k.op("act", lambda e, pst=pst: e.copy(out=b["prT"][1].t[:], in_=pst.t[:]), reads=[pst.b],
                         writes=[b["prT"][1].b])
            if ATT_STOP == 4:
                continue
            pov = cm.ps_x[1]
            for j in range(8):
                k.op("pe", lambda e, j=j, g=g: e.matmul(out=pov.t[:, j * 64:(j + 1) * 64], lhsT=b["prT"][0].t[:, j * 128:(j + 1) * 128],
                                                        rhs=Vp.t[:, g * 64:(g + 1) * 64], start=True, stop=False),
                     reads=[b["prT"][0].b, Vp.b], writes=[pov.b], tok=False)
                k.op("pe", lambda e, j=j, g=g: e.matmul(out=pov.t[:, j * 64:(j + 1) * 64], lhsT=b["prT"][1].t[:, j * 128:(j + 1) * 128],
                                                        rhs=Vc.t[:, g * 64:(g + 1) * 64], start=False, stop=True),
                     reads=[b["prT"][1].b, Vc.b], writes=[pov.b], tok=(j == 7))
            k.op("dve", lambda e, g=g: e.tensor_tensor(
                out=o.t[:, g * 512:(g + 1) * 512].rearrange("p (j d) -> p j d", d=64),
                in0=pov.t[:, :].rearrange("p (j d) -> p j d", d=64),
                in1=rden.t[:].unsqueeze(2).to_broadcast([128, 8, 64]), op=ALU.mult),
                reads=[pov.b, rden.b], writes=[o.b])
        ro = (i - 1) * 128
        k.dma("pool", lambda e: e.dma_start(out=att_d[ro:ro + 128, :], in_=o.t[:]), reads=[o.b], writes=[att_bufs[i - 1]])


def build_test_att(ntiles):
    nc = bass.Bass("TRN2", target_bir_lowering=False)
    hin_d = nc.dram_tensor("h_in", [ntiles * 128, D], F32, kind="ExternalInput").ap()
    att_d = nc.dram_tensor("att_out", [(ntiles - 1) * 128, D], F32, kind="ExternalOutput").ap()
    nq_d = nc.dram_tensor("nq_in", [128, 16], F32, kind="ExternalInput").ap()
    nkv_d = nc.dram_tensor("nkv_in", [128, 16], F32, kind="ExternalInput").ap()
    wq_d = nc.dram_tensor("wq_in", [D, D], F32, kind="ExternalInput").ap()
    wkv_d = nc.dram_tensor("wkv_in", [D, 512], F32, kind="ExternalInput").ap()
    sinks_d = nc.dram_tensor("sinks_in", [32], F32, kind="ExternalInput").ap()
    mask_d = nc.dram_tensor("mask_in", [128, 256], F32, kind="ExternalInput").ap()
    mask1_d = nc.dram_tensor("mask1_in", [128, 256], F32, kind="ExternalInput").ap()
    id_d = nc.dram_tensor("ident_in", [128, 128], F32, kind="ExternalInput").ap()
    with ExitStack() as stack:
        k = K(nc, stack)
        cm = Common(k, id_d)
        b = att_alloc(k)
        hs_bufs = [Buf("hs%d" % i) for i in range(ntiles)]
        att_bufs = [Buf("at%d" % i) for i in range(ntiles)]
        phase_att(k, cm, b, hin_d, hs_bufs, ntiles, att_d, att_bufs, nq_d, nkv_d, wq_d, wkv_d, sinks_d, mask_d, mask1_d)
        drain(k)
        print("ninst", k.ninst, "nsem", k.nsem)
    return nc


def band_masks():
    t = np.arange(128)[:, None]
    kk = np.arange(256)[None, :]
    valid = (kk >= t + 1) & (kk <= t + 128)
    m = np.where(valid, 0.0, -1e30).astype(np.float32)
    m1 = m.copy()
    m1[:, :128] = -1e30
    return m, m1


def mlstm_alloc(k):
    mk = lambda name, shape, dt=F32: T(k, name, shape, dt)
    W = mk("mW", [128, 16, 1538], BF16)
    nw = mk("mnw", [128, 16])
    gb = mk("mgb", [128, 2])
    gb15 = mk("mgb15", [128, 2])
    hnw = mk("mhnw", [128, 512])
    tri = mk("mtri", [128, 128])
    sel = mk("msel", [128, 128])
    cmask = mk("mcmask", [128, 128])
    ones = mk("mones", [128, 128])
    one1 = mk("mone1", [128, 1])
    onesb = mk("monesb", [128, 1], BF16)
    C = mk("mC", [128, 2, 512])
    Cb = mk("mCb", [128, 2, 512], BF16)
    n_ = mk("mn", [128, 2])
    nb = mk("mnb", [128, 2], BF16)
    mprev = mk("mmprev", [128, 1])
    xs = [mk("mx%d" % i, [128, 2048]) for i in range(2)]
    ss, sq, rstd = mk("mss", [128, 1]), mk("msq", [128, 1]), mk("mrstd", [128, 1])
    xn = mk("mxn", [128, 2048], BF16)
    xT = mk("mxT", [128, 2048], BF16)
    qkb = mk("mqkb", [128, 512], BF16)
    qkT = mk("mqkT", [128, 512], BF16)
    vb = mk("mvb", [128, 512], BF16)
    sog = mk("msog", [128, 512])
    gs = mk("mgs", [128, 2])
    th = mk("mth", [128, 2])
    li, z, ez, sp, lf = mk("mli", [128, 1]), mk("mz", [128, 1]), mk("mez", [128, 1]), mk("msp", [128, 1]), mk("mlf", [128, 1])
    bcs, gvec, mrow, u, negu, mt = (mk("mbcs", [128, 1]), mk("mgvec", [128, 1]), mk("mmrow", [128, 1]), mk("mu", [128, 1]),
                                    mk("mnegu", [128, 1]), mk("mmt", [128, 1]))
    dg = mk("mdg", [128, 128])
    A = mk("mA", [128, 128])
    wintra = mk("mwintra", [128, 128])
    wia, winter = mk("mwia", [128, 1]), mk("mwinter", [128, 1])
    Pb = mk("mPb", [128, 128], BF16)
    PT = mk("mPT", [128, 128], BF16)
    dintra = mk("mdintra", [128, 1])
    tmp = mk("mtmp", [128, 512])
    num = mk("mnum", [128, 512])
    den, aden, emt, dmax, rden = mk("mden", [128, 1]), mk("maden", [128, 1]), mk("memt", [128, 1]), mk("mdmax", [128, 1]), mk("mrden", [128, 1])
    ssn, t1, sqv, rstd2, sc = mk("mssn", [128, 1]), mk("mt1", [128, 1]), mk("msqv", [128, 1]), mk("mrstd2", [128, 1]), mk("msc", [128, 1])
    ots = [mk("mot%d" % i, [128, 512]) for i in range(2)]
    mb = mk("mmb", [128, 2])
    last2 = mk("mlast2", [128, 2])
    dlt, wstate, deca, decay = mk("mdlt", [128, 1]), mk("mwstate", [128, 1]), mk("mdeca", [128, 1]), mk("mdecay", [128, 1])
    kw = mk("mkw", [128, 256], BF16)

    padneg = mk("mpadneg", [128, 80])
    kall = mk("mkall", [128, 48, 256], BF16)
    xnB = mk("mxnB", [128, 2048], BF16)
    xTB = mk("mxTB", [128, 2048], BF16)
    ssB, sqB, rstdB = mk("mssB", [128, 1]), mk("msqB", [128, 1]), mk("mrstdB", [128, 1])
    vall = mk("mvall", [128, 48, 512], BF16)
    gall = mk("mgall", [128, 48, 2])
    pv = {nm: mk("mpv_" + nm, [128, 48]) for nm in ("thi", "li", "thf", "ez", "sp", "lf", "bloc", "btot", "off", "bg", "wv", "wst")}
    pM, pMall, pMp, pnegMp, pBend = mk("mpM", [128, 1]), mk("mpMall", [128, 1]), mk("mpMp", [128, 1]), mk("mpnegMp", [128, 1]), mk("mpBend", [128, 1])
    npm = mk("mnpm", [128, 80])
    return dict(xnB=xnB, xTB=xTB, ssB=ssB, sqB=sqB, rstdB=rstdB, kall=kall, vall=vall, gall=gall, pv=pv, pM=pM, pMall=pMall, pMp=pMp, pnegMp=pnegMp, pBend=pBend, W=W, nw=nw, gb=gb, gb15=gb15, hnw=hnw, tri=tri, sel=sel, cmask=cmask, ones=ones, one1=one1, onesb=onesb, C=C, Cb=Cb, n_=n_, nb=nb, mprev=mprev, xs=xs, ss=ss, sq=sq, rstd=rstd, xn=xn, xT=xT, qkb=qkb, qkT=qkT, vb=vb, sog=sog, gs=gs, th=th, li=li, z=z, ez=ez, sp=sp, lf=lf, bcs=bcs, gvec=gvec, mrow=mrow, u=u, negu=negu, mt=mt, dg=dg, A=A, wintra=wintra, wia=wia, winter=winter, Pb=Pb, PT=PT, dintra=dintra, tmp=tmp, num=num, den=den, aden=aden, emt=emt, dmax=dmax, rden=rden, ssn=ssn, t1=t1, sqv=sqv, rstd2=rstd2, sc=sc, ots=ots, mb=mb, last2=last2, dlt=dlt, wstate=wstate, deca=deca, decay=decay, kw=kw, padneg=padneg, npm=npm)


def phase_mlstm(k, cm, m, xw_d, w4_d, nw_d, gb4_d, hnw_d, tri_d, sel_d, cmask_d, padneg_d, npm_d, nchunks, out_from,
                hg_d, hg_bufs, bg=None):
    g = globals()
    loc = dict(m)
    W = m["W"]
    nw = m["nw"]
    gb = m["gb"]
    gb15 = m["gb15"]
    hnw = m["hnw"]
    tri = m["tri"]
    sel = m["sel"]
    cmask = m["cmask"]
    ones = m["ones"]
    one1 = m["one1"]
    onesb = m["onesb"]
    C = m["C"]
    Cb = m["Cb"]
    n_ = m["n_"]
    nb = m["nb"]
    mprev = m["mprev"]
    xs = m["xs"]
    ss = m["ss"]
    sq = m["sq"]
    rstd = m["rstd"]
    xn = m["xn"]
    xT = m["xT"]
    qkb = m["qkb"]
    qkT = m["qkT"]
    vb = m["vb"]
    sog = m["sog"]
    gs = m["gs"]
    th = m["th"]
    li = m["li"]
    z = m["z"]
    ez = m["ez"]
    sp = m["sp"]
    lf = m["lf"]
    bcs = m["bcs"]
    gvec = m["gvec"]
    mrow = m["mrow"]
    u = m["u"]
    negu = m["negu"]
    mt = m["mt"]
    dg = m["dg"]
    A = m["A"]
    wintra = m["wintra"]
    wia = m["wia"]
    winter = m["winter"]
    Pb = m["Pb"]
    PT = m["PT"]
    dintra = m["dintra"]
    tmp = m["tmp"]
    num = m["num"]
    den = m["den"]
    aden = m["aden"]
    emt = m["emt"]
    dmax = m["dmax"]
    rden = m["rden"]
    ssn = m["ssn"]
    t1 = m["t1"]
    sqv = m["sqv"]
    rstd2 = m["rstd2"]
    sc = m["sc"]
    ots = m["ots"]
    mb = m["mb"]
    last2 = m["last2"]
    dlt = m["dlt"]
    wstate = m["wstate"]
    deca = m["deca"]
    decay = m["decay"]
    kw = m["kw"]
    padneg = m["padneg"]
    npm = m["npm"]

    dl = lambda t, src: k.dma("sp", lambda e: e.dma_start(out=t.t[:], in_=src), writes=[t.b])
    dl(nw, nw_d[:, :])
    dl(tri, tri_d[:, :])
    dl(sel, sel_d[:, :])
    dl(cmask, cmask_d[:, :])
    k.dma("sp", lambda e: e.dma_start(out=padneg.t[:, :nchunks], in_=padneg_d[:, :]), writes=[padneg.b])
    k.dma("sp", lambda e: e.dma_start(out=npm.t[:, :nchunks], in_=npm_d[:, :]), writes=[npm.b])
    for t_, val in ((ones, 1.0), (one1, 1.0), (onesb, 1.0)):
        k.op("pool", lambda e, t_=t_, val=val: e.memset(t_.t[:], val), writes=[t_.b])
    eps_ap = cm_eps(k)
    P0, P1, P2, P3 = cm.ps_o
    X0, X1 = cm.ps_x
    for hd in range(4):
        dl(gb, gb4_d[hd, :, :])
        k.dma("sp", lambda e: e.dma_start(out=hnw.t[:], in_=hnw_d[hd * 512:(hd + 1) * 512].partition_broadcast(128)),
              writes=[hnw.b])
        k.op("dve", lambda e: e.tensor_scalar(out=gb15.t[:], in0=gb.t[:], scalar1=1.0 / 15.0, scalar2=None, op0=ALU.mult),
             reads=[gb.b], writes=[gb15.b])
        for t_, val in ((C, 0.0), (Cb, 0.0), (n_, 0.0), (nb, 0.0), (mprev, 0.0)):
            k.op("pool", lambda e, t_=t_, val=val: e.memset(t_.t[:], val), writes=[t_.b])
        load_w(k, cm, W, w4_d[hd], 1538, 16, scale=nw)
        NP = out_from
        kall, vall, gall, pv = m["kall"], m["vall"], m["gall"], m["pv"]
        pM, pMall, pMp, pnegMp, pBend = m["pM"], m["pMall"], m["pMp"], m["pnegMp"], m["pBend"]
        for c in range(NP):
            r0 = c * 128
            xt = xs[c % 2]
            k.dma("sp", lambda e: e.dma_start(out=xt.t[:], in_=xw_d[r0:r0 + 128, :]), writes=[xt.b])
            if bg is not None:
                bg()
            xn_, xT_, ss_, sq_, rstd_ = (xn, xT, ss, sq, rstd) if c % 2 == 0 else (m["xnB"], m["xTB"], m["ssB"], m["sqB"], m["rstdB"])
            rms_rstd(k, xt, xn_, ss_, sq_, rstd_)
            k.op("act", lambda e: e.activation(out=xn_.t[:], in_=xt.t[:], func=AF.Copy, scale=rstd_.t[:, 0:1]),
                 reads=[xt.b, rstd_.b], writes=[xn_.b])
            transpose_tile(k, cm, xn_, xT_)
            pk = P0 if c % 2 == 0 else P2
            pvv = P1 if c % 2 == 0 else P3
            for kc in range(16):
                k.op("pe", lambda e, kc=kc: e.matmul(out=pk.t[:, 0:256], lhsT=xT_.t[:, kc * 128:(kc + 1) * 128],
                                                     rhs=W.t[:, kc, 256:512], start=(kc == 0), stop=(kc == 15)),
                     reads=[xT_.b, W.b], writes=[pk.b], tok=(kc == 15))
            mm_group(k, pvv, xT_.t, xT_.b, W, 512, 512)
            mm_group(k, X0, xT_.t, xT_.b, W, 1536, 2)
            k.op("dve", lambda e: e.tensor_copy(out=kall.t[:, c, :], in_=pk.t[:, 0:256]), reads=[pk.b], writes=[kall.b])
            k.op("act", lambda e: e.copy(out=vall.t[:, c, :], in_=pvv.t[:]), reads=[pvv.b], writes=[vall.b])
            k.op("dve", lambda e: e.tensor_copy(out=gall.t[:, c, :], in_=X0.t[:, 0:2]), reads=[X0.b], writes=[gall.b])
        if NP > 0:
            o1 = lambda eng, fn, r, w: k.op(eng, fn, reads=[t_.b for t_ in r], writes=[t_.b for t_ in w])
            def _sl(t_):
                v_ = V(t_.t[:, 0:NP], None)
                v_.b = t_.b
                return v_
            thi, liA, thf, ezA, spA, lfA = [_sl(pv[n]) for n in ("thi", "li", "thf", "ez", "sp", "lf")]
            bloc, btot, off, bgl, wv, wst = [_sl(pv[n]) for n in ("bloc", "btot", "off", "bg", "wv", "wst")]
            gall = V(m["gall"].t[:, 0:NP, :], None)
            gall.b = m["gall"].b
            kall = V(m["kall"].t[:, 0:NP, :], None)
            kall.b = m["kall"].b
            o1("act", lambda e: e.activation(out=thi.t[:], in_=gall.t[:, :, 0], func=AF.Tanh, scale=1.0 / 15.0, bias=gb15.t[:, 0:1]),
               [gall, gb15], [thi])
            o1("act", lambda e: e.activation(out=thf.t[:], in_=gall.t[:, :, 1], func=AF.Tanh, scale=1.0 / 15.0, bias=gb15.t[:, 1:2]),
               [gall, gb15], [thf])
            o1("dve", lambda e: e.tensor_scalar(out=liA.t[:], in0=thi.t[:], scalar1=15.0, scalar2=None, op0=ALU.mult), [thi], [liA])
            o1("dve", lambda e: e.tensor_tensor(out=liA.t[:], in0=liA.t[:], in1=padneg.t[:, 0:NP], op=ALU.add), [padneg], [liA])
            o1("act", lambda e: e.activation(out=ezA.t[:], in_=thf.t[:], func=AF.Exp, scale=-15.0), [thf], [ezA])
            o1("act", lambda e: e.activation(out=spA.t[:], in_=ezA.t[:], func=AF.Ln, bias=one1.t[:, 0:1]), [ezA, one1], [spA])
            o1("dve", lambda e: e.tensor_tensor(out=lfA.t[:], in0=spA.t[:], in1=npm.t[:, 0:NP], op=ALU.mult), [spA, npm], [lfA])
            k.op("pe", lambda e: e.matmul(out=X1.t[:, 0:NP], lhsT=tri.t[:], rhs=lfA.t[:], start=True, stop=True),
                 reads=[tri.b, lfA.b], writes=[X1.b])
            k.op("dve", lambda e: e.tensor_copy(out=bloc.t[:], in_=X1.t[:, 0:NP]), reads=[X1.b], writes=[bloc.b])
            k.op("pe", lambda e: e.matmul(out=X1.t[:, 64:64 + NP], lhsT=sel.t[:], rhs=bloc.t[:], start=True, stop=True),
                 reads=[sel.b, bloc.b], writes=[X1.b])
            k.op("dve", lambda e: e.tensor_copy(out=btot.t[:], in_=X1.t[:, 64:64 + NP]), reads=[X1.b], writes=[btot.b])
            k.op("pool", lambda e: e.memset(off.t[:], 0.0), writes=[off.b])
            for c in range(1, NP):
                k.op("dve", lambda e, c=c: e.tensor_tensor(out=off.t[:, c:c + 1], in0=off.t[:, c - 1:c], in1=btot.t[:, c - 1:c], op=ALU.add),
                     reads=[btot.b], writes=[off.b])
            o1("dve", lambda e: e.tensor_tensor(out=bgl.t[:], in0=off.t[:], in1=bloc.t[:], op=ALU.add), [off, bloc], [bgl])
            o1("dve", lambda e: e.tensor_tensor(out=wv.t[:], in0=liA.t[:], in1=bgl.t[:], op=ALU.subtract), [liA, bgl], [wv])
            o1("dve", lambda e: e.tensor_reduce(out=pM.t[:], in_=wv.t[:], axis=AX.X, op=ALU.max), [wv], [pM])
            o1("dve", lambda e: e.tensor_scalar(out=dg.t[:], in0=cm.identf.t[:], scalar1=pM.t[:, 0:1], scalar2=None, op0=ALU.mult),
               [cm.identf, pM], [dg])
            k.op("pe", lambda e: e.matmul(out=X1.t[:, 128:256], lhsT=ones.t[:], rhs=dg.t[:], start=True, stop=True),
                 reads=[ones.b, dg.b], writes=[X1.b])
            k.op("dve", lambda e: e.tensor_reduce(out=pMp.t[:], in_=X1.t[:, 128:256], axis=AX.X, op=ALU.max), reads=[X1.b], writes=[pMp.b])
            o1("dve", lambda e: e.tensor_scalar(out=pMp.t[:], in0=pMp.t[:], scalar1=0.0, scalar2=None, op0=ALU.max), [], [pMp])
            o1("dve", lambda e: e.tensor_scalar(out=pnegMp.t[:], in0=pMp.t[:], scalar1=-1.0, scalar2=None, op0=ALU.mult), [pMp], [pnegMp])
            o1("dve", lambda e: e.tensor_tensor(out=pBend.t[:], in0=off.t[:, NP - 1:NP], in1=btot.t[:, NP - 1:NP], op=ALU.add),
               [off, btot], [pBend])
            o1("dve", lambda e: e.tensor_tensor(out=mprev.t[:], in0=pBend.t[:], in1=pMp.t[:], op=ALU.add), [pBend, pMp], [mprev])
            o1("act", lambda e: e.activation(out=wst.t[:], in_=wv.t[:], func=AF.Exp, bias=pnegMp.t[:, 0:1]), [wv, pnegMp], [wst])
            o1("dve", lambda e: e.tensor_tensor(out=kall.t[:], in0=kall.t[:], in1=wst.t[:].unsqueeze(2).to_broadcast([128, NP, 256]),
                                                op=ALU.mult), [wst], [kall])
            for dc, pc in ((0, P2), (1, P3)):
                for c in range(NP):
                    k.op("pe", lambda e, c=c, dc=dc, pc=pc: e.matmul(out=pc.t[:], lhsT=kall.t[:, c, dc * 128:(dc + 1) * 128],
                                                                     rhs=vall.t[:, c, :], start=(c == 0), stop=(c == NP - 1)),
                         reads=[kall.b, vall.b], writes=[pc.b], tok=(c == NP - 1))
            for dc in range(2):
                for c in range(NP):
                    k.op("pe", lambda e, c=c, dc=dc: e.matmul(out=X0.t[:, 16 + dc:17 + dc], lhsT=kall.t[:, c, dc * 128:(dc + 1) * 128],
                                                              rhs=onesb.t[:], start=(c == 0), stop=(c == NP - 1)),
                         reads=[kall.b, onesb.b], writes=[X0.b], tok=(c == NP - 1))
            k.op("dve", lambda e: e.tensor_copy(out=C.t[:, 0, :], in_=P2.t[:]), reads=[P2.b], writes=[C.b])
            k.op("dve", lambda e: e.tensor_copy(out=C.t[:, 1, :], in_=P3.t[:]), reads=[P3.b], writes=[C.b])
            k.op("act", lambda e: e.copy(out=Cb.t[:], in_=C.t[:]), reads=[C.b], writes=[Cb.b])
            k.op("dve", lambda e: e.tensor_copy(out=n_.t[:], in_=X0.t[:, 16:18]), reads=[X0.b], writes=[n_.b])
            k.op("dve", lambda e: e.tensor_copy(out=nb.t[:], in_=n_.t[:]), reads=[n_.b], writes=[nb.b])
        for c in range(NP, nchunks):
            r0 = c * 128
            xt = xs[c % 2]
            ot = ots[c % 2]
            k.dma("sp", lambda e: e.dma_start(out=xt.t[:], in_=xw_d[r0:r0 + 128, :]), writes=[xt.b])
            if bg is not None:
                bg()
            rms_rstd(k, xt, xn, ss, sq, rstd)
            k.op("act", lambda e: e.activation(out=xn.t[:], in_=xt.t[:], func=AF.Copy, scale=rstd.t[:, 0:1]),
                 reads=[xt.b, rstd.b], writes=[xn.b])
            transpose_tile(k, cm, xn, xT)
            full = c >= out_from
            if full:
                mm_group(k, P0, xT.t, xT.b, W, 0, 512)
            else:
                for kc in range(16):
                    k.op("pe", lambda e, kc=kc: e.matmul(out=P0.t[:, 256:512], lhsT=xT.t[:, kc * 128:(kc + 1) * 128],
                                                         rhs=W.t[:, kc, 256:512], start=(kc == 0), stop=(kc == 15)),
                         reads=[xT.b, W.b], writes=[P0.b], tok=(kc == 15))
            mm_group(k, P1, xT.t, xT.b, W, 512, 512)
            if full:
                mm_group(k, P2, xT.t, xT.b, W, 1024, 512)
            mm_group(k, X0, xT.t, xT.b, W, 1536, 2)
            if full:
                k.op("act", lambda e: e.activation(out=qkb.t[:, 0:256], in_=P0.t[:, 0:256], func=AF.Copy, scale=1.0 / 16.0),
                     reads=[P0.b], writes=[qkb.b])
            k.op("dve", lambda e: e.tensor_copy(out=qkb.t[:, 256:512], in_=P0.t[:, 256:512]), reads=[P0.b], writes=[qkb.b])
            k.op("dve", lambda e: e.tensor_copy(out=vb.t[:], in_=P1.t[:]), reads=[P1.b], writes=[vb.b])
            if full:
                k.op("act", lambda e: e.activation(out=sog.t[:], in_=P2.t[:], func=AF.Sigmoid), reads=[P2.b], writes=[sog.b])
                k.op("pool", lambda e: e.tensor_tensor(out=sog.t[:], in0=sog.t[:], in1=hnw.t[:], op=ALU.mult),
                     reads=[hnw.b], writes=[sog.b])
            k.op("dve", lambda e: e.tensor_copy(out=gs.t[:], in_=X0.t[:, 0:2]), reads=[X0.b], writes=[gs.b])
            if full:
                transpose_tile(k, cm, qkb, qkT, nchunks=4)
            for col in range(2):
                k.op("act", lambda e, col=col: e.activation(out=th.t[:, col:col + 1], in_=gs.t[:, col:col + 1], func=AF.Tanh,
                                                            scale=1.0 / 15.0, bias=gb15.t[:, col:col + 1]),
                     reads=[gs.b, gb15.b], writes=[th.b])
            k.op("dve", lambda e: e.tensor_scalar(out=li.t[:], in0=th.t[:, 0:1], scalar1=15.0, scalar2=None, op0=ALU.mult),
                 reads=[th.b], writes=[li.b])
            k.op("dve", lambda e: e.tensor_tensor(out=li.t[:], in0=li.t[:], in1=padneg.t[:, c:c + 1], op=ALU.add),
                 reads=[padneg.b], writes=[li.b])
            k.op("act", lambda e: e.activation(out=ez.t[:], in_=th.t[:, 1:2], func=AF.Exp, scale=-15.0), reads=[th.b], writes=[ez.b])
            k.op("act", lambda e: e.activation(out=sp.t[:], in_=ez.t[:], func=AF.Ln, bias=one1.t[:, 0:1]),
                 reads=[ez.b, one1.b], writes=[sp.b])
            k.op("dve", lambda e: e.tensor_tensor(out=lf.t[:], in0=sp.t[:], in1=npm.t[:, c:c + 1], op=ALU.mult),
                 reads=[sp.b, npm.b], writes=[lf.b])
            k.op("pe", lambda e: e.matmul(out=X0.t[:, 4:5], lhsT=tri.t[:], rhs=lf.t[:], start=True, stop=True),
                 reads=[tri.b, lf.b], writes=[X0.b])
            k.op("dve", lambda e: e.tensor_copy(out=bcs.t[:], in_=X0.t[:, 4:5]), reads=[X0.b], writes=[bcs.b])
            k.op("dve", lambda e: e.tensor_tensor(out=gvec.t[:], in0=li.t[:], in1=bcs.t[:], op=ALU.subtract),
                 reads=[li.b, bcs.b], writes=[gvec.b])
            k.op("dve", lambda e: e.tensor_scalar(out=dg.t[:], in0=cm.identf.t[:], scalar1=gvec.t[:, 0:1], scalar2=None, op0=ALU.mult),
                 reads=[cm.identf.b, gvec.b], writes=[dg.b])
            k.op("pe", lambda e: e.matmul(out=X1.t[:, 0:128], lhsT=ones.t[:], rhs=dg.t[:], start=True, stop=True),
                 reads=[ones.b, dg.b], writes=[X1.b])
            k.op("dve", lambda e: e.tensor_tensor(out=A.t[:], in0=X1.t[:, 0:128], in1=cmask.t[:], op=ALU.add),
                 reads=[X1.b, cmask.b], writes=[A.b])
            k.op("dve", lambda e: e.tensor_reduce(out=mrow.t[:], in_=A.t[:], axis=AX.X, op=ALU.max), reads=[A.b], writes=[mrow.b])
            k.op("dve", lambda e: e.tensor_tensor(out=u.t[:], in0=mrow.t[:], in1=mprev.t[:], op=ALU.max),
                 reads=[mrow.b, mprev.b], writes=[u.b])
            k.op("dve", lambda e: e.tensor_scalar(out=negu.t[:], in0=u.t[:], scalar1=-1.0, scalar2=None, op0=ALU.mult),
                 reads=[u.b], writes=[negu.b])
            k.op("dve", lambda e: e.tensor_tensor(out=mt.t[:], in0=bcs.t[:], in1=u.t[:], op=ALU.add), reads=[bcs.b, u.b], writes=[mt.b])
            if full:
                k.op("act", lambda e: e.activation(out=wintra.t[:], in_=A.t[:], func=AF.Exp, bias=negu.t[:, 0:1]),
                     reads=[A.b, negu.b], writes=[wintra.b])
                k.op("dve", lambda e: e.tensor_tensor(out=wia.t[:], in0=mprev.t[:], in1=u.t[:], op=ALU.subtract),
                     reads=[mprev.b, u.b], writes=[wia.b])
                k.op("act", lambda e: e.activation(out=winter.t[:], in_=wia.t[:], func=AF.Exp), reads=[wia.b], writes=[winter.b])
                for dc in range(2):
                    k.op("pe", lambda e, dc=dc: e.matmul(out=X1.t[:, 128:256], lhsT=qkT.t[:, dc * 128:(dc + 1) * 128],
                                                         rhs=qkT.t[:, (2 + dc) * 128:(3 + dc) * 128], start=(dc == 0), stop=(dc == 1)),
                         reads=[qkT.b], writes=[X1.b], tok=(dc == 1))
                k.op("dve", lambda e: e.scalar_tensor_tensor(out=Pb.t[:], in0=X1.t[:, 128:256], scalar=1.0, in1=wintra.t[:],
                                                             op0=ALU.mult, op1=ALU.mult, accum_out=dintra.t[:, 0:1]),
                     reads=[X1.b, wintra.b], writes=[Pb.b, dintra.b])
                pst = cm.ps_tr[0]
                k.op("pe", lambda e: e.transpose(out=pst.t[:, 0:128], in_=Pb.t[:], identity=cm.ident.t[:]),
                     reads=[Pb.b, cm.ident.b], writes=[pst.b])
                k.op("dve", lambda e: e.tensor_copy(out=PT.t[:], in_=pst.t[:, 0:128]), reads=[pst.b], writes=[PT.b])
                for dc in range(2):
                    k.op("pe", lambda e, dc=dc: e.matmul(out=P0.t[:], lhsT=qkT.t[:, dc * 128:(dc + 1) * 128], rhs=Cb.t[:, dc, :],
                                                         start=(dc == 0), stop=(dc == 1)),
                         reads=[qkT.b, Cb.b], writes=[P0.b], tok=(dc == 1))
                for dc in range(2):
                    k.op("pe", lambda e, dc=dc: e.matmul(out=X0.t[:, 12:13], lhsT=qkT.t[:, dc * 128:(dc + 1) * 128], rhs=nb.t[:, dc:dc + 1],
                                                         start=(dc == 0), stop=(dc == 1)),
                         reads=[qkT.b, nb.b], writes=[X0.b], tok=(dc == 1))
                k.op("pe", lambda e: e.matmul(out=P1.t[:], lhsT=PT.t[:], rhs=vb.t[:], start=True, stop=True),
                     reads=[PT.b, vb.b], writes=[P1.b])
                k.op("act", lambda e: e.activation(out=tmp.t[:], in_=P0.t[:], func=AF.Copy, scale=winter.t[:, 0:1]),
                     reads=[P0.b, winter.b], writes=[tmp.b])
                k.op("dve", lambda e: e.tensor_tensor(out=num.t[:], in0=P1.t[:], in1=tmp.t[:], op=ALU.add),
                     reads=[P1.b, tmp.b], writes=[num.b])
                k.op("dve", lambda e: e.scalar_tensor_tensor(out=den.t[:], in0=X0.t[:, 12:13], scalar=winter.t[:, 0:1], in1=dintra.t[:],
                                                             op0=ALU.mult, op1=ALU.add),
                     reads=[X0.b, winter.b, dintra.b], writes=[den.b])
                k.op("dve", lambda e: e.tensor_scalar(out=aden.t[:], in0=den.t[:], scalar1=-1.0, scalar2=None, op0=ALU.mult),
                     reads=[den.b], writes=[aden.b])
                k.op("dve", lambda e: e.tensor_tensor(out=aden.t[:], in0=aden.t[:], in1=den.t[:], op=ALU.max),
                     reads=[den.b], writes=[aden.b])
                k.op("act", lambda e: e.activation(out=emt.t[:], in_=mt.t[:], func=AF.Exp, scale=-1.0), reads=[mt.b], writes=[emt.b])
                k.op("dve", lambda e: e.tensor_tensor(out=dmax.t[:], in0=aden.t[:], in1=emt.t[:], op=ALU.max),
                     reads=[aden.b, emt.b], writes=[dmax.b])
                k.op("dve", lambda e: e.reciprocal(out=rden.t[:], in_=dmax.t[:]), reads=[dmax.b], writes=[rden.b])
                k.op("act", lambda e: e.activation(out=tmp.t[:], in_=num.t[:], func=AF.Square, accum_out=ssn.t[:, 0:1]),
                     reads=[num.b], writes=[tmp.b, ssn.b])
                k.op("dve", lambda e: e.tensor_tensor(out=t1.t[:], in0=ssn.t[:], in1=rden.t[:], op=ALU.mult), reads=[ssn.b, rden.b], writes=[t1.b])
                k.op("dve", lambda e: e.tensor_tensor(out=t1.t[:], in0=t1.t[:], in1=rden.t[:], op=ALU.mult), reads=[rden.b], writes=[t1.b])
                k.op("act", lambda e: e.activation(out=sqv.t[:], in_=t1.t[:], func=AF.Sqrt, scale=1.0 / 512.0, bias=eps_ap),
                     reads=[t1.b], writes=[sqv.b])
                k.op("dve", lambda e: e.reciprocal(out=rstd2.t[:], in_=sqv.t[:]), reads=[sqv.b], writes=[rstd2.b])
                k.op("dve", lambda e: e.tensor_tensor(out=sc.t[:], in0=rden.t[:], in1=rstd2.t[:], op=ALU.mult), reads=[rden.b, rstd2.b], writes=[sc.b])
                k.op("dve", lambda e: e.scalar_tensor_tensor(out=ot.t[:], in0=num.t[:], scalar=sc.t[:, 0:1], in1=sog.t[:],
                                                             op0=ALU.mult, op1=ALU.mult),
                     reads=[num.b, sc.b, sog.b], writes=[ot.b])
                ti = c - out_from
                k.dma("pool", lambda e: e.dma_start(out=hg_d[ti * 128:(ti + 1) * 128, hd * 512:(hd + 1) * 512], in_=ot.t[:]),
                      reads=[ot.b], writes=[hg_bufs[ti]])
            k.op("dve", lambda e: e.tensor_copy(out=mb.t[:, 0:1], in_=mt.t[:]), reads=[mt.b], writes=[mb.b])
            k.op("dve", lambda e: e.tensor_copy(out=mb.t[:, 1:2], in_=bcs.t[:]), reads=[bcs.b], writes=[mb.b])
            k.op("pe", lambda e: e.matmul(out=X0.t[:, 8:10], lhsT=sel.t[:], rhs=mb.t[:], start=True, stop=True),
                 reads=[sel.b, mb.b], writes=[X0.b])
            k.op("dve", lambda e: e.tensor_copy(out=last2.t[:], in_=X0.t[:, 8:10]), reads=[X0.b], writes=[last2.b])
            k.op("dve", lambda e: e.tensor_tensor(out=dlt.t[:], in0=last2.t[:, 1:2], in1=last2.t[:, 0:1], op=ALU.subtract),
                 reads=[last2.b], writes=[dlt.b])
            k.op("act", lambda e: e.activation(out=wstate.t[:], in_=gvec.t[:], func=AF.Exp, bias=dlt.t[:, 0:1]),
                 reads=[gvec.b, dlt.b], writes=[wstate.b])
            k.op("dve", lambda e: e.tensor_tensor(out=deca.t[:], in0=dlt.t[:], in1=mprev.t[:], op=ALU.add),
                 reads=[dlt.b, mprev.b], writes=[deca.b])
            k.op("act", lambda e: e.activation(out=decay.t[:], in_=deca.t[:], func=AF.Exp), reads=[deca.b], writes=[decay.b])
            k.op("dve", lambda e: e.tensor_scalar(out=kw.t[:], in0=qkb.t[:, 256:512], scalar1=wstate.t[:, 0:1], scalar2=None, op0=ALU.mult),
                 reads=[qkb.b, wstate.b], writes=[kw.b])
            k.op("pe", lambda e: e.matmul(out=P2.t[:], lhsT=kw.t[:, 0:128], rhs=vb.t[:], start=True, stop=True),
                 reads=[kw.b, vb.b], writes=[P2.b])
            k.op("pe", lambda e: e.matmul(out=P3.t[:], lhsT=kw.t[:, 128:256], rhs=vb.t[:], start=True, stop=True),
                 reads=[kw.b, vb.b], writes=[P3.b])
            k.op("pe", lambda e: e.matmul(out=X0.t[:, 16:17], lhsT=kw.t[:, 0:128], rhs=onesb.t[:], start=True, stop=True),
                 reads=[kw.b, onesb.b], writes=[X0.b], tok=False)
            k.op("pe", lambda e: e.matmul(out=X0.t[:, 17:18], lhsT=kw.t[:, 128:256], rhs=onesb.t[:], start=True, stop=True),
                 reads=[kw.b, onesb.b], writes=[X0.b])
            k.op("dve", lambda e: e.scalar_tensor_tensor(out=C.t[:, 0, :], in0=C.t[:, 0, :], scalar=decay.t[:, 0:1], in1=P2.t[:],
                                                         op0=ALU.mult, op1=ALU.add), reads=[decay.b, P2.b], writes=[C.b])
            k.op("dve", lambda e: e.scalar_tensor_tensor(out=C.t[:, 1, :], in0=C.t[:, 1, :], scalar=decay.t[:, 0:1], in1=P3.t[:],
                                                         op0=ALU.mult, op1=ALU.add), reads=[decay.b, P3.b], writes=[C.b])
            k.op("act", lambda e: e.copy(out=Cb.t[:], in_=C.t[:]), reads=[C.b], writes=[Cb.b])
            k.op("dve", lambda e: e.scalar_tensor_tensor(out=n_.t[:], in0=n_.t[:], scalar=decay.t[:, 0:1], in1=X0.t[:, 16:18],
                                                         op0=ALU.mult, op1=ALU.add), reads=[decay.b, X0.b], writes=[n_.b])
            k.op("dve", lambda e: e.tensor_copy(out=nb.t[:], in_=n_.t[:]), reads=[n_.b], writes=[nb.b])
            k.op("dve", lambda e: e.tensor_copy(out=mprev.t[:], in_=last2.t[:, 0:1]), reads=[last2.b], writes=[mprev.b])


def mlstm_consts():
    s = np.arange(128)[:, None]
    t = np.arange(128)[None, :]
    tri = (s <= t).astype(np.float32)
    sel = np.zeros((128, 128), np.float32)
    sel[127, :] = 1.0
    cmask = np.where(t <= s, 0.0, -1e30).astype(np.float32)
    return tri, sel, cmask


def mlstm_inmap(x_b, a_norm, a_w_in, gate_bias, head_norm, hd):
    o0, o1, o2, o3 = 1024, 2048, 4096, 6144
    cols = np.concatenate([np.arange(hd * 256, (hd + 1) * 256), o0 + np.arange(hd * 256, (hd + 1) * 256),
                           o1 + np.arange(hd * 512, (hd + 1) * 512), o2 + np.arange(hd * 512, (hd + 1) * 512),
                           np.array([o3 + hd, o3 + 4 + hd])])
    tri, sel, cmask = mlstm_consts()
    return {
        "x_in": np.ascontiguousarray(x_b),
        "w_in": np.ascontiguousarray(a_w_in[:, cols]),
        "nw_in": np.ascontiguousarray(a_norm.reshape(16, 128).T),
        "gb_in": np.ascontiguousarray(np.broadcast_to(gate_bias[:, hd][None, :], (128, 2))).astype(np.float32),
        "hnw_in": np.ascontiguousarray(head_norm[hd * 512:(hd + 1) * 512]),
        "tri_in": tri, "sel_in": sel, "cmask_in": cmask, "ident_in": np.eye(128, dtype=np.float32),
    }


NT0 = 17
NT1 = 16
NW = 65


def build_fused():
    nc = bass.Bass("TRN2", target_bir_lowering=False)
    di = lambda name, shape: nc.dram_tensor(name, list(shape), F32, kind="ExternalInput").ap()
    xw_d = di("xw", [NW * 128, D])
    p0_d = di("p0_sh", [NT0 * 128, 256])
    p1_d = di("p1_sh", [NT1 * 128, 256])
    w4_d = di("mw4", [4, D, 1538])
    mnw_d = di("mnw", [128, 16])
    gb4_d = di("mgb4", [4, 128, 2])
    hnw_d = di("mhnw", [D])
    tri_d = di("tri_in", [128, 128])
    sel_d = di("sel_in", [128, 128])
    cmask_d = di("cmask_in", [128, 128])
    padneg_d = di("padneg", [128, NW])
    npm_d = di("npm", [128, NW])
    wout0_d = di("wout0", [D, D])
    wout1_d = di("wout1", [D, D])
    peer = []
    for L in range(2):
        peer.append(dict(cw=di("cw%d" % L, [D]), wq=di("pwq%d" % L, [D, D]), k1=di("k1T%d" % L, [8, 128, 128]),
                         k2=di("k2T%d" % L, [8, 128, 128]), u=di("u%d" % L, [16384, D]), v=di("v%d" % L, [16384, D])))
    ple = []
    for L in range(2):
        ple.append(dict(nw=di("enw%d" % L, [128, 16]), wg=di("ewg%d" % L, [D, D]), wp=di("ewp%d" % L, [256, D])))
    fw_d = di("fw", [D])
    nq_d = di("nq", [128, 16])
    nkv_d = di("nkv", [128, 16])
    bwq_d = di("bwq", [D, D])
    wkv_d = di("wkv", [D, 512])
    sinks_d = di("sinks", [32])
    mask_d = di("mask", [128, 256])
    mask1_d = di("mask1", [128, 256])
    id_d = di("ident_in", [128, 128])
    pcc_d = di("pcc_in", [128, 32])
    out_d = nc.dram_tensor("out", [NT1 * 128, D], F32, kind="ExternalOutput").ap()
    hg_d = nc.dram_tensor("hg_scr", [NT0 * 128, D], F32, kind="Internal").ap()
    hs_d = nc.dram_tensor("hs_scr", [NT0 * 128, D], F32, kind="Internal").ap()
    att_d = nc.dram_tensor("att_scr", [NT1 * 128, D], F32, kind="Internal").ap()
    x0 = (NW - NT0) * 128
    tbl = []
    for L in range(2):
        for nm in ("u", "v"):
            tb = nc.dram_tensor("tb_%s%d" % (nm, L), [16384, D], BF16, kind="Internal").ap()
            tbl.append((peer[L][nm], tb))
            peer[L][nm + "b"] = tb
    with ExitStack() as stack:
        k = K(nc, stack)
        cm = Common(k, id_d)
        cm.pcc_d = pcc_d
        cm_eps(k)
        hg_bufs = [Buf("hg%d" % i) for i in range(NT0)]
        hs_bufs = [Buf("hs%d" % i) for i in range(NT0)]
        att_bufs = [Buf("at%d" % i) for i in range(NT1)]

        def phase(alloc, fn):
            with ExitStack() as ps:
                k.stack = ps
                b = alloc(k)
                fn(b)
                k.barrier()
            k.stack = stack

        conv = [(src_, dst_, i) for (src_, dst_) in tbl for i in range(128)]
        conv.reverse()

        def run_mlstm(m):
            stg = [T(k, "cstg%d" % i, [128, 2048], BF16) for i in range(2)]
            cnt = [0]

            def bg(n=2):
                for _ in range(n):
                    if not conv:
                        return
                    src_, dst_, i = conv.pop()
                    s = stg[cnt[0] % 2]
                    cnt[0] += 1
                    k.dma("pool", lambda e: e.dma_start(out=s.t[:], in_=src_[i * 128:(i + 1) * 128, :]), writes=[s.b])
                    k.dma("sp", lambda e: e.dma_start(out=dst_[i * 128:(i + 1) * 128, :], in_=s.t[:]), reads=[s.b])
            phase_mlstm(k, cm, m, xw_d, w4_d, mnw_d, gb4_d, hnw_d, tri_d, sel_d, cmask_d,
                        padneg_d, npm_d, NW, NW - NT0, hg_d, hg_bufs, bg=bg)
            while conv:
                bg()
        phase(mlstm_alloc, run_mlstm)
        phase(mmres_alloc, lambda b: phase_mmres(k, cm, b, wout0_d, NT0, hg_d, 0, hg_bufs, xw_d, x0, None, hs_d, 0, hs_bufs))
        phase(peer_alloc, lambda b: phase_peer(k, cm, b, hs_d, hs_bufs, list(range(NT0)), peer[0]["cw"], peer[0]["wq"],
                                               peer[0]["k1"], peer[0]["k2"], peer[0]["ub"], peer[0]["vb"]))
        phase(ple_alloc, lambda b: phase_ple(k, cm, b, hs_d, hs_bufs, list(range(NT0)), p0_d, 0, ple[0]["nw"], ple[0]["wg"],
                                             ple[0]["wp"]))
        phase(att_alloc, lambda b: phase_att(k, cm, b, hs_d, hs_bufs, NT0, att_d, att_bufs, nq_d, nkv_d, bwq_d, wkv_d,
                                             sinks_d, mask_d, mask1_d))
        phase(mmres_alloc, lambda b: phase_mmres(k, cm, b, wout1_d, NT1, att_d, 0, att_bufs, hs_d, 128, hs_bufs[1:],
                                                 hs_d, 128, hs_bufs[1:]))
        phase(peer_alloc, lambda b: phase_peer(k, cm, b, hs_d, hs_bufs, list(range(1, NT0)), peer[1]["cw"], peer[1]["wq"],
                                               peer[1]["k1"], peer[1]["k2"], peer[1]["ub"], peer[1]["vb"]))
        phase(ple_alloc, lambda b: phase_ple(k, cm, b, hs_d, hs_bufs, list(range(1, NT0)), p1_d, -128, ple[1]["nw"],
                                             ple[1]["wg"], ple[1]["wp"], final=(fw_d, out_d, 0)))
        drain(k)
        print("fused ninst", k.ninst, "nsem", k.nsem)
    return nc


def _r16(v):
    return np.ascontiguousarray(np.asarray(v, np.float32).reshape(16, 128).T)


def kernel(x, p, a_norm, a_w_in, a_gate_bias, a_head_norm, a_w_out, kv_norm, w_kv,
           b_norm, b_w_q, b_sinks, b_w_out, c_norm, peer_w_q, peer_k1, peer_k2,
           peer_u, peer_v, ple_norm, ple_w_gate, ple_w_proj, final_norm):
    f = lambda a: np.ascontiguousarray(np.asarray(a, dtype=np.float32))
    x = f(x)
    p = f(p)
    B, S, _ = x.shape
    nc = build_fused()
    m, m1 = band_masks()
    tri, sel, cmask = mlstm_consts()
    a_w_in0 = f(a_w_in)[0]
    gbias = f(a_gate_bias)[0]
    o0, o1, o2, o3 = 1024, 2048, 4096, 6144
    w4 = np.zeros((4, D, 1538), np.float32)
    gb4 = np.zeros((4, 128, 2), np.float32)
    for hd in range(4):
        cols = np.concatenate([np.arange(hd * 256, (hd + 1) * 256), o0 + np.arange(hd * 256, (hd + 1) * 256),
                               o1 + np.arange(hd * 512, (hd + 1) * 512), o2 + np.arange(hd * 512, (hd + 1) * 512),
                               np.array([o3 + hd, o3 + 4 + hd])])
        w4[hd] = a_w_in0[:, cols]
        gb4[hd] = np.broadcast_to(gbias[:, hd][None, :], (128, 2))
    shared = {
        "mw4": w4, "mnw": _r16(f(a_norm)[0]), "mgb4": gb4, "mhnw": f(a_head_norm)[0],
        "tri_in": tri, "sel_in": sel, "cmask_in": cmask,
        "wout0": f(a_w_out)[0], "wout1": f(b_w_out)[0], "fw": f(final_norm),
        "nq": _r16(f(b_norm)[0]), "nkv": _r16(kv_norm), "bwq": f(b_w_q)[0], "wkv": f(w_kv), "sinks": f(b_sinks)[0],
        "mask": m, "ident_in": np.eye(128, dtype=np.float32), "pcc_in": peer_consts(),
    }
    for L in range(2):
        shared["cw%d" % L] = f(c_norm)[L]
        shared["pwq%d" % L] = f(peer_w_q)[L]
        shared["k1T%d" % L] = np.ascontiguousarray(f(peer_k1)[L].transpose(0, 2, 1))
        shared["k2T%d" % L] = np.ascontiguousarray(f(peer_k2)[L].transpose(0, 2, 1))
        shared["u%d" % L] = f(peer_u)[L]
        shared["v%d" % L] = f(peer_v)[L]
        shared["enw%d" % L] = _r16(f(ple_norm)[L])
        shared["ewg%d" % L] = f(ple_w_gate)[L]
        shared["ewp%d" % L] = f(ple_w_proj)[L]
    TPC = S // 4
    WT = NW * 128
    maps = []
    for c in range(NCORES):
        bb, qd = c // 4, c % 4
        e0 = (qd + 1) * TPC
        npad = max(0, WT - e0)

        def window(arr, width, ntok):
            o = np.zeros((ntok, width), np.float32)
            lo = e0 - ntok
            if lo < 0:
                o[-lo:] = arr[0:e0]
            else:
                o[:] = arr[lo:e0]
            return o
        mp = dict(shared)
        mp["xw"] = window(x[bb], D, WT)
        mp["p0_sh"] = window(p[0, bb], 256, NT0 * 128)
        mp["p1_sh"] = np.ascontiguousarray(p[1, bb, e0 - TPC:e0])
        padc = npad // 128
        pn = np.zeros((128, NW), np.float32)
        pn[:, :padc] = -1e30
        pm = -np.ones((128, NW), np.float32)
        pm[:, :padc] = 0.0
        mp["padneg"] = pn
        mp["npm"] = pm
        mp["mask1"] = m1 if qd == 0 else m
        maps.append(mp)
    res = run_bass_kernel_spmd(nc, maps, core_ids=list(range(NCORES)))
    out = np.zeros((B, S, D), np.float32)
    for c in range(NCORES):
        bb, qd = c // 4, c % 4
        out[bb, qd * TPC:(qd + 1) * TPC] = res.results[c]["out"]
    return out
```

```python
from contextlib import ExitStack
import numpy as np
import concourse.bass as bass
import concourse.mybir as mybir
from concourse.bass_utils import run_bass_kernel_spmd

F32 = mybir.dt.float32
BF16 = mybir.dt.bfloat16
I32 = mybir.dt.int32
U32 = mybir.dt.uint32
AF = mybir.ActivationFunctionType
ALU = mybir.AluOpType
AX = mybir.AxisListType

SEM_LIMIT = 20000
ATT_STOP = 0
D = 2048
NCORES = 8


class Buf:
    __slots__ = ("name", "wr", "rd", "pend")

    def __init__(self, name):
        self.name = name
        self.wr = None
        self.rd = []
        self.pend = False


class T:
    def __init__(self, k, name, shape, dtype, psum=False):
        self.t = k.ps(name, shape, dtype) if psum else k.sb(name, shape, dtype)
        self.b = Buf(name)


class K:
    def __init__(self, nc, stack):
        self.nc = nc
        self.stack = stack
        self.sem_stack = stack
        self.eng = {"pe": nc.tensor, "act": nc.scalar, "dve": nc.vector,
                    "pool": nc.gpsimd, "sp": nc.sync}
        self.csem = {}
        self.known = {e: {} for e in self.eng}
        self.sems = {}
        self.nsem = 0
        self.dq = {}
        self.drr = {}
        self.ninst = 0
        self.pe_pend_r = []
        self.pe_pend_w = []
        self.retired = []

    def barrier(self):
        assert not self.pe_pend_r and not self.pe_pend_w
        toks = [(cs[0], cs[1], e) for e, cs in self.csem.items()]
        for q, slots in self.dq.items():
            for s in slots:
                if s[1] > 0:
                    toks.append((s[0], s[1], "dma"))
        for e in self.eng:
            for t in toks:
                if t[2] == e == "pe":
                    continue
                self._need(e, t)

    def new_sem(self, tag):
        key = "%s_%d" % (tag, self.nsem)
        self.nsem += 1
        h = self.sem_stack.enter_context(self.nc.semaphore(key))
        self.sems[key] = h
        return key

    def sb(self, name, shape, dtype):
        self.ntens = getattr(self, "ntens", 0) + 1
        return self.stack.enter_context(self.nc.sbuf_tensor("%s_%d" % (name, self.ntens), list(shape), dtype))

    def ps(self, name, shape, dtype):
        self.ntens = getattr(self, "ntens", 0) + 1
        return self.stack.enter_context(self.nc.psum_tensor("%s_%d" % (name, self.ntens), list(shape), dtype))

    def _need(self, e, tok):
        if tok is None:
            return
        key, val, teng = tok
        if teng == "pe" and e == "pe":
            return
        if self.known[e].get(key, 0) >= val:
            return
        self.eng[e].wait_ge(self.sems[key], val)
        self.known[e][key] = val

    def _waits(self, e, reads, writes):
        for b in reads:
            if b.pend and e != "pe":
                raise RuntimeError("pending PE access on %s" % b.name)
            self._need(e, b.wr)
        for b in writes:
            if b.pend and e != "pe":
                raise RuntimeError("pending PE access on %s" % b.name)
            self._need(e, b.wr)
            for t in b.rd:
                self._need(e, t)

    def _update(self, tok, reads, writes):
        for b in reads:
            b.rd = [t for t in b.rd if t[0] != tok[0]] + [tok]
        for b in writes:
            b.wr = tok
            b.rd = []

    def op(self, e, fn, reads=(), writes=(), tok=True):
        self._waits(e, reads, writes)
        inst = fn(self.eng[e])
        self.ninst += 1
        if not tok:
            assert e == "pe"
            for b in reads:
                b.pend = True
                self.pe_pend_r.append(b)
            for b in writes:
                b.pend = True
                self.pe_pend_w.append(b)
            return None
        cs = self.csem.get(e)
        if cs is None or cs[1] >= SEM_LIMIT:
            cs = [self.new_sem("c" + e), 0]
            self.csem[e] = cs
        cs[1] += 1
        inst.then_inc(self.sems[cs[0]], 1)
        t = (cs[0], cs[1], e)
        if e == "pe" and (self.pe_pend_r or self.pe_pend_w):
            for b in self.pe_pend_r + self.pe_pend_w:
                b.pend = False
            self._update(t, self.pe_pend_r, self.pe_pend_w)
            self.pe_pend_r = []
            self.pe_pend_w = []
        self._update(t, reads, writes)
        return t

    def dma(self, q, fn, reads=(), writes=(), nslots=8):
        self._waits(q, reads, writes)
        if q not in self.dq:
            self.dq[q] = [[self.new_sem("d" + q), 0] for _ in range(nslots)]
            self.drr[q] = 0
        slots = self.dq[q]
        i = self.drr[q]
        self.drr[q] = (i + 1) % len(slots)
        s = slots[i]
        if s[1] > 0:
            self._need(q, (s[0], s[1], "dma"))
        if s[1] >= 30000:
            self.retired.append((s[0], s[1], "dma"))
            s[0] = self.new_sem("d" + q)
            s[1] = 0
        inst = fn(self.eng[q])
        s[1] += 16
        inst.then_inc(self.sems[s[0]], 16)
        t = (s[0], s[1], "dma")
        self._update(t, reads, writes)
        self.ninst += 1
        return t

    def finish(self, bufs):
        for b in bufs:
            self._need("sp", b.wr)


class Common:
    def __init__(self, k, ident_d):
        self.k = k
        self.ident = T(k, "ident", [128, 128], BF16)
        self.identf = T(k, "identf", [128, 128], F32)
        k.dma("sp", lambda e: e.dma_start(out=self.identf.t[:], in_=ident_d[:, :]), writes=[self.identf.b])
        k.op("dve", lambda e: e.tensor_copy(out=self.ident.t[:], in_=self.identf.t[:]),
             reads=[self.identf.b], writes=[self.ident.b])
        self.wstage = [T(k, "wstage%d" % i, [128, 2048], F32) for i in range(2)]
        self.nw = 0
        self.psA = k.ps("psA", [128, 2048], F32)
        self.psB = k.ps("psB", [128, 2048], F32)
        self.ps_o = [V(self.psA[:, i * 512:(i + 1) * 512], "bank%d" % i) for i in range(4)]
        self.ps_tr = [V(self.psB[:, i * 512:(i + 1) * 512].bitcast(BF16), "bank%d" % (4 + i)) for i in range(2)]
        self.ps_x = [V(self.psB[:, (2 + i) * 512:(3 + i) * 512], "bank%d" % (6 + i)) for i in range(2)]
        self.ps_y = [V(self.psB[:, i * 512:(i + 1) * 512], None) for i in range(4)]
        self.ps_y[0].b = self.ps_tr[0].b
        self.ps_y[1].b = self.ps_tr[1].b
        self.ps_y[2].b = self.ps_x[0].b
        self.ps_y[3].b = self.ps_x[1].b
        self.bankA = [v.b for v in self.ps_o]
        self.bankB = [v.b for v in self.ps_y]


class V:
    def __init__(self, ap, name):
        self.t = ap
        self.b = Buf(name) if name is not None else None


def load_w(k, cm, W, wd, ncols, kchunks, scale=None, col0=0):
    for kc in range(kchunks):
        for c0 in range(0, ncols, 2048):
            cw = min(2048, ncols - c0)
            st = cm.wstage[cm.nw % 2]
            cm.nw += 1
            k.dma("sp", lambda e, st=st, kc=kc, c0=c0, cw=cw: e.dma_start(
                out=st.t[:, :cw], in_=wd[kc * 128:(kc + 1) * 128, c0:c0 + cw]), writes=[st.b])
            if scale is None:
                k.op("pool", lambda e, st=st, kc=kc, c0=c0, cw=cw: e.tensor_copy(
                    out=W.t[:, kc, col0 + c0:col0 + c0 + cw], in_=st.t[:, :cw]),
                    reads=[st.b], writes=[W.b])
            else:
                k.op("pool", lambda e, st=st, kc=kc, c0=c0, cw=cw: e.tensor_scalar(
                    out=W.t[:, kc, col0 + c0:col0 + c0 + cw], in0=st.t[:, :cw],
                    scalar1=scale.t[:, kc:kc + 1], scalar2=None, op0=ALU.mult),
                    reads=[st.b, scale.b], writes=[W.b])


def transpose_tile(k, cm, src_bf, dstT, nchunks=16):
    for half in range((nchunks + 7) // 8):
        pst = cm.ps_tr[half % 2]
        n = min(8, nchunks - half * 8)
        for j in range(n):
            kc = half * 8 + j
            k.op("pe", lambda e, j=j, kc=kc, pst=pst: e.transpose(
                out=pst.t[:, j * 128:(j + 1) * 128], in_=src_bf.t[:, kc * 128:(kc + 1) * 128],
                identity=cm.ident.t[:]), reads=[src_bf.b, cm.ident.b], writes=[pst.b], tok=(j == n - 1))
        k.op("dve", lambda e, half=half, pst=pst, n=n: e.tensor_copy(
            out=dstT.t[:, half * 1024:half * 1024 + n * 128], in_=pst.t[:, :n * 128]),
            reads=[pst.b], writes=[dstT.b])


def mm_group(k, out_ps, lhsT_t, lhsT_b, W, col0, ncols, kchunks=16):
    for kc in range(kchunks):
        k.op("pe", lambda e, kc=kc: e.matmul(
            out=out_ps.t[:, :ncols], lhsT=lhsT_t[:, kc * 128:(kc + 1) * 128],
            rhs=W.t[:, kc, col0:col0 + ncols], start=(kc == 0), stop=(kc == kchunks - 1)),
            reads=[lhsT_b, W.b], writes=[out_ps.b], tok=(kc == kchunks - 1))


def mmres_alloc(k):
    return {
        "W": T(k, "rW", [128, 16, 2048], BF16),
        "a": [T(k, "ra%d" % i, [128, 2048], F32) for i in range(2)],
        "x": [T(k, "rx%d" % i, [128, 2048], F32) for i in range(2)],
        "ab": T(k, "rab", [128, 2048], BF16),
        "aT": T(k, "raT", [128, 2048], BF16),
        "o": [T(k, "ro%d" % i, [128, 2048], F32) for i in range(2)],
    }


def phase_mmres(k, cm, bufs, w_d, ntiles, a_d, a_row0, a_bufs, x_d, x_row0, x_bufs, out_d, out_row0, out_bufs):
    W = bufs["W"]
    load_w(k, cm, W, w_d, 2048, 16)
    for i in range(ntiles):
        a = bufs["a"][i % 2]
        xt = bufs["x"][i % 2]
        ab = bufs["ab"]
        aT = bufs["aT"]
        ot = bufs["o"][i % 2]
        ra, rx, ro = a_row0 + i * 128, x_row0 + i * 128, out_row0 + i * 128
        k.dma("sp", lambda e: e.dma_start(out=a.t[:], in_=a_d[ra:ra + 128, :]),
              reads=[a_bufs[i]] if a_bufs else [], writes=[a.b])
        k.dma("sp", lambda e: e.dma_start(out=xt.t[:], in_=x_d[rx:rx + 128, :]),
              reads=[x_bufs[i]] if x_bufs else [], writes=[xt.b])
        k.op("act", lambda e: e.copy(out=ab.t[:], in_=a.t[:]), reads=[a.b], writes=[ab.b])
        transpose_tile(k, cm, ab, aT)
        for ng in range(4):
            mm_group(k, cm.ps_o[ng], aT.t, aT.b, W, ng * 512, 512)
            k.op("dve", lambda e, ng=ng: e.tensor_tensor(
                out=ot.t[:, ng * 512:(ng + 1) * 512], in0=cm.ps_o[ng].t[:],
                in1=xt.t[:, ng * 512:(ng + 1) * 512], op=ALU.add),
                reads=[cm.ps_o[ng].b, xt.b], writes=[ot.b])
        k.dma("pool", lambda e: e.dma_start(out=out_d[ro:ro + 128, :], in_=ot.t[:]), reads=[ot.b],
              writes=[out_bufs[i]] if out_bufs else [])


def build_test_b1(ntiles):
    nc = bass.Bass("TRN2", target_bir_lowering=False)
    x_d = nc.dram_tensor("x_in", [ntiles * 128, D], F32, kind="ExternalInput").ap()
    hg_d = nc.dram_tensor("hg_in", [ntiles * 128, D], F32, kind="ExternalInput").ap()
    w_d = nc.dram_tensor("w_in", [D, D], F32, kind="ExternalInput").ap()
    id_d = nc.dram_tensor("ident_in", [128, 128], F32, kind="ExternalInput").ap()
    o_d = nc.dram_tensor("out", [ntiles * 128, D], F32, kind="ExternalOutput").ap()
    with ExitStack() as stack:
        k = K(nc, stack)
        cm = Common(k, id_d)
        bufs = mmres_alloc(k)
        phase_mmres(k, cm, bufs, w_d, ntiles, hg_d, 0, None, x_d, 0, None, o_d, 0, None)
        drain(k)
        print("ninst", k.ninst, "nsem", k.nsem)
    return nc


def rms_rstd(k, h, junk, ss, sq, rstd, ncols=D):
    k.op("act", lambda e: e.activation(out=junk.t[:, :ncols], in_=h.t[:, :ncols], func=AF.Square,
                                       accum_out=ss.t[:, 0:1]),
         reads=[h.b], writes=[junk.b, ss.b])
    k.op("act", lambda e: e.activation(out=sq.t[:, 0:1], in_=ss.t[:, 0:1], func=AF.Sqrt,
                                       scale=1.0 / ncols, bias=cm_eps(k)),
         reads=[ss.b], writes=[sq.b])
    k.op("dve", lambda e: e.reciprocal(out=rstd.t[:, 0:1], in_=sq.t[:, 0:1]), reads=[sq.b], writes=[rstd.b])


_EPS_T = {}


def cm_eps(k):
    if id(k) not in _EPS_T:
        t = T(k, "eps_c", [128, 1], F32)
        k.op("pool", lambda e: e.memset(t.t[:], 1e-6), writes=[t.b])
        _EPS_T.clear()
        _EPS_T[id(k)] = t
    t = _EPS_T[id(k)]
    k._need("act", t.b.wr)
    return t.t[:, 0:1]


def peer_alloc(k):
    b = {}
    b["Wq"] = T(k, "pWq", [128, 16, 2048], BF16)
    b["kT"] = T(k, "pkT", [128, 16, 128], BF16)
    b["cw"] = T(k, "pcw", [128, 2048], F32)
    b["h"] = T(k, "ph", [128, 2048], F32)
    b["junk"] = T(k, "pjunk", [128, 2048], BF16)
    b["ss"] = T(k, "pss", [128, 1], F32)
    b["sq"] = T(k, "psq", [128, 1], F32)
    b["rstd"] = T(k, "prstd", [128, 1], F32)
    b["xn"] = T(k, "pxn", [128, 2048], BF16)
    b["xT"] = T(k, "pxT", [128, 2048], BF16)
    b["qb"] = T(k, "pqb", [128, 2048], BF16)
    b["qT"] = b["xT"]
    b["S"] = T(k, "pS", [128, 2048], F32)
    b["kst"] = V(b["S"].t[:].rearrange("p (c n) -> p c n", n=128), None)
    b["kst"].b = b["S"].b
    b["v"] = T(k, "pv", [128, 16, 16], F32)
    b["ix"] = T(k, "pix", [128, 16, 16], U32)
    b["ixf"] = T(k, "pixf", [128, 16, 16], F32)
    b["cand"] = T(k, "pcand", [128, 8, 256], F32)
    b["cand2"] = T(k, "pcand2", [128, 8, 256], F32)
    b["cidx"] = T(k, "pcidx", [128, 8, 256], F32)
    b["ts"] = T(k, "pts", [128, 8, 16], F32)
    b["pos"] = T(k, "ppos", [128, 8, 16], U32)
    b["pcc"] = T(k, "ppcc", [128, 32], F32)
    b["tsm"] = T(k, "ptsm", [128, 8, 16], F32)
    b["e"] = T(k, "pe_", [128, 8, 16], F32)
    b["esum"] = T(k, "pesum", [128, 8], F32)
    b["rinv"] = T(k, "prinv", [128, 8], F32)
    b["g"] = T(k, "pg", [128, 128], F32)
    b["E"] = T(k, "pE", [128, 16, 256], F32)
    b["eidx"] = T(k, "peidx", [128, 128], F32)
    b["idxT"] = T(k, "pidxT", [128, 128], I32)
    b["gT"] = T(k, "pgT", [128, 128], F32)
    b["actv"] = T(k, "pactv", [128, 128], F32)
    b["gel"] = T(k, "pgel", [128, 128], F32)
    b["coefT"] = T(k, "pcoefT", [128, 128], F32)
    b["W2"] = T(k, "pW2", [128, 256], BF16)
    b["ue"] = [T(k, "pue%d" % i, [128, 2048], BF16) for i in range(4)]
    b["ve"] = [T(k, "pve%d" % i, [128, 2048], BF16) for i in range(4)]
    b["lh"] = [T(k, "plh%d" % i, [128, 128], BF16) for i in range(2)]
    b["hi"] = b["tsm"]
    b["lo"] = b["e"]
    for nm, src_ in (("posf", "actv"), ("esel", "gel")):
        v_ = V(b[src_].t[:].rearrange("p (h k) -> p h k", k=16), None)
        v_.b = b[src_].b
        b[nm] = v_
    Eb = b["E"].t[:].rearrange("p a c -> p (a c)").bitcast(BF16)
    b["E_alias"] = []
    for i in range(4):
        v_ = V(Eb[:, i * 2048:(i + 1) * 2048], "ueA%d" % i)
        b["ue"].append(v_)
        b["E_alias"].append(v_.b)
    for nm in ("cand2", "cidx"):
        cb = b[nm].t[:].rearrange("p a c -> p (a c)").bitcast(BF16)
        b[nm + "_alias"] = []
        for i in range(2):
            v_ = V(cb[:, i * 2048:(i + 1) * 2048], "veA_%s%d" % (nm, i))
            b["ve"].append(v_)
            b[nm + "_alias"].append(v_.b)
    b["o"] = b["S"]
    k.op("pool", lambda e: e.memset(b["W2"].t[:], 0.0), writes=[b["W2"].b])
    k.op("pool", lambda e: e.memset(b["W2"].t[:, 127:128], 1.0), writes=[b["W2"].b])
    return b


def phase_peer(k, cm, b, hs_d, hs_bufs, tiles, cw_d, wq_d, k1T_d, k2T_d, u_d, v_d):
    load_w(k, cm, b["Wq"], wq_d, 2048, 16)
    k.dma("sp", lambda e: e.dma_start(out=b["cw"].t[:], in_=cw_d.partition_broadcast(128)), writes=[b["cw"].b])
    k.dma("sp", lambda e: e.dma_start(out=b["pcc"].t[:], in_=cm.pcc_d[:, :]), writes=[b["pcc"].b])
    for half, kd in enumerate((k1T_d, k2T_d)):
        for h in range(8):
            c = 2 * h + half
            k.dma("sp", lambda e, c=c, h=h, kd=kd: e.dma_start(out=b["kst"].t[:, c, :], in_=kd[h, :, :]),
                  writes=[b["kst"].b])
    k.op("pool", lambda e: e.tensor_copy(out=b["kT"].t[:], in_=b["kst"].t[:]), reads=[b["kst"].b], writes=[b["kT"].b])

    h_, junk, ss, sq, rstd = b["h"], b["junk"], b["ss"], b["sq"], b["rstd"]
    xn, xT, qb, qT, S = b["xn"], b["xT"], b["qb"], b["qT"], b["S"]
    v, ix, ixf, cand, cand2, cidx, ts = b["v"], b["ix"], b["ixf"], b["cand"], b["cand2"], b["cidx"], b["ts"]
    for i in tiles:
        r0 = i * 128
        k.dma("sp", lambda e: e.dma_start(out=h_.t[:], in_=hs_d[r0:r0 + 128, :]), reads=[hs_bufs[i]], writes=[h_.b])
        rms_rstd(k, h_, junk, ss, sq, rstd)
        k.op("dve", lambda e: e.scalar_tensor_tensor(out=xn.t[:], in0=h_.t[:], scalar=rstd.t[:, 0:1], in1=b["cw"].t[:],
                                                     op0=ALU.mult, op1=ALU.mult),
             reads=[h_.b, rstd.b, b["cw"].b], writes=[xn.b])
        transpose_tile(k, cm, xn, xT)
        for ng in range(4):
            mm_group(k, cm.ps_o[ng], xT.t, xT.b, b["Wq"], ng * 512, 512)
            k.op("act", lambda e, ng=ng: e.copy(out=qb.t[:, ng * 512:(ng + 1) * 512], in_=cm.ps_o[ng].t[:]),
                 reads=[cm.ps_o[ng].b], writes=[qb.b])
        transpose_tile(k, cm, qb, qT)
        for c in range(16):
            po = cm.ps_o[c // 4]
            k.op("pe", lambda e, c=c, po=po: e.matmul(out=po.t[:, (c % 4) * 128:(c % 4 + 1) * 128],
                                                      lhsT=qT.t[:, c * 128:(c + 1) * 128], rhs=b["kT"].t[:, c, :],
                                                      start=True, stop=True),
                 reads=[qT.b, b["kT"].b], writes=[po.b], tok=(c % 4 == 3))
        for ng in range(4):
            k.op("act", lambda e, ng=ng: e.copy(out=S.t[:, ng * 512:(ng + 1) * 512], in_=cm.ps_o[ng].t[:]),
                 reads=[cm.ps_o[ng].b], writes=[S.b])
        for c in range(16):
            Sc = S.t[:, c * 128:(c + 1) * 128]
            k.op("dve", lambda e, c=c, Sc=Sc: e.max(out=v.t[:, c, 0:8], in_=Sc), reads=[S.b], writes=[v.b])
            k.op("dve", lambda e, c=c, Sc=Sc: e.max_index(out=ix.t[:, c, 0:8], in_max=v.t[:, c, 0:8], in_values=Sc),
                 reads=[S.b, v.b], writes=[ix.b])
            k.op("dve", lambda e, c=c, Sc=Sc: e.match_replace(out=Sc, in_to_replace=v.t[:, c, 0:8], in_values=Sc,
                                                              imm_value=-1e30),
                 reads=[v.b], writes=[S.b])
            k.op("dve", lambda e, c=c, Sc=Sc: e.max(out=v.t[:, c, 8:16], in_=Sc), reads=[S.b], writes=[v.b])
            k.op("dve", lambda e, c=c, Sc=Sc: e.max_index(out=ix.t[:, c, 8:16], in_max=v.t[:, c, 8:16], in_values=Sc),
                 reads=[S.b, v.b], writes=[ix.b])
        k.op("dve", lambda e: e.tensor_copy(out=ixf.t[:], in_=ix.t[:]), reads=[ix.b], writes=[ixf.b])
        vv = v.t[:].rearrange("p (h two) k -> p h two k", two=2)
        iv = ixf.t[:].rearrange("p (h two) k -> p h two k", two=2)
        cand4 = cand.t[:].rearrange("p h (i j) -> p h i j", j=16)
        k.op("dve", lambda e: e.tensor_tensor(out=cand4, in0=vv[:, :, 0, :].unsqueeze(3).to_broadcast([128, 8, 16, 16]),
                                              in1=vv[:, :, 1, :].unsqueeze(2).to_broadcast([128, 8, 16, 16]), op=ALU.add),
             reads=[v.b], writes=[cand.b])
        k.op("dve", lambda e: e.tensor_scalar(out=iv[:, :, 0, :], in0=iv[:, :, 0, :], scalar1=128.0, scalar2=None,
                                              op0=ALU.mult), reads=[ixf.b], writes=[ixf.b])
        pos, posf, hi, lo = b["pos"], b["posf"], b["hi"], b["lo"]
        for h in range(8):
            ch = cand.t[:, h, :]
            k.op("dve", lambda e, h=h, ch=ch: e.max(out=ts.t[:, h, 0:8], in_=ch), reads=[cand.b], writes=[ts.b])
            k.op("dve", lambda e, h=h, ch=ch: e.max_index(out=pos.t[:, h, 0:8], in_max=ts.t[:, h, 0:8], in_values=ch),
                 reads=[cand.b, ts.b], writes=[pos.b])
            k.op("dve", lambda e, h=h, ch=ch: e.match_replace(out=ch, in_to_replace=ts.t[:, h, 0:8], in_values=ch,
                                                              imm_value=-1e30), reads=[ts.b], writes=[cand.b])
            k.op("dve", lambda e, h=h, ch=ch: e.max(out=ts.t[:, h, 8:16], in_=ch), reads=[cand.b], writes=[ts.b])
            k.op("dve", lambda e, h=h, ch=ch: e.max_index(out=pos.t[:, h, 8:16], in_max=ts.t[:, h, 8:16], in_values=ch),
                 reads=[cand.b, ts.b], writes=[pos.b])
        k.op("dve", lambda e: e.tensor_copy(out=posf.t[:], in_=pos.t[:]), reads=[pos.b], writes=[posf.b])
        sc1 = cand2.t[:].rearrange("p h (k i) -> p h k i", i=16)
        sc2 = cidx.t[:].rearrange("p h (k i) -> p h k i", i=16)
        s1w = [cand2.b] + b["cand2_alias"]
        s2w = [cidx.b] + b["cidx_alias"]
        thr = b["pcc"].t[:, 0:16]
        io16 = b["pcc"].t[:, 16:32]
        bc_k = lambda ap3: ap3.unsqueeze(3).to_broadcast([128, 8, 16, 16])
        bc_c = lambda ap1: ap1.unsqueeze(1).unsqueeze(1).to_broadcast([128, 8, 16, 16])
        k.op("dve", lambda e: e.tensor_tensor(out=sc1, in0=bc_k(posf.t[:]), in1=bc_c(thr), op=ALU.is_ge),
             reads=[posf.b, b["pcc"].b], writes=s1w)
        k.op("dve", lambda e: e.tensor_reduce(out=hi.t[:], in_=sc1, axis=AX.X, op=ALU.add), reads=[cand2.b], writes=[hi.b])
        k.op("dve", lambda e: e.scalar_tensor_tensor(out=lo.t[:].rearrange("p h k -> p (h k)"),
                                                     in0=hi.t[:].rearrange("p h k -> p (h k)"), scalar=-16.0,
                                                     in1=posf.t[:].rearrange("p h k -> p (h k)"), op0=ALU.mult, op1=ALU.add),
             reads=[hi.b, posf.b], writes=[lo.b])
        eidx3 = b["eidx"].t[:].rearrange("p (h k) -> p h k", k=16)
        esel = b["esel"]
        for which, (src_, scr, scw, dst) in enumerate(((hi, sc1, s1w, eidx3), (lo, sc2, s2w, esel.t[:]))):
            tab = iv[:, :, which, :].unsqueeze(2).to_broadcast([128, 8, 16, 16])
            k.op("dve", lambda e, src_=src_, scr=scr: e.tensor_tensor(out=scr, in0=bc_k(src_.t[:]), in1=bc_c(io16), op=ALU.is_equal),
                 reads=[src_.b, b["pcc"].b], writes=scw)
            k.op("dve", lambda e, scr=scr, tab=tab: e.tensor_tensor(out=scr, in0=scr, in1=tab, op=ALU.mult),
                 reads=[ixf.b], writes=scw)
            k.op("dve", lambda e, scr=scr, dst=dst: e.tensor_reduce(out=dst, in_=scr, axis=AX.X, op=ALU.add),
                 reads=[scw[0]], writes=[b["eidx"].b if which == 0 else esel.b])
        k.op("dve", lambda e: e.tensor_tensor(out=eidx3, in0=eidx3, in1=esel.t[:], op=ALU.add), reads=[esel.b], writes=[b["eidx"].b])
        k.op("dve", lambda e: e.tensor_scalar(out=b["eidx"].t[:], in0=b["eidx"].t[:], scalar1=16383.0, scalar2=0.0,
                                              op0=ALU.min, op1=ALU.max), reads=[], writes=[b["eidx"].b])
        k.op("dve", lambda e: e.tensor_tensor(out=b["tsm"].t[:], in0=ts.t[:], in1=ts.t[:, :, 0:1].to_broadcast([128, 8, 16]),
                                              op=ALU.subtract), reads=[ts.b], writes=[b["tsm"].b])
        k.op("act", lambda e: e.activation(out=b["e"].t[:], in_=b["tsm"].t[:], func=AF.Exp),
             reads=[b["tsm"].b], writes=[b["e"].b])
        k.op("dve", lambda e: e.tensor_reduce(out=b["esum"].t[:], in_=b["e"].t[:], axis=AX.X, op=ALU.add),
             reads=[b["e"].b], writes=[b["esum"].b])
        k.op("dve", lambda e: e.reciprocal(out=b["rinv"].t[:], in_=b["esum"].t[:]), reads=[b["esum"].b], writes=[b["rinv"].b])
        g3 = b["g"].t[:].rearrange("p (h k) -> p h k", k=16)
        k.op("dve", lambda e: e.tensor_tensor(out=g3, in0=b["e"].t[:], in1=b["rinv"].t[:].unsqueeze(2).to_broadcast([128, 8, 16]),
                                              op=ALU.mult), reads=[b["e"].b, b["rinv"].b], writes=[b["g"].b])
        px0, px1 = cm.ps_x[0], cm.ps_x[1]
        k.op("pe", lambda e: e.transpose(out=px0.t[:, 0:128], in_=b["eidx"].t[:], identity=cm.identf.t[:]),
             reads=[b["eidx"].b, cm.identf.b], writes=[px0.b])
        k.op("dve", lambda e: e.tensor_copy(out=b["idxT"].t[:], in_=px0.t[:, 0:128]), reads=[px0.b], writes=[b["idxT"].b])
        k.op("pe", lambda e: e.transpose(out=px1.t[:, 0:128], in_=b["g"].t[:], identity=cm.identf.t[:]),
             reads=[b["g"].b, cm.identf.b], writes=[px1.b])
        k.op("dve", lambda e: e.tensor_copy(out=b["gT"].t[:], in_=px1.t[:, 0:128]), reads=[px1.b], writes=[b["gT"].b])
        actv = b["actv"]
        for t in range(128):
            ue = b["ue"][t % 8]
            k.dma("pool", lambda e, t=t, ue=ue: e.indirect_dma_start(
                out=ue.t[:], out_offset=None, in_=u_d[:, :],
                in_offset=bass.IndirectOffsetOnAxis(ap=b["idxT"].t[:, t:t + 1], axis=0)),
                reads=[b["idxT"].b], writes=[ue.b])
            pbk = cm.ps_o if t % 2 == 0 else cm.ps_y
            pall = cm.psA if t % 2 == 0 else cm.psB
            pbufs = cm.bankA if t % 2 == 0 else cm.bankB
            for ng in range(4):
                k.op("pe", lambda e, t=t, ng=ng, pbk=pbk: e.matmul(
                    out=pbk[ng].t[:], lhsT=cm.ident.t[:, t:t + 1].to_broadcast([128, 128]),
                    rhs=xn.t[:, ng * 512:(ng + 1) * 512], start=True, stop=True),
                    reads=[cm.ident.b, xn.b], writes=[pbk[ng].b], tok=(ng == 3))
            k.op("dve", lambda e, t=t, ue=ue, pall=pall: e.scalar_tensor_tensor(
                out=junk.t[:], in0=ue.t[:], scalar=1.0, in1=pall[:, :], op0=ALU.mult, op1=ALU.mult,
                accum_out=actv.t[:, t:t + 1]),
                reads=[ue.b] + pbufs, writes=[junk.b, actv.b])
        k.op("act", lambda e: e.activation(out=b["gel"].t[:], in_=actv.t[:], func=AF.Gelu),
             reads=[actv.b], writes=[b["gel"].b])
        k.op("dve", lambda e: e.tensor_tensor(out=b["coefT"].t[:], in0=b["gel"].t[:], in1=b["gT"].t[:], op=ALU.mult),
             reads=[b["gel"].b, b["gT"].b], writes=[b["coefT"].b])
        for t in range(128):
            ve = b["ve"][t % 8]
            lh = b["lh"][t % 2]
            k.dma("pool", lambda e, t=t, ve=ve: e.indirect_dma_start(
                out=ve.t[:], out_offset=None, in_=v_d[:, :],
                in_offset=bass.IndirectOffsetOnAxis(ap=b["idxT"].t[:, t:t + 1], axis=0)),
                reads=[b["idxT"].b], writes=[ve.b])
            k.op("dve", lambda e, t=t, lh=lh: e.tensor_scalar(
                out=lh.t[:], in0=b["W2"].t[:, 127 - t:255 - t], scalar1=b["coefT"].t[:, t:t + 1], scalar2=None,
                op0=ALU.mult), reads=[b["W2"].b, b["coefT"].b], writes=[lh.b])
            for ng in range(4):
                k.op("pe", lambda e, t=t, ng=ng, lh=lh, ve=ve: e.matmul(
                    out=cm.ps_y[ng].t[:], lhsT=lh.t[:], rhs=ve.t[:, ng * 512:(ng + 1) * 512],
                    start=(t == 0), stop=(t == 127)),
                    reads=[lh.b, ve.b], writes=[cm.ps_y[ng].b], tok=(ng == 3))
        o = b["o"]
        k.op("dve", lambda e: e.tensor_tensor(out=o.t[:], in0=cm.psB[:, :], in1=h_.t[:], op=ALU.add),
             reads=[h_.b] + cm.bankB, writes=[o.b])
        k.dma("pool", lambda e: e.dma_start(out=hs_d[r0:r0 + 128, :], in_=o.t[:]), reads=[o.b], writes=[hs_bufs[i]])


def drain(k):
    for q, slots in k.dq.items():
        for s in slots:
            if s[1] > 0:
                k._need("sp", (s[0], s[1], "dma"))


def peer_consts():
    c = np.zeros((128, 32), np.float32)
    c[:, 0:16] = 16.0 * np.arange(1, 17)[None, :]
    c[:, 16:32] = np.arange(16)[None, :]
    return c


def build_test_peer(ntiles):
    nc = bass.Bass("TRN2", target_bir_lowering=False)
    pcc_d = nc.dram_tensor("pcc_in", [128, 32], F32, kind="ExternalInput").ap()
    hs_d = nc.dram_tensor("hs_io", [ntiles * 128, D], F32, kind="ExternalOutput").ap()
    hin_d = nc.dram_tensor("h_in", [ntiles * 128, D], F32, kind="ExternalInput").ap()
    cw_d = nc.dram_tensor("cw_in", [D], F32, kind="ExternalInput").ap()
    wq_d = nc.dram_tensor("wq_in", [D, D], F32, kind="ExternalInput").ap()
    k1_d = nc.dram_tensor("k1T_in", [8, 128, 128], F32, kind="ExternalInput").ap()
    k2_d = nc.dram_tensor("k2T_in", [8, 128, 128], F32, kind="ExternalInput").ap()
    u_d = nc.dram_tensor("u_in", [16384, D], F32, kind="ExternalInput").ap()
    v_d = nc.dram_tensor("v_in", [16384, D], F32, kind="ExternalInput").ap()
    id_d = nc.dram_tensor("ident_in", [128, 128], F32, kind="ExternalInput").ap()
    with ExitStack() as stack:
        k = K(nc, stack)
        cm = Common(k, id_d)
        cm.pcc_d = pcc_d
        b = peer_alloc(k)
        hs_bufs = [Buf("hs%d" % i) for i in range(ntiles)]
        for i in range(ntiles):
            k.dma("sp", lambda e, i=i: e.dma_start(out=b["o"].t[:], in_=hin_d[i * 128:(i + 1) * 128, :]), writes=[b["o"].b])
            k.dma("sp", lambda e, i=i: e.dma_start(out=hs_d[i * 128:(i + 1) * 128, :], in_=b["o"].t[:]), reads=[b["o"].b],
                  writes=[hs_bufs[i]])
        phase_peer(k, cm, b, hs_d, hs_bufs, list(range(ntiles)), cw_d, wq_d, k1_d, k2_d, u_d, v_d)
        drain(k)
        print("ninst", k.ninst, "nsem", k.nsem)
    return nc


def ple_alloc(k):
    b = {}
    b["Wg"] = T(k, "eWg", [128, 16, 2048], BF16)
    b["Wp"] = T(k, "eWp", [128, 2, 2048], BF16)
    b["nw"] = T(k, "enw", [128, 16], F32)
    b["fw"] = T(k, "efw", [128, 2048], F32)
    b["h"] = [T(k, "eh%d" % i, [128, 2048], F32) for i in range(2)]
    b["p"] = [T(k, "ep%d" % i, [128, 256], F32) for i in range(2)]
    b["pb"] = T(k, "epb", [128, 256], BF16)
    b["pT"] = T(k, "epT", [128, 256], BF16)
    b["junk"] = T(k, "ejunk", [128, 2048], BF16)
    b["ss"] = T(k, "ess", [128, 1], F32)
    b["sq"] = T(k, "esq", [128, 1], F32)
    b["rstd"] = T(k, "erstd", [128, 1], F32)
    b["xn"] = T(k, "exn", [128, 2048], BF16)
    b["xT"] = T(k, "exT", [128, 2048], BF16)
    b["sig"] = T(k, "esig", [128, 2048], F32)
    b["o"] = [T(k, "eo%d" % i, [128, 2048], F32) for i in range(2)]
    b["o2"] = [T(k, "eo2%d" % i, [128, 2048], F32) for i in range(2)]
    return b


def phase_ple(k, cm, b, hs_d, hs_bufs, tiles, p_d, prow0, nw_d, wg_d, wp_d, final=None):
    k.dma("sp", lambda e: e.dma_start(out=b["nw"].t[:], in_=nw_d[:, :]), writes=[b["nw"].b])
    load_w(k, cm, b["Wg"], wg_d, 2048, 16, scale=b["nw"])
    load_w(k, cm, b["Wp"], wp_d, 2048, 2)
    if final is not None:
        fw_d, out_d, orow0 = final
        k.dma("sp", lambda e: e.dma_start(out=b["fw"].t[:], in_=fw_d.partition_broadcast(128)), writes=[b["fw"].b])
    junk, ss, sq, rstd, xn, xT = b["junk"], b["ss"], b["sq"], b["rstd"], b["xn"], b["xT"]
    for n, i in enumerate(tiles):
        r0 = i * 128
        h_ = b["h"][n % 2]
        p_ = b["p"][n % 2]
        o = b["o"][n % 2]
        k.dma("sp", lambda e: e.dma_start(out=h_.t[:], in_=hs_d[r0:r0 + 128, :]), reads=[hs_bufs[i]], writes=[h_.b])
        k.dma("sp", lambda e: e.dma_start(out=p_.t[:], in_=p_d[prow0 + r0:prow0 + r0 + 128, :]), writes=[p_.b])
        rms_rstd(k, h_, junk, ss, sq, rstd)
        k.op("act", lambda e: e.activation(out=xn.t[:], in_=h_.t[:], func=AF.Copy, scale=rstd.t[:, 0:1]),
             reads=[h_.b, rstd.b], writes=[xn.b])
        transpose_tile(k, cm, xn, xT)
        k.op("pool", lambda e: e.tensor_copy(out=b["pb"].t[:], in_=p_.t[:]), reads=[p_.b], writes=[b["pb"].b])
        transpose_tile(k, cm, b["pb"], b["pT"], nchunks=2)
        for ng in range(4):
            mm_group(k, cm.ps_o[ng], xT.t, xT.b, b["Wg"], ng * 512, 512)
            px = cm.ps_x[ng % 2]
            mm_group(k, px, b["pT"].t, b["pT"].b, b["Wp"], ng * 512, 512, kchunks=2)
            sl = slice(ng * 512, (ng + 1) * 512)
            k.op("act", lambda e, ng=ng, sl=sl: e.activation(out=b["sig"].t[:, sl], in_=cm.ps_o[ng].t[:], func=AF.Sigmoid),
                 reads=[cm.ps_o[ng].b], writes=[b["sig"].b])
            k.op("dve", lambda e, sl=sl, px=px: e.tensor_tensor(out=b["sig"].t[:, sl], in0=px.t[:], in1=b["sig"].t[:, sl],
                                                                op=ALU.mult), reads=[px.b], writes=[b["sig"].b])
        k.op("pool", lambda e: e.tensor_tensor(out=o.t[:], in0=b["sig"].t[:], in1=h_.t[:], op=ALU.add),
             reads=[b["sig"].b, h_.b], writes=[o.b])
        if final is None:
            k.dma("pool", lambda e: e.dma_start(out=hs_d[r0:r0 + 128, :], in_=o.t[:]), reads=[o.b], writes=[hs_bufs[i]])
        else:
            o2 = b["o2"][n % 2]
            rms_rstd(k, o, junk, ss, sq, rstd)
            k.op("dve", lambda e: e.scalar_tensor_tensor(out=o2.t[:], in0=o.t[:], scalar=rstd.t[:, 0:1], in1=b["fw"].t[:],
                                                         op0=ALU.mult, op1=ALU.mult),
                 reads=[o.b, rstd.b, b["fw"].b], writes=[o2.b])
            rr = orow0 + n * 128
            k.dma("pool", lambda e: e.dma_start(out=out_d[rr:rr + 128, :], in_=o2.t[:]), reads=[o2.b])


def build_test_ple(ntiles, final):
    nc = bass.Bass("TRN2", target_bir_lowering=False)
    hs_d = nc.dram_tensor("hs_io", [ntiles * 128, D], F32, kind="ExternalOutput").ap()
    out_d = nc.dram_tensor("out", [ntiles * 128, D], F32, kind="ExternalOutput").ap()
    hin_d = nc.dram_tensor("h_in", [ntiles * 128, D], F32, kind="ExternalInput").ap()
    p_d = nc.dram_tensor("p_in", [ntiles * 128, 256], F32, kind="ExternalInput").ap()
    nw_d = nc.dram_tensor("nw_in", [128, 16], F32, kind="ExternalInput").ap()
    fw_d = nc.dram_tensor("fw_in", [D], F32, kind="ExternalInput").ap()
    wg_d = nc.dram_tensor("wg_in", [D, D], F32, kind="ExternalInput").ap()
    wp_d = nc.dram_tensor("wp_in", [256, D], F32, kind="ExternalInput").ap()
    id_d = nc.dram_tensor("ident_in", [128, 128], F32, kind="ExternalInput").ap()
    with ExitStack() as stack:
        k = K(nc, stack)
        cm = Common(k, id_d)
        b = ple_alloc(k)
        hs_bufs = [Buf("hs%d" % i) for i in range(ntiles)]
        for i in range(ntiles):
            k.dma("sp", lambda e, i=i: e.dma_start(out=b["sig"].t[:], in_=hin_d[i * 128:(i + 1) * 128, :]), writes=[b["sig"].b])
            k.dma("sp", lambda e, i=i: e.dma_start(out=hs_d[i * 128:(i + 1) * 128, :], in_=b["sig"].t[:]), reads=[b["sig"].b],
                  writes=[hs_bufs[i]])
        phase_ple(k, cm, b, hs_d, hs_bufs, list(range(ntiles)), p_d, 0, nw_d, wg_d, wp_d,
                  final=(fw_d, out_d, 0) if final else None)
        drain(k)
        print("ninst", k.ninst, "nsem", k.nsem)
    return nc


def att_alloc(k):
    b = {}
    b["Wq"] = T(k, "aWq", [128, 16, 2048], BF16)
    b["Wkv"] = T(k, "aWkv", [128, 16, 512], BF16)
    b["nq"] = T(k, "anq", [128, 16], F32)
    b["nkv"] = T(k, "ankv", [128, 16], F32)
    b["mask"] = T(k, "amask", [128, 256], F32)
    b["mask1"] = T(k, "amask1", [128, 256], F32)
    b["sinks"] = T(k, "asinks", [128, 32], F32)
    b["h"] = [T(k, "ah%d" % i, [128, 2048], F32) for i in range(2)]
    b["junk"] = T(k, "ajunk", [128, 2048], BF16)
    b["ss"] = T(k, "ass", [128, 1], F32)
    b["sq"] = T(k, "asq", [128, 1], F32)
    b["rstd"] = T(k, "arstd", [128, 1], F32)
    b["xn"] = T(k, "axn", [128, 2048], BF16)
    b["xT"] = T(k, "axT", [128, 2048], BF16)
    b["qb"] = T(k, "aqb", [128, 2048], BF16)
    b["qT"] = T(k, "aqT", [128, 2048], BF16)
    b["Kdup"] = T(k, "aKdup", [128, 8, 128], BF16)
    b["kT"] = [T(k, "akT%d" % i, [128, 1024], BF16) for i in range(2)]
    b["V"] = [T(k, "aV%d" % i, [128, 256], BF16) for i in range(2)]
    b["sm"] = T(k, "asm", [128, 8, 256], F32)
    b["pr"] = T(k, "apr", [128, 8, 256], BF16)
    b["prT"] = [T(k, "aprT%d" % i, [128, 1024], BF16) for i in range(2)]
    b["mx"] = T(k, "amx", [128, 8], F32)
    b["rs"] = T(k, "ars", [128, 8], F32)
    b["es"] = T(k, "aes", [128, 8], F32)
    b["rden"] = T(k, "arden", [128, 8], F32)
    b["o"] = [T(k, "ao%d" % i, [128, 2048], F32) for i in range(2)]
    return b


def phase_att(k, cm, b, hs_d, hs_bufs, ntiles, att_d, att_bufs, nq_d, nkv_d, wq_d, wkv_d, sinks_d, mask_d, mask1_d):
    k.dma("sp", lambda e: e.dma_start(out=b["nq"].t[:], in_=nq_d[:, :]), writes=[b["nq"].b])
    k.dma("sp", lambda e: e.dma_start(out=b["nkv"].t[:], in_=nkv_d[:, :]), writes=[b["nkv"].b])
    k.dma("sp", lambda e: e.dma_start(out=b["mask"].t[:], in_=mask_d[:, :]), writes=[b["mask"].b])
    k.dma("sp", lambda e: e.dma_start(out=b["mask1"].t[:], in_=mask1_d[:, :]), writes=[b["mask1"].b])
    k.dma("sp", lambda e: e.dma_start(out=b["sinks"].t[:], in_=sinks_d.partition_broadcast(128)), writes=[b["sinks"].b])
    load_w(k, cm, b["Wq"], wq_d, 2048, 16, scale=b["nq"])
    load_w(k, cm, b["Wkv"], wkv_d, 512, 16, scale=b["nkv"])
    k.op("pool", lambda e: e.memset(b["Kdup"].t[:], 0.0), writes=[b["Kdup"].b])
    junk, ss, sq, rstd, xn, xT, qb, qT = b["junk"], b["ss"], b["sq"], b["rstd"], b["xn"], b["xT"], b["qb"], b["qT"]
    sm, pr, mx, rs, es, rden = b["sm"], b["pr"], b["mx"], b["rs"], b["es"], b["rden"]
    for i in range(ntiles):
        r0 = i * 128
        h_ = b["h"][i % 2]
        kTc, kTp = b["kT"][i % 2], b["kT"][(i + 1) % 2]
        Vc, Vp = b["V"][i % 2], b["V"][(i + 1) % 2]
        k.dma("sp", lambda e: e.dma_start(out=h_.t[:], in_=hs_d[r0:r0 + 128, :]), reads=[hs_bufs[i]], writes=[h_.b])
        rms_rstd(k, h_, junk, ss, sq, rstd)
        k.op("act", lambda e: e.activation(out=xn.t[:], in_=h_.t[:], func=AF.Copy, scale=rstd.t[:, 0:1]),
             reads=[h_.b, rstd.b], writes=[xn.b])
        transpose_tile(k, cm, xn, xT)
        pkv = cm.ps_x[0]
        mm_group(k, pkv, xT.t, xT.b, b["Wkv"], 0, 512)
        kview = pkv.t[:, 0:256].rearrange("p (g d) -> p g d", d=64)
        kz4 = b["Kdup"].t[:].rearrange("p (g two) d -> p g two d", two=2)
        k.op("act", lambda e: e.copy(out=kz4[:, :, 0, 0:64], in_=kview), reads=[pkv.b], writes=[b["Kdup"].b])
        k.op("dve", lambda e: e.tensor_copy(out=kz4[:, :, 1, 64:128], in_=kview), reads=[pkv.b], writes=[b["Kdup"].b])
        k.op("act", lambda e: e.copy(out=Vc.t[:], in_=pkv.t[:, 256:512]), reads=[pkv.b], writes=[Vc.b])
        kd2 = V(b["Kdup"].t[:].rearrange("p g d -> p (g d)"), None)
        kd2.b = b["Kdup"].b
        transpose_tile(k, cm, kd2, kTc, nchunks=8)
        if i == 0:
            continue
        for ng in range(4):
            mm_group(k, cm.ps_o[ng], xT.t, xT.b, b["Wq"], ng * 512, 512)
            k.op("act", lambda e, ng=ng: e.activation(out=qb.t[:, ng * 512:(ng + 1) * 512], in_=cm.ps_o[ng].t[:],
                                                      func=AF.Copy, scale=0.125),
                 reads=[cm.ps_o[ng].b], writes=[qb.b])
        transpose_tile(k, cm, qb, qT)
        if ATT_STOP == 1:
            continue
        msk = b["mask1"] if i == 1 else b["mask"]
        o = b["o"][i % 2]
        for g in range(4):
            for j in range(8):
                hq = 8 * g + j
                c = hq // 2
                off = (hq % 2) * 64
                po = cm.ps_o[j // 2]
                cb = (j % 2) * 256
                k.op("pe", lambda e, c=c, off=off, po=po, cb=cb, g=g: e.matmul(
                    out=po.t[:, cb:cb + 128], lhsT=qT.t[:, c * 128:(c + 1) * 128],
                    rhs=kTp.t[:, (2 * g + off // 64) * 128:(2 * g + off // 64 + 1) * 128], start=True, stop=True),
                    reads=[qT.b, kTp.b], writes=[po.b], tok=False)
                k.op("pe", lambda e, c=c, off=off, po=po, cb=cb, g=g: e.matmul(
                    out=po.t[:, cb + 128:cb + 256], lhsT=qT.t[:, c * 128:(c + 1) * 128],
                    rhs=kTc.t[:, (2 * g + off // 64) * 128:(2 * g + off // 64 + 1) * 128], start=True, stop=True),
                    reads=[qT.b, kTc.b], writes=[po.b], tok=(j % 2 == 1))
            if ATT_STOP == 2:
                k.op("pe", lambda e: e.transpose(out=cm.ps_x[1].t[:, 0:128], in_=cm.identf.t[:], identity=cm.identf.t[:]),
                     reads=[cm.identf.b], writes=[cm.ps_x[1].b])
                continue
            sA = cm.psA[:, :].rearrange("p (j n) -> p j n", n=256)
            k.op("dve", lambda e: e.tensor_tensor(out=sm.t[:], in0=sA, in1=msk.t[:].unsqueeze(1).to_broadcast([128, 8, 256]),
                                                  op=ALU.add), reads=cm.bankA + [msk.b], writes=[sm.b])
            k.op("dve", lambda e: e.tensor_reduce(out=mx.t[:], in_=sm.t[:], axis=AX.X, op=ALU.max), reads=[sm.b], writes=[mx.b])
            k.op("dve", lambda e, g=g: e.tensor_tensor(out=mx.t[:], in0=mx.t[:], in1=b["sinks"].t[:, 8 * g:8 * g + 8], op=ALU.max),
                 reads=[b["sinks"].b], writes=[mx.b])
            k.op("dve", lambda e: e.tensor_tensor(out=sm.t[:], in0=sm.t[:], in1=mx.t[:].unsqueeze(2).to_broadcast([128, 8, 256]),
                                                  op=ALU.subtract), reads=[mx.b], writes=[sm.b])
            k.op("act", lambda e: e.activation(out=pr.t[:], in_=sm.t[:], func=AF.Exp), reads=[sm.b], writes=[pr.b])
            k.op("dve", lambda e: e.tensor_reduce(out=rs.t[:], in_=pr.t[:], axis=AX.X, op=ALU.add), reads=[pr.b], writes=[rs.b])
            k.op("dve", lambda e, g=g: e.tensor_tensor(out=es.t[:], in0=b["sinks"].t[:, 8 * g:8 * g + 8], in1=mx.t[:],
                                                       op=ALU.subtract), reads=[b["sinks"].b, mx.b], writes=[es.b])
            k.op("act", lambda e: e.activation(out=es.t[:], in_=es.t[:], func=AF.Exp), reads=[], writes=[es.b])
            k.op("dve", lambda e: e.tensor_tensor(out=rs.t[:], in0=rs.t[:], in1=es.t[:], op=ALU.add), reads=[es.b], writes=[rs.b])
            k.op("dve", lambda e: e.reciprocal(out=rden.t[:], in_=rs.t[:]), reads=[rs.b], writes=[rden.b])
            if ATT_STOP == 3:
                continue
            for half in range(2):
                pst = cm.ps_tr[half]
                for j in range(8):
                    k.op("pe", lambda e, j=j, half=half, pst=pst: e.transpose(
                        out=pst.t[:, j * 128:(j + 1) * 128], in_=pr.t[:, j, half * 128:(half + 1) * 128],
                        identity=cm.ident.t[:]), reads=[pr.b, cm.ident.b], writes=[pst.b], tok=(j == 7))
                eng = "dve" if half == 0 else "act"
                if half == 0:
                    k.op("dve", lambda e, pst=pst: e.tensor_copy(out=b["prT"][0].t[:], in_=pst.t[:]), reads=[pst.b],
                         writes=[b["prT"][0].b])
                else:
                    k.op("act", lambda e, pst=pst: e.copy(out=b["prT"][1].t[:], in_=pst.t[:]), reads=[pst.b],
                         writes=[b["prT"][1].b])
            if ATT_STOP == 4:
                continue
            pov = cm.ps_x[1]
            for j in range(8):
                k.op("pe", lambda e, j=j, g=g: e.matmul(out=pov.t[:, j * 64:(j + 1) * 64], lhsT=b["prT"][0].t[:, j * 128:(j + 1) * 128],
                                                        rhs=Vp.t[:, g * 64:(g + 1) * 64], start=True, stop=False),
                     reads=[b["prT"][0].b, Vp.b], writes=[pov.b], tok=False)
                k.op("pe", lambda e, j=j, g=g: e.matmul(out=pov.t[:, j * 64:(j + 1) * 64], lhsT=b["prT"][1].t[:, j * 128:(j + 1) * 128],
                                                        rhs=Vc.t[:, g * 64:(g + 1) * 64], start=False, stop=True),
                     reads=[b["prT"][1].b, Vc.b], writes=[pov.b], tok=(j == 7))
            k.op("dve", lambda e, g=g: e.tensor_tensor(
                out=o.t[:, g * 512:(g + 1) * 512].rearrange("p (j d) -> p j d", d=64),
                in0=pov.t[:, :].rearrange("p (j d) -> p j d", d=64),
                in1=rden.t[:].unsqueeze(2).to_broadcast([128, 8, 64]), op=ALU.mult),
                reads=[pov.b, rden.b], writes=[o.b])
        ro = (i - 1) * 128
        k.dma("pool", lambda e: e.dma_start(out=att_d[ro:ro + 128, :], in_=o.t[:]), reads=[o.b], writes=[att_bufs[i - 1]])


def build_test_att(ntiles):
    nc = bass.Bass("TRN2", target_bir_lowering=False)
    hin_d = nc.dram_tensor("h_in", [ntiles * 128, D], F32, kind="ExternalInput").ap()
    att_d = nc.dram_tensor("att_out", [(ntiles - 1) * 128, D], F32, kind="ExternalOutput").ap()
    nq_d = nc.dram_tensor("nq_in", [128, 16], F32, kind="ExternalInput").ap()
    nkv_d = nc.dram_tensor("nkv_in", [128, 16], F32, kind="ExternalInput").ap()
    wq_d = nc.dram_tensor("wq_in", [D, D], F32, kind="ExternalInput").ap()
    wkv_d = nc.dram_tensor("wkv_in", [D, 512], F32, kind="ExternalInput").ap()
    sinks_d = nc.dram_tensor("sinks_in", [32], F32, kind="ExternalInput").ap()
    mask_d = nc.dram_tensor("mask_in", [128, 256], F32, kind="ExternalInput").ap()
    mask1_d = nc.dram_tensor("mask1_in", [128, 256], F32, kind="ExternalInput").ap()
    id_d = nc.dram_tensor("ident_in", [128, 128], F32, kind="ExternalInput").ap()
    with ExitStack() as stack:
        k = K(nc, stack)
        cm = Common(k, id_d)
        b = att_alloc(k)
        hs_bufs = [Buf("hs%d" % i) for i in range(ntiles)]
        att_bufs = [Buf("at%d" % i) for i in range(ntiles)]
        phase_att(k, cm, b, hin_d, hs_bufs, ntiles, att_d, att_bufs, nq_d, nkv_d, wq_d, wkv_d, sinks_d, mask_d, mask1_d)
        drain(k)
        print("ninst", k.ninst, "nsem", k.nsem)
    return nc


def band_masks():
    t = np.arange(128)[:, None]
    kk = np.arange(256)[None, :]
    valid = (kk >= t + 1) & (kk <= t + 128)
    m = np.where(valid, 0.0, -1e30).astype(np.float32)
    m1 = m.copy()
    m1[:, :128] = -1e30
    return m, m1


def mlstm_alloc(k):
    mk = lambda name, shape, dt=F32: T(k, name, shape, dt)
    W = mk("mW", [128, 16, 1538], BF16)
    nw = mk("mnw", [128, 16])
    gb = mk("mgb", [128, 2])
    gb15 = mk("mgb15", [128, 2])
    hnw = mk("mhnw", [128, 512])
    tri = mk("mtri", [128, 128])
    sel = mk("msel", [128, 128])
    cmask = mk("mcmask", [128, 128])
    ones = mk("mones", [128, 128])
    one1 = mk("mone1", [128, 1])
    onesb = mk("monesb", [128, 1], BF16)
    C = mk("mC", [128, 2, 512])
    Cb = mk("mCb", [128, 2, 512], BF16)
    n_ = mk("mn", [128, 2])
    nb = mk("mnb", [128, 2], BF16)
    mprev = mk("mmprev", [128, 1])
    xs = [mk("mx%d" % i, [128, 2048]) for i in range(2)]
    ss, sq, rstd = mk("mss", [128, 1]), mk("msq", [128, 1]), mk("mrstd", [128, 1])
    xn = mk("mxn", [128, 2048], BF16)
    xT = mk("mxT", [128, 2048], BF16)
    qkb = mk("mqkb", [128, 512], BF16)
    qkT = mk("mqkT", [128, 512], BF16)
    vb = mk("mvb", [128, 512], BF16)
    sog = mk("msog", [128, 512])
    gs = mk("mgs", [128, 2])
    th = mk("mth", [128, 2])
    li, z, ez, sp, lf = mk("mli", [128, 1]), mk("mz", [128, 1]), mk("mez", [128, 1]), mk("msp", [128, 1]), mk("mlf", [128, 1])
    bcs, gvec, mrow, u, negu, mt = (mk("mbcs", [128, 1]), mk("mgvec", [128, 1]), mk("mmrow", [128, 1]), mk("mu", [128, 1]),
                                    mk("mnegu", [128, 1]), mk("mmt", [128, 1]))
    dg = mk("mdg", [128, 128])
    A = mk("mA", [128, 128])
    wintra = mk("mwintra", [128, 128])
    wia, winter = mk("mwia", [128, 1]), mk("mwinter", [128, 1])
    Pb = mk("mPb", [128, 128], BF16)
    PT = mk("mPT", [128, 128], BF16)
    dintra = mk("mdintra", [128, 1])
    tmp = mk("mtmp", [128, 512])
    num = mk("mnum", [128, 512])
    den, aden, emt, dmax, rden = mk("mden", [128, 1]), mk("maden", [128, 1]), mk("memt", [128, 1]), mk("mdmax", [128, 1]), mk("mrden", [128, 1])
    ssn, t1, sqv, rstd2, sc = mk("mssn", [128, 1]), mk("mt1", [128, 1]), mk("msqv", [128, 1]), mk("mrstd2", [128, 1]), mk("msc", [128, 1])
    ots = [mk("mot%d" % i, [128, 512]) for i in range(2)]
    mb = mk("mmb", [128, 2])
    last2 = mk("mlast2", [128, 2])
    dlt, wstate, deca, decay = mk("mdlt", [128, 1]), mk("mwstate", [128, 1]), mk("mdeca", [128, 1]), mk("mdecay", [128, 1])
    kw = mk("mkw", [128, 256], BF16)

    padneg = mk("mpadneg", [128, 80])
    kall = mk("mkall", [128, 48, 256], BF16)
    xnB = mk("mxnB", [128, 2048], BF16)
    xTB = mk("mxTB", [128, 2048], BF16)
    ssB, sqB, rstdB = mk("mssB", [128, 1]), mk("msqB", [128, 1]), mk("mrstdB", [128, 1])
    vall = mk("mvall", [128, 48, 512], BF16)
    gall = mk("mgall", [128, 48, 2])
    pv = {nm: mk("mpv_" + nm, [128, 48]) for nm in ("thi", "li", "thf", "ez", "sp", "lf", "bloc", "btot", "off", "bg", "wv", "wst")}
    pM, pMall, pMp, pnegMp, pBend = mk("mpM", [128, 1]), mk("mpMall", [128, 1]), mk("mpMp", [128, 1]), mk("mpnegMp", [128, 1]), mk("mpBend", [128, 1])
    npm = mk("mnpm", [128, 80])
    return dict(xnB=xnB, xTB=xTB, ssB=ssB, sqB=sqB, rstdB=rstdB, kall=kall, vall=vall, gall=gall, pv=pv, pM=pM, pMall=pMall, pMp=pMp, pnegMp=pnegMp, pBend=pBend, W=W, nw=nw, gb=gb, gb15=gb15, hnw=hnw, tri=tri, sel=sel, cmask=cmask, ones=ones, one1=one1, onesb=onesb, C=C, Cb=Cb, n_=n_, nb=nb, mprev=mprev, xs=xs, ss=ss, sq=sq, rstd=rstd, xn=xn, xT=xT, qkb=qkb, qkT=qkT, vb=vb, sog=sog, gs=gs, th=th, li=li, z=z, ez=ez, sp=sp, lf=lf, bcs=bcs, gvec=gvec, mrow=mrow, u=u, negu=negu, mt=mt, dg=dg, A=A, wintra=wintra, wia=wia, winter=winter, Pb=Pb, PT=PT, dintra=dintra, tmp=tmp, num=num, den=den, aden=aden, emt=emt, dmax=dmax, rden=rden, ssn=ssn, t1=t1, sqv=sqv, rstd2=rstd2, sc=sc, ots=ots, mb=mb, last2=last2, dlt=dlt, wstate=wstate, deca=deca, decay=decay, kw=kw, padneg=padneg, npm=npm)


def phase_mlstm(k, cm, m, xw_d, w4_d, nw_d, gb4_d, hnw_d, tri_d, sel_d, cmask_d, padneg_d, npm_d, nchunks, out_from,
                hg_d, hg_bufs, bg=None):
    g = globals()
    loc = dict(m)
    W = m["W"]
    nw = m["nw"]
    gb = m["gb"]
    gb15 = m["gb15"]
    hnw = m["hnw"]
    tri = m["tri"]
    sel = m["sel"]
    cmask = m["cmask"]
    ones = m["ones"]
    one1 = m["one1"]
    onesb = m["onesb"]
    C = m["C"]
    Cb = m["Cb"]
    n_ = m["n_"]
    nb = m["nb"]
    mprev = m["mprev"]
    xs = m["xs"]
    ss = m["ss"]
    sq = m["sq"]
    rstd = m["rstd"]
    xn = m["xn"]
    xT = m["xT"]
    qkb = m["qkb"]
    qkT = m["qkT"]
    vb = m["vb"]
    sog = m["sog"]
    gs = m["gs"]
    th = m["th"]
    li = m["li"]
    z = m["z"]
    ez = m["ez"]
    sp = m["sp"]
    lf = m["lf"]
    bcs = m["bcs"]
    gvec = m["gvec"]
    mrow = m["mrow"]
    u = m["u"]
    negu = m["negu"]
    mt = m["mt"]
    dg = m["dg"]
    A = m["A"]
    wintra = m["wintra"]
    wia = m["wia"]
    winter = m["winter"]
    Pb = m["Pb"]
    PT = m["PT"]
    dintra = m["dintra"]
    tmp = m["tmp"]
    num = m["num"]
    den = m["den"]
    aden = m["aden"]
    emt = m["emt"]
    dmax = m["dmax"]
    rden = m["rden"]
    ssn = m["ssn"]
    t1 = m["t1"]
    sqv = m["sqv"]
    rstd2 = m["rstd2"]
    sc = m["sc"]
    ots = m["ots"]
    mb = m["mb"]
    last2 = m["last2"]
    dlt = m["dlt"]
    wstate = m["wstate"]
    deca = m["deca"]
    decay = m["decay"]
    kw = m["kw"]
    padneg = m["padneg"]
    npm = m["npm"]

    dl = lambda t, src: k.dma("sp", lambda e: e.dma_start(out=t.t[:], in_=src), writes=[t.b])
    dl(nw, nw_d[:, :])
    dl(tri, tri_d[:, :])
    dl(sel, sel_d[:, :])
    dl(cmask, cmask_d[:, :])
    k.dma("sp", lambda e: e.dma_start(out=padneg.t[:, :nchunks], in_=padneg_d[:, :]), writes=[padneg.b])
    k.dma("sp", lambda e: e.dma_start(out=npm.t[:, :nchunks], in_=npm_d[:, :]), writes=[npm.b])
    for t_, val in ((ones, 1.0), (one1, 1.0), (onesb, 1.0)):
        k.op("pool", lambda e, t_=t_, val=val: e.memset(t_.t[:], val), writes=[t_.b])
    eps_ap = cm_eps(k)
    P0, P1, P2, P3 = cm.ps_o
    X0, X1 = cm.ps_x
    for hd in range(4):
        dl(gb, gb4_d[hd, :, :])
        k.dma("sp", lambda e: e.dma_start(out=hnw.t[:], in_=hnw_d[hd * 512:(hd + 1) * 512].partition_broadcast(128)),
              writes=[hnw.b])
        k.op("dve", lambda e: e.tensor_scalar(out=gb15.t[:], in0=gb.t[:], scalar1=1.0 / 15.0, scalar2=None, op0=ALU.mult),
             reads=[gb.b], writes=[gb15.b])
        for t_, val in ((C, 0.0), (Cb, 0.0), (n_, 0.0), (nb, 0.0), (mprev, 0.0)):
            k.op("pool", lambda e, t_=t_, val=val: e.memset(t_.t[:], val), writes=[t_.b])
        load_w(k, cm, W, w4_d[hd], 1538, 16, scale=nw)
        NP = out_from
        kall, vall, gall, pv = m["kall"], m["vall"], m["gall"], m["pv"]
        pM, pMall, pMp, pnegMp, pBend = m["pM"], m["pMall"], m["pMp"], m["pnegMp"], m["pBend"]
        def p1_bufs(c):
            return (xn, xT, ss, sq, rstd) if c % 2 == 0 else (m["xnB"], m["xTB"], m["ssB"], m["sqB"], m["rstdB"])

        def p1_norm(c):
            r0 = c * 128
            xt = xs[c % 2]
            xn_, xT_, ss_, sq_, rstd_ = p1_bufs(c)
            k.dma("sp", lambda e: e.dma_start(out=xt.t[:], in_=xw_d[r0:r0 + 128, :]), writes=[xt.b])
            if bg is not None:
                bg()
            rms_rstd(k, xt, xn_, ss_, sq_, rstd_)
            k.op("act", lambda e: e.activation(out=xn_.t[:], in_=xt.t[:], func=AF.Copy, scale=rstd_.t[:, 0:1]),
                 reads=[xt.b, rstd_.b], writes=[xn_.b])

        def p1_tr(c):
            xn_, xT_ = p1_bufs(c)[0:2]
            transpose_tile(k, cm, xn_, xT_)

        def p1_mm(c):
            xT_ = p1_bufs(c)[1]
            pk = P0 if c % 2 == 0 else P2
            pvv = P1 if c % 2 == 0 else P3
            for kc in range(16):
                k.op("pe", lambda e, kc=kc: e.matmul(out=pk.t[:, 0:256], lhsT=xT_.t[:, kc * 128:(kc + 1) * 128],
                                                     rhs=W.t[:, kc, 256:512], start=(kc == 0), stop=(kc == 15)),
                     reads=[xT_.b, W.b], writes=[pk.b], tok=(kc == 15))
            mm_group(k, pvv, xT_.t, xT_.b, W, 512, 512)
            mm_group(k, X0, xT_.t, xT_.b, W, 1536, 2)

        def p1_evac(c):
            pk = P0 if c % 2 == 0 else P2
            pvv = P1 if c % 2 == 0 else P3
            k.op("dve", lambda e: e.tensor_copy(out=kall.t[:, c, :], in_=pk.t[:, 0:256]), reads=[pk.b], writes=[kall.b])
            k.op("act", lambda e: e.copy(out=vall.t[:, c, :], in_=pvv.t[:]), reads=[pvv.b], writes=[vall.b])
            k.op("dve", lambda e: e.tensor_copy(out=gall.t[:, c, :], in_=X0.t[:, 0:2]), reads=[X0.b], writes=[gall.b])

        if NP > 0:
            p1_norm(0)
            p1_tr(0)
            for c in range(NP):
                if c + 1 < NP:
                    p1_norm(c + 1)
                p1_mm(c)
                if c + 1 < NP:
                    p1_tr(c + 1)
                p1_evac(c)
        if NP > 0:
            o1 = lambda eng, fn, r, w: k.op(eng, fn, reads=[t_.b for t_ in r], writes=[t_.b for t_ in w])
            def _sl(t_):
                v_ = V(t_.t[:, 0:NP], None)
                v_.b = t_.b
                return v_
            thi, liA, thf, ezA, spA, lfA = [_sl(pv[n]) for n in ("thi", "li", "thf", "ez", "sp", "lf")]
            bloc, btot, off, bgl, wv, wst = [_sl(pv[n]) for n in ("bloc", "btot", "off", "bg", "wv", "wst")]
            gall = V(m["gall"].t[:, 0:NP, :], None)
            gall.b = m["gall"].b
            kall = V(m["kall"].t[:, 0:NP, :], None)
            kall.b = m["kall"].b
            o1("act", lambda e: e.activation(out=thi.t[:], in_=gall.t[:, :, 0], func=AF.Tanh, scale=1.0 / 15.0, bias=gb15.t[:, 0:1]),
               [gall, gb15], [thi])
            o1("act", lambda e: e.activation(out=thf.t[:], in_=gall.t[:, :, 1], func=AF.Tanh, scale=1.0 / 15.0, bias=gb15.t[:, 1:2]),
               [gall, gb15], [thf])
            o1("dve", lambda e: e.tensor_scalar(out=liA.t[:], in0=thi.t[:], scalar1=15.0, scalar2=None, op0=ALU.mult), [thi], [liA])
            o1("dve", lambda e: e.tensor_tensor(out=liA.t[:], in0=liA.t[:], in1=padneg.t[:, 0:NP], op=ALU.add), [padneg], [liA])
            o1("act", lambda e: e.activation(out=ezA.t[:], in_=thf.t[:], func=AF.Exp, scale=-15.0), [thf], [ezA])
            o1("act", lambda e: e.activation(out=spA.t[:], in_=ezA.t[:], func=AF.Ln, bias=one1.t[:, 0:1]), [ezA, one1], [spA])
            o1("dve", lambda e: e.tensor_tensor(out=lfA.t[:], in0=spA.t[:], in1=npm.t[:, 0:NP], op=ALU.mult), [spA, npm], [lfA])
            k.op("pe", lambda e: e.matmul(out=X1.t[:, 0:NP], lhsT=tri.t[:], rhs=lfA.t[:], start=True, stop=True),
                 reads=[tri.b, lfA.b], writes=[X1.b])
            k.op("dve", lambda e: e.tensor_copy(out=bloc.t[:], in_=X1.t[:, 0:NP]), reads=[X1.b], writes=[bloc.b])
            k.op("pe", lambda e: e.matmul(out=X1.t[:, 64:64 + NP], lhsT=sel.t[:], rhs=bloc.t[:], start=True, stop=True),
                 reads=[sel.b, bloc.b], writes=[X1.b])
            k.op("dve", lambda e: e.tensor_copy(out=btot.t[:], in_=X1.t[:, 64:64 + NP]), reads=[X1.b], writes=[btot.b])
            k.op("pool", lambda e: e.memset(off.t[:], 0.0), writes=[off.b])
            for c in range(1, NP):
                k.op("dve", lambda e, c=c: e.tensor_tensor(out=off.t[:, c:c + 1], in0=off.t[:, c - 1:c], in1=btot.t[:, c - 1:c], op=ALU.add),
                     reads=[btot.b], writes=[off.b])
            o1("dve", lambda e: e.tensor_tensor(out=bgl.t[:], in0=off.t[:], in1=bloc.t[:], op=ALU.add), [off, bloc], [bgl])
            o1("dve", lambda e: e.tensor_tensor(out=wv.t[:], in0=liA.t[:], in1=bgl.t[:], op=ALU.subtract), [liA, bgl], [wv])
            o1("dve", lambda e: e.tensor_reduce(out=pM.t[:], in_=wv.t[:], axis=AX.X, op=ALU.max), [wv], [pM])
            o1("dve", lambda e: e.tensor_scalar(out=dg.t[:], in0=cm.identf.t[:], scalar1=pM.t[:, 0:1], scalar2=None, op0=ALU.mult),
               [cm.identf, pM], [dg])
            k.op("pe", lambda e: e.matmul(out=X1.t[:, 128:256], lhsT=ones.t[:], rhs=dg.t[:], start=True, stop=True),
                 reads=[ones.b, dg.b], writes=[X1.b])
            k.op("dve", lambda e: e.tensor_reduce(out=pMp.t[:], in_=X1.t[:, 128:256], axis=AX.X, op=ALU.max), reads=[X1.b], writes=[pMp.b])
            o1("dve", lambda e: e.tensor_scalar(out=pMp.t[:], in0=pMp.t[:], scalar1=0.0, scalar2=None, op0=ALU.max), [], [pMp])
            o1("dve", lambda e: e.tensor_scalar(out=pnegMp.t[:], in0=pMp.t[:], scalar1=-1.0, scalar2=None, op0=ALU.mult), [pMp], [pnegMp])
            o1("dve", lambda e: e.tensor_tensor(out=pBend.t[:], in0=off.t[:, NP - 1:NP], in1=btot.t[:, NP - 1:NP], op=ALU.add),
               [off, btot], [pBend])
            o1("dve", lambda e: e.tensor_tensor(out=mprev.t[:], in0=pBend.t[:], in1=pMp.t[:], op=ALU.add), [pBend, pMp], [mprev])
            o1("act", lambda e: e.activation(out=wst.t[:], in_=wv.t[:], func=AF.Exp, bias=pnegMp.t[:, 0:1]), [wv, pnegMp], [wst])
            o1("dve", lambda e: e.tensor_tensor(out=kall.t[:], in0=kall.t[:], in1=wst.t[:].unsqueeze(2).to_broadcast([128, NP, 256]),
                                                op=ALU.mult), [wst], [kall])
            for dc, pc in ((0, P2), (1, P3)):
                for c in range(NP):
                    k.op("pe", lambda e, c=c, dc=dc, pc=pc: e.matmul(out=pc.t[:], lhsT=kall.t[:, c, dc * 128:(dc + 1) * 128],
                                                                     rhs=vall.t[:, c, :], start=(c == 0), stop=(c == NP - 1)),
                         reads=[kall.b, vall.b], writes=[pc.b], tok=(c == NP - 1))
            for dc in range(2):
                for c in range(NP):
                    k.op("pe", lambda e, c=c, dc=dc: e.matmul(out=X0.t[:, 16 + dc:17 + dc], lhsT=kall.t[:, c, dc * 128:(dc + 1) * 128],
                                                              rhs=onesb.t[:], start=(c == 0), stop=(c == NP - 1)),
                         reads=[kall.b, onesb.b], writes=[X0.b], tok=(c == NP - 1))
            k.op("dve", lambda e: e.tensor_copy(out=C.t[:, 0, :], in_=P2.t[:]), reads=[P2.b], writes=[C.b])
            k.op("dve", lambda e: e.tensor_copy(out=C.t[:, 1, :], in_=P3.t[:]), reads=[P3.b], writes=[C.b])
            k.op("act", lambda e: e.copy(out=Cb.t[:], in_=C.t[:]), reads=[C.b], writes=[Cb.b])
            k.op("dve", lambda e: e.tensor_copy(out=n_.t[:], in_=X0.t[:, 16:18]), reads=[X0.b], writes=[n_.b])
            k.op("dve", lambda e: e.tensor_copy(out=nb.t[:], in_=n_.t[:]), reads=[n_.b], writes=[nb.b])
        for c in range(NP, nchunks):
            r0 = c * 128
            xt = xs[c % 2]
            ot = ots[c % 2]
            k.dma("sp", lambda e: e.dma_start(out=xt.t[:], in_=xw_d[r0:r0 + 128, :]), writes=[xt.b])
            if bg is not None:
                bg(6)
            rms_rstd(k, xt, xn, ss, sq, rstd)
            k.op("act", lambda e: e.activation(out=xn.t[:], in_=xt.t[:], func=AF.Copy, scale=rstd.t[:, 0:1]),
                 reads=[xt.b, rstd.b], writes=[xn.b])
            transpose_tile(k, cm, xn, xT)
            full = c >= out_from
            if full:
                mm_group(k, P0, xT.t, xT.b, W, 0, 512)
            else:
                for kc in range(16):
                    k.op("pe", lambda e, kc=kc: e.matmul(out=P0.t[:, 256:512], lhsT=xT.t[:, kc * 128:(kc + 1) * 128],
                                                         rhs=W.t[:, kc, 256:512], start=(kc == 0), stop=(kc == 15)),
                         reads=[xT.b, W.b], writes=[P0.b], tok=(kc == 15))
            mm_group(k, P1, xT.t, xT.b, W, 512, 512)
            if full:
                mm_group(k, P2, xT.t, xT.b, W, 1024, 512)
            mm_group(k, X0, xT.t, xT.b, W, 1536, 2)
            if full:
                k.op("act", lambda e: e.activation(out=qkb.t[:, 0:256], in_=P0.t[:, 0:256], func=AF.Copy, scale=1.0 / 16.0),
                     reads=[P0.b], writes=[qkb.b])
            k.op("dve", lambda e: e.tensor_copy(out=qkb.t[:, 256:512], in_=P0.t[:, 256:512]), reads=[P0.b], writes=[qkb.b])
            k.op("dve", lambda e: e.tensor_copy(out=vb.t[:], in_=P1.t[:]), reads=[P1.b], writes=[vb.b])
            if full:
                k.op("act", lambda e: e.activation(out=sog.t[:], in_=P2.t[:], func=AF.Sigmoid), reads=[P2.b], writes=[sog.b])
                k.op("pool", lambda e: e.tensor_tensor(out=sog.t[:], in0=sog.t[:], in1=hnw.t[:], op=ALU.mult),
                     reads=[hnw.b], writes=[sog.b])
            k.op("dve", lambda e: e.tensor_copy(out=gs.t[:], in_=X0.t[:, 0:2]), reads=[X0.b], writes=[gs.b])
            if full:
                transpose_tile(k, cm, qkb, qkT, nchunks=4)
            for col in range(2):
                k.op("act", lambda e, col=col: e.activation(out=th.t[:, col:col + 1], in_=gs.t[:, col:col + 1], func=AF.Tanh,
                                                            scale=1.0 / 15.0, bias=gb15.t[:, col:col + 1]),
                     reads=[gs.b, gb15.b], writes=[th.b])
            k.op("dve", lambda e: e.tensor_scalar(out=li.t[:], in0=th.t[:, 0:1], scalar1=15.0, scalar2=None, op0=ALU.mult),
                 reads=[th.b], writes=[li.b])
            k.op("dve", lambda e: e.tensor_tensor(out=li.t[:], in0=li.t[:], in1=padneg.t[:, c:c + 1], op=ALU.add),
                 reads=[padneg.b], writes=[li.b])
            k.op("act", lambda e: e.activation(out=ez.t[:], in_=th.t[:, 1:2], func=AF.Exp, scale=-15.0), reads=[th.b], writes=[ez.b])
            k.op("act", lambda e: e.activation(out=sp.t[:], in_=ez.t[:], func=AF.Ln, bias=one1.t[:, 0:1]),
                 reads=[ez.b, one1.b], writes=[sp.b])
            k.op("dve", lambda e: e.tensor_tensor(out=lf.t[:], in0=sp.t[:], in1=npm.t[:, c:c + 1], op=ALU.mult),
                 reads=[sp.b, npm.b], writes=[lf.b])
            k.op("pe", lambda e: e.matmul(out=X0.t[:, 4:5], lhsT=tri.t[:], rhs=lf.t[:], start=True, stop=True),
                 reads=[tri.b, lf.b], writes=[X0.b])
            k.op("dve", lambda e: e.tensor_copy(out=bcs.t[:], in_=X0.t[:, 4:5]), reads=[X0.b], writes=[bcs.b])
            k.op("dve", lambda e: e.tensor_tensor(out=gvec.t[:], in0=li.t[:], in1=bcs.t[:], op=ALU.subtract),
                 reads=[li.b, bcs.b], writes=[gvec.b])
            k.op("dve", lambda e: e.tensor_scalar(out=dg.t[:], in0=cm.identf.t[:], scalar1=gvec.t[:, 0:1], scalar2=None, op0=ALU.mult),
                 reads=[cm.identf.b, gvec.b], writes=[dg.b])
            k.op("pe", lambda e: e.matmul(out=X1.t[:, 0:128], lhsT=ones.t[:], rhs=dg.t[:], start=True, stop=True),
                 reads=[ones.b, dg.b], writes=[X1.b])
            k.op("dve", lambda e: e.tensor_tensor(out=A.t[:], in0=X1.t[:, 0:128], in1=cmask.t[:], op=ALU.add),
                 reads=[X1.b, cmask.b], writes=[A.b])
            k.op("dve", lambda e: e.tensor_reduce(out=mrow.t[:], in_=A.t[:], axis=AX.X, op=ALU.max), reads=[A.b], writes=[mrow.b])
            k.op("dve", lambda e: e.tensor_tensor(out=u.t[:], in0=mrow.t[:], in1=mprev.t[:], op=ALU.max),
                 reads=[mrow.b, mprev.b], writes=[u.b])
            k.op("dve", lambda e: e.tensor_scalar(out=negu.t[:], in0=u.t[:], scalar1=-1.0, scalar2=None, op0=ALU.mult),
                 reads=[u.b], writes=[negu.b])
            k.op("dve", lambda e: e.tensor_tensor(out=mt.t[:], in0=bcs.t[:], in1=u.t[:], op=ALU.add), reads=[bcs.b, u.b], writes=[mt.b])
            if full:
                k.op("act", lambda e: e.activation(out=wintra.t[:], in_=A.t[:], func=AF.Exp, bias=negu.t[:, 0:1]),
                     reads=[A.b, negu.b], writes=[wintra.b])
                k.op("dve", lambda e: e.tensor_tensor(out=wia.t[:], in0=mprev.t[:], in1=u.t[:], op=ALU.subtract),
                     reads=[mprev.b, u.b], writes=[wia.b])
                k.op("act", lambda e: e.activation(out=winter.t[:], in_=wia.t[:], func=AF.Exp), reads=[wia.b], writes=[winter.b])
                for dc in range(2):
                    k.op("pe", lambda e, dc=dc: e.matmul(out=X1.t[:, 128:256], lhsT=qkT.t[:, dc * 128:(dc + 1) * 128],
                                                         rhs=qkT.t[:, (2 + dc) * 128:(3 + dc) * 128], start=(dc == 0), stop=(dc == 1)),
                         reads=[qkT.b], writes=[X1.b], tok=(dc == 1))
                k.op("dve", lambda e: e.scalar_tensor_tensor(out=Pb.t[:], in0=X1.t[:, 128:256], scalar=1.0, in1=wintra.t[:],
                                                             op0=ALU.mult, op1=ALU.mult, accum_out=dintra.t[:, 0:1]),
                     reads=[X1.b, wintra.b], writes=[Pb.b, dintra.b])
                pst = cm.ps_tr[0]
                k.op("pe", lambda e: e.transpose(out=pst.t[:, 0:128], in_=Pb.t[:], identity=cm.ident.t[:]),
                     reads=[Pb.b, cm.ident.b], writes=[pst.b])
                k.op("dve", lambda e: e.tensor_copy(out=PT.t[:], in_=pst.t[:, 0:128]), reads=[pst.b], writes=[PT.b])
                for dc in range(2):
                    k.op("pe", lambda e, dc=dc: e.matmul(out=P0.t[:], lhsT=qkT.t[:, dc * 128:(dc + 1) * 128], rhs=Cb.t[:, dc, :],
                                                         start=(dc == 0), stop=(dc == 1)),
                         reads=[qkT.b, Cb.b], writes=[P0.b], tok=(dc == 1))
                for dc in range(2):
                    k.op("pe", lambda e, dc=dc: e.matmul(out=X0.t[:, 12:13], lhsT=qkT.t[:, dc * 128:(dc + 1) * 128], rhs=nb.t[:, dc:dc + 1],
                                                         start=(dc == 0), stop=(dc == 1)),
                         reads=[qkT.b, nb.b], writes=[X0.b], tok=(dc == 1))
                k.op("pe", lambda e: e.matmul(out=P1.t[:], lhsT=PT.t[:], rhs=vb.t[:], start=True, stop=True),
                     reads=[PT.b, vb.b], writes=[P1.b])
                k.op("act", lambda e: e.activation(out=tmp.t[:], in_=P0.t[:], func=AF.Copy, scale=winter.t[:, 0:1]),
                     reads=[P0.b, winter.b], writes=[tmp.b])
                k.op("dve", lambda e: e.tensor_tensor(out=num.t[:], in0=P1.t[:], in1=tmp.t[:], op=ALU.add),
                     reads=[P1.b, tmp.b], writes=[num.b])
                k.op("dve", lambda e: e.scalar_tensor_tensor(out=den.t[:], in0=X0.t[:, 12:13], scalar=winter.t[:, 0:1], in1=dintra.t[:],
                                                             op0=ALU.mult, op1=ALU.add),
                     reads=[X0.b, winter.b, dintra.b], writes=[den.b])
                k.op("dve", lambda e: e.tensor_scalar(out=aden.t[:], in0=den.t[:], scalar1=-1.0, scalar2=None, op0=ALU.mult),
                     reads=[den.b], writes=[aden.b])
                k.op("dve", lambda e: e.tensor_tensor(out=aden.t[:], in0=aden.t[:], in1=den.t[:], op=ALU.max),
                     reads=[den.b], writes=[aden.b])
                k.op("act", lambda e: e.activation(out=emt.t[:], in_=mt.t[:], func=AF.Exp, scale=-1.0), reads=[mt.b], writes=[emt.b])
                k.op("dve", lambda e: e.tensor_tensor(out=dmax.t[:], in0=aden.t[:], in1=emt.t[:], op=ALU.max),
                     reads=[aden.b, emt.b], writes=[dmax.b])
                k.op("dve", lambda e: e.reciprocal(out=rden.t[:], in_=dmax.t[:]), reads=[dmax.b], writes=[rden.b])
                k.op("act", lambda e: e.activation(out=tmp.t[:], in_=num.t[:], func=AF.Square, accum_out=ssn.t[:, 0:1]),
                     reads=[num.b], writes=[tmp.b, ssn.b])
                k.op("dve", lambda e: e.tensor_tensor(out=t1.t[:], in0=ssn.t[:], in1=rden.t[:], op=ALU.mult), reads=[ssn.b, rden.b], writes=[t1.b])
                k.op("dve", lambda e: e.tensor_tensor(out=t1.t[:], in0=t1.t[:], in1=rden.t[:], op=ALU.mult), reads=[rden.b], writes=[t1.b])
                k.op("act", lambda e: e.activation(out=sqv.t[:], in_=t1.t[:], func=AF.Sqrt, scale=1.0 / 512.0, bias=eps_ap),
                     reads=[t1.b], writes=[sqv.b])
                k.op("dve", lambda e: e.reciprocal(out=rstd2.t[:], in_=sqv.t[:]), reads=[sqv.b], writes=[rstd2.b])
                k.op("dve", lambda e: e.tensor_tensor(out=sc.t[:], in0=rden.t[:], in1=rstd2.t[:], op=ALU.mult), reads=[rden.b, rstd2.b], writes=[sc.b])
                k.op("dve", lambda e: e.scalar_tensor_tensor(out=ot.t[:], in0=num.t[:], scalar=sc.t[:, 0:1], in1=sog.t[:],
                                                             op0=ALU.mult, op1=ALU.mult),
                     reads=[num.b, sc.b, sog.b], writes=[ot.b])
                ti = c - out_from
                k.dma("pool", lambda e: e.dma_start(out=hg_d[ti * 128:(ti + 1) * 128, hd * 512:(hd + 1) * 512], in_=ot.t[:]),
                      reads=[ot.b], writes=[hg_bufs[ti]])
            k.op("dve", lambda e: e.tensor_copy(out=mb.t[:, 0:1], in_=mt.t[:]), reads=[mt.b], writes=[mb.b])
            k.op("dve", lambda e: e.tensor_copy(out=mb.t[:, 1:2], in_=bcs.t[:]), reads=[bcs.b], writes=[mb.b])
            k.op("pe", lambda e: e.matmul(out=X0.t[:, 8:10], lhsT=sel.t[:], rhs=mb.t[:], start=True, stop=True),
                 reads=[sel.b, mb.b], writes=[X0.b])
            k.op("dve", lambda e: e.tensor_copy(out=last2.t[:], in_=X0.t[:, 8:10]), reads=[X0.b], writes=[last2.b])
            k.op("dve", lambda e: e.tensor_tensor(out=dlt.t[:], in0=last2.t[:, 1:2], in1=last2.t[:, 0:1], op=ALU.subtract),
                 reads=[last2.b], writes=[dlt.b])
            k.op("act", lambda e: e.activation(out=wstate.t[:], in_=gvec.t[:], func=AF.Exp, bias=dlt.t[:, 0:1]),
                 reads=[gvec.b, dlt.b], writes=[wstate.b])
            k.op("dve", lambda e: e.tensor_tensor(out=deca.t[:], in0=dlt.t[:], in1=mprev.t[:], op=ALU.add),
                 reads=[dlt.b, mprev.b], writes=[deca.b])
            k.op("act", lambda e: e.activation(out=decay.t[:], in_=deca.t[:], func=AF.Exp), reads=[deca.b], writes=[decay.b])
            k.op("dve", lambda e: e.tensor_scalar(out=kw.t[:], in0=qkb.t[:, 256:512], scalar1=wstate.t[:, 0:1], scalar2=None, op0=ALU.mult),
                 reads=[qkb.b, wstate.b], writes=[kw.b])
            k.op("pe", lambda e: e.matmul(out=P2.t[:], lhsT=kw.t[:, 0:128], rhs=vb.t[:], start=True, stop=True),
                 reads=[kw.b, vb.b], writes=[P2.b])
            k.op("pe", lambda e: e.matmul(out=P3.t[:], lhsT=kw.t[:, 128:256], rhs=vb.t[:], start=True, stop=True),
                 reads=[kw.b, vb.b], writes=[P3.b])
            k.op("pe", lambda e: e.matmul(out=X0.t[:, 16:17], lhsT=kw.t[:, 0:128], rhs=onesb.t[:], start=True, stop=True),
                 reads=[kw.b, onesb.b], writes=[X0.b], tok=False)
            k.op("pe", lambda e: e.matmul(out=X0.t[:, 17:18], lhsT=kw.t[:, 128:256], rhs=onesb.t[:], start=True, stop=True),
                 reads=[kw.b, onesb.b], writes=[X0.b])
            k.op("dve", lambda e: e.scalar_tensor_tensor(out=C.t[:, 0, :], in0=C.t[:, 0, :], scalar=decay.t[:, 0:1], in1=P2.t[:],
                                                         op0=ALU.mult, op1=ALU.add), reads=[decay.b, P2.b], writes=[C.b])
            k.op("dve", lambda e: e.scalar_tensor_tensor(out=C.t[:, 1, :], in0=C.t[:, 1, :], scalar=decay.t[:, 0:1], in1=P3.t[:],
                                                         op0=ALU.mult, op1=ALU.add), reads=[decay.b, P3.b], writes=[C.b])
            k.op("act", lambda e: e.copy(out=Cb.t[:], in_=C.t[:]), reads=[C.b], writes=[Cb.b])
            k.op("dve", lambda e: e.scalar_tensor_tensor(out=n_.t[:], in0=n_.t[:], scalar=decay.t[:, 0:1], in1=X0.t[:, 16:18],
                                                         op0=ALU.mult, op1=ALU.add), reads=[decay.b, X0.b], writes=[n_.b])
            k.op("dve", lambda e: e.tensor_copy(out=nb.t[:], in_=n_.t[:]), reads=[n_.b], writes=[nb.b])
            k.op("dve", lambda e: e.tensor_copy(out=mprev.t[:], in_=last2.t[:, 0:1]), reads=[last2.b], writes=[mprev.b])


def mlstm_consts():
    s = np.arange(128)[:, None]
    t = np.arange(128)[None, :]
    tri = (s <= t).astype(np.float32)
    sel = np.zeros((128, 128), np.float32)
    sel[127, :] = 1.0
    cmask = np.where(t <= s, 0.0, -1e30).astype(np.float32)
    return tri, sel, cmask


def mlstm_inmap(x_b, a_norm, a_w_in, gate_bias, head_norm, hd):
    o0, o1, o2, o3 = 1024, 2048, 4096, 6144
    cols = np.concatenate([np.arange(hd * 256, (hd + 1) * 256), o0 + np.arange(hd * 256, (hd + 1) * 256),
                           o1 + np.arange(hd * 512, (hd + 1) * 512), o2 + np.arange(hd * 512, (hd + 1) * 512),
                           np.array([o3 + hd, o3 + 4 + hd])])
    tri, sel, cmask = mlstm_consts()
    return {
        "x_in": np.ascontiguousarray(x_b),
        "w_in": np.ascontiguousarray(a_w_in[:, cols]),
        "nw_in": np.ascontiguousarray(a_norm.reshape(16, 128).T),
        "gb_in": np.ascontiguousarray(np.broadcast_to(gate_bias[:, hd][None, :], (128, 2))).astype(np.float32),
        "hnw_in": np.ascontiguousarray(head_norm[hd * 512:(hd + 1) * 512]),
        "tri_in": tri, "sel_in": sel, "cmask_in": cmask, "ident_in": np.eye(128, dtype=np.float32),
    }


NT0 = 17
NT1 = 16
NW = 65


def build_fused():
    nc = bass.Bass("TRN2", target_bir_lowering=False)
    di = lambda name, shape: nc.dram_tensor(name, list(shape), F32, kind="ExternalInput").ap()
    xw_d = di("xw", [NW * 128, D])
    p0_d = di("p0_sh", [NT0 * 128, 256])
    p1_d = di("p1_sh", [NT1 * 128, 256])
    w4_d = di("mw4", [4, D, 1538])
    mnw_d = di("mnw", [128, 16])
    gb4_d = di("mgb4", [4, 128, 2])
    hnw_d = di("mhnw", [D])
    tri_d = di("tri_in", [128, 128])
    sel_d = di("sel_in", [128, 128])
    cmask_d = di("cmask_in", [128, 128])
    padneg_d = di("padneg", [128, NW])
    npm_d = di("npm", [128, NW])
    wout0_d = di("wout0", [D, D])
    wout1_d = di("wout1", [D, D])
    peer = []
    for L in range(2):
        peer.append(dict(cw=di("cw%d" % L, [D]), wq=di("pwq%d" % L, [D, D]), k1=di("k1T%d" % L, [8, 128, 128]),
                         k2=di("k2T%d" % L, [8, 128, 128]), u=di("u%d" % L, [16384, D]), v=di("v%d" % L, [16384, D])))
    ple = []
    for L in range(2):
        ple.append(dict(nw=di("enw%d" % L, [128, 16]), wg=di("ewg%d" % L, [D, D]), wp=di("ewp%d" % L, [256, D])))
    fw_d = di("fw", [D])
    nq_d = di("nq", [128, 16])
    nkv_d = di("nkv", [128, 16])
    bwq_d = di("bwq", [D, D])
    wkv_d = di("wkv", [D, 512])
    sinks_d = di("sinks", [32])
    mask_d = di("mask", [128, 256])
    mask1_d = di("mask1", [128, 256])
    id_d = di("ident_in", [128, 128])
    pcc_d = di("pcc_in", [128, 32])
    out_d = nc.dram_tensor("out", [NT1 * 128, D], F32, kind="ExternalOutput").ap()
    hg_d = nc.dram_tensor("hg_scr", [NT0 * 128, D], F32, kind="Internal").ap()
    hs_d = nc.dram_tensor("hs_scr", [NT0 * 128, D], F32, kind="Internal").ap()
    att_d = nc.dram_tensor("att_scr", [NT1 * 128, D], F32, kind="Internal").ap()
    x0 = (NW - NT0) * 128
    tbl = []
    for L in range(2):
        for nm in ("u", "v"):
            tb = nc.dram_tensor("tb_%s%d" % (nm, L), [16384, D], BF16, kind="Internal").ap()
            tbl.append((peer[L][nm], tb))
            peer[L][nm + "b"] = tb
    with ExitStack() as stack:
        k = K(nc, stack)
        cm = Common(k, id_d)
        cm.pcc_d = pcc_d
        cm_eps(k)
        hg_bufs = [Buf("hg%d" % i) for i in range(NT0)]
        hs_bufs = [Buf("hs%d" % i) for i in range(NT0)]
        att_bufs = [Buf("at%d" % i) for i in range(NT1)]

        def phase(alloc, fn):
            with ExitStack() as ps:
                k.stack = ps
                b = alloc(k)
                fn(b)
                k.barrier()
            k.stack = stack

        conv = [(src_, dst_, i) for (src_, dst_) in tbl for i in range(128)]
        conv.reverse()

        def run_mlstm(m):
            stg = [T(k, "cstg%d" % i, [128, 2048], BF16) for i in range(2)]
            cnt = [0]

            pend = []

            def flush_store():
                if pend:
                    s_, dst_, i_ = pend.pop()
                    k.dma("pool", lambda e: e.dma_start(out=dst_[i_ * 128:(i_ + 1) * 128, :], in_=s_.t[:]), reads=[s_.b])

            def bg(n=1):
                for _ in range(n):
                    if not conv:
                        flush_store()
                        return
                    src_, dst_, i = conv.pop()
                    s = stg[cnt[0] % 2]
                    cnt[0] += 1
                    k.dma("pool", lambda e: e.dma_start(out=s.t[:], in_=src_[i * 128:(i + 1) * 128, :]), writes=[s.b])
                    flush_store()
                    pend.append((s, dst_, i))
            phase_mlstm(k, cm, m, xw_d, w4_d, mnw_d, gb4_d, hnw_d, tri_d, sel_d, cmask_d,
                        padneg_d, npm_d, NW, NW - NT0, hg_d, hg_bufs, bg=bg)
            while conv:
                bg()
            flush_store()
        phase(mlstm_alloc, run_mlstm)
        phase(mmres_alloc, lambda b: phase_mmres(k, cm, b, wout0_d, NT0, hg_d, 0, hg_bufs, xw_d, x0, None, hs_d, 0, hs_bufs))
        phase(peer_alloc, lambda b: phase_peer(k, cm, b, hs_d, hs_bufs, list(range(NT0)), peer[0]["cw"], peer[0]["wq"],
                                               peer[0]["k1"], peer[0]["k2"], peer[0]["ub"], peer[0]["vb"]))
        phase(ple_alloc, lambda b: phase_ple(k, cm, b, hs_d, hs_bufs, list(range(NT0)), p0_d, 0, ple[0]["nw"], ple[0]["wg"],
                                             ple[0]["wp"]))
        phase(att_alloc, lambda b: phase_att(k, cm, b, hs_d, hs_bufs, NT0, att_d, att_bufs, nq_d, nkv_d, bwq_d, wkv_d,
                                             sinks_d, mask_d, mask1_d))
        phase(mmres_alloc, lambda b: phase_mmres(k, cm, b, wout1_d, NT1, att_d, 0, att_bufs, hs_d, 128, hs_bufs[1:],
                                                 hs_d, 128, hs_bufs[1:]))
        phase(peer_alloc, lambda b: phase_peer(k, cm, b, hs_d, hs_bufs, list(range(1, NT0)), peer[1]["cw"], peer[1]["wq"],
                                               peer[1]["k1"], peer[1]["k2"], peer[1]["ub"], peer[1]["vb"]))
        phase(ple_alloc, lambda b: phase_ple(k, cm, b, hs_d, hs_bufs, list(range(1, NT0)), p1_d, -128, ple[1]["nw"],
                                             ple[1]["wg"], ple[1]["wp"], final=(fw_d, out_d, 0)))
        drain(k)
        print("fused ninst", k.ninst, "nsem", k.nsem)
    return nc


def _r16(v):
    return np.ascontiguousarray(np.asarray(v, np.float32).reshape(16, 128).T)


def kernel(x, p, a_norm, a_w_in, a_gate_bias, a_head_norm, a_w_out, kv_norm, w_kv,
           b_norm, b_w_q, b_sinks, b_w_out, c_norm, peer_w_q, peer_k1, peer_k2,
           peer_u, peer_v, ple_norm, ple_w_gate, ple_w_proj, final_norm):
    f = lambda a: np.ascontiguousarray(np.asarray(a, dtype=np.float32))
    x = f(x)
    p = f(p)
    B, S, _ = x.shape
    nc = build_fused()
    m, m1 = band_masks()
    tri, sel, cmask = mlstm_consts()
    a_w_in0 = f(a_w_in)[0]
    gbias = f(a_gate_bias)[0]
    o0, o1, o2, o3 = 1024, 2048, 4096, 6144
    w4 = np.zeros((4, D, 1538), np.float32)
    gb4 = np.zeros((4, 128, 2), np.float32)
    for hd in range(4):
        cols = np.concatenate([np.arange(hd * 256, (hd + 1) * 256), o0 + np.arange(hd * 256, (hd + 1) * 256),
                               o1 + np.arange(hd * 512, (hd + 1) * 512), o2 + np.arange(hd * 512, (hd + 1) * 512),
                               np.array([o3 + hd, o3 + 4 + hd])])
        w4[hd] = a_w_in0[:, cols]
        gb4[hd] = np.broadcast_to(gbias[:, hd][None, :], (128, 2))
    shared = {
        "mw4": w4, "mnw": _r16(f(a_norm)[0]), "mgb4": gb4, "mhnw": f(a_head_norm)[0],
        "tri_in": tri, "sel_in": sel, "cmask_in": cmask,
        "wout0": f(a_w_out)[0], "wout1": f(b_w_out)[0], "fw": f(final_norm),
        "nq": _r16(f(b_norm)[0]), "nkv": _r16(kv_norm), "bwq": f(b_w_q)[0], "wkv": f(w_kv), "sinks": f(b_sinks)[0],
        "mask": m, "ident_in": np.eye(128, dtype=np.float32), "pcc_in": peer_consts(),
    }
    for L in range(2):
        shared["cw%d" % L] = f(c_norm)[L]
        shared["pwq%d" % L] = f(peer_w_q)[L]
        shared["k1T%d" % L] = np.ascontiguousarray(f(peer_k1)[L].transpose(0, 2, 1))
        shared["k2T%d" % L] = np.ascontiguousarray(f(peer_k2)[L].transpose(0, 2, 1))
        shared["u%d" % L] = f(peer_u)[L]
        shared["v%d" % L] = f(peer_v)[L]
        shared["enw%d" % L] = _r16(f(ple_norm)[L])
        shared["ewg%d" % L] = f(ple_w_gate)[L]
        shared["ewp%d" % L] = f(ple_w_proj)[L]
    TPC = S // 4
    WT = NW * 128
    maps = []
    for c in range(NCORES):
        bb, qd = c // 4, c % 4
        e0 = (qd + 1) * TPC
        npad = max(0, WT - e0)

        def window(arr, width, ntok):
            o = np.zeros((ntok, width), np.float32)
            lo = e0 - ntok
            if lo < 0:
                o[-lo:] = arr[0:e0]
            else:
                o[:] = arr[lo:e0]
            return o
        mp = dict(shared)
        mp["xw"] = window(x[bb], D, WT)
        mp["p0_sh"] = window(p[0, bb], 256, NT0 * 128)
        mp["p1_sh"] = np.ascontiguousarray(p[1, bb, e0 - TPC:e0])
        padc = npad // 128
        pn = np.zeros((128, NW), np.float32)
        pn[:, :padc] = -1e30
        pm = -np.ones((128, NW), np.float32)
        pm[:, :padc] = 0.0
        mp["padneg"] = pn
        mp["npm"] = pm
        mp["mask1"] = m1 if qd == 0 else m
        maps.append(mp)
    res = run_bass_kernel_spmd(nc, maps, core_ids=list(range(NCORES)))
    out = np.zeros((B, S, D), np.float32)
    for c in range(NCORES):
        bb, qd = c // 4, c % 4
        out[bb, qd * TPC:(qd + 1) * TPC] = res.results[c]["out"]
    return out
```
